# Optimizing a Trainium2 kernel written in Bass

```python
import jax, jax.numpy as jnp
from jax import lax
import numpy as np

D_MODEL = 2048
BATCH = 1
SEQ = 8192
DEPTH = 1
DEC_BATCH = 128
DEC_SEQ = 4
PAST_LEN = 2048
PAGE_SIZE = 128

HG_HEADS = 8
HG_DK = 128
HG_DV = 128
HG_WIDTH = HG_HEADS * HG_DK
HG_VWIDTH = HG_HEADS * HG_DV
HG_CHUNK = 64
NSA_HEADS = 16
NSA_KV_HEADS = 2
NSA_GROUP = NSA_HEADS // NSA_KV_HEADS
HEAD_DIM = 64
NSA_WIDTH = NSA_HEADS * HEAD_DIM
KV_WIDTH = NSA_KV_HEADS * HEAD_DIM
CMP_BLOCK = 32
CMP_STRIDE = 16
CMP_HIDDEN = 256
SLC_BLOCK = 64
SLC_TOP = 16
WINDOW = 512
Q_BLOCK = 128
FORCE_SCORE = 1.0e4
D_FF = 5504
EPS = 1e-6
IN_WIDTHS = (HG_WIDTH, HG_WIDTH, HG_VWIDTH, HG_VWIDTH, NSA_WIDTH,
             KV_WIDTH, KV_WIDTH, KV_WIDTH, KV_WIDTH, KV_WIDTH, KV_WIDTH,
             3 * NSA_HEADS, D_MODEL, D_MODEL)
D_IN = sum(IN_WIDTHS)

kernel_name = 'hybrid_hgrn2_nsa_macaron_step'


def rmsnorm(x, g):
    xf = x.astype(jnp.float32)
    y = xf * lax.rsqrt(jnp.mean(xf * xf, axis=-1, keepdims=True) + EPS)
    return (y * g.astype(jnp.float32)).astype(x.dtype)


def half_ffn(x, n_pre, n_post, w_gate, w_up, w_down):
    h = rmsnorm(x, n_pre)
    y = (jax.nn.silu(h @ w_gate) * (h @ w_up)) @ w_down
    return x + 0.5 * rmsnorm(y, n_post)


def alibi_slopes():
    return jnp.asarray(np.power(2.0, -8.0 * np.arange(1, NSA_HEADS + 1) / NSA_HEADS), dtype=jnp.float32)


def in_project(h, w_in):
    points = np.cumsum(IN_WIDTHS)[:-1].tolist()
    parts = jnp.split(h @ w_in, points, axis=-1)
    hg_in = (parts[0], parts[1], parts[2], parts[3])
    nsa_in = (parts[4], (parts[5], parts[6], parts[7], parts[8], parts[9], parts[10]), parts[11])
    return hg_in, nsa_in, parts[12], parts[13]


def hgrn2_scan(q, logf, k, v, s0):
    B, T, H, _ = q.shape
    C = min(HG_CHUNK, T)
    n_chunks = -(-T // C)
    pad = n_chunks * C - T

    def prep(a):
        a = jnp.pad(a.astype(jnp.float32), ((0, 0), (0, pad), (0, 0), (0, 0)))
        return a.reshape(B, n_chunks, C, H, a.shape[-1]).transpose(1, 0, 2, 3, 4)

    qc, fc, kc, vc = prep(q), prep(logf), prep(k), prep(v)
    causal = jnp.tril(jnp.ones((C, C), dtype=bool))[None, :, :, None, None]

    def step(S, inp):
        qi, fi, ki, vi = inp
        b = jnp.cumsum(fi, axis=1)
        o_inter = jnp.einsum('bchk,bhkv->bchv', qi * jnp.exp(b), S)
        diff = b[:, :, None] - b[:, None, :]
        decay = jnp.exp(jnp.where(causal, diff, -jnp.inf))
        A = jnp.einsum('bthk,bshk,btshk->bhts', qi, ki, decay)
        o_intra = jnp.einsum('bhts,bshv->bthv', A, vi)
        b_last = b[:, -1]
        k_dec = ki * jnp.exp(b_last[:, None] - b)
        S_new = jnp.exp(b_last)[..., None] * S + jnp.einsum('bshk,bshv->bhkv', k_dec, vi)
        return S_new, o_inter + o_intra

    S_fin, o = lax.scan(step, s0.astype(jnp.float32), (qc, fc, kc, vc))
    o = o.transpose(1, 0, 2, 3, 4).reshape(B, n_chunks * C, H, HG_DV)[:, :T]
    return o, S_fin


def hgrn2_branch(q_raw, f_raw, i_raw, g_raw, lb, g_norm, s0):
    B, T, _ = q_raw.shape
    f = lb + (1.0 - lb) * jax.nn.sigmoid(f_raw.astype(jnp.float32))
    logf = jnp.log(f)
    k = 1.0 - f
    q = jax.nn.silu(q_raw.astype(jnp.float32))
    heads = lambda a: a.reshape(B, T, HG_HEADS, -1)
    o, S = hgrn2_scan(heads(q), heads(logf), heads(k), heads(i_raw), s0)
    o = rmsnorm(o, g_norm) * jax.nn.silu(heads(g_raw).astype(jnp.float32))
    return o.reshape(B, T, HG_VWIDTH).astype(q_raw.dtype), S


def nsa_heads(q_raw, kv_raws, g_raw):
    B, T, _ = q_raw.shape
    q = q_raw.reshape(B, T, NSA_HEADS, HEAD_DIM) * (HEAD_DIM ** -0.5)
    kvs = tuple(a.reshape(B, T, NSA_KV_HEADS, HEAD_DIM) for a in kv_raws)
    gates = jax.nn.sigmoid(g_raw.astype(jnp.float32)).reshape(B, T, NSA_HEADS, 3)
    return q, gates, kvs


def compress(k, pos_emb, w1, w2):
    B, L, N, D = k.shape
    nc = (L - CMP_BLOCK) // CMP_STRIDE + 1
    idx = jnp.arange(nc)[:, None] * CMP_STRIDE + jnp.arange(CMP_BLOCK)[None, :]
    blocks = k[:, idx] + pos_emb[None, None, :, None, :]
    flat = blocks.transpose(0, 1, 3, 2, 4).reshape(B, nc, N, CMP_BLOCK * D)
    return jax.nn.gelu(flat @ w1) @ w2


def to_blocks(k):
    B, L, N, D = k.shape
    ns = -(-L // SLC_BLOCK)
    k = jnp.pad(k, ((0, 0), (0, ns * SLC_BLOCK - L), (0, 0), (0, 0)))
    return k.reshape(B, ns, SLC_BLOCK, N, D).transpose(0, 3, 1, 2, 4)


def overlap_matrix(nc, ns):
    cs = jnp.arange(nc)[:, None] * CMP_STRIDE
    ss = jnp.arange(ns)[None, :] * SLC_BLOCK
    return ((cs <= ss + SLC_BLOCK - 1) & (cs + CMP_BLOCK - 1 >= ss)).astype(jnp.float32)


def masked_attend(s, mask, v, eq):
    s = jnp.where(mask, s, -jnp.inf)
    m = jnp.max(s, axis=-1, keepdims=True)
    m = jnp.where(jnp.isfinite(m), m, 0.0)
    p = jnp.where(mask, jnp.exp(s - m), 0.0)
    p = p / jnp.maximum(jnp.sum(p, axis=-1, keepdims=True), 1e-30)
    return p, jnp.einsum(eq, p, v.astype(jnp.float32))


def nsa_block(q, qpos, gates, kc, vc, cpos, ks_blocks, vs_blocks, kw, vw, wpos):
    B, Tq, H, D = q.shape
    nc = kc.shape[1]
    ns = ks_blocks.shape[2]
    slopes = alibi_slopes().reshape(NSA_KV_HEADS, NSA_GROUP, 1, 1)
    qg = q.astype(jnp.float32).reshape(B, Tq, NSA_KV_HEADS, NSA_GROUP, D)
    dist_c = (qpos[:, None] - cpos[None, :]).astype(jnp.float32)
    s_c = jnp.einsum('btngd,bcnd->bngtc', qg, kc.astype(jnp.float32)) - slopes * dist_c
    p_c, o_c = masked_attend(s_c, dist_c >= 0, vc, 'bngtc,bcnd->btngd')
    score = jnp.einsum('bngtc,cs->bnts', p_c, overlap_matrix(nc, ns))
    blk = jnp.arange(ns)[None, :]
    qblk = (qpos // SLC_BLOCK)[:, None]
    forced = (blk == 0) | (blk == qblk) | (blk == qblk - 1)
    valid = blk * SLC_BLOCK <= qpos[:, None]
    score = jnp.where(valid, jnp.where(forced, FORCE_SCORE, score), -jnp.inf)
    n_top = min(SLC_TOP, ns)
    _, idx = lax.top_k(score, n_top)
    idx_flat = idx.reshape(B, NSA_KV_HEADS, Tq * n_top)
    b_ix = jnp.arange(B)[:, None, None]
    n_ix = jnp.arange(NSA_KV_HEADS)[None, :, None]
    ks = ks_blocks[b_ix, n_ix, idx_flat].reshape(B, NSA_KV_HEADS, Tq, n_top * SLC_BLOCK, D)
    vs = vs_blocks[b_ix, n_ix, idx_flat].reshape(B, NSA_KV_HEADS, Tq, n_top * SLC_BLOCK, D)
    spos = (idx[..., None] * SLC_BLOCK + jnp.arange(SLC_BLOCK)).reshape(B, NSA_KV_HEADS, Tq, n_top * SLC_BLOCK)
    dist_s = (qpos[:, None] - spos).astype(jnp.float32)[:, :, None]
    s_s = jnp.einsum('btngd,bntsd->bngts', qg, ks.astype(jnp.float32)) - slopes * dist_s
    _, o_s = masked_attend(s_s, dist_s >= 0, vs, 'bngts,bntsd->btngd')
    dist_w = qpos[:, None] - wpos[None, :]
    mask_w = (dist_w >= 0) & (dist_w < WINDOW) & (wpos[None, :] >= 0)
    s_w = jnp.einsum('btngd,bwnd->bngtw', qg, kw.astype(jnp.float32)) - slopes * dist_w.astype(jnp.float32)
    _, o_w = masked_attend(s_w, mask_w, vw, 'bngtw,bwnd->btngd')
    o = jnp.stack([o_c, o_s, o_w], axis=-2).reshape(B, Tq, H, 3, D)
    return jnp.einsum('bthr,bthrd->bthd', gates, o)


def nsa_prompt(q, gates, kvs, cmp_k, cmp_v):
    B, T, H, D = q.shape
    k_c, v_c, k_s, v_s, k_w, v_w = kvs
    kc = compress(k_c, *cmp_k)
    vc = compress(v_c, *cmp_v)
    cpos = jnp.arange(kc.shape[1]) * CMP_STRIDE + CMP_BLOCK - 1
    ks_blocks, vs_blocks = to_blocks(k_s), to_blocks(v_s)
    kw_pad = jnp.pad(k_w, ((0, 0), (WINDOW, 0), (0, 0), (0, 0)))
    vw_pad = jnp.pad(v_w, ((0, 0), (WINDOW, 0), (0, 0), (0, 0)))
    nq = T // Q_BLOCK
    qb = q.reshape(B, nq, Q_BLOCK, H, D).swapaxes(0, 1)
    gb = gates.reshape(B, nq, Q_BLOCK, H, 3).swapaxes(0, 1)

    def step(args):
        i, q_i, g_i = args
        start = i * Q_BLOCK
        qpos = start + jnp.arange(Q_BLOCK)
        kw = lax.dynamic_slice_in_dim(kw_pad, start, WINDOW + Q_BLOCK, axis=1)
        vw = lax.dynamic_slice_in_dim(vw_pad, start, WINDOW + Q_BLOCK, axis=1)
        wpos = start - WINDOW + jnp.arange(WINDOW + Q_BLOCK)
        return nsa_block(q_i, qpos, g_i, kc, vc, cpos, ks_blocks, vs_blocks, kw, vw, wpos)

    o = lax.map(step, (jnp.arange(nq), qb, gb))
    o = o.swapaxes(0, 1).reshape(B, T, NSA_WIDTH)
    kv_rows = jnp.stack([k_c, v_c, k_s, v_s], axis=2)
    wl = min(WINDOW, T)
    win_rows = jnp.stack([k_w, v_w], axis=2)[:, T - wl:]
    return o, kv_rows, win_rows


def nsa_sample(q, gates, kvs, cache_kv_l, cache_win_l, page_table, cmp_k, cmp_v):
    B, Tn, H, D = q.shape
    k_c, v_c, k_s, v_s, k_w, v_w = kvs
    past_len = page_table.shape[1] * cache_kv_l.shape[1]
    past = cache_kv_l[page_table].reshape(B, past_len, 4, NSA_KV_HEADS, HEAD_DIM)
    new_rows = jnp.stack([k_c, v_c, k_s, v_s], axis=2)
    full = jnp.concatenate([past, new_rows.astype(past.dtype)], axis=1)
    kc = compress(full[:, :, 0], *cmp_k)
    vc = compress(full[:, :, 1], *cmp_v)
    cpos = jnp.arange(kc.shape[1]) * CMP_STRIDE + CMP_BLOCK - 1
    ks_blocks, vs_blocks = to_blocks(full[:, :, 2]), to_blocks(full[:, :, 3])
    wl = cache_win_l.shape[1]
    wfull = jnp.concatenate([cache_win_l, jnp.stack([k_w, v_w], axis=2).astype(cache_win_l.dtype)], axis=1)
    wpos = past_len - wl + jnp.arange(wl + Tn)
    qpos = past_len + jnp.arange(Tn)
    o = nsa_block(q, qpos, gates, kc, vc, cpos, ks_blocks, vs_blocks, wfull[:, :, 0], wfull[:, :, 1], wpos)
    return o.reshape(B, Tn, NSA_WIDTH), new_rows, wfull[:, Tn:]


def merge_out(o_hg, o_nsa, gate_a, gate_b, w_proj_hg, w_proj_nsa, w_out):
    y = jax.nn.sigmoid(gate_a) * (o_hg @ w_proj_hg) + jax.nn.sigmoid(gate_b) * (o_nsa @ w_proj_nsa)
    return y @ w_out


def setup_inputs(seed: int = 0) -> dict:
    key = jax.random.key(seed)
    keys = list(jax.random.split(key, 48))
    cnt = [0]

    def nk():
        cnt[0] += 1
        return keys[cnt[0] - 1]

    def nrm(shape, scale):
        return jax.random.normal(nk(), shape, jnp.float32) * scale

    def gain(shape):
        return 1.0 + nrm(shape, 0.1)

    n_pages = PAST_LEN // PAGE_SIZE
    n_pool = (DEC_BATCH * n_pages * 5) // 4
    win_len = min(WINDOW, PAST_LEN)
    inp = {}
    inp['x_prompt'] = nrm((BATCH, SEQ, D_MODEL), 1.0)
    inp['x_sample'] = nrm((DEC_BATCH, DEC_SEQ, D_MODEL), 1.0)
    inp['cache_kv'] = nrm((DEPTH, n_pool, PAGE_SIZE, 4, NSA_KV_HEADS, HEAD_DIM), 1.0)
    inp['cache_win'] = nrm((DEPTH, DEC_BATCH, win_len, 2, NSA_KV_HEADS, HEAD_DIM), 1.0)
    inp['state_hgrn'] = nrm((DEPTH, DEC_BATCH, HG_HEADS, HG_DK, HG_DV), 0.3)
    inp['page_table'] = jax.random.permutation(nk(), n_pool)[: DEC_BATCH * n_pages].reshape(DEC_BATCH, n_pages).astype(jnp.int32)
    inp['norm_pre1'] = gain((DEPTH, D_MODEL))
    inp['norm_post1'] = gain((DEPTH, D_MODEL))
    inp['ff1_gate'] = nrm((DEPTH, D_MODEL, D_FF), D_MODEL ** -0.5)
    inp['ff1_up'] = nrm((DEPTH, D_MODEL, D_FF), D_MODEL ** -0.5)
    inp['ff1_down'] = nrm((DEPTH, D_FF, D_MODEL), D_FF ** -0.5)
    inp['norm_pre2'] = gain((DEPTH, D_MODEL))
    inp['norm_post2'] = gain((DEPTH, D_MODEL))
    inp['w_in'] = nrm((DEPTH, D_MODEL, D_IN), D_MODEL ** -0.5)
    inp['hg_lb'] = nrm((DEPTH + 1, HG_WIDTH), 0.5)
    inp['hg_gnorm'] = gain((DEPTH, HG_DV))
    inp['cmp_pos_k'] = nrm((DEPTH, CMP_BLOCK, HEAD_DIM), 0.5)
    inp['cmp_w1_k'] = nrm((DEPTH, CMP_BLOCK * HEAD_DIM, CMP_HIDDEN), (CMP_BLOCK * HEAD_DIM) ** -0.5)
    inp['cmp_w2_k'] = nrm((DEPTH, CMP_HIDDEN, HEAD_DIM), CMP_HIDDEN ** -0.5)
    inp['cmp_pos_v'] = nrm((DEPTH, CMP_BLOCK, HEAD_DIM), 0.5)
    inp['cmp_w1_v'] = nrm((DEPTH, CMP_BLOCK * HEAD_DIM, CMP_HIDDEN), (CMP_BLOCK * HEAD_DIM) ** -0.5)
    inp['cmp_w2_v'] = nrm((DEPTH, CMP_HIDDEN, HEAD_DIM), CMP_HIDDEN ** -0.5)
    inp['w_proj_hg'] = nrm((DEPTH, HG_VWIDTH, D_MODEL), HG_VWIDTH ** -0.5)
    inp['w_proj_nsa'] = nrm((DEPTH, NSA_WIDTH, D_MODEL), NSA_WIDTH ** -0.5)
    inp['w_out'] = nrm((DEPTH, D_MODEL, D_MODEL), D_MODEL ** -0.5)
    inp['norm_pre3'] = gain((DEPTH, D_MODEL))
    inp['norm_post3'] = gain((DEPTH, D_MODEL))
    inp['ff2_gate'] = nrm((DEPTH, D_MODEL, D_FF), D_MODEL ** -0.5)
    inp['ff2_up'] = nrm((DEPTH, D_MODEL, D_FF), D_MODEL ** -0.5)
    inp['ff2_down'] = nrm((DEPTH, D_FF, D_MODEL), D_FF ** -0.5)
    return inp


def reference(x_prompt, x_sample, cache_kv, cache_win, state_hgrn, page_table,
              norm_pre1, norm_post1, ff1_gate, ff1_up, ff1_down,
              norm_pre2, norm_post2, w_in, hg_lb, hg_gnorm,
              cmp_pos_k, cmp_w1_k, cmp_w2_k, cmp_pos_v, cmp_w1_v, cmp_w2_v,
              w_proj_hg, w_proj_nsa, w_out,
              norm_pre3, norm_post3, ff2_gate, ff2_up, ff2_down):
    lb_all = jnp.cumsum(jax.nn.softmax(hg_lb.astype(jnp.float32), axis=0), axis=0)[:DEPTH]
    xp, xs = x_prompt, x_sample
    kv_p, kv_s, win_p, win_s, st_p, st_s = [], [], [], [], [], []
    for l in range(DEPTH):
        xp = half_ffn(xp, norm_pre1[l], norm_post1[l], ff1_gate[l], ff1_up[l], ff1_down[l])
        xs = half_ffn(xs, norm_pre1[l], norm_post1[l], ff1_gate[l], ff1_up[l], ff1_down[l])
        cmp_k = (cmp_pos_k[l], cmp_w1_k[l], cmp_w2_k[l])
        cmp_v = (cmp_pos_v[l], cmp_w1_v[l], cmp_w2_v[l])
        h = rmsnorm(xp, norm_pre2[l])
        hg_in, nsa_in, gate_a, gate_b = in_project(h, w_in[l])
        s0 = jnp.zeros((xp.shape[0], HG_HEADS, HG_DK, HG_DV), jnp.float32)
        o_hg, s_new = hgrn2_branch(*hg_in, lb_all[l], hg_gnorm[l], s0)
        q, gates, kvs = nsa_heads(*nsa_in)
        o_nsa, kv_rows, win_rows = nsa_prompt(q, gates, kvs, cmp_k, cmp_v)
        mix = merge_out(o_hg, o_nsa.astype(h.dtype), gate_a, gate_b, w_proj_hg[l], w_proj_nsa[l], w_out[l])
        xp = xp + rmsnorm(mix, norm_post2[l])
        kv_p.append(kv_rows)
        win_p.append(win_rows)
        st_p.append(s_new)
        h = rmsnorm(xs, norm_pre2[l])
        hg_in, nsa_in, gate_a, gate_b = in_project(h, w_in[l])
        o_hg, s_new = hgrn2_branch(*hg_in, lb_all[l], hg_gnorm[l], state_hgrn[l])
        q, gates, kvs = nsa_heads(*nsa_in)
        o_nsa, kv_rows, win_buf = nsa_sample(q, gates, kvs, cache_kv[l], cache_win[l], page_table, cmp_k, cmp_v)
        mix = merge_out(o_hg, o_nsa.astype(h.dtype), gate_a, gate_b, w_proj_hg[l], w_proj_nsa[l], w_out[l])
        xs = xs + rmsnorm(mix, norm_post2[l])
        kv_s.append(kv_rows)
        win_s.append(win_buf)
        st_s.append(s_new)
        xp = half_ffn(xp, norm_pre3[l], norm_post3[l], ff2_gate[l], ff2_up[l], ff2_down[l])
        xs = half_ffn(xs, norm_pre3[l], norm_post3[l], ff2_gate[l], ff2_up[l], ff2_down[l])
    kv_prompt = jnp.stack(kv_p, axis=0)
    kv_sample = jnp.stack(kv_s, axis=0)
    win_prompt = jnp.stack(win_p, axis=0)
    win_sample = jnp.stack(win_s, axis=0)
    hgrn_prompt = jnp.stack(st_p, axis=0)
    hgrn_sample = jnp.stack(st_s, axis=0)
    return (xp, xs, kv_prompt, kv_sample, win_prompt, win_sample, hgrn_prompt, hgrn_sample)
```

```python
from contextlib import ExitStack
import numpy as np
import concourse.bass as bass
import concourse.mybir as mybir
from concourse.bass_utils import run_bass_kernel_spmd

F32 = mybir.dt.float32
BF16 = mybir.dt.bfloat16
I32 = mybir.dt.int32
AF = mybir.ActivationFunctionType
ALU = mybir.AluOpType
AX = mybir.AxisListType

NCORES = 8
D = 2048
DFF = 5504
T_P = 1024
T_S = 64
T_ALL = T_P + T_S
KC = D // 128
FC = DFF // 128
EPS = 1e-6
NPROJ = 5936
D_IN = 10032

COMPUTE = ("pe", "act", "dve", "pool")
NDMASEM = 8
CUT = 9


def region(ap):
    name = ap.tensor.name
    space = str(ap.space)
    aplist = ap.ap
    off = int(ap.offset)
    if space == "PSUM":
        return (name, 0, 128, 0, 1 << 30)
    if space == "SB":
        pstep, pcount = aplist[0]
        if pstep == 0:
            p0, foff, pcount = 0, off, 128
        else:
            p0 = off // pstep
            foff = off % pstep
        ext = 1
        for s, c in aplist[1:]:
            ext += (c - 1) * abs(s)
        return (name, p0, p0 + pcount, foff, foff + ext)
    ext = 1
    for s, c in aplist:
        ext += (c - 1) * abs(s)
    return (name, 0, 1, off, off + ext)


def overlap(a, b):
    return a[1] < b[2] and b[1] < a[2] and a[3] < b[4] and b[3] < a[4]


def covers(a, b):
    return a[1] <= b[1] and a[2] >= b[2] and a[3] <= b[3] and a[4] >= b[4]


class Ins:
    __slots__ = ("eng", "fn", "deps", "need_inc", "cnt", "is_dma", "dsem", "dval", "idx", "prewait", "inc")

    def __init__(self, eng, fn, is_dma):
        self.eng = eng
        self.fn = fn
        self.deps = set()
        self.need_inc = False
        self.cnt = None
        self.is_dma = is_dma
        self.dsem = None
        self.dval = None
        self.prewait = None
        self.inc = 16


class Prog:
    def __init__(self, nc):
        self.nc = nc
        self.ins = []
        self.hist = {}

    def _track(self, I, reads, writes):
        idx = I.idx
        ins = self.ins
        rr_ = [region(a) for a in reads if str(a.space) != "PSUM"]
        wr_ = [region(a) for a in writes] + [region(a) for a in reads if str(a.space) == "PSUM"]
        for r in rr_:
            h = self.hist.setdefault(r[0], [])
            for (rr, j, w) in h:
                if w and overlap(r, rr):
                    J = ins[j]
                    if J.eng == "pe" and I.eng == "pe" and not I.is_dma and not J.is_dma:
                        continue
                    I.deps.add(j)
        for r in wr_:
            h = self.hist.setdefault(r[0], [])
            for (rr, j, w) in h:
                if overlap(r, rr):
                    J = ins[j]
                    if J.eng == I.eng and not I.is_dma and not J.is_dma:
                        continue
                    I.deps.add(j)
        if len(I.deps) > 1:
            best = {}
            keep = set()
            for j in I.deps:
                J = ins[j]
                if J.is_dma:
                    keep.add(j)
                elif best.get(J.eng, -1) < j:
                    best[J.eng] = j
            keep.update(best.values())
            I.deps = keep
        for r in rr_:
            h = self.hist[r[0]]
            if not I.is_dma:
                h[:] = [e for e in h if e[2] or e[0] != r or ins[e[1]].eng != I.eng or ins[e[1]].is_dma]
            h.append((r, idx, False))
        for r in wr_:
            h = self.hist[r[0]]
            h[:] = [e for e in h if not covers(r, e[0])]
            h.append((r, idx, True))

    def op(self, eng, fn, reads=(), writes=()):
        I = Ins(eng, fn, False)
        I.idx = len(self.ins)
        self.ins.append(I)
        self._track(I, reads, writes)
        return I

    def dma(self, q, out, in_, **kw):
        def fn(e, out=out, in_=in_, kw=kw):
            return e.dma_start(out=out, in_=in_, **kw)
        I = Ins(q, fn, True)
        I.idx = len(self.ins)
        self.ins.append(I)
        self._track(I, [in_], [out])
        return I

    def emit(self, stack):
        nc = self.nc
        ins = self.ins
        for I in ins:
            for j in I.deps:
                ins[j].need_inc = True
        csem = {e: stack.enter_context(nc.semaphore("c_" + e)) for e in COMPUTE}
        dsems = {q: [stack.enter_context(nc.semaphore("d_%s_%d" % (q, i))) for i in range(NDMASEM)]
                 for q in ("sp", "act", "pool")}
        cnt = {e: 0 for e in COMPUTE}
        dq_n = {q: 0 for q in dsems}
        dq_val = {q: [0] * NDMASEM for q in dsems}
        dq_last = {q: [None] * NDMASEM for q in dsems}
        for I in ins:
            if I.is_dma:
                k = dq_n[I.eng] % NDMASEM
                dq_n[I.eng] += 1
                I.prewait = dq_last[I.eng][k]
                dq_val[I.eng][k] += I.inc
                I.dsem = dsems[I.eng][k]
                I.dval = dq_val[I.eng][k]
                dq_last[I.eng][k] = I.idx
            elif I.need_inc:
                cnt[I.eng] += 1
                I.cnt = cnt[I.eng]
        self.maxcnt = dict(cnt)
        streams = {e: [] for e in ("pe", "act", "dve", "pool", "sp")}
        for I in ins:
            streams[I.eng].append(I)
        block = stack.enter_context(nc.Block())

        def run_stream(ename, e):
            waited = {}

            def wait_for(j):
                J = ins[j]
                if J.is_dma:
                    key = ("d", J.eng, id(J.dsem))
                    if waited.get(key, 0) >= J.dval:
                        return
                    waited[key] = J.dval
                    e.wait_ge(J.dsem, J.dval)
                else:
                    key = ("c", J.eng)
                    if waited.get(key, 0) >= J.cnt:
                        return
                    waited[key] = J.cnt
                    e.wait_ge(csem[J.eng], J.cnt)

            for I in streams[ename]:
                for j in sorted(I.deps):
                    wait_for(j)
                if I.is_dma and I.prewait is not None:
                    wait_for(I.prewait)
                bi = I.fn(e)
                if I.is_dma:
                    bi.then_inc(I.dsem, I.inc)
                elif I.need_inc:
                    bi.then_inc(csem[I.eng], 1)
            if ename == "sp":
                for q in dsems:
                    for k in range(NDMASEM):
                        if dq_val[q][k] > 0:
                            e.wait_ge(dsems[q][k], dq_val[q][k])
                for ce in COMPUTE:
                    if cnt[ce] > 0:
                        e.wait_ge(csem[ce], cnt[ce])

        @block.tensor
        def _(e):
            run_stream("pe", e)

        @block.scalar
        def _(e):
            run_stream("act", e)

        @block.vector
        def _(e):
            run_stream("dve", e)

        @block.gpsimd
        def _(e):
            run_stream("pool", e)

        @block.sync
        def _(e):
            run_stream("sp", e)


class Bld:
    def __init__(self, nc):
        self.nc = nc
        self.P = Prog(nc)
        self.st = ExitStack()
        self._n = 0

    def sb(self, name, shape, dt):
        return self.st.enter_context(self.nc.sbuf_tensor(name, shape, dt))

    def ps(self, name, shape, dt=F32):
        return self.st.enter_context(self.nc.psum_tensor(name, shape, dt))

    def mm(self, out, lhsT, rhs, start=True, stop=True):
        self.P.op("pe", lambda e: e.matmul(out, lhsT=lhsT, rhs=rhs, start=start, stop=stop),
                  [lhsT, rhs] + ([] if start else [out]), [out])

    def tr(self, out, in_, ident):
        self.P.op("pe", lambda e: e.transpose(out=out, in_=in_, identity=ident), [in_, ident], [out])

    def act(self, out, in_, func, bias=None, scale=None, accum=None, eng="act"):
        kw = {}
        rd = [in_]
        wr = [out]
        if bias is not None:
            kw["bias"] = bias
            if not isinstance(bias, (int, float)):
                rd.append(bias)
        if scale is not None:
            kw["scale"] = scale
            if not isinstance(scale, (int, float)):
                rd.append(scale)
        if accum is not None:
            kw["accum_out"] = accum
            wr.append(accum)
        self.P.op("act", lambda e: e.activation(out=out, in_=in_, func=func, **kw), rd, wr)

    def tt(self, out, a, b, op, eng="dve"):
        self.P.op(eng, lambda e: e.tensor_tensor(out=out, in0=a, in1=b, op=op), [a, b], [out])

    def ts(self, out, a, s1, op0, s2=None, op1=None, eng="dve", accum=None):
        rd = [a]
        wr = [out]
        if not isinstance(s1, (int, float)):
            rd.append(s1)
        if s2 is not None and not isinstance(s2, (int, float)):
            rd.append(s2)
        kw = {}
        if op1 is not None:
            kw["op1"] = op1
        if accum is not None:
            kw["accum_out"] = accum
            wr.append(accum)
        self.P.op(eng, lambda e: e.tensor_scalar(out=out, in0=a, scalar1=s1, scalar2=s2, op0=op0, **kw), rd, wr)

    def stt(self, out, a, s, b, op0, op1):
        rd = [a, b]
        if not isinstance(s, (int, float)):
            rd.append(s)
        self.P.op("dve", lambda e: e.scalar_tensor_tensor(out=out, in0=a, scalar=s, in1=b, op0=op0, op1=op1), rd, [out])

    def copy(self, out, in_, eng="dve"):
        if eng == "act":
            self.P.op("act", lambda e: e.copy(out=out, in_=in_), [in_], [out])
        else:
            self.P.op(eng, lambda e: e.tensor_copy(out=out, in_=in_), [in_], [out])

    def recip(self, out, in_):
        self.P.op("dve", lambda e: e.reciprocal(out=out, in_=in_), [in_], [out])

    def memset(self, ap, v, eng="pool"):
        self.P.op(eng, lambda e: e.memset(ap, v), [], [ap])

    def dma(self, q, out, in_, **kw):
        self.P.dma(q, out, in_, **kw)

    def finish(self):
        self.P.emit(self.st)
        self.st.close()


class Dense:
    def __init__(self, b, ident_dram):
        self.b = b
        nc = b.nc
        self.identf = b.sb("identf", [128, 128], F32)
        self.ident = b.sb("ident", [128, 128], BF16)
        b.dma("sp", self.identf[:], ident_dram)
        b.copy(self.ident[:], self.identf[:])
        self.hT = b.sb("hT", [128, KC, 576], BF16)
        self.aT = b.sb("aT", [128, FC, 576], BF16)
        self.ybuf = b.sb("ybuf", [128, 5, D], BF16)
        self.xt = [b.sb("xt%d" % i, [128, D], F32) for i in range(2)]
        self.hn = [b.sb("hn%d" % i, [128, D], BF16) for i in range(2)]
        self.junk = b.sb("junk", [128, D], BF16)
        self.st4 = b.sb("st4", [128, 8], F32)
        self.wA = [b.sb("wA%d" % i, [128, KC, 256], BF16) for i in range(2)]
        self.wB = [b.sb("wB%d" % i, [128, KC, 256], BF16) for i in range(2)]
        self.wD = [b.sb("wD%d" % i, [128, FC, 256], BF16) for i in range(2)]
        self.sg = [b.sb("sg%d" % i, [128, 512], BF16) for i in range(2)]
        self.ev = [b.sb("ev%d" % i, [128, 256], F32) for i in range(2)]
        self.psT = [b.ps("psT%d" % i, [128, 8, 128], BF16) for i in range(2)]
        self.psG = [b.ps("psG%d" % i, [128, 512]) for i in range(2)]
        self.psU = [b.ps("psU%d" % i, [128, 512]) for i in range(2)]
        self.psO = [b.ps("psO%d" % i, [128, 512]) for i in range(2)]
        self.nT = 0
        self.nG = 0
        self.nO = 0
        self.nX = 0
        self.nW = 0
        self.nE = 0

    def rstd(self, src, rows, col):
        b = self.b
        ss = self.st4[0:rows, col:col + 1]
        b.act(self.junk[0:rows, :], src, AF.Square, accum=ss)
        b.act(ss, ss, AF.Sqrt, bias=EPS, scale=1.0 / D)
        b.recip(ss, ss)
        return ss

    def norm_to_hT(self, src, rows, tok0, gcol):
        b = self.b
        r = self.rstd(src, rows, 0)
        hn = self.hn[self.nX % 2]
        self.nX += 1
        b.ts(hn[0:rows, :], src, r, ALU.mult)
        if CUT >= 3:
            self.to_T(hn, rows, self.hT, tok0, gcol)

    def to_T(self, src, rows, dstT, tok0, gcol=None, nk=KC):
        b = self.b
        for k4 in range(0, nk, 4):
            pt = self.psT[self.nT % 2]
            self.nT += 1
            for j in range(4):
                kc = k4 + j
                b.tr(pt[:, j, 0:rows], src[0:rows, kc * 128:(kc + 1) * 128], self.ident[0:rows, 0:rows])
            if gcol is None:
                b.copy(dstT[:, k4:k4 + 4, tok0:tok0 + rows], pt[:, 0:4, 0:rows])
            else:
                for j in range(4):
                    kc = k4 + j
                    b.act(dstT[:, kc, tok0:tok0 + rows], pt[:, j, 0:rows], AF.Copy, scale=gcol[:, kc:kc + 1])

    def load_w(self, dst, w_dram, c0, w, nk):
        src = w_dram.rearrange("(kc p) f -> p kc f", p=128)[:, :, c0:c0 + w]
        self.b.dma("pool", dst[:, 0:nk, 0:w], src)

    def gate_up(self, wg, wu, groups):
        b = self.b
        nblk = (DFF + 255) // 256
        blocks = [(i * 256, min(256, DFF - i * 256)) for i in range(nblk)]

        def issue(i):
            c0, w = blocks[i]
            self.load_w(self.wA[i % 2], wg, c0, w, KC)
            self.load_w(self.wB[i % 2], wu, c0, w, KC)
        issue(0)
        for i, (c0, w) in enumerate(blocks):
            if i + 1 < nblk:
                issue(i + 1)
            wa, wb = self.wA[i % 2], self.wB[i % 2]
            for fl in range(w // 128):
                fc = c0 // 128 + fl
                for (t0, n) in groups:
                    pg = self.psG[self.nG % 2]
                    pu = self.psU[self.nG % 2]
                    sg = self.sg[self.nG % 2]
                    self.nG += 1
                    for kc in range(KC):
                        b.mm(pg[:, 0:n], wa[:, kc, fl * 128:(fl + 1) * 128], self.hT[:, kc, t0:t0 + n],
                             start=(kc == 0), stop=(kc == KC - 1))
                    for kc in range(KC):
                        b.mm(pu[:, 0:n], wb[:, kc, fl * 128:(fl + 1) * 128], self.hT[:, kc, t0:t0 + n],
                             start=(kc == 0), stop=(kc == KC - 1))
                    b.act(sg[:, 0:n], pg[:, 0:n], AF.Silu)
                    b.tt(self.aT[:, fc, t0:t0 + n], sg[:, 0:n], pu[:, 0:n], ALU.mult)

    def down(self, wd, tiles):
        b = self.b
        nblk = D // 256

        def issue(i):
            src = wd.rearrange("(fc p) d -> p fc d", p=128)[:, :, i * 256:(i + 1) * 256]
            b.dma("pool", self.wD[i % 2][:], src)
        issue(0)
        for i in range(nblk):
            if i + 1 < nblk:
                issue(i + 1)
            w = self.wD[i % 2]
            for ti, (t0, rows) in enumerate(tiles):
                po = self.psO[self.nO % 2]
                self.nO += 1
                for fc in range(FC):
                    b.mm(po[0:rows, 0:256], self.aT[:, fc, t0:t0 + rows], w[:, fc, :], start=(fc == 0), stop=(fc == FC - 1))
                b.copy(self.ybuf[0:rows, ti, i * 256:(i + 1) * 256], po[0:rows, 0:256], eng="act")

    def proj(self, srcT, nk, w_dram, cols, tiles, sink):
        b = self.b
        c_lo, c_hi = cols
        nblk = (c_hi - c_lo + 255) // 256
        blocks = [(c_lo + i * 256, min(256, c_hi - c_lo - i * 256)) for i in range(nblk)]

        def issue(i):
            c0, w = blocks[i]
            self.load_w(self.wA[(self.nW + i) % 2], w_dram, c0, w, nk)
        issue(0)
        for i, (c0, w) in enumerate(blocks):
            if i + 1 < nblk:
                issue(i + 1)
            wt = self.wA[(self.nW + i) % 2]
            for ti, (t0, rows) in enumerate(tiles):
                po = self.psO[self.nO % 2]
                self.nO += 1
                for kc in range(nk):
                    b.mm(po[0:rows, 0:w], srcT[:, kc, t0:t0 + rows], wt[:, kc, 0:w], start=(kc == 0), stop=(kc == nk - 1))
                sink(ti, t0, rows, c0, w, po)
        self.nW += nblk


def half_tiles(h):
    if h == 0:
        return [(i * 128, 128, i * 128) for i in range(4)] + [(512, 64, T_P)]
    return [(i * 128, 128, 512 + i * 128) for i in range(4)]


def half_groups(h):
    return [(0, 512), (512, 64)] if h == 0 else [(0, 512)]


def build_l1(stages=('win', 'norm', 'gateup', 'down', 'resid', 'proj'), halves=(0, 1)):
    nc = bass.Bass("TRN2", target_bir_lowering=False)
    dt = lambda n, s, k="ExternalInput": nc.dram_tensor(n, s, F32, kind=k).ap()
    x = dt("x", [T_ALL, D])
    wg = dt("wg", [D, DFF]) if 'gateup' in stages else None
    wu = dt("wu", [D, DFF]) if 'gateup' in stages else None
    wd = dt("wd", [DFF, D]) if 'down' in stages else None
    win = dt("win", [D, NPROJ]) if 'proj' in stages else None
    gcols = dt("gcols", [128, 2 * KC])
    gpost = dt("gpost", [128, D])
    identd = dt("identd", [128, 128])
    cwin = dt("cwin", [16, 512, 256])
    x1o = dt("x1o", [T_ALL, D], "ExternalOutput")
    projo = dt("projo", [T_ALL, NPROJ], "ExternalOutput")
    wino = dt("wino", [16, 512, 256], "ExternalOutput")

    b = Bld(nc)
    dn = Dense(b, identd)
    gc = b.sb("gc", [128, 2 * KC], F32)
    gp = b.sb("gp", [128, D], F32)
    b.dma("sp", gc[:], gcols)
    b.dma("sp", gp[:], gpost)
    for bi in range(16 if 'win' in stages else 0):
        b.dma("act", wino[bi, 0:508, :], cwin[bi, 4:512, :])

    for h in halves:
        tiles = half_tiles(h)
        for (t0, rows, g0) in (tiles if 'norm' in stages else []):
            xt = dn.xt[dn.nX % 2]
            b.dma("sp", xt[0:rows, :], x[g0:g0 + rows, :])
            dn.norm_to_hT(xt[0:rows, :], rows, t0, gc[:, 0:KC])
        if 'gateup' in stages:
            dn.gate_up(wg, wu, half_groups(h))
        if 'down' in stages:
            dn.down(wd, [(t0, rows) for (t0, rows, g0) in tiles])
        for ti, (t0, rows, g0) in enumerate(tiles if 'resid' in stages else []):
            xt = dn.xt[dn.nX % 2]
            b.dma("sp", xt[0:rows, :], x[g0:g0 + rows, :])
            y = dn.ybuf[0:rows, ti, :]
            r = dn.rstd(y, rows, 1)
            tmp = dn.hn[(dn.nX + 1) % 2]
            b.stt(tmp[0:rows, :], y, r, gp[0:rows, :], ALU.mult, ALU.mult)
            b.stt(xt[0:rows, :], tmp[0:rows, :], 0.5, xt[0:rows, :], ALU.mult, ALU.add)
            b.dma("sp", x1o[g0:g0 + rows, :], xt[0:rows, :])
            dn.norm_to_hT(xt[0:rows, :], rows, t0, gc[:, KC:2 * KC])

        def sink(ti, t0, rows, c0, w, po, tiles=tiles):
            ev = dn.ev[dn.nE % 2]
            dn.nE += 1
            b.copy(ev[0:rows, 0:w], po[0:rows, 0:w], eng="act")
            g0 = tiles[ti][2]
            b.dma("sp", projo[g0:g0 + rows, c0:c0 + w], ev[0:rows, 0:w])
            if g0 == T_P and c0 == 5632:
                for bi in range(16):
                    b.dma("sp", wino[bi, 508:512, :], ev[bi * 4:(bi + 1) * 4, 0:256])
        if 'proj' in stages:
            dn.proj(dn.hT, KC, win, (0, NPROJ), [(t0, rows) for (t0, rows, g0) in tiles], sink)
    b.finish()
    return nc


def _bcast(v):
    return np.ascontiguousarray(np.broadcast_to(np.asarray(v, np.float32).reshape(1, -1), (128, v.size)))


def _cols(v):
    return np.ascontiguousarray(np.asarray(v, np.float32).reshape(-1, 128).T)


def run_l1(inp):
    nc = build_l1()
    xp = inp["x_prompt"][0]
    xs = inp["x_sample"].reshape(-1, D)
    ident = np.eye(128, dtype=np.float32)
    gcols = np.concatenate([_cols(inp["norm_pre1"][0]), _cols(inp["norm_pre2"][0])], axis=1)
    gpost = _bcast(inp["norm_post1"][0])
    win = np.ascontiguousarray(inp["w_in"][0][:, :NPROJ])
    maps = []
    for c in range(NCORES):
        maps.append({
            "x": np.ascontiguousarray(np.concatenate([xp[c * T_P:(c + 1) * T_P], xs[c * T_S:(c + 1) * T_S]], 0)),
            "wg": inp["ff1_gate"][0], "wu": inp["ff1_up"][0], "wd": inp["ff1_down"][0], "win": win,
            "gcols": gcols, "gpost": gpost, "identd": ident,
            "cwin": np.ascontiguousarray(inp["cache_win"][0, c * 16:(c + 1) * 16].reshape(16, 512, 256)),
        })
    res = run_bass_kernel_spmd(nc, maps, core_ids=list(range(NCORES)))
    return res.results


def kernel(**inp):
    inp = {k: np.asarray(v) for k, v in inp.items()}
    r1 = run_l1(inp)
    proj_p = np.concatenate([r["projo"][:T_P] for r in r1], 0)
    proj_s = np.concatenate([r["projo"][T_P:] for r in r1], 0)
    x1_p = np.concatenate([r["x1o"][:T_P] for r in r1], 0)
    x1_s = np.concatenate([r["x1o"][T_P:] for r in r1], 0)
    kv_prompt = proj_p[:, 5120:5632].reshape(1, 1, 8192, 4, 2, 64)
    kv_sample = proj_s[:, 5120:5632].reshape(1, 128, 4, 4, 2, 64)
    win_prompt = proj_p[8192 - 512:, 5632:5888].reshape(1, 1, 512, 2, 2, 64)
    win_sample = np.concatenate([r["wino"] for r in r1], 0).reshape(1, 128, 512, 2, 2, 64)
    ohg_p, ohg_s, hg_p, hg_s = run_l2h(inp, proj_p, proj_s)
    ons_p = run_l2n_prompt(inp, proj_p)
    ons_s = run_l2n_sample(inp, proj_s)
    y_p, y_s = run_l3(inp, x1_p, x1_s, ohg_p, ohg_s, ons_p, ons_s)
    y_prompt = y_p.reshape(1, 8192, D)
    y_sample = y_s.reshape(128, 4, D)
    return (y_prompt, y_sample, np.ascontiguousarray(kv_prompt), np.ascontiguousarray(kv_sample),
            np.ascontiguousarray(win_prompt), win_sample, hg_p, hg_s)


CH = 32
SEG = 2048


def build_l2h(do_prompt=True, do_sample=True):
    nc = bass.Bass("TRN2", target_bir_lowering=False)
    dt = lambda n, s, k="ExternalInput": nc.dram_tensor(n, s, F32, kind=k).ap()
    qT = dt("qT", [128, 8192])
    fT = dt("fT", [128, 8192])
    v32 = dt("v32", [32, 256, 128])
    g32 = dt("g32", [32, 256, 128])
    lbc = dt("lbc", [128, 2])
    qTs = dt("qTs", [128, 512])
    fTs = dt("fTs", [128, 512])
    v4 = dt("v4", [4, 128, 128])
    g4 = dt("g4", [4, 128, 128])
    lbs = dt("lbs", [128, 16])
    st0 = dt("st0", [16, 8, 128, 128])
    gn32 = dt("gn32", [32, 128])
    rmask = dt("rmask", [128, SEG])
    rmask4 = dt("rmask4", [128, 512])
    trid = dt("trid", [32, 32])
    identd = dt("identd", [128, 128])
    o32 = dt("o32", [32, 256, 128], "ExternalOutput")
    Sp = dt("Sp", [128, 128], "ExternalOutput")
    o4 = dt("o4", [4, 128, 128], "ExternalOutput")
    Ss = dt("Ss", [16, 8, 128, 128], "ExternalOutput")

    b = Bld(nc)
    identf = b.sb("identf", [128, 128], F32)
    ident = b.sb("ident", [128, 128], BF16)
    b.dma("sp", identf[:], identd)
    b.copy(ident[:], identf[:])
    tri = b.sb("tri", [32, 32], F32)
    b.dma("sp", tri[:], trid)
    gn = b.sb("gn", [32, 128], F32)
    b.dma("sp", gn[:], gn32)
    rm = b.sb("rm", [128, SEG], F32)
    b.dma("sp", rm[:], rmask)
    rm4 = b.sb("rm4", [128, 512], F32)
    b.dma("sp", rm4[:], rmask4)
    lbr = b.sb("lbr", [128, 16], F32)
    lbv = b.sb("lbv", [128, 8], F32)
    oml = b.sb("oml", [128, 8], F32)

    qr = b.sb("qr", [128, SEG], F32)
    fr = b.sb("fr", [128, SEG], F32)
    bc = b.sb("bc", [128, SEG], F32)
    kk = b.sb("kk", [128, SEG], F32)
    t1 = b.sb("t1", [128, SEG], F32)
    qb = b.sb("qb", [128, SEG], BF16)
    kb = b.sb("kb", [128, SEG], BF16)
    kd = b.sb("kd", [128, SEG], BF16)
    ebl = b.sb("ebl", [128, 128], F32)
    vs = b.sb("vs", [32, 64, 128], BF16)
    gs = b.sb("gs", [32, 64, 128], F32)
    oa = b.sb("oa", [32, 64, 128], F32)
    sq = b.sb("sq", [32, 64, 128], F32)
    rs = b.sb("rs", [32, 128], F32)
    S = [b.sb("S%d" % i, [128, 128], F32) for i in range(3)]
    Sbf = [b.sb("Sbf%d" % i, [128, 128], BF16) for i in range(3)]
    atm = [b.sb("atm%d" % i, [32, 32], BF16) for i in range(2)]
    kdT = [b.sb("kdT%d" % i, [32, 128], BF16) for i in range(2)]
    psA = [b.ps("psA%d" % i, [128, 512]) for i in range(2)]
    psK = [b.ps("psK%d" % i, [128, 1024], BF16) for i in range(2)]
    psO = [b.ps("psO%d" % i, [128, 512]) for i in range(2)]
    psS = [b.ps("psS%d" % i, [128, 512]) for i in range(2)]
    cnt = [0]

    def prep(q_src, f_src, n, nh, C, rmk):
        per = n // nh
        nch = n // C
        v3 = lambda t: t[:, 0:n].rearrange("p (h m) -> p h m", h=nh)
        lb_bc = lbv[:, 0:nh].unsqueeze(2).broadcast_to([128, nh, per])
        oml_bc = oml[:, 0:nh].unsqueeze(2).broadcast_to([128, nh, per])
        b.dma("sp", qr[:, 0:n], q_src)
        b.dma("act", fr[:, 0:n], f_src)
        b.act(t1[:, 0:n], fr[:, 0:n], AF.Sigmoid)
        b.tt(v3(t1), v3(t1), oml_bc, ALU.mult)
        b.tt(v3(fr), v3(t1), lb_bc, ALU.add)
        b.ts(kk[:, 0:n], fr[:, 0:n], -1.0, ALU.mult, 1.0, ALU.add)
        b.act(t1[:, 0:n], fr[:, 0:n], AF.Ln)
        b.P.op("dve", lambda e: e.tensor_tensor_scan(out=bc[:, 0:n], data0=rmk[:, 0:n], data1=t1[:, 0:n],
                                                     initial=0.0, op0=ALU.mult, op1=ALU.add),
               [rmk[:, 0:n], t1[:, 0:n]], [bc[:, 0:n]])
        b.act(fr[:, 0:n], qr[:, 0:n], AF.Silu)
        b.act(t1[:, 0:n], bc[:, 0:n], AF.Exp)
        b.tt(qb[:, 0:n], fr[:, 0:n], t1[:, 0:n], ALU.mult)
        b.act(t1[:, 0:n], bc[:, 0:n], AF.Exp, scale=-1.0)
        b.tt(kb[:, 0:n], kk[:, 0:n], t1[:, 0:n], ALU.mult)
        bc3 = bc[:, 0:n].rearrange("p (c m) -> p c m", m=C)
        bl_bc = bc3[:, :, C - 1:C].broadcast_to([128, nch, C])
        b.tt(t1[:, 0:n].rearrange("p (c m) -> p c m", m=C), bl_bc, bc3, ALU.subtract)
        b.act(t1[:, 0:n], t1[:, 0:n], AF.Exp)
        b.tt(kd[:, 0:n], kk[:, 0:n], t1[:, 0:n], ALU.mult)
        b.act(ebl[:, 0:nch], bc3[:, :, C - 1], AF.Exp)

    def chunk(ci, C, Sx, Sbx, cg=None):
        k = cnt[0] % 2
        cnt[0] += 1
        cg = ci if cg is None else cg
        cs = slice(cg * C, (cg + 1) * C)
        pa, pk, po, pS = psA[k], psK[k], psO[k], psS[k]
        b.mm(pa[0:C, 0:C], kb[:, cs], qb[:, cs])
        b.tt(atm[k][0:C, 0:C], pa[0:C, 0:C], tri[0:C, 0:C], ALU.mult)
        b.tr(pk[0:C, 0:128], kd[:, cs], ident[:, :])
        b.copy(kdT[k][0:C, :], pk[0:C, 0:128], eng="act")
        b.mm(po[0:C, 0:128], atm[k][0:C, 0:C], vs[0:C, ci, :], start=True, stop=False)
        b.mm(po[0:C, 0:128], qb[:, cs], Sbx[:], start=False, stop=True)
        b.copy(oa[0:C, ci, :], po[0:C, 0:128], eng="act")
        b.mm(pS[:, 0:128], kdT[k][0:C, :], vs[0:C, ci, :])
        b.stt(Sx[:], Sx[:], ebl[:, cg:cg + 1], pS[:, 0:128], ALU.mult, ALU.add)

    def post(C, nch, g_src, o_dst):
        b.dma("sp", gs[0:C, 0:nch, :], g_src)
        b.tt(sq[0:C, 0:nch, :], oa[0:C, 0:nch, :], oa[0:C, 0:nch, :], ALU.mult)
        b.P.op("dve", lambda e: e.reduce_sum(out=rs[0:C, 0:nch], in_=sq[0:C, 0:nch, :], axis=AX.X),
               [sq[0:C, 0:nch, :]], [rs[0:C, 0:nch]])
        b.act(rs[0:C, 0:nch], rs[0:C, 0:nch], AF.Sqrt, bias=EPS, scale=1.0 / 128)
        b.recip(rs[0:C, 0:nch], rs[0:C, 0:nch])
        b.tt(oa[0:C, 0:nch, :], oa[0:C, 0:nch, :], rs[0:C, 0:nch].unsqueeze(2).broadcast_to([C, nch, 128]), ALU.mult)
        b.tt(oa[0:C, 0:nch, :], oa[0:C, 0:nch, :], gn[0:C, :].unsqueeze(1).broadcast_to([C, nch, 128]), ALU.mult)
        b.act(gs[0:C, 0:nch, :], gs[0:C, 0:nch, :], AF.Silu)
        b.tt(oa[0:C, 0:nch, :], oa[0:C, 0:nch, :], gs[0:C, 0:nch, :], ALU.mult)
        b.dma("sp", o_dst, oa[0:C, 0:nch, :])

    def lower_bounds(src, nh):
        b.dma("sp", lbr[:, 0:2 * nh], src)
        l3 = lbr[:, 0:2 * nh].rearrange("p (h r) -> p h r", r=2)
        b.tt(lbv[:, 0:nh], l3[:, :, 0], l3[:, :, 1], ALU.subtract)
        b.act(lbv[:, 0:nh], lbv[:, 0:nh], AF.Sigmoid)
        b.ts(oml[:, 0:nh], lbv[:, 0:nh], -1.0, ALU.mult, 1.0, ALU.add)

    if do_prompt:
        lower_bounds(lbc, 1)
        b.memset(S[0][:], 0.0)
        b.memset(Sbf[0][:], 0.0)
        nseg = 8192 // SEG
        cps = SEG // CH
        for sg_ in range(nseg):
            prep(qT[:, sg_ * SEG:(sg_ + 1) * SEG], fT[:, sg_ * SEG:(sg_ + 1) * SEG], SEG, 1, CH, rm)
            b.dma("pool", vs[:, 0:cps, :], v32[:, sg_ * cps:(sg_ + 1) * cps, :])
            for ci in range(cps):
                chunk(ci, CH, S[0], Sbf[0])
                b.copy(Sbf[0][:], S[0][:], eng="act")
            post(CH, cps, g32[:, sg_ * cps:(sg_ + 1) * cps, :], o32[:, sg_ * cps:(sg_ + 1) * cps, :])
        b.dma("sp", Sp, S[0][:])
    if do_sample:
        lower_bounds(lbs, 8)
        prep(qTs, fTs, 512, 8, 4, rm4)
        for grp in range(2):
            b.dma("pool", vs[0:4, 0:64, :], v4[:, grp * 64:(grp + 1) * 64, :])
            for jl in range(64):
                j = grp * 64 + jl
                h_, b_ = j // 16, j % 16
                Sx, Sbx = S[j % 3], Sbf[j % 3]
                b.dma("sp", Sx[:], st0[b_, h_])
                b.copy(Sbx[:], Sx[:], eng="act")
                chunk(jl, 4, Sx, Sbx, cg=j)
                b.dma("sp", Ss[b_, h_], Sx[:])
            post(4, 64, g4[:, grp * 64:(grp + 1) * 64, :], o4[:, grp * 64:(grp + 1) * 64, :])
    b.finish()
    return nc


def l2h_consts():
    rmask = np.ones((128, SEG), np.float32)
    rmask[:, ::CH] = 0.0
    rmask4 = np.ones((128, 512), np.float32)
    rmask4[:, ::4] = 0.0
    tri = np.triu(np.ones((32, 32), np.float32))
    return {"rmask": rmask, "rmask4": rmask4, "trid": tri, "identd": np.eye(128, dtype=np.float32)}


def run_l2h(inp, proj_p, proj_s):
    nc = build_l2h()
    cst = l2h_consts()
    gn32 = _bcast(inp["hg_gnorm"][0])[:32]
    hg_lb = inp["hg_lb"]
    maps = []
    ps4 = proj_s.reshape(128, 4, NPROJ)
    for c in range(NCORES):
        hs = slice(c * 128, (c + 1) * 128)
        m = dict(cst)
        m["gn32"] = np.ascontiguousarray(gn32)
        m["qT"] = np.ascontiguousarray(proj_p[:, 0 * 1024:][:, hs].T)
        m["fT"] = np.ascontiguousarray(proj_p[:, 1 * 1024:][:, hs].T)
        m["v32"] = np.ascontiguousarray(proj_p[:, 2 * 1024:][:, hs].reshape(256, 32, 128).transpose(1, 0, 2))
        m["g32"] = np.ascontiguousarray(proj_p[:, 3 * 1024:][:, hs].reshape(256, 32, 128).transpose(1, 0, 2))
        m["lbc"] = np.ascontiguousarray(hg_lb[:, hs].T)
        sb = ps4[c * 16:(c + 1) * 16]
        part = lambda k: sb[:, :, k * 1024:(k + 1) * 1024].reshape(16, 4, 8, 128)
        m["qTs"] = np.ascontiguousarray(part(0).transpose(3, 2, 0, 1).reshape(128, 512))
        m["fTs"] = np.ascontiguousarray(part(1).transpose(3, 2, 0, 1).reshape(128, 512))
        m["v4"] = np.ascontiguousarray(part(2).transpose(1, 2, 0, 3).reshape(4, 128, 128))
        m["g4"] = np.ascontiguousarray(part(3).transpose(1, 2, 0, 3).reshape(4, 128, 128))
        m["lbs"] = np.ascontiguousarray(hg_lb.reshape(2, 8, 128).transpose(2, 1, 0).reshape(128, 16))
        m["st0"] = np.ascontiguousarray(inp["state_hgrn"][0, c * 16:(c + 1) * 16])
        maps.append(m)
    res = run_bass_kernel_spmd(nc, maps, core_ids=list(range(NCORES)))
    r = res.results
    o_p = np.concatenate([r[c]["o32"].transpose(1, 0, 2).reshape(8192, 128) for c in range(NCORES)], axis=1)
    hg_p = np.stack([r[c]["Sp"] for c in range(NCORES)], 0).reshape(1, 1, 8, 128, 128)
    o_s = np.concatenate([r[c]["o4"].reshape(4, 8, 16, 128).transpose(2, 0, 1, 3).reshape(16, 4, 1024)
                          for c in range(NCORES)], 0)
    hg_s = np.concatenate([r[c]["Ss"] for c in range(NCORES)], 0).reshape(1, 128, 8, 128, 128)
    return o_p, o_s.reshape(512, 1024), hg_p, hg_s


def build_l3():
    nc = bass.Bass("TRN2", target_bir_lowering=False)
    dt = lambda n, s, k="ExternalInput": nc.dram_tensor(n, s, F32, kind=k).ap()
    x1 = dt("x1", [T_ALL, D])
    ohgT = dt("ohgT", [1024, T_ALL])
    onsT = dt("onsT", [1024, T_ALL])
    wgab = dt("wgab", [D, 2 * D])
    wphg = dt("wphg", [1024, D])
    wpns = dt("wpns", [1024, D])
    wout = dt("wout", [D, D])
    wg = dt("wg", [D, DFF])
    wu = dt("wu", [D, DFF])
    wd = dt("wd", [DFF, D])
    gcols = dt("gcols", [128, 2 * KC])
    gpost2 = dt("gpost2", [128, D])
    gpost3 = dt("gpost3", [128, D])
    identd = dt("identd", [128, 128])
    yo = dt("yo", [T_ALL, D], "ExternalOutput")
    x2s = nc.dram_tensor("x2s", [T_ALL, D], F32).ap()

    b = Bld(nc)
    dn = Dense(b, identd)
    gc = b.sb("gc", [128, 2 * KC], F32)
    gp = b.sb("gp", [128, D], F32)
    b.dma("sp", gc[:], gcols)
    oT = [dn.aT[:, 0:8, :], dn.aT[:, 8:16, :]]
    sga = dn.sg

    for h in range(2):
        tiles = half_tiles(h)
        ntok = 576 if h == 0 else 512
        for (t0, rows, g0) in tiles:
            xt = dn.xt[dn.nX % 2]
            b.dma("sp", xt[0:rows, :], x1[g0:g0 + rows, :])
            dn.norm_to_hT(xt[0:rows, :], rows, t0, gc[:, 0:KC])
        for src, dst in ((ohgT, oT[0]), (onsT, oT[1])):
            s3 = src.rearrange("(kc p) t -> p kc t", p=128)
            if h == 0:
                b.dma("pool", dst[:, :, 0:512], s3[:, :, 0:512])
                b.dma("pool", dst[:, :, 512:576], s3[:, :, T_P:T_P + 64])
            else:
                b.dma("pool", dst[:, :, 0:512], s3[:, :, 512:1024])
        nblk = D // 256

        def issue(i):
            dn.load_w(dn.wA[i % 2], wgab, i * 256, 256, KC)
            dn.load_w(dn.wB[i % 2], wgab, D + i * 256, 256, KC)
            dn.load_w(dn.wD[i % 2][:, 0:8, :], wphg, i * 256, 256, 8)
            dn.load_w(dn.wD[i % 2][:, 8:16, :], wpns, i * 256, 256, 8)
        issue(0)
        for i in range(nblk):
            if i + 1 < nblk:
                issue(i + 1)
            wa, wb, wc = dn.wA[i % 2], dn.wB[i % 2], dn.wD[i % 2]
            for ti, (t0, rows, g0) in enumerate(tiles):
                k = dn.nG % 2
                dn.nG += 1
                pga, pgb, ph, pn = dn.psG[k], dn.psU[k], dn.psO[0], dn.psO[1]
                for kc in range(KC):
                    b.mm(pga[0:rows, 0:256], dn.hT[:, kc, t0:t0 + rows], wa[:, kc, :], start=(kc == 0), stop=(kc == KC - 1))
                for kc in range(KC):
                    b.mm(pgb[0:rows, 0:256], dn.hT[:, kc, t0:t0 + rows], wb[:, kc, :], start=(kc == 0), stop=(kc == KC - 1))
                for kc in range(8):
                    b.mm(ph[0:rows, 0:256], oT[0][:, kc, t0:t0 + rows], wc[:, kc, :], start=(kc == 0), stop=(kc == 7))
                for kc in range(8):
                    b.mm(pn[0:rows, 0:256], oT[1][:, kc, t0:t0 + rows], wc[:, 8 + kc, :], start=(kc == 0), stop=(kc == 7))
                s1, s2 = dn.ev[0], dn.ev[1]
                b.act(s1[0:rows, :], pga[0:rows, 0:256], AF.Sigmoid)
                b.act(s2[0:rows, :], pgb[0:rows, 0:256], AF.Sigmoid)
                b.tt(s1[0:rows, :], s1[0:rows, :], ph[0:rows, 0:256], ALU.mult)
                b.tt(s2[0:rows, :], s2[0:rows, :], pn[0:rows, 0:256], ALU.mult)
                b.tt(dn.ybuf[0:rows, ti, i * 256:(i + 1) * 256], s1[0:rows, :], s2[0:rows, :], ALU.add)
        for ti, (t0, rows, g0) in enumerate(tiles):
            dn.to_T(dn.ybuf[:, ti, :], rows, dn.hT, t0)

        def sink(ti, t0, rows, c0, w, po):
            b.copy(dn.ybuf[0:rows, ti, c0:c0 + w], po[0:rows, 0:w], eng="act")
        dn.proj(dn.hT, KC, wout, (0, D), [(t0, rows) for (t0, rows, g0) in tiles], sink)
        b.dma("sp", gp[:], gpost2)
        for ti, (t0, rows, g0) in enumerate(tiles):
            xt = dn.xt[dn.nX % 2]
            b.dma("sp", xt[0:rows, :], x1[g0:g0 + rows, :])
            y = dn.ybuf[0:rows, ti, :]
            r = dn.rstd(y, rows, 1)
            tmp = dn.hn[(dn.nX + 1) % 2]
            b.stt(tmp[0:rows, :], y, r, gp[0:rows, :], ALU.mult, ALU.mult)
            b.tt(xt[0:rows, :], tmp[0:rows, :], xt[0:rows, :], ALU.add)
            b.dma("sp", x2s[g0:g0 + rows, :], xt[0:rows, :])
            dn.norm_to_hT(xt[0:rows, :], rows, t0, gc[:, KC:2 * KC])
        dn.gate_up(wg, wu, half_groups(h))
        dn.down(wd, [(t0, rows) for (t0, rows, g0) in tiles])
        b.dma("sp", gp[:], gpost3)
        for ti, (t0, rows, g0) in enumerate(tiles):
            xt = dn.xt[dn.nX % 2]
            dn.nX += 1
            b.dma("sp", xt[0:rows, :], x2s[g0:g0 + rows, :])
            y = dn.ybuf[0:rows, ti, :]
            r = dn.rstd(y, rows, 1)
            tmp = dn.hn[ti % 2]
            b.stt(tmp[0:rows, :], y, r, gp[0:rows, :], ALU.mult, ALU.mult)
            b.stt(xt[0:rows, :], tmp[0:rows, :], 0.5, xt[0:rows, :], ALU.mult, ALU.add)
            b.dma("sp", yo[g0:g0 + rows, :], xt[0:rows, :])
    b.finish()
    return nc


def run_l3(inp, x1_p, x1_s, ohg_p, ohg_s, ons_p, ons_s):
    nc = build_l3()
    gcols = np.concatenate([_cols(inp["norm_pre2"][0]), _cols(inp["norm_pre3"][0])], axis=1)
    wgab = np.ascontiguousarray(inp["w_in"][0][:, NPROJ:])
    base = {"wgab": wgab, "wphg": inp["w_proj_hg"][0], "wpns": inp["w_proj_nsa"][0], "wout": inp["w_out"][0],
            "wg": inp["ff2_gate"][0], "wu": inp["ff2_up"][0], "wd": inp["ff2_down"][0], "gcols": gcols,
            "gpost2": _bcast(inp["norm_post2"][0]), "gpost3": _bcast(inp["norm_post3"][0]),
            "identd": np.eye(128, dtype=np.float32)}
    maps = []
    for c in range(NCORES):
        m = dict(base)
        ps, ss = slice(c * T_P, (c + 1) * T_P), slice(c * T_S, (c + 1) * T_S)
        m["x1"] = np.ascontiguousarray(np.concatenate([x1_p[ps], x1_s[ss]], 0))
        m["ohgT"] = np.ascontiguousarray(np.concatenate([ohg_p[ps], ohg_s[ss]], 0).T)
        m["onsT"] = np.ascontiguousarray(np.concatenate([ons_p[ps], ons_s[ss]], 0).T)
        maps.append(m)
    res = run_bass_kernel_spmd(nc, maps, core_ids=list(range(NCORES)))
    y_p = np.concatenate([r["yo"][:T_P] for r in res.results], 0)
    y_s = np.concatenate([r["yo"][T_P:] for r in res.results], 0)
    return y_p, y_s


NEG = -32768.0
SLOPES = np.power(2.0, -8.0 * np.arange(1, 17) / 16).astype(np.float64)
GELU_C = 1.5957691216057308


def nsa_prompt_consts(core):
    tiles = [core + 8 * j for j in range(8)]
    p = np.arange(128)
    c = {}
    c["gmat"] = (np.arange(128)[:, None] == (np.arange(8192)[None, :] // 64)).astype(np.float32)
    cs = np.arange(512)[:, None] * 16
    ss = np.arange(128)[None, :] * 64
    ov = ((cs <= ss + 63) & (cs + 31 >= ss)).astype(np.float32)
    ov[511] = 0.0
    c["ovl"] = np.ascontiguousarray(ov.reshape(4, 128, 128).transpose(1, 0, 2))
    r = np.arange(72) - (7 - core)
    c["btab"] = np.ascontiguousarray((SLOPES[None, :, None] * (p[:, None, None] - 64 - 128 * r[None, None, :])).astype(np.float32).reshape(128, 16 * 72))
    cb = np.zeros((128, 8, 16, 4), np.float64)
    cm = np.zeros((128, 8, 2, 128), np.float32)
    keep = np.zeros((128, 8, 128), np.float32)
    add = np.zeros((128, 8, 128), np.float32)
    tt = np.arange(128)
    blk = np.arange(128)
    for j, i in enumerate(tiles):
        t0 = 128 * i
        for ct in range(4):
            cb[:, j, :, ct] = SLOPES[None, :] * (16 * (128 * ct + p[:, None]) + 31 - (t0 + 64))
        nct = i // 16 + 1
        for rr in range(2):
            ct = nct - 1 - rr
            if ct < 0:
                continue
            cpos = 16 * (128 * ct + p) + 31
            ok = (cpos[:, None] <= (t0 + tt)[None, :]) & ((128 * ct + p) < 511)[:, None]
            cm[:, j, rr, :] = np.where(ok, 0.0, NEG)
        qpos = t0 + tt
        qb = qpos // 64
        valid = blk[None, :] <= qb[:, None]
        f0 = blk[None, :] == 0
        f1 = blk[None, :] == qb[:, None]
        f2 = blk[None, :] == (qb[:, None] - 1)
        forced = f0 | f1 | f2
        keep[:, j, :] = (valid & ~forced).astype(np.float32)
        a = np.where(valid, 0.0, -1e30)
        a = np.where(f2, 1e4, a)
        a = np.where(f1, 2e4, a)
        a = np.where(f0, 3e4, a)
        add[:, j, :] = a
    c["cbias"] = np.ascontiguousarray(cb.astype(np.float32).reshape(128, 8 * 16 * 4))
    c["cmask"] = np.ascontiguousarray(cm.reshape(128, 8 * 2 * 128))
    c["keepm"] = np.ascontiguousarray(keep.reshape(128, 8 * 128))
    c["addm"] = np.ascontiguousarray(add.reshape(128, 8 * 128))
    causal = np.where(p[:, None] <= tt[None, :], 0.0, NEG).astype(np.float32)
    wlow = np.where(p[:, None] > tt[None, :], 0.0, NEG).astype(np.float32)
    zero = np.zeros((128, 128), np.float32)
    full = np.full((128, 128), NEG, np.float32)
    dms = [zero if q < core else (causal if q == core else full) for q in range(8)]
    dmw = []
    for q in range(12):
        if q < core or q > core + 4:
            dmw.append(full)
        elif q == core:
            dmw.append(wlow)
        elif q == core + 4:
            dmw.append(causal)
        else:
            dmw.append(zero)
    c["dms"] = np.ascontiguousarray(np.stack(dms, 1).reshape(128, 8 * 128))
    c["dmw"] = np.ascontiguousarray(np.stack(dmw, 1).reshape(128, 12 * 128))
    c["identd"] = np.eye(128, dtype=np.float32)
    return c


class Nsa:
    def __init__(self, b, nc, dt):
        self.b = b
        identd = dt("identd", [128, 128])
        self.identf = b.sb("identf", [128, 128], F32)
        self.ident = b.sb("ident", [128, 128], BF16)
        b.dma("sp", self.identf[:], identd)
        b.copy(self.ident[:], self.identf[:])
        self.w1 = {}
        self.w2 = {}
        self.posT = {}
        for kind in ("k", "v"):
            w1d = dt("w1" + kind, [128, 32, 256])
            w2d = dt("w2" + kind, [128, 4, 128])
            pd = dt("pos" + kind, [128, 32])
            self.w1[kind] = b.sb("s_w1" + kind, [128, 32, 256], BF16)
            self.w2[kind] = b.sb("s_w2" + kind, [128, 4, 128], BF16)
            self.posT[kind] = b.sb("s_pos" + kind, [128, 32], BF16)
            b.dma("pool", self.w1[kind][:], w1d)
            b.dma("pool", self.w2[kind][:], w2d)
            b.dma("pool", self.posT[kind][:], pd)
        self.bcol = b.sb("bcol", [128, 4], F32)
        self.xs = [b.sb("xs%d" % i, [128, 2064], BF16) for i in range(2)]
        self.gh = [b.sb("gh%d" % i, [128, 128], BF16) for i in range(4)]
        self.tx = b.sb("tx", [128, 128], F32)
        self.tu = b.sb("tu", [128, 128], F32)
        self.psS = [b.ps("psS%d" % i, [128, 512]) for i in range(2)]
        self.psAcc = [b.ps("psAcc%d" % i, [128, 512]) for i in range(2)]
        self.psH = [b.ps("psH%d" % i, [128, 512]) for i in range(2)]
        self.psK2 = b.ps("psK2", [128, 512])
        self.psT = b.ps("psT", [128, 1024], BF16)
        self.PT = [b.sb("PT%d" % i, [128, 128], BF16) for i in range(3)]
        self.nS = 0
        self.nA = 0
        self.nH = 0
        self.nP = 0
        self.nX = 0
        self.bias_done = False

    def cmp_bias(self):
        b = self.b
        for ki, kind in enumerate(("k", "v")):
            for hc in range(2):
                ph = self.psH[self.nH % 2]
                self.nH += 1
                for l in range(32):
                    b.mm(ph[:, 0:1], self.w1[kind][0:64, l, hc * 128:(hc + 1) * 128], self.posT[kind][0:64, l:l + 1],
                         start=(l == 0), stop=(l == 31))
                b.copy(self.bcol[:, ki * 2 + hc:ki * 2 + hc + 1], ph[:, 0:1], eng="act")

    def compress_tile(self, kind, src_dram_cols, N, kdst=None, vdst=None, xs_ap=None):
        b = self.b
        ki = 0 if kind == "k" else 1
        L = 16 * (N - 1) + 32
        if xs_ap is not None:
            xs = xs_ap
        else:
            xs = self.xs[self.nX % 2]
            self.nX += 1
            b.dma("pool", xs[:, 0:L], src_dram_cols)
        for n in range(2):
            for hc in range(2):
                ph = self.psH[self.nH % 2]
                self.nH += 1
                for l in range(32):
                    b.mm(ph[:, 0:N], self.w1[kind][n * 64:(n + 1) * 64, l, hc * 128:(hc + 1) * 128],
                         xs[n * 64:(n + 1) * 64, l:l + 16 * (N - 1) + 1:16], start=(l == 0), stop=(l == 31))
                tx, tu, gh = self.tx, self.tu, self.gh[n * 2 + hc]
                b.act(tx[:, 0:N], ph[:, 0:N], AF.Identity, bias=self.bcol[:, ki * 2 + hc:ki * 2 + hc + 1])
                b.tt(tu[:, 0:N], tx[:, 0:N], tx[:, 0:N], ALU.mult)
                b.ts(tu[:, 0:N], tu[:, 0:N], 0.044715, ALU.mult, 1.0, ALU.add)
                b.tt(tu[:, 0:N], tu[:, 0:N], tx[:, 0:N], ALU.mult)
                b.act(tu[:, 0:N], tu[:, 0:N], AF.Sigmoid, scale=GELU_C)
                b.tt(gh[:, 0:N], tx[:, 0:N], tu[:, 0:N], ALU.mult)
        pk = self.psK2
        if kind == "k":
            for q in range(4):
                b.mm(pk[:, 0:N], self.w2[kind][:, q, :], self.gh[q][:, 0:N], start=(q == 0), stop=(q == 3))
            b.copy(kdst, pk[:, 0:N], eng="act")
        else:
            for q in range(4):
                b.mm(pk[0:N, 0:128], self.gh[q][:, 0:N], self.w2[kind][:, q, :], start=(q == 0), stop=(q == 3))
            b.copy(vdst[0], pk[0:N, 0:64], eng="act")
            b.copy(vdst[1], pk[0:N, 64:128], eng="act")

    def branch(self, steps, qrhs, nq, ncols, scale=0.125):
        b = self.b
        pacc = self.psAcc[self.nA % 2]
        self.nA += 1
        ns = len(steps)
        for si, st in enumerate(steps):
            ps = self.psS[self.nS % 2]
            self.nS += 1
            ex = st.get("extra", [])
            rows = st.get("rows", 128)
            b.mm(ps[0:rows, 0:nq], st["k"], qrhs, start=True, stop=(len(ex) == 0))
            for ei, (l_, r_) in enumerate(ex):
                b.mm(ps[0:rows, 0:nq], l_, r_, start=False, stop=(ei == len(ex) - 1))
            pt = self.PT[self.nP % 3]
            self.nP += 1
            if st.get("bias") is not None:
                b.act(pt[0:rows, 0:nq], ps[0:rows, 0:nq], AF.Exp, bias=st["bias"], scale=scale)
            else:
                b.act(pt[0:rows, 0:nq], ps[0:rows, 0:nq], AF.Exp, scale=scale)
            b.mm(pacc[0:nq, 0:ncols], pt[0:rows, 0:nq], st["v"], start=(si == 0), stop=(si == ns - 1))
        return pacc


def build_l2n():
    nc = bass.Bass("TRN2", target_bir_lowering=False)
    dt = lambda n, s, k="ExternalInput": nc.dram_tensor(n, s, F32, kind=k).ap()
    b = Bld(nc)
    ns = Nsa(b, nc, dt)
    qTd = dt("qT", [128, 8, 1024])
    gated = dt("gates", [128, 8, 48])
    KsTd = dt("KsT", [128, 8192])
    KwTd = dt("KwT", [128, 8192])
    KcTd = dt("KcT", [128, 8192])
    VcTd = dt("VcT", [128, 8192])
    Vsd = dt("Vs", [128, 64, 128])
    Vwd = dt("Vw", [128, 64, 128])
    gmatd = dt("gmat", [128, 8192])
    ovld = dt("ovl", [128, 4, 128])
    btabd = dt("btab", [128, 16 * 72])
    cbiasd = dt("cbias", [128, 512])
    cmaskd = dt("cmask", [128, 2048])
    keepd = dt("keepm", [128, 1024])
    addd = dt("addm", [128, 1024])
    dmsd = dt("dms", [128, 8 * 128])
    dmwd = dt("dmw", [128, 12 * 128])
    onso = dt("ons", [128, 8, 1024], "ExternalOutput")

    sbt = b.sb
    KsT = sbt("s_KsT", [128, 8192], BF16)
    KwT = sbt("s_KwT", [128, 8192], BF16)
    Vs = sbt("Vsa", [128, 64, 2, 65], BF16)
    Vw = sbt("Vwa", [128, 64, 2, 65], BF16)
    G = sbt("G", [128, 8192], BF16)
    qT = sbt("qTb", [128, 8, 1024], BF16)
    KCT = sbt("KCT", [128, 512], BF16)
    VCO = sbt("VCO", [128, 4, 2, 193], BF16)
    btab = sbt("s_btab", [128, 16 * 72], F32)
    cbias = sbt("s_cbias", [128, 512], F32)
    cmask = sbt("s_cmask", [128, 2048], BF16)
    keepm = sbt("s_keepm", [128, 1024], F32)
    addm = sbt("s_addm", [128, 1024], F32)
    dms = sbt("s_dms", [128, 8, 128], BF16)
    dmw = sbt("s_dmw", [128, 12, 128], BF16)
    gts = sbt("gts", [128, 8, 48], F32)
    sc = sbt("sc", [128, 2, 128], F32)
    s2 = sbt("s2", [128, 128], F32)
    s3 = sbt("s3", [128, 128], F32)
    m8 = sbt("m8", [128, 16], F32)
    nm = sbt("nm", [128, 128], BF16)
    nmT = sbt("nmT", [128, 2, 128], BF16)
    rd = sbt("rd", [128, 8], F32)
    oacc = [sbt("oacc%d" % i, [128, 1024], F32) for i in range(2)]

    for d_, s_ in ((KsT, KsTd), (KwT, KwTd), (G, gmatd)):
        for q in range(4):
            b.dma("pool", d_[:, q * 2048:(q + 1) * 2048], s_[:, q * 2048:(q + 1) * 2048])
    b.memset(Vs[:], 1.0)
    b.memset(Vw[:], 1.0)
    b.memset(VCO[:], 0.0)
    b.memset(KCT[:], 0.0)
    for d_, s_ in ((Vs, Vsd), (Vw, Vwd)):
        for q in range(4):
            b.dma("pool", d_[:, q * 16:(q + 1) * 16, :, 0:64], s_[:, q * 16:(q + 1) * 16, :].rearrange("p k (n d) -> p k n d", n=2))
    b.dma("pool", qT[:], qTd)
    b.dma("sp", btab[:], btabd)
    b.dma("sp", cbias[:], cbiasd)
    b.dma("pool", cmask[:], cmaskd)
    b.dma("sp", keepm[:], keepd)
    b.dma("sp", addm[:], addd)
    b.dma("pool", dms[:], dmsd.rearrange("p (q t) -> p q t", q=8))
    b.dma("pool", dmw[:], dmwd.rearrange("p (q t) -> p q t", q=12))
    b.dma("sp", gts[:], gated)
    b.act(gts[:], gts[:], AF.Sigmoid)

    ns.cmp_bias()
    b.memset(VCO[:, :, :, 64:65], 1.0)
    for n in range(2):
        b.dma("pool", VCO[:, :, n, 65:193], ovld)
    for ct in range(4):
        N = 128 if ct < 3 else 127
        L = 16 * (N - 1) + 32
        ns.compress_tile("k", KcTd[:, ct * 2048:ct * 2048 + L], N, kdst=KCT[:, ct * 128:ct * 128 + N])
        ns.compress_tile("v", VcTd[:, ct * 2048:ct * 2048 + L], N,
                         vdst=[VCO[0:N, ct, 0, 0:64], VCO[0:N, ct, 1, 0:64]])

    for j in range(8):
        oa = oacc[j % 2]
        qs = slice(j * 128, (j + 1) * 128)
        nct = j // 2 + 1
        for n in range(2):
            pb = slice(n * 64, (n + 1) * 64)
            b.memset(sc[:, n, :], 0.0, eng="dve")
            for g in range(8):
                h = n * 8 + g
                steps = []
                for ct in range(nct):
                    st = {"k": KCT[pb, ct * 128:(ct + 1) * 128], "v": VCO[:, ct, n, :],
                          "bias": cbias[:, (j * 16 + h) * 4 + ct:(j * 16 + h) * 4 + ct + 1]}
                    rr = nct - 1 - ct
                    if rr < 2:
                        st["extra"] = [(ns.ident[:, :], cmask[:, (j * 2 + rr) * 128:(j * 2 + rr + 1) * 128])]
                    steps.append(st)
                pc = ns.branch(steps, qT[pb, g, qs], 128, 193)
                b.ts(rd[:, 0:1], pc[:, 64:65], 1e-30, ALU.max)
                b.recip(rd[:, 0:1], rd[:, 0:1])
                b.stt(sc[:, n, :], pc[:, 65:193], rd[:, 0:1], sc[:, n, :], ALU.mult, ALU.add)
                b.tt(rd[:, 1:2], rd[:, 0:1], gts[:, j, h * 3:h * 3 + 1], ALU.mult)
                b.ts(oa[:, h * 64:(h + 1) * 64], pc[:, 0:64], rd[:, 1:2], ALU.mult)
            b.tt(s2[:], sc[:, n, :], keepm[:, j * 128:(j + 1) * 128], ALU.mult)
            b.tt(s2[:], s2[:], addm[:, j * 128:(j + 1) * 128], ALU.add)
            b.P.op("dve", lambda e: e.max(out=m8[:, 0:8], in_=s2[:]), [s2[:]], [m8[:, 0:8]])
            b.P.op("dve", lambda e: e.match_replace(out=s3[:], in_to_replace=m8[:, 0:8], in_values=s2[:], imm_value=-1e30),
                   [s2[:], m8[:, 0:8]], [s3[:]])
            b.P.op("dve", lambda e: e.max(out=m8[:, 8:16], in_=s3[:]), [s3[:]], [m8[:, 8:16]])
            b.ts(s3[:], s2[:], m8[:, 15:16], ALU.is_ge)
            b.ts(nm[:], s3[:], -NEG, ALU.mult, NEG, ALU.add)
            b.tr(ns.psT[:, 0:128], nm[:], ns.ident[:, :])
            b.copy(nmT[:, n, :], ns.psT[:, 0:128])
        for h in range(16):
            n, g = h // 8, h % 8
            pb = slice(n * 64, (n + 1) * 64)
            steps = []
            for kt in range(8 * j + 8):
                ex = [(G[:, kt * 128:(kt + 1) * 128], nmT[:, n, :])]
                if kt >= 8 * j:
                    ex.append((ns.ident[:, :], dms[:, kt - 8 * j, :]))
                rp = 8 * j + 7 - kt
                steps.append({"k": KsT[pb, kt * 128:(kt + 1) * 128], "v": Vs[:, kt, n, :], "extra": ex,
                              "bias": btab[:, h * 72 + rp:h * 72 + rp + 1]})
            pc = ns.branch(steps, qT[pb, g, qs], 128, 65)
            b.ts(rd[:, 2:3], pc[:, 64:65], 1e-30, ALU.max)
            b.recip(rd[:, 2:3], rd[:, 2:3])
            b.tt(rd[:, 3:4], rd[:, 2:3], gts[:, j, h * 3 + 1:h * 3 + 2], ALU.mult)
            b.stt(oa[:, h * 64:(h + 1) * 64], pc[:, 0:64], rd[:, 3:4], oa[:, h * 64:(h + 1) * 64], ALU.mult, ALU.add)
            steps = []
            for q in range(12):
                kt = 8 * j - 4 + q
                if kt < 0:
                    continue
                ex = [(ns.ident[:, :], dmw[:, q, :])]
                rp = 11 - q
                steps.append({"k": KwT[pb, kt * 128:(kt + 1) * 128], "v": Vw[:, kt, n, :], "extra": ex,
                              "bias": btab[:, h * 72 + rp:h * 72 + rp + 1]})
            pc = ns.branch(steps, qT[pb, g, qs], 128, 65)
            b.ts(rd[:, 4:5], pc[:, 64:65], 1e-30, ALU.max)
            b.recip(rd[:, 4:5], rd[:, 4:5])
            b.tt(rd[:, 5:6], rd[:, 4:5], gts[:, j, h * 3 + 2:h * 3 + 3], ALU.mult)
            b.stt(oa[:, h * 64:(h + 1) * 64], pc[:, 0:64], rd[:, 5:6], oa[:, h * 64:(h + 1) * 64], ALU.mult, ALU.add)
        b.dma("sp", onso[:, j, :], oa[:])
    b.finish()
    return nc


def nsa_cmp_weights(inp):
    m = {}
    for kind in ("k", "v"):
        w1 = inp["cmp_w1_" + kind][0].reshape(32, 64, 256).transpose(1, 0, 2)
        m["w1" + kind] = np.ascontiguousarray(np.concatenate([w1, w1], 0))
        w2 = inp["cmp_w2_" + kind][0].reshape(2, 128, 64)
        w2p = np.zeros((128, 4, 128), np.float32)
        for n in range(2):
            for hc in range(2):
                w2p[:, n * 2 + hc, n * 64:(n + 1) * 64] = w2[hc]
        m["w2" + kind] = w2p
        pT = inp["cmp_pos_" + kind][0].T
        m["pos" + kind] = np.ascontiguousarray(np.concatenate([pT, pT], 0))
    return m


def run_l2n_prompt(inp, proj_p):
    q = proj_p[:, 4096:5120]
    kv = proj_p[:, 5120:5888].reshape(8192, 6, 128)
    gates = proj_p[:, 5888:5936]
    cw = nsa_cmp_weights(inp)
    T = lambda a: np.ascontiguousarray(a.T)
    tok = lambda a: np.ascontiguousarray(a.reshape(64, 128, 128).transpose(1, 0, 2))
    shared = {"KcT": T(kv[:, 0]), "VcT": T(kv[:, 1]), "KsT": T(kv[:, 2]), "Vs": tok(kv[:, 3]),
              "KwT": T(kv[:, 4]), "Vw": tok(kv[:, 5])}
    shared.update(cw)
    outs = []
    ncs = []
    maps = []
    for c in range(NCORES):
        tiles = [c + 8 * j for j in range(8)]
        m = dict(shared)
        m.update(nsa_prompt_consts(c))
        qc = np.stack([q[128 * i:128 * (i + 1)] for i in tiles], 0)
        qr = qc.reshape(8, 128, 2, 8, 64).transpose(2, 4, 3, 0, 1).reshape(128, 8, 1024)
        m["qT"] = np.ascontiguousarray(qr)
        m["gates"] = np.ascontiguousarray(np.stack([gates[128 * i:128 * (i + 1)] for i in tiles], 1))
        maps.append(m)
    nc = build_l2n()
    res = run_bass_kernel_spmd(nc, maps, core_ids=list(range(NCORES)))
    o = np.zeros((8192, 1024), np.float32)
    for c in range(NCORES):
        r = res.results[c]["ons"]
        for j in range(8):
            i = c + 8 * j
            o[128 * i:128 * (i + 1)] = r[:, j, :]
    return o


U32 = mybir.dt.uint32
PAST = 2048
SCUT = None
NEGF = -30000.0


def nsa_sample_consts():
    c = {}
    p = np.arange(128)
    c["gs"] = (np.arange(128)[:, None] == (np.arange(17 * 128)[None, :] // 64)).astype(np.float32)
    cs = np.arange(128)[:, None] * 16
    ss = np.arange(64)[None, :] * 64
    ov = ((cs <= ss + 63) & (cs + 31 >= ss)).astype(np.float32)
    ov[127] = 0.0
    ov[:, 33:] = 0.0
    c["ovs"] = ov
    t = np.arange(4)
    bc = np.zeros((128, 2, 8, 4), np.float64)
    bs = np.zeros((128, 17, 2, 8, 4), np.float64)
    bw = np.zeros((128, 5, 2, 8, 4), np.float64)
    for n in range(2):
        for g in range(8):
            sl = SLOPES[n * 8 + g]
            qpos = PAST + t
            cpos = 16 * p + 31
            bc[:, n, g, :] = np.where((p < 127)[:, None], -sl * (qpos[None, :] - cpos[:, None]), NEGF)
            for tile in range(17):
                spos = 128 * tile + p
                dist = qpos[None, :] - spos[:, None]
                bs[:, tile, n, g, :] = np.where(dist >= 0, -sl * dist, NEGF)
            for tile in range(5):
                wpos = PAST - 512 + 128 * tile + p
                dist = qpos[None, :] - wpos[:, None]
                ok = (dist >= 0) & (dist < 512)
                if tile == 4:
                    ok &= (p < 4)[:, None]
                bw[:, tile, n, g, :] = np.where(ok, -sl * dist, NEGF)
    c["biasc"] = np.ascontiguousarray(bc.astype(np.float32).reshape(128, 64))
    c["biass"] = np.ascontiguousarray(bs.astype(np.float32).reshape(128, 17 * 64))
    c["biasw"] = np.ascontiguousarray(bw.astype(np.float32).reshape(128, 5 * 64))
    blk = np.arange(64)
    forced0, forced1, forced2 = blk == 0, blk == 32, blk == 31
    valid = blk <= 32
    keep = (valid & ~(forced0 | forced1 | forced2)).astype(np.float32)
    a = np.where(valid, 0.0, -1e30)
    a = np.where(forced2, 1e4, a)
    a = np.where(forced1, 2e4, a)
    a = np.where(forced0, 3e4, a)
    c["keeps"] = np.ascontiguousarray(np.broadcast_to(keep[None, :], (4, 64)).astype(np.float32))
    c["adds"] = np.ascontiguousarray(np.broadcast_to(a[None, :], (4, 64)).astype(np.float32))
    sel = np.zeros((32, 4), np.float32)
    for g in range(8):
        for tt in range(4):
            sel[g * 4 + tt, tt] = 1.0
    c["selm"] = sel
    c["pcol"] = p.astype(np.float32).reshape(128, 1)
    c["identd"] = np.eye(128, dtype=np.float32)
    return c


def build_l2s(n_pool=2560, nb=16):
    nc = bass.Bass("TRN2", target_bir_lowering=False)
    dt = lambda n, s, k="ExternalInput": nc.dram_tensor(n, s, F32, kind=k).ap()
    b = Bld(nc)
    ns = Nsa(b, nc, dt)
    cache = dt("cache", [n_pool * 128, 512])
    ptab = nc.dram_tensor("ptab", [1, nb * 16], I32, kind="ExternalInput").ap()
    cwin = dt("cwin", [nb, 512, 256])
    qTd = dt("qTs", [128, nb, 32])
    ksnd = dt("ksn", [128, nb, 4])
    kwnd = dt("kwn", [128, nb, 4])
    vsnd = dt("vsn", [4, nb, 128])
    vwnd = dt("vwn", [4, nb, 128])
    gtd = dt("gts", [32, nb, 2, 3])
    gsd = dt("gs", [128, 17 * 128])
    ovsd = dt("ovs", [128, 64])
    bcd = dt("biasc", [128, 64])
    bsd = dt("biass", [128, 17 * 64])
    bwd = dt("biasw", [128, 5 * 64])
    keepd = dt("keeps", [4, 64])
    addd = dt("adds", [4, 64])
    seld = dt("selm", [32, 4])
    pcold = dt("pcol", [128, 1])
    onso = dt("ons", [nb, 2, 32, 64], "ExternalOutput")

    sbt = b.sb
    Gs = sbt("s_gs", [128, 17 * 128], BF16)
    b.dma("pool", Gs[:], gsd)
    biasc = sbt("s_bc", [128, 2, 32], F32)
    biass = sbt("s_bs", [128, 17, 2, 32], F32)
    biasw = sbt("s_bw", [128, 5, 2, 32], F32)
    b.dma("sp", biasc[:], bcd.rearrange("p (n q) -> p n q", n=2))
    b.dma("sp", biass[:], bsd.rearrange("p (k n q) -> p k n q", k=17, n=2))
    b.dma("sp", biasw[:], bwd.rearrange("p (k n q) -> p k n q", k=5, n=2))
    keeps = sbt("s_keep", [4, 64], F32)
    adds = sbt("s_add", [4, 64], F32)
    selm = sbt("s_sel", [32, 4], F32)
    pcol = sbt("s_pcol", [128, 1], F32)
    b.dma("sp", keeps[:], keepd)
    b.dma("sp", adds[:], addd)
    b.dma("sp", selm[:], seld)
    b.dma("sp", pcol[:], pcold)
    qT = sbt("s_qT", [128, nb, 32], BF16)
    ksn = sbt("s_ksn", [128, nb, 4], BF16)
    kwn = sbt("s_kwn", [128, nb, 4], BF16)
    vsn = sbt("s_vsn", [4, nb, 128], BF16)
    vwn = sbt("s_vwn", [4, nb, 128], BF16)
    gts = sbt("s_gts", [32, nb, 2, 3], F32)
    b.dma("pool", qT[:], qTd)
    b.dma("pool", ksn[:], ksnd)
    b.dma("pool", kwn[:], kwnd)
    b.dma("pool", vsn[:], vsnd)
    b.dma("pool", vwn[:], vwnd)
    b.dma("sp", gts[:], gtd)
    b.act(gts[:], gts[:], AF.Sigmoid)
    pti = sbt("pti", [128, nb * 16], I32)
    idx = sbt("idx", [128, nb * 16], U32)
    b.dma("sp", pti[:], ptab.partition_broadcast(128))
    b.ts(idx[:], pti[:], 128.0, ALU.mult, pcol[:, 0:1], ALU.add)

    gth = [sbt("gth%d" % i, [128, 512], F32) for i in range(3)]
    wth = [sbt("wth%d" % i, [128, 256], F32) for i in range(2)]
    KcT = [sbt("KcT%d" % i, [128, 2048], BF16) for i in range(2)]
    VcT = [sbt("VcT%d" % i, [128, 2048], BF16) for i in range(2)]
    KsT = [sbt("KsTs%d" % i, [128, 2052], BF16) for i in range(2)]
    KwT = [sbt("KwTs%d" % i, [128, 516], BF16) for i in range(2)]
    Vs = [sbt("Vss%d" % i, [128, 17, 2, 65], BF16) for i in range(2)]
    Vw = [sbt("Vws%d" % i, [128, 5, 2, 65], BF16) for i in range(2)]
    KCTs = sbt("KCTs", [128, 128], BF16)
    VCOs = sbt("VCOs", [128, 2, 129], BF16)
    tmpf = [sbt("tmpf%d" % i, [128, 32], F32) for i in range(2)]
    xn = sbt("xn", [32, 64], F32)
    s2 = sbt("s2s", [4, 64], F32)
    s3 = sbt("s3s", [4, 64], F32)
    m8 = sbt("m8s", [4, 16], F32)
    nmf = sbt("nmf", [4, 64], F32)
    nmT = sbt("nmTs", [128, 8, 4], BF16)
    rd = sbt("rds", [32, 8], F32)
    oac = [sbt("oacs%d" % i, [32, 64], F32) for i in range(2)]
    for i in range(2):
        b.memset(Vs[i][:], 1.0)
        b.memset(Vw[i][:], 1.0)
    b.memset(KCTs[:], 0.0)
    b.memset(nmT[:], 0.0)
    b.memset(VCOs[:], 0.0)
    b.memset(VCOs[:, :, 64:65], 1.0)
    for n in range(2):
        b.dma("pool", VCOs[:, n, 65:129], ovsd)
    ns.cmp_bias()
    nt = [0]

    def step(kT, rows, q, mask, bias, v, pacc, first, last, ncols):
        ps = ns.psS[ns.nS % 2]
        ns.nS += 1
        b.mm(ps[0:rows, 0:32], kT, q, start=True, stop=(mask is None))
        if mask is not None:
            b.mm(ps[0:rows, 0:32], mask[0], mask[1], start=False, stop=True)
        tf = tmpf[nt[0] % 2]
        nt[0] += 1
        b.stt(tf[0:rows, :], ps[0:rows, 0:32], 0.125, bias, ALU.mult, ALU.add)
        pt = ns.PT[ns.nP % 3]
        ns.nP += 1
        b.act(pt[0:rows, 0:32], tf[0:rows, :], AF.Exp)
        b.mm(pacc[0:32, 0:ncols], pt[0:rows, 0:32], v, start=first, stop=last)

    for bi in range(nb):
        k2 = bi % 2
        for pg in range(16):
            gt = gth[(bi * 16 + pg) % 3]
            col = bi * 16 + pg
            b.P.custom = None
            I = Ins("pool", (lambda e, gt=gt, col=col: e.indirect_dma_start(
                out=gt[:], out_offset=None, in_=cache,
                in_offset=bass.IndirectOffsetOnAxis(ap=idx[:, col:col + 1], axis=0))), True)
            I.idx = len(b.P.ins)
            b.P.ins.append(I)
            b.P._track(I, [idx[:, col:col + 1], cache], [gt[:]])
            ph = ns.psH[ns.nH % 2]
            ns.nH += 1
            for q3 in range(3):
                b.tr(ph[:, q3 * 128:(q3 + 1) * 128], gt[:, q3 * 128:(q3 + 1) * 128], ns.identf[:, :])
            cs = slice(pg * 128, (pg + 1) * 128)
            b.copy(KcT[k2][:, cs], ph[:, 0:128], eng="act")
            b.copy(VcT[k2][:, cs], ph[:, 128:256])
            b.copy(KsT[k2][:, cs], ph[:, 256:384], eng="act")
            b.copy(Vs[k2][:, pg, :, 0:64], gt[:, 384:512].rearrange("p (n d) -> p n d", n=2))
        b.copy(KsT[k2][:, 2048:2052], ksn[:, bi, :])
        b.copy(Vs[k2][0:4, 16, :, 0:64], vsn[0:4, bi, :].rearrange("p (n d) -> p n d", n=2))
        for wt in range(4):
            wtile = wth[wt % 2]
            b.dma("sp", wtile[:], cwin[bi, wt * 128:(wt + 1) * 128, :])
            ph = ns.psH[ns.nH % 2]
            ns.nH += 1
            b.tr(ph[:, 0:128], wtile[:, 0:128], ns.identf[:, :])
            b.copy(KwT[k2][:, wt * 128:(wt + 1) * 128], ph[:, 0:128], eng="act")
            b.copy(Vw[k2][:, wt, :, 0:64], wtile[:, 128:256].rearrange("p (n d) -> p n d", n=2))
        b.copy(KwT[k2][:, 512:516], kwn[:, bi, :])
        b.copy(Vw[k2][0:4, 4, :, 0:64], vwn[0:4, bi, :].rearrange("p (n d) -> p n d", n=2))
        if SCUT == "A":
            continue
        ns.compress_tile("k", None, 127, kdst=KCTs[:, 0:127], xs_ap=KcT[k2])
        ns.compress_tile("v", None, 127, vdst=[VCOs[0:127, 0, 0:64], VCOs[0:127, 1, 0:64]], xs_ap=VcT[k2])
        if SCUT == "B":
            continue
        for n in range(2):
            pb = slice(n * 64, (n + 1) * 64)
            q = qT[pb, bi, :]
            oa = oac[n]
            pacc = ns.psAcc[ns.nA % 2]
            ns.nA += 1
            step(KCTs[pb, 0:128], 128, q, None, biasc[:, n, :], VCOs[:, n, :], pacc, True, True, 129)
            b.ts(rd[:, 0:1], pacc[0:32, 64:65], 1e-30, ALU.max)
            b.recip(rd[:, 0:1], rd[:, 0:1])
            b.ts(xn[:], pacc[0:32, 65:129], rd[:, 0:1], ALU.mult)
            b.tt(rd[:, 1:2], rd[:, 0:1], gts[:, bi, n, 0:1], ALU.mult)
            b.ts(oa[:], pacc[0:32, 0:64], rd[:, 1:2], ALU.mult)
            if SCUT == "C":
                continue
            pk = ns.psK2
            b.mm(pk[0:4, 0:64], selm[:, :], xn[:, :])
            b.tt(s2[:], pk[0:4, 0:64], keeps[:], ALU.mult)
            b.tt(s2[:], s2[:], adds[:], ALU.add)
            b.P.op("dve", lambda e: e.max(out=m8[:, 0:8], in_=s2[:]), [s2[:]], [m8[:, 0:8]])
            b.P.op("dve", lambda e: e.match_replace(out=s3[:], in_to_replace=m8[:, 0:8], in_values=s2[:], imm_value=-1e30),
                   [s2[:], m8[:, 0:8]], [s3[:]])
            b.P.op("dve", lambda e: e.max(out=m8[:, 8:16], in_=s3[:]), [s3[:]], [m8[:, 8:16]])
            b.ts(s3[:], s2[:], m8[:, 15:16], ALU.is_ge)
            b.ts(nmf[:], s3[:], -NEG, ALU.mult, NEG, ALU.add)
            b.tr(pk[0:64, 64:68], nmf[:, :], ns.identf[0:4, 0:4])
            b.copy(nmT[0:64, :, :], pk[0:64, 64:68].unsqueeze(1).broadcast_to([64, 8, 4]))
            nmv = nmT[:].rearrange("p g t -> p (g t)")
            if SCUT == "D":
                continue
            pacc = ns.psAcc[ns.nA % 2]
            ns.nA += 1
            for tile in range(17):
                rows = 128 if tile < 16 else 4
                cs = slice(tile * 128, tile * 128 + rows)
                step(KsT[k2][pb, cs], rows, q, (Gs[:, cs], nmv), biass[0:rows, tile, n, :], Vs[k2][0:rows, tile, n, :],
                     pacc, tile == 0, tile == 16, 65)
            b.ts(rd[:, 2:3], pacc[0:32, 64:65], 1e-30, ALU.max)
            b.recip(rd[:, 2:3], rd[:, 2:3])
            b.tt(rd[:, 3:4], rd[:, 2:3], gts[:, bi, n, 1:2], ALU.mult)
            b.stt(oa[:], pacc[0:32, 0:64], rd[:, 3:4], oa[:], ALU.mult, ALU.add)
            if SCUT == "E":
                continue
            pacc = ns.psAcc[ns.nA % 2]
            ns.nA += 1
            for tile in range(5):
                rows = 128 if tile < 4 else 4
                cs = slice(tile * 128, tile * 128 + rows)
                step(KwT[k2][pb, cs], rows, q, None, biasw[0:rows, tile, n, :], Vw[k2][0:rows, tile, n, :],
                     pacc, tile == 0, tile == 4, 65)
            b.ts(rd[:, 4:5], pacc[0:32, 64:65], 1e-30, ALU.max)
            b.recip(rd[:, 4:5], rd[:, 4:5])
            b.tt(rd[:, 5:6], rd[:, 4:5], gts[:, bi, n, 2:3], ALU.mult)
            b.stt(oa[:], pacc[0:32, 0:64], rd[:, 5:6], oa[:], ALU.mult, ALU.add)
            b.dma("sp", onso[bi, n], oa[:])
    b.finish()
    return nc


def run_l2n_sample(inp, proj_s, nb=16):
    cache = inp["cache_kv"][0]
    n_pool = cache.shape[0]
    cache2 = np.ascontiguousarray(cache.reshape(n_pool * 128, 512))
    cst = nsa_sample_consts()
    cst.update(nsa_cmp_weights(inp))
    ps = proj_s.reshape(128, 4, NPROJ)
    nc = build_l2s(n_pool, nb)
    maps = []
    for c in range(NCORES):
        sb = ps[c * nb:(c + 1) * nb]
        m = dict(cst)
        m["cache"] = cache2
        m["ptab"] = np.ascontiguousarray(inp["page_table"][c * nb:(c + 1) * nb].reshape(1, nb * 16).astype(np.int32))
        m["cwin"] = np.ascontiguousarray(inp["cache_win"][0, c * nb:(c + 1) * nb].reshape(nb, 512, 256))
        q = sb[:, :, 4096:5120].reshape(nb, 4, 2, 8, 64)
        m["qTs"] = np.ascontiguousarray(q.transpose(2, 4, 0, 3, 1).reshape(128, nb, 32))
        kv = sb[:, :, 5120:5888].reshape(nb, 4, 6, 128)
        m["ksn"] = np.ascontiguousarray(kv[:, :, 2].transpose(2, 0, 1))
        m["kwn"] = np.ascontiguousarray(kv[:, :, 4].transpose(2, 0, 1))
        m["vsn"] = np.ascontiguousarray(kv[:, :, 3].transpose(1, 0, 2))
        m["vwn"] = np.ascontiguousarray(kv[:, :, 5].transpose(1, 0, 2))
        g = sb[:, :, 5888:5936].reshape(nb, 4, 2, 8, 3)
        m["gts"] = np.ascontiguousarray(g.transpose(3, 1, 0, 2, 4).reshape(32, nb, 2, 3))
        maps.append(m)
    res = run_bass_kernel_spmd(nc, maps, core_ids=list(range(NCORES)))
    outs = []
    for c in range(NCORES):
        r = res.results[c]["ons"].reshape(nb, 2, 8, 4, 64)
        outs.append(r.transpose(0, 3, 1, 2, 4).reshape(nb * 4, 1024))
    return np.concatenate(outs, 0)
```

```python
from contextlib import ExitStack
import numpy as np
import concourse.bass as bass
import concourse.mybir as mybir
from concourse.bass_utils import run_bass_kernel_spmd

F32 = mybir.dt.float32
BF16 = mybir.dt.bfloat16
I32 = mybir.dt.int32
AF = mybir.ActivationFunctionType
ALU = mybir.AluOpType
AX = mybir.AxisListType

NCORES = 8
D = 2048
DFF = 5504
T_P = 1024
T_S = 64
T_ALL = T_P + T_S
KC = D // 128
FC = DFF // 128
EPS = 1e-6
NPROJ = 5936
D_IN = 10032

COMPUTE = ("pe", "act", "dve", "pool")
NDMASEM = 8
CUT = 9


def region(ap):
    name = ap.tensor.name
    space = str(ap.space)
    aplist = ap.ap
    off = int(ap.offset)
    if space == "PSUM":
        return (name, 0, 128, 0, 1 << 30)
    if space == "SB":
        pstep, pcount = aplist[0]
        if pstep == 0:
            p0, foff, pcount = 0, off, 128
        else:
            p0 = off // pstep
            foff = off % pstep
        ext = 1
        for s, c in aplist[1:]:
            ext += (c - 1) * abs(s)
        return (name, p0, p0 + pcount, foff, foff + ext)
    ext = 1
    for s, c in aplist:
        ext += (c - 1) * abs(s)
    return (name, 0, 1, off, off + ext)


def overlap(a, b):
    return a[1] < b[2] and b[1] < a[2] and a[3] < b[4] and b[3] < a[4]


def covers(a, b):
    return a[1] <= b[1] and a[2] >= b[2] and a[3] <= b[3] and a[4] >= b[4]


class Ins:
    __slots__ = ("eng", "fn", "deps", "need_inc", "cnt", "is_dma", "dsem", "dval", "idx", "prewait", "inc")

    def __init__(self, eng, fn, is_dma):
        self.eng = eng
        self.fn = fn
        self.deps = set()
        self.need_inc = False
        self.cnt = None
        self.is_dma = is_dma
        self.dsem = None
        self.dval = None
        self.prewait = None
        self.inc = 16


class Prog:
    def __init__(self, nc):
        self.nc = nc
        self.ins = []
        self.hist = {}

    def _track(self, I, reads, writes):
        idx = I.idx
        ins = self.ins
        rr_ = [region(a) for a in reads if str(a.space) != "PSUM"]
        wr_ = [region(a) for a in writes] + [region(a) for a in reads if str(a.space) == "PSUM"]
        for r in rr_:
            h = self.hist.setdefault(r[0], [])
            for (rr, j, w) in h:
                if w and overlap(r, rr):
                    J = ins[j]
                    if J.eng == "pe" and I.eng == "pe" and not I.is_dma and not J.is_dma:
                        continue
                    I.deps.add(j)
        for r in wr_:
            h = self.hist.setdefault(r[0], [])
            for (rr, j, w) in h:
                if overlap(r, rr):
                    J = ins[j]
                    if J.eng == I.eng and not I.is_dma and not J.is_dma:
                        continue
                    I.deps.add(j)
        if len(I.deps) > 1:
            best = {}
            keep = set()
            for j in I.deps:
                J = ins[j]
                if J.is_dma:
                    keep.add(j)
                elif best.get(J.eng, -1) < j:
                    best[J.eng] = j
            keep.update(best.values())
            I.deps = keep
        for r in rr_:
            h = self.hist[r[0]]
            if not I.is_dma:
                h[:] = [e for e in h if e[2] or e[0] != r or ins[e[1]].eng != I.eng or ins[e[1]].is_dma]
            h.append((r, idx, False))
        for r in wr_:
            h = self.hist[r[0]]
            h[:] = [e for e in h if not covers(r, e[0])]
            h.append((r, idx, True))

    def op(self, eng, fn, reads=(), writes=()):
        I = Ins(eng, fn, False)
        I.idx = len(self.ins)
        self.ins.append(I)
        self._track(I, reads, writes)
        return I

    def dma(self, q, out, in_, **kw):
        def fn(e, out=out, in_=in_, kw=kw):
            return e.dma_start(out=out, in_=in_, **kw)
        I = Ins(q, fn, True)
        I.idx = len(self.ins)
        self.ins.append(I)
        self._track(I, [in_], [out])
        return I

    def emit(self, stack):
        nc = self.nc
        ins = self.ins
        for I in ins:
            for j in I.deps:
                ins[j].need_inc = True
        csem = {e: stack.enter_context(nc.semaphore("c_" + e)) for e in COMPUTE}
        dsems = {q: [stack.enter_context(nc.semaphore("d_%s_%d" % (q, i))) for i in range(NDMASEM)]
                 for q in ("sp", "act", "pool")}
        cnt = {e: 0 for e in COMPUTE}
        dq_n = {q: 0 for q in dsems}
        dq_val = {q: [0] * NDMASEM for q in dsems}
        dq_last = {q: [None] * NDMASEM for q in dsems}
        for I in ins:
            if I.is_dma:
                k = dq_n[I.eng] % NDMASEM
                dq_n[I.eng] += 1
                I.prewait = dq_last[I.eng][k]
                dq_val[I.eng][k] += I.inc
                I.dsem = dsems[I.eng][k]
                I.dval = dq_val[I.eng][k]
                dq_last[I.eng][k] = I.idx
            elif I.need_inc:
                cnt[I.eng] += 1
                I.cnt = cnt[I.eng]
        self.maxcnt = dict(cnt)
        streams = {e: [] for e in ("pe", "act", "dve", "pool", "sp")}
        for I in ins:
            streams[I.eng].append(I)
        block = stack.enter_context(nc.Block())

        def run_stream(ename, e):
            waited = {}

            def wait_for(j):
                J = ins[j]
                if J.is_dma:
                    key = ("d", J.eng, id(J.dsem))
                    if waited.get(key, 0) >= J.dval:
                        return
                    waited[key] = J.dval
                    e.wait_ge(J.dsem, J.dval)
                else:
                    key = ("c", J.eng)
                    if waited.get(key, 0) >= J.cnt:
                        return
                    waited[key] = J.cnt
                    e.wait_ge(csem[J.eng], J.cnt)

            for I in streams[ename]:
                for j in sorted(I.deps):
                    wait_for(j)
                if I.is_dma and I.prewait is not None:
                    wait_for(I.prewait)
                bi = I.fn(e)
                if I.is_dma:
                    bi.then_inc(I.dsem, I.inc)
                elif I.need_inc:
                    bi.then_inc(csem[I.eng], 1)
            if ename == "sp":
                for q in dsems:
                    for k in range(NDMASEM):
                        if dq_val[q][k] > 0:
                            e.wait_ge(dsems[q][k], dq_val[q][k])
                for ce in COMPUTE:
                    if cnt[ce] > 0:
                        e.wait_ge(csem[ce], cnt[ce])

        @block.tensor
        def _(e):
            run_stream("pe", e)

        @block.scalar
        def _(e):
            run_stream("act", e)

        @block.vector
        def _(e):
            run_stream("dve", e)

        @block.gpsimd
        def _(e):
            run_stream("pool", e)

        @block.sync
        def _(e):
            run_stream("sp", e)


class Bld:
    def __init__(self, nc):
        self.nc = nc
        self.P = Prog(nc)
        self.st = ExitStack()
        self._n = 0

    def sb(self, name, shape, dt):
        return self.st.enter_context(self.nc.sbuf_tensor(name, shape, dt))

    def ps(self, name, shape, dt=F32):
        return self.st.enter_context(self.nc.psum_tensor(name, shape, dt))

    def mm(self, out, lhsT, rhs, start=True, stop=True, skip=False):
        if skip:
            self.P.op("pe", lambda e: e.matmul(out, lhsT=lhsT, rhs=rhs, start=start, stop=stop, skip_group_check=True),
                      [lhsT, rhs, out], [out])
        else:
            self.P.op("pe", lambda e: e.matmul(out, lhsT=lhsT, rhs=rhs, start=start, stop=stop),
                      [lhsT, rhs] + ([] if start else [out]), [out])

    def tr(self, out, in_, ident):
        self.P.op("pe", lambda e: e.transpose(out=out, in_=in_, identity=ident), [in_, ident], [out])

    def act(self, out, in_, func, bias=None, scale=None, accum=None, eng="act"):
        kw = {}
        rd = [in_]
        wr = [out]
        if bias is not None:
            kw["bias"] = bias
            if not isinstance(bias, (int, float)):
                rd.append(bias)
        if scale is not None:
            kw["scale"] = scale
            if not isinstance(scale, (int, float)):
                rd.append(scale)
        if accum is not None:
            kw["accum_out"] = accum
            wr.append(accum)
        self.P.op("act", lambda e: e.activation(out=out, in_=in_, func=func, **kw), rd, wr)

    def tt(self, out, a, b, op, eng="dve"):
        self.P.op(eng, lambda e: e.tensor_tensor(out=out, in0=a, in1=b, op=op), [a, b], [out])

    def ts(self, out, a, s1, op0, s2=None, op1=None, eng="dve", accum=None):
        rd = [a]
        wr = [out]
        if not isinstance(s1, (int, float)):
            rd.append(s1)
        if s2 is not None and not isinstance(s2, (int, float)):
            rd.append(s2)
        kw = {}
        if op1 is not None:
            kw["op1"] = op1
        if accum is not None:
            kw["accum_out"] = accum
            wr.append(accum)
        self.P.op(eng, lambda e: e.tensor_scalar(out=out, in0=a, scalar1=s1, scalar2=s2, op0=op0, **kw), rd, wr)

    def stt(self, out, a, s, b, op0, op1):
        rd = [a, b]
        if not isinstance(s, (int, float)):
            rd.append(s)
        self.P.op("dve", lambda e: e.scalar_tensor_tensor(out=out, in0=a, scalar=s, in1=b, op0=op0, op1=op1), rd, [out])

    def copy(self, out, in_, eng="dve"):
        if eng == "act":
            self.P.op("act", lambda e: e.copy(out=out, in_=in_), [in_], [out])
        else:
            self.P.op(eng, lambda e: e.tensor_copy(out=out, in_=in_), [in_], [out])

    def recip(self, out, in_):
        self.P.op("dve", lambda e: e.reciprocal(out=out, in_=in_), [in_], [out])

    def memset(self, ap, v, eng="pool"):
        self.P.op(eng, lambda e: e.memset(ap, v), [], [ap])

    def dma(self, q, out, in_, **kw):
        self.P.dma(q, out, in_, **kw)

    def finish(self):
        self.P.emit(self.st)
        self.st.close()


class Dense:
    def __init__(self, b, ident_dram):
        self.b = b
        nc = b.nc
        self.identf = b.sb("identf", [128, 128], F32)
        self.ident = b.sb("ident", [128, 128], BF16)
        b.dma("sp", self.identf[:], ident_dram)
        b.copy(self.ident[:], self.identf[:])
        self.hT = b.sb("hT", [128, KC, 576], BF16)
        self.aT = b.sb("aT", [128, FC, 576], BF16)
        self.ybuf = b.sb("ybuf", [128, 5, D], BF16)
        self.xt = [b.sb("xt%d" % i, [128, D], F32) for i in range(2)]
        self.hn = [b.sb("hn%d" % i, [128, D], BF16) for i in range(2)]
        self.junk = b.sb("junk", [128, D], BF16)
        self.st4 = b.sb("st4", [128, 8], F32)
        self.wA = [b.sb("wA%d" % i, [128, KC, 256], BF16) for i in range(2)]
        self.wB = [b.sb("wB%d" % i, [128, KC, 256], BF16) for i in range(2)]
        self.wD = [b.sb("wD%d" % i, [128, FC, 256], BF16) for i in range(2)]
        self.sg = [b.sb("sg%d" % i, [128, 512], BF16) for i in range(2)]
        self.ev = [b.sb("ev%d" % i, [128, 256], F32) for i in range(2)]
        self.psT = [b.ps("psT%d" % i, [128, 8, 128], BF16) for i in range(2)]
        self.psG = [b.ps("psG%d" % i, [128, 512]) for i in range(2)]
        self.psU = [b.ps("psU%d" % i, [128, 512]) for i in range(2)]
        self.psO = [b.ps("psO%d" % i, [128, 512]) for i in range(2)]
        self.nT = 0
        self.nG = 0
        self.nO = 0
        self.nX = 0
        self.nW = 0
        self.nE = 0

    def rstd(self, src, rows, col):
        b = self.b
        ss = self.st4[0:rows, col:col + 1]
        b.act(self.junk[0:rows, :], src, AF.Square, accum=ss)
        b.act(ss, ss, AF.Sqrt, bias=EPS, scale=1.0 / D)
        b.recip(ss, ss)
        return ss

    def norm_to_hT(self, src, rows, tok0, gcol):
        b = self.b
        r = self.rstd(src, rows, 0)
        hn = self.hn[self.nX % 2]
        self.nX += 1
        b.ts(hn[0:rows, :], src, r, ALU.mult)
        if CUT >= 3:
            self.to_T(hn, rows, self.hT, tok0, gcol)

    def to_T(self, src, rows, dstT, tok0, gcol=None, nk=KC):
        b = self.b
        for k4 in range(0, nk, 4):
            pt = self.psT[self.nT % 2]
            self.nT += 1
            for j in range(4):
                kc = k4 + j
                b.tr(pt[:, j, 0:rows], src[0:rows, kc * 128:(kc + 1) * 128], self.ident[0:rows, 0:rows])
            if gcol is None:
                b.copy(dstT[:, k4:k4 + 4, tok0:tok0 + rows], pt[:, 0:4, 0:rows])
            else:
                for j in range(4):
                    kc = k4 + j
                    b.act(dstT[:, kc, tok0:tok0 + rows], pt[:, j, 0:rows], AF.Copy, scale=gcol[:, kc:kc + 1])

    def load_w(self, dst, w_dram, c0, w, nk):
        src = w_dram.rearrange("(kc p) f -> p kc f", p=128)[:, :, c0:c0 + w]
        self.b.dma("pool", dst[:, 0:nk, 0:w], src)

    def gate_up(self, wg, wu, groups):
        b = self.b
        nblk = (DFF + 255) // 256
        blocks = [(i * 256, min(256, DFF - i * 256)) for i in range(nblk)]

        def issue(i):
            c0, w = blocks[i]
            self.load_w(self.wA[i % 2], wg, c0, w, KC)
            self.load_w(self.wB[i % 2], wu, c0, w, KC)
        issue(0)
        for i, (c0, w) in enumerate(blocks):
            if i + 1 < nblk:
                issue(i + 1)
            wa, wb = self.wA[i % 2], self.wB[i % 2]
            for fl in range(w // 128):
                fc = c0 // 128 + fl
                for (t0, n) in groups:
                    pg = self.psG[self.nG % 2]
                    pu = self.psU[self.nG % 2]
                    sg = self.sg[self.nG % 2]
                    self.nG += 1
                    for kc in range(KC):
                        b.mm(pg[:, 0:n], wa[:, kc, fl * 128:(fl + 1) * 128], self.hT[:, kc, t0:t0 + n],
                             start=(kc == 0), stop=(kc == KC - 1))
                    for kc in range(KC):
                        b.mm(pu[:, 0:n], wb[:, kc, fl * 128:(fl + 1) * 128], self.hT[:, kc, t0:t0 + n],
                             start=(kc == 0), stop=(kc == KC - 1))
                    b.act(sg[:, 0:n], pg[:, 0:n], AF.Silu)
                    b.tt(self.aT[:, fc, t0:t0 + n], sg[:, 0:n], pu[:, 0:n], ALU.mult)

    def down(self, wd, tiles):
        b = self.b
        nblk = D // 256

        def issue(i):
            src = wd.rearrange("(fc p) d -> p fc d", p=128)[:, :, i * 256:(i + 1) * 256]
            b.dma("pool", self.wD[i % 2][:], src)
        issue(0)
        for i in range(nblk):
            if i + 1 < nblk:
                issue(i + 1)
            w = self.wD[i % 2]
            for ti, (t0, rows) in enumerate(tiles):
                po = self.psO[self.nO % 2]
                self.nO += 1
                for fc in range(FC):
                    b.mm(po[0:rows, 0:256], self.aT[:, fc, t0:t0 + rows], w[:, fc, :], start=(fc == 0), stop=(fc == FC - 1))
                b.copy(self.ybuf[0:rows, ti, i * 256:(i + 1) * 256], po[0:rows, 0:256], eng="act")

    def proj(self, srcT, nk, w_dram, cols, tiles, sink):
        b = self.b
        c_lo, c_hi = cols
        nblk = (c_hi - c_lo + 255) // 256
        blocks = [(c_lo + i * 256, min(256, c_hi - c_lo - i * 256)) for i in range(nblk)]

        def issue(i):
            c0, w = blocks[i]
            self.load_w(self.wA[(self.nW + i) % 2], w_dram, c0, w, nk)
        issue(0)
        for i, (c0, w) in enumerate(blocks):
            if i + 1 < nblk:
                issue(i + 1)
            wt = self.wA[(self.nW + i) % 2]
            for ti, (t0, rows) in enumerate(tiles):
                po = self.psO[self.nO % 2]
                self.nO += 1
                for kc in range(nk):
                    b.mm(po[0:rows, 0:w], srcT[:, kc, t0:t0 + rows], wt[:, kc, 0:w], start=(kc == 0), stop=(kc == nk - 1))
                sink(ti, t0, rows, c0, w, po)
        self.nW += nblk


def half_tiles(h):
    if h == 0:
        return [(i * 128, 128, i * 128) for i in range(4)] + [(512, 64, T_P)]
    return [(i * 128, 128, 512 + i * 128) for i in range(4)]


def half_groups(h):
    return [(0, 512), (512, 64)] if h == 0 else [(0, 512)]


def build_l1(stages=('win', 'norm', 'gateup', 'down', 'resid', 'proj'), halves=(0, 1)):
    nc = bass.Bass("TRN2", target_bir_lowering=False)
    dt = lambda n, s, k="ExternalInput": nc.dram_tensor(n, s, F32, kind=k).ap()
    x = dt("x", [T_ALL, D])
    wg = dt("wg", [D, DFF]) if 'gateup' in stages else None
    wu = dt("wu", [D, DFF]) if 'gateup' in stages else None
    wd = dt("wd", [DFF, D]) if 'down' in stages else None
    win = dt("win", [D, NPROJ]) if 'proj' in stages else None
    gcols = dt("gcols", [128, 2 * KC])
    gpost = dt("gpost", [128, D])
    identd = dt("identd", [128, 128])
    cwin = dt("cwin", [16, 512, 256])
    x1o = dt("x1o", [T_ALL, D], "ExternalOutput")
    projo = dt("projo", [T_ALL, NPROJ], "ExternalOutput")
    wino = dt("wino", [16, 512, 256], "ExternalOutput")

    b = Bld(nc)
    dn = Dense(b, identd)
    gc = b.sb("gc", [128, 2 * KC], F32)
    gp = b.sb("gp", [128, D], F32)
    b.dma("sp", gc[:], gcols)
    b.dma("sp", gp[:], gpost)
    for bi in range(16 if 'win' in stages else 0):
        b.dma("act", wino[bi, 0:508, :], cwin[bi, 4:512, :])

    for h in halves:
        tiles = half_tiles(h)
        for (t0, rows, g0) in (tiles if 'norm' in stages else []):
            xt = dn.xt[dn.nX % 2]
            b.dma("sp", xt[0:rows, :], x[g0:g0 + rows, :])
            dn.norm_to_hT(xt[0:rows, :], rows, t0, gc[:, 0:KC])
        if 'gateup' in stages:
            dn.gate_up(wg, wu, half_groups(h))
        if 'down' in stages:
            dn.down(wd, [(t0, rows) for (t0, rows, g0) in tiles])
        for ti, (t0, rows, g0) in enumerate(tiles if 'resid' in stages else []):
            xt = dn.xt[dn.nX % 2]
            b.dma("sp", xt[0:rows, :], x[g0:g0 + rows, :])
            y = dn.ybuf[0:rows, ti, :]
            r = dn.rstd(y, rows, 1)
            tmp = dn.hn[(dn.nX + 1) % 2]
            b.stt(tmp[0:rows, :], y, r, gp[0:rows, :], ALU.mult, ALU.mult)
            b.stt(xt[0:rows, :], tmp[0:rows, :], 0.5, xt[0:rows, :], ALU.mult, ALU.add)
            b.dma("sp", x1o[g0:g0 + rows, :], xt[0:rows, :])
            dn.norm_to_hT(xt[0:rows, :], rows, t0, gc[:, KC:2 * KC])

        def sink(ti, t0, rows, c0, w, po, tiles=tiles):
            ev = dn.ev[dn.nE % 2]
            dn.nE += 1
            b.copy(ev[0:rows, 0:w], po[0:rows, 0:w], eng="act")
            g0 = tiles[ti][2]
            b.dma("sp", projo[g0:g0 + rows, c0:c0 + w], ev[0:rows, 0:w])
            if g0 == T_P and c0 == 5632:
                for bi in range(16):
                    b.dma("sp", wino[bi, 508:512, :], ev[bi * 4:(bi + 1) * 4, 0:256])
        if 'proj' in stages:
            dn.proj(dn.hT, KC, win, (0, NPROJ), [(t0, rows) for (t0, rows, g0) in tiles], sink)
    b.finish()
    return nc


def _bcast(v):
    return np.ascontiguousarray(np.broadcast_to(np.asarray(v, np.float32).reshape(1, -1), (128, v.size)))


def _cols(v):
    return np.ascontiguousarray(np.asarray(v, np.float32).reshape(-1, 128).T)


def run_l1(inp):
    nc = build_l1()
    xp = inp["x_prompt"][0]
    xs = inp["x_sample"].reshape(-1, D)
    ident = np.eye(128, dtype=np.float32)
    gcols = np.concatenate([_cols(inp["norm_pre1"][0]), _cols(inp["norm_pre2"][0])], axis=1)
    gpost = _bcast(inp["norm_post1"][0])
    win = np.ascontiguousarray(inp["w_in"][0][:, :NPROJ])
    maps = []
    for c in range(NCORES):
        maps.append({
            "x": np.ascontiguousarray(np.concatenate([xp[c * T_P:(c + 1) * T_P], xs[c * T_S:(c + 1) * T_S]], 0)),
            "wg": inp["ff1_gate"][0], "wu": inp["ff1_up"][0], "wd": inp["ff1_down"][0], "win": win,
            "gcols": gcols, "gpost": gpost, "identd": ident,
            "cwin": np.ascontiguousarray(inp["cache_win"][0, c * 16:(c + 1) * 16].reshape(16, 512, 256)),
        })
    res = run_bass_kernel_spmd(nc, maps, core_ids=list(range(NCORES)))
    return res.results


def kernel(**inp):
    inp = {k: np.asarray(v) for k, v in inp.items()}
    r1 = run_l1(inp)
    proj_p = np.concatenate([r["projo"][:T_P] for r in r1], 0)
    proj_s = np.concatenate([r["projo"][T_P:] for r in r1], 0)
    x1_p = np.concatenate([r["x1o"][:T_P] for r in r1], 0)
    x1_s = np.concatenate([r["x1o"][T_P:] for r in r1], 0)
    kv_prompt = proj_p[:, 5120:5632].reshape(1, 1, 8192, 4, 2, 64)
    kv_sample = proj_s[:, 5120:5632].reshape(1, 128, 4, 4, 2, 64)
    win_prompt = proj_p[8192 - 512:, 5632:5888].reshape(1, 1, 512, 2, 2, 64)
    win_sample = np.concatenate([r["wino"] for r in r1], 0).reshape(1, 128, 512, 2, 2, 64)
    ohg_p, ohg_s, hg_p, hg_s = run_l2h(inp, proj_p, proj_s)
    ons_p = run_l2n_prompt(inp, proj_p)
    ons_s = run_l2n_sample(inp, proj_s)
    y_p, y_s = run_l3(inp, x1_p, x1_s, ohg_p, ohg_s, ons_p, ons_s)
    y_prompt = y_p.reshape(1, 8192, D)
    y_sample = y_s.reshape(128, 4, D)
    return (y_prompt, y_sample, np.ascontiguousarray(kv_prompt), np.ascontiguousarray(kv_sample),
            np.ascontiguousarray(win_prompt), win_sample, hg_p, hg_s)


CH = 32
SEG = 2048


def build_l2h(do_prompt=True, do_sample=True):
    nc = bass.Bass("TRN2", target_bir_lowering=False)
    dt = lambda n, s, k="ExternalInput": nc.dram_tensor(n, s, F32, kind=k).ap()
    qT = dt("qT", [128, 8192])
    fT = dt("fT", [128, 8192])
    v32 = dt("v32", [32, 256, 128])
    g32 = dt("g32", [32, 256, 128])
    lbc = dt("lbc", [128, 2])
    qTs = dt("qTs", [128, 512])
    fTs = dt("fTs", [128, 512])
    v4 = dt("v4", [4, 128, 128])
    g4 = dt("g4", [4, 128, 128])
    lbs = dt("lbs", [128, 16])
    st0 = dt("st0", [16, 8, 128, 128])
    gn32 = dt("gn32", [32, 128])
    rmask = dt("rmask", [128, SEG])
    rmask4 = dt("rmask4", [128, 512])
    trid = dt("trid", [32, 32])
    identd = dt("identd", [128, 128])
    o32 = dt("o32", [32, 256, 128], "ExternalOutput")
    Sp = dt("Sp", [128, 128], "ExternalOutput")
    o4 = dt("o4", [4, 128, 128], "ExternalOutput")
    Ss = dt("Ss", [16, 8, 128, 128], "ExternalOutput")

    b = Bld(nc)
    identf = b.sb("identf", [128, 128], F32)
    ident = b.sb("ident", [128, 128], BF16)
    b.dma("sp", identf[:], identd)
    b.copy(ident[:], identf[:])
    tri = b.sb("tri", [32, 32], F32)
    b.dma("sp", tri[:], trid)
    gn = b.sb("gn", [32, 128], F32)
    b.dma("sp", gn[:], gn32)
    rm = b.sb("rm", [128, SEG], F32)
    b.dma("sp", rm[:], rmask)
    rm4 = b.sb("rm4", [128, 512], F32)
    b.dma("sp", rm4[:], rmask4)
    lbr = b.sb("lbr", [128, 16], F32)
    lbv = b.sb("lbv", [128, 8], F32)
    oml = b.sb("oml", [128, 8], F32)

    qr = b.sb("qr", [128, SEG], F32)
    fr = b.sb("fr", [128, SEG], F32)
    bc = b.sb("bc", [128, SEG], F32)
    kk = b.sb("kk", [128, SEG], F32)
    t1 = b.sb("t1", [128, SEG], F32)
    qb = b.sb("qb", [128, SEG], BF16)
    kb = b.sb("kb", [128, SEG], BF16)
    kd = b.sb("kd", [128, SEG], BF16)
    ebl = b.sb("ebl", [128, 128], F32)
    vs = b.sb("vs", [32, 64, 128], BF16)
    gs = b.sb("gs", [32, 64, 128], F32)
    oa = b.sb("oa", [32, 64, 128], F32)
    sq = b.sb("sq", [32, 64, 128], F32)
    rs = b.sb("rs", [32, 128], F32)
    S = [b.sb("S%d" % i, [128, 128], F32) for i in range(3)]
    Sbf = [b.sb("Sbf%d" % i, [128, 128], BF16) for i in range(3)]
    atm = [b.sb("atm%d" % i, [32, 32], BF16) for i in range(2)]
    kdT = [b.sb("kdT%d" % i, [32, 128], BF16) for i in range(2)]
    psA = [b.ps("psA%d" % i, [128, 512]) for i in range(2)]
    psK = [b.ps("psK%d" % i, [128, 1024], BF16) for i in range(2)]
    psO = [b.ps("psO%d" % i, [128, 512]) for i in range(2)]
    psS = [b.ps("psS%d" % i, [128, 512]) for i in range(2)]
    cnt = [0]

    def prep(q_src, f_src, n, nh, C, rmk):
        per = n // nh
        nch = n // C
        v3 = lambda t: t[:, 0:n].rearrange("p (h m) -> p h m", h=nh)
        lb_bc = lbv[:, 0:nh].unsqueeze(2).broadcast_to([128, nh, per])
        oml_bc = oml[:, 0:nh].unsqueeze(2).broadcast_to([128, nh, per])
        b.dma("sp", qr[:, 0:n], q_src)
        b.dma("act", fr[:, 0:n], f_src)
        b.act(t1[:, 0:n], fr[:, 0:n], AF.Sigmoid)
        b.tt(v3(t1), v3(t1), oml_bc, ALU.mult)
        b.tt(v3(fr), v3(t1), lb_bc, ALU.add)
        b.ts(kk[:, 0:n], fr[:, 0:n], -1.0, ALU.mult, 1.0, ALU.add)
        b.act(t1[:, 0:n], fr[:, 0:n], AF.Ln)
        b.P.op("dve", lambda e: e.tensor_tensor_scan(out=bc[:, 0:n], data0=rmk[:, 0:n], data1=t1[:, 0:n],
                                                     initial=0.0, op0=ALU.mult, op1=ALU.add),
               [rmk[:, 0:n], t1[:, 0:n]], [bc[:, 0:n]])
        b.act(fr[:, 0:n], qr[:, 0:n], AF.Silu)
        b.act(t1[:, 0:n], bc[:, 0:n], AF.Exp)
        b.tt(qb[:, 0:n], fr[:, 0:n], t1[:, 0:n], ALU.mult)
        b.act(t1[:, 0:n], bc[:, 0:n], AF.Exp, scale=-1.0)
        b.tt(kb[:, 0:n], kk[:, 0:n], t1[:, 0:n], ALU.mult)
        bc3 = bc[:, 0:n].rearrange("p (c m) -> p c m", m=C)
        bl_bc = bc3[:, :, C - 1:C].broadcast_to([128, nch, C])
        b.tt(t1[:, 0:n].rearrange("p (c m) -> p c m", m=C), bl_bc, bc3, ALU.subtract)
        b.act(t1[:, 0:n], t1[:, 0:n], AF.Exp)
        b.tt(kd[:, 0:n], kk[:, 0:n], t1[:, 0:n], ALU.mult)
        b.act(ebl[:, 0:nch], bc3[:, :, C - 1], AF.Exp)

    def chunk(ci, C, Sin, Sout, Sbx, cg=None):
        k = cnt[0] % 2
        cnt[0] += 1
        cg = ci if cg is None else cg
        cs = slice(cg * C, (cg + 1) * C)
        pa, pk, po, pS = psA[k], psK[k], psO[k], psS[k]
        b.mm(pa[0:C, 0:C], kb[:, cs], qb[:, cs])
        b.tt(atm[k][0:C, 0:C], pa[0:C, 0:C], tri[0:C, 0:C], ALU.mult)
        b.tr(pk[0:C, 0:128], kd[:, cs], ident[:, :])
        b.copy(kdT[k][0:C, :], pk[0:C, 0:128], eng="act")
        b.mm(pS[:, 0:128], kdT[k][0:C, :], vs[0:C, ci, :])
        b.stt(Sout[:], Sin[:], ebl[:, cg:cg + 1], pS[:, 0:128], ALU.mult, ALU.add)
        b.mm(po[0:C, 0:128], atm[k][0:C, 0:C], vs[0:C, ci, :], start=True, stop=False)
        b.mm(po[0:C, 0:128], qb[:, cs], Sbx[:], start=False, stop=True)
        b.copy(oa[0:C, ci, :], po[0:C, 0:128], eng="act")

    def post(C, nch, g_src, o_dst):
        b.dma("sp", gs[0:C, 0:nch, :], g_src)
        b.tt(sq[0:C, 0:nch, :], oa[0:C, 0:nch, :], oa[0:C, 0:nch, :], ALU.mult)
        b.P.op("dve", lambda e: e.reduce_sum(out=rs[0:C, 0:nch], in_=sq[0:C, 0:nch, :], axis=AX.X),
               [sq[0:C, 0:nch, :]], [rs[0:C, 0:nch]])
        b.act(rs[0:C, 0:nch], rs[0:C, 0:nch], AF.Sqrt, bias=EPS, scale=1.0 / 128)
        b.recip(rs[0:C, 0:nch], rs[0:C, 0:nch])
        b.tt(oa[0:C, 0:nch, :], oa[0:C, 0:nch, :], rs[0:C, 0:nch].unsqueeze(2).broadcast_to([C, nch, 128]), ALU.mult)
        b.tt(oa[0:C, 0:nch, :], oa[0:C, 0:nch, :], gn[0:C, :].unsqueeze(1).broadcast_to([C, nch, 128]), ALU.mult)
        b.act(gs[0:C, 0:nch, :], gs[0:C, 0:nch, :], AF.Silu)
        b.tt(oa[0:C, 0:nch, :], oa[0:C, 0:nch, :], gs[0:C, 0:nch, :], ALU.mult)
        b.dma("sp", o_dst, oa[0:C, 0:nch, :])

    def lower_bounds(src, nh):
        b.dma("sp", lbr[:, 0:2 * nh], src)
        l3 = lbr[:, 0:2 * nh].rearrange("p (h r) -> p h r", r=2)
        b.tt(lbv[:, 0:nh], l3[:, :, 0], l3[:, :, 1], ALU.subtract)
        b.act(lbv[:, 0:nh], lbv[:, 0:nh], AF.Sigmoid)
        b.ts(oml[:, 0:nh], lbv[:, 0:nh], -1.0, ALU.mult, 1.0, ALU.add)

    if do_prompt:
        lower_bounds(lbc, 1)
        b.memset(S[0][:], 0.0)
        b.memset(Sbf[0][:], 0.0)
        nseg = 8192 // SEG
        cps = SEG // CH
        for sg_ in range(nseg):
            prep(qT[:, sg_ * SEG:(sg_ + 1) * SEG], fT[:, sg_ * SEG:(sg_ + 1) * SEG], SEG, 1, CH, rm)
            b.dma("pool", vs[:, 0:cps, :], v32[:, sg_ * cps:(sg_ + 1) * cps, :])
            for ci in range(cps):
                gi = sg_ * cps + ci
                chunk(ci, CH, S[gi % 2], S[(gi + 1) % 2], Sbf[gi % 3])
                b.copy(Sbf[(gi + 1) % 3][:], S[(gi + 1) % 2][:], eng="act")
            post(CH, cps, g32[:, sg_ * cps:(sg_ + 1) * cps, :], o32[:, sg_ * cps:(sg_ + 1) * cps, :])
        b.dma("sp", Sp, S[(8192 // CH) % 2][:])
    if do_sample:
        lower_bounds(lbs, 8)
        prep(qTs, fTs, 512, 8, 4, rm4)
        for grp in range(2):
            b.dma("pool", vs[0:4, 0:64, :], v4[:, grp * 64:(grp + 1) * 64, :])
            for jl in range(64):
                j = grp * 64 + jl
                h_, b_ = j // 16, j % 16
                Sx, Sbx = S[j % 3], Sbf[j % 3]
                b.dma("sp", Sx[:], st0[b_, h_])
                b.copy(Sbx[:], Sx[:], eng="act")
                chunk(jl, 4, Sx, Sx, Sbx, cg=j)
                b.dma("sp", Ss[b_, h_], Sx[:])
            post(4, 64, g4[:, grp * 64:(grp + 1) * 64, :], o4[:, grp * 64:(grp + 1) * 64, :])
    b.finish()
    return nc


def l2h_consts():
    rmask = np.ones((128, SEG), np.float32)
    rmask[:, ::CH] = 0.0
    rmask4 = np.ones((128, 512), np.float32)
    rmask4[:, ::4] = 0.0
    tri = np.triu(np.ones((32, 32), np.float32))
    return {"rmask": rmask, "rmask4": rmask4, "trid": tri, "identd": np.eye(128, dtype=np.float32)}


def run_l2h(inp, proj_p, proj_s):
    nc = build_l2h()
    cst = l2h_consts()
    gn32 = _bcast(inp["hg_gnorm"][0])[:32]
    hg_lb = inp["hg_lb"]
    maps = []
    ps4 = proj_s.reshape(128, 4, NPROJ)
    for c in range(NCORES):
        hs = slice(c * 128, (c + 1) * 128)
        m = dict(cst)
        m["gn32"] = np.ascontiguousarray(gn32)
        m["qT"] = np.ascontiguousarray(proj_p[:, 0 * 1024:][:, hs].T)
        m["fT"] = np.ascontiguousarray(proj_p[:, 1 * 1024:][:, hs].T)
        m["v32"] = np.ascontiguousarray(proj_p[:, 2 * 1024:][:, hs].reshape(256, 32, 128).transpose(1, 0, 2))
        m["g32"] = np.ascontiguousarray(proj_p[:, 3 * 1024:][:, hs].reshape(256, 32, 128).transpose(1, 0, 2))
        m["lbc"] = np.ascontiguousarray(hg_lb[:, hs].T)
        sb = ps4[c * 16:(c + 1) * 16]
        part = lambda k: sb[:, :, k * 1024:(k + 1) * 1024].reshape(16, 4, 8, 128)
        m["qTs"] = np.ascontiguousarray(part(0).transpose(3, 2, 0, 1).reshape(128, 512))
        m["fTs"] = np.ascontiguousarray(part(1).transpose(3, 2, 0, 1).reshape(128, 512))
        m["v4"] = np.ascontiguousarray(part(2).transpose(1, 2, 0, 3).reshape(4, 128, 128))
        m["g4"] = np.ascontiguousarray(part(3).transpose(1, 2, 0, 3).reshape(4, 128, 128))
        m["lbs"] = np.ascontiguousarray(hg_lb.reshape(2, 8, 128).transpose(2, 1, 0).reshape(128, 16))
        m["st0"] = np.ascontiguousarray(inp["state_hgrn"][0, c * 16:(c + 1) * 16])
        maps.append(m)
    res = run_bass_kernel_spmd(nc, maps, core_ids=list(range(NCORES)))
    r = res.results
    o_p = np.concatenate([r[c]["o32"].transpose(1, 0, 2).reshape(8192, 128) for c in range(NCORES)], axis=1)
    hg_p = np.stack([r[c]["Sp"] for c in range(NCORES)], 0).reshape(1, 1, 8, 128, 128)
    o_s = np.concatenate([r[c]["o4"].reshape(4, 8, 16, 128).transpose(2, 0, 1, 3).reshape(16, 4, 1024)
                          for c in range(NCORES)], 0)
    hg_s = np.concatenate([r[c]["Ss"] for c in range(NCORES)], 0).reshape(1, 128, 8, 128, 128)
    return o_p, o_s.reshape(512, 1024), hg_p, hg_s


def build_l3():
    nc = bass.Bass("TRN2", target_bir_lowering=False)
    dt = lambda n, s, k="ExternalInput": nc.dram_tensor(n, s, F32, kind=k).ap()
    x1 = dt("x1", [T_ALL, D])
    ohgT = dt("ohgT", [1024, T_ALL])
    onsT = dt("onsT", [1024, T_ALL])
    wgab = dt("wgab", [D, 2 * D])
    wphg = dt("wphg", [1024, D])
    wpns = dt("wpns", [1024, D])
    wout = dt("wout", [D, D])
    wg = dt("wg", [D, DFF])
    wu = dt("wu", [D, DFF])
    wd = dt("wd", [DFF, D])
    gcols = dt("gcols", [128, 2 * KC])
    gpost2 = dt("gpost2", [128, D])
    gpost3 = dt("gpost3", [128, D])
    identd = dt("identd", [128, 128])
    yo = dt("yo", [T_ALL, D], "ExternalOutput")
    x2s = nc.dram_tensor("x2s", [T_ALL, D], F32).ap()

    b = Bld(nc)
    dn = Dense(b, identd)
    gc = b.sb("gc", [128, 2 * KC], F32)
    gp = b.sb("gp", [128, D], F32)
    b.dma("sp", gc[:], gcols)
    oT = [dn.aT[:, 0:8, :], dn.aT[:, 8:16, :]]
    sga = dn.sg

    for h in range(2):
        tiles = half_tiles(h)
        ntok = 576 if h == 0 else 512
        for (t0, rows, g0) in tiles:
            xt = dn.xt[dn.nX % 2]
            b.dma("sp", xt[0:rows, :], x1[g0:g0 + rows, :])
            dn.norm_to_hT(xt[0:rows, :], rows, t0, gc[:, 0:KC])
        for src, dst in ((ohgT, oT[0]), (onsT, oT[1])):
            s3 = src.rearrange("(kc p) t -> p kc t", p=128)
            if h == 0:
                b.dma("pool", dst[:, :, 0:512], s3[:, :, 0:512])
                b.dma("pool", dst[:, :, 512:576], s3[:, :, T_P:T_P + 64])
            else:
                b.dma("pool", dst[:, :, 0:512], s3[:, :, 512:1024])
        nblk = D // 256

        def issue(i):
            dn.load_w(dn.wA[i % 2], wgab, i * 256, 256, KC)
            dn.load_w(dn.wB[i % 2], wgab, D + i * 256, 256, KC)
            dn.load_w(dn.wD[i % 2][:, 0:8, :], wphg, i * 256, 256, 8)
            dn.load_w(dn.wD[i % 2][:, 8:16, :], wpns, i * 256, 256, 8)
        issue(0)
        for i in range(nblk):
            if i + 1 < nblk:
                issue(i + 1)
            wa, wb, wc = dn.wA[i % 2], dn.wB[i % 2], dn.wD[i % 2]
            for ti, (t0, rows, g0) in enumerate(tiles):
                k = dn.nG % 2
                dn.nG += 1
                pga, pgb, ph, pn = dn.psG[k], dn.psU[k], dn.psO[0], dn.psO[1]
                for kc in range(KC):
                    b.mm(pga[0:rows, 0:256], dn.hT[:, kc, t0:t0 + rows], wa[:, kc, :], start=(kc == 0), stop=(kc == KC - 1))
                for kc in range(KC):
                    b.mm(pgb[0:rows, 0:256], dn.hT[:, kc, t0:t0 + rows], wb[:, kc, :], start=(kc == 0), stop=(kc == KC - 1))
                for kc in range(8):
                    b.mm(ph[0:rows, 0:256], oT[0][:, kc, t0:t0 + rows], wc[:, kc, :], start=(kc == 0), stop=(kc == 7))
                for kc in range(8):
                    b.mm(pn[0:rows, 0:256], oT[1][:, kc, t0:t0 + rows], wc[:, 8 + kc, :], start=(kc == 0), stop=(kc == 7))
                s1, s2 = dn.ev[0], dn.ev[1]
                b.act(s1[0:rows, :], pga[0:rows, 0:256], AF.Sigmoid)
                b.act(s2[0:rows, :], pgb[0:rows, 0:256], AF.Sigmoid)
                b.tt(s1[0:rows, :], s1[0:rows, :], ph[0:rows, 0:256], ALU.mult)
                b.tt(s2[0:rows, :], s2[0:rows, :], pn[0:rows, 0:256], ALU.mult)
                b.tt(dn.ybuf[0:rows, ti, i * 256:(i + 1) * 256], s1[0:rows, :], s2[0:rows, :], ALU.add)
        for ti, (t0, rows, g0) in enumerate(tiles):
            dn.to_T(dn.ybuf[:, ti, :], rows, dn.hT, t0)

        def sink(ti, t0, rows, c0, w, po):
            b.copy(dn.ybuf[0:rows, ti, c0:c0 + w], po[0:rows, 0:w], eng="act")
        dn.proj(dn.hT, KC, wout, (0, D), [(t0, rows) for (t0, rows, g0) in tiles], sink)
        b.dma("sp", gp[:], gpost2)
        for ti, (t0, rows, g0) in enumerate(tiles):
            xt = dn.xt[dn.nX % 2]
            b.dma("sp", xt[0:rows, :], x1[g0:g0 + rows, :])
            y = dn.ybuf[0:rows, ti, :]
            r = dn.rstd(y, rows, 1)
            tmp = dn.hn[(dn.nX + 1) % 2]
            b.stt(tmp[0:rows, :], y, r, gp[0:rows, :], ALU.mult, ALU.mult)
            b.tt(xt[0:rows, :], tmp[0:rows, :], xt[0:rows, :], ALU.add)
            b.dma("sp", x2s[g0:g0 + rows, :], xt[0:rows, :])
            dn.norm_to_hT(xt[0:rows, :], rows, t0, gc[:, KC:2 * KC])
        dn.gate_up(wg, wu, half_groups(h))
        dn.down(wd, [(t0, rows) for (t0, rows, g0) in tiles])
        b.dma("sp", gp[:], gpost3)
        for ti, (t0, rows, g0) in enumerate(tiles):
            xt = dn.xt[dn.nX % 2]
            dn.nX += 1
            b.dma("sp", xt[0:rows, :], x2s[g0:g0 + rows, :])
            y = dn.ybuf[0:rows, ti, :]
            r = dn.rstd(y, rows, 1)
            tmp = dn.hn[ti % 2]
            b.stt(tmp[0:rows, :], y, r, gp[0:rows, :], ALU.mult, ALU.mult)
            b.stt(xt[0:rows, :], tmp[0:rows, :], 0.5, xt[0:rows, :], ALU.mult, ALU.add)
            b.dma("sp", yo[g0:g0 + rows, :], xt[0:rows, :])
    b.finish()
    return nc


def run_l3(inp, x1_p, x1_s, ohg_p, ohg_s, ons_p, ons_s):
    nc = build_l3()
    gcols = np.concatenate([_cols(inp["norm_pre2"][0]), _cols(inp["norm_pre3"][0])], axis=1)
    wgab = np.ascontiguousarray(inp["w_in"][0][:, NPROJ:])
    base = {"wgab": wgab, "wphg": inp["w_proj_hg"][0], "wpns": inp["w_proj_nsa"][0], "wout": inp["w_out"][0],
            "wg": inp["ff2_gate"][0], "wu": inp["ff2_up"][0], "wd": inp["ff2_down"][0], "gcols": gcols,
            "gpost2": _bcast(inp["norm_post2"][0]), "gpost3": _bcast(inp["norm_post3"][0]),
            "identd": np.eye(128, dtype=np.float32)}
    maps = []
    for c in range(NCORES):
        m = dict(base)
        ps, ss = slice(c * T_P, (c + 1) * T_P), slice(c * T_S, (c + 1) * T_S)
        m["x1"] = np.ascontiguousarray(np.concatenate([x1_p[ps], x1_s[ss]], 0))
        m["ohgT"] = np.ascontiguousarray(np.concatenate([ohg_p[ps], ohg_s[ss]], 0).T)
        m["onsT"] = np.ascontiguousarray(np.concatenate([ons_p[ps], ons_s[ss]], 0).T)
        maps.append(m)
    res = run_bass_kernel_spmd(nc, maps, core_ids=list(range(NCORES)))
    y_p = np.concatenate([r["yo"][:T_P] for r in res.results], 0)
    y_s = np.concatenate([r["yo"][T_P:] for r in res.results], 0)
    return y_p, y_s


NEG = -32768.0
SLOPES = np.power(2.0, -8.0 * np.arange(1, 17) / 16).astype(np.float64)
GELU_C = 1.5957691216057308


def nsa_prompt_consts(core):
    tiles = [core + 8 * j for j in range(8)]
    p = np.arange(128)
    c = {}
    c["gmat"] = (np.arange(128)[:, None] == (np.arange(8192)[None, :] // 64)).astype(np.float32)
    cs = np.arange(512)[:, None] * 16
    ss = np.arange(128)[None, :] * 64
    ov = ((cs <= ss + 63) & (cs + 31 >= ss)).astype(np.float32)
    ov[511] = 0.0
    c["ovl"] = np.ascontiguousarray(ov.reshape(4, 128, 128).transpose(1, 0, 2))
    r = np.arange(72) - (7 - core)
    c["btab"] = np.ascontiguousarray((SLOPES[None, :, None] * (p[:, None, None] - 64 - 128 * r[None, None, :])).astype(np.float32).reshape(128, 16 * 72))
    cb = np.zeros((128, 8, 16, 4), np.float64)
    cm = np.zeros((128, 8, 2, 128), np.float32)
    keep = np.zeros((128, 8, 128), np.float32)
    add = np.zeros((128, 8, 128), np.float32)
    tt = np.arange(128)
    blk = np.arange(128)
    for j, i in enumerate(tiles):
        t0 = 128 * i
        for ct in range(4):
            cb[:, j, :, ct] = SLOPES[None, :] * (16 * (128 * ct + p[:, None]) + 31 - (t0 + 64))
        nct = i // 16 + 1
        for rr in range(2):
            ct = nct - 1 - rr
            if ct < 0:
                continue
            cpos = 16 * (128 * ct + p) + 31
            ok = (cpos[:, None] <= (t0 + tt)[None, :]) & ((128 * ct + p) < 511)[:, None]
            cm[:, j, rr, :] = np.where(ok, 0.0, NEG)
        qpos = t0 + tt
        qb = qpos // 64
        valid = blk[None, :] <= qb[:, None]
        f0 = blk[None, :] == 0
        f1 = blk[None, :] == qb[:, None]
        f2 = blk[None, :] == (qb[:, None] - 1)
        forced = f0 | f1 | f2
        keep[:, j, :] = (valid & ~forced).astype(np.float32)
        a = np.where(valid, 0.0, -1e30)
        a = np.where(f2, 1e4, a)
        a = np.where(f1, 2e4, a)
        a = np.where(f0, 3e4, a)
        add[:, j, :] = a
    c["cbias"] = np.ascontiguousarray(cb.astype(np.float32).reshape(128, 8 * 16 * 4))
    c["cmask"] = np.ascontiguousarray(cm.reshape(128, 8 * 2 * 128))
    c["keepm"] = np.ascontiguousarray(keep.reshape(128, 8 * 128))
    c["addm"] = np.ascontiguousarray(add.reshape(128, 8 * 128))
    causal = np.where(p[:, None] <= tt[None, :], 0.0, NEG).astype(np.float32)
    wlow = np.where(p[:, None] > tt[None, :], 0.0, NEG).astype(np.float32)
    zero = np.zeros((128, 128), np.float32)
    full = np.full((128, 128), NEG, np.float32)
    dms = [zero if q < core else (causal if q == core else full) for q in range(8)]
    dmw = []
    for q in range(12):
        if q < core or q > core + 4:
            dmw.append(full)
        elif q == core:
            dmw.append(wlow)
        elif q == core + 4:
            dmw.append(causal)
        else:
            dmw.append(zero)
    c["dms"] = np.ascontiguousarray(np.stack(dms, 1).reshape(128, 8 * 128))
    c["dmw"] = np.ascontiguousarray(np.stack(dmw, 1).reshape(128, 12 * 128))
    c["identd"] = np.eye(128, dtype=np.float32)
    import ml_dtypes
    bf = lambda x: np.asarray(x, np.float64).astype(ml_dtypes.bfloat16).astype(np.float64)
    tab = np.zeros((5, 72, 16), np.float64)
    rr = np.arange(72) - (7 - core)
    for h in range(16):
        a = 8.0 * SLOPES[h]
        a0 = bf(a)
        a1 = bf(a - a0)
        cc = -1024.0 * rr * SLOPES[h]
        c0 = bf(cc)
        c1 = bf(cc - c0)
        c2 = bf(cc - c0 - c1)
        tab[0, :, h] = a0
        tab[1, :, h] = a1
        tab[2, :, h] = c0
        tab[3, :, h] = c1
        tab[4, :, h] = c2
    c["btab5"] = np.ascontiguousarray(tab.astype(np.float32).reshape(5, 72 * 16))
    bl = np.ones((5, 128), np.float32)
    bl[0] = p - 64
    bl[1] = p - 64
    c["biasl"] = bl
    return c


class Nsa:
    def __init__(self, b, nc, dt):
        self.b = b
        identd = dt("identd", [128, 128])
        self.identf = b.sb("identf", [128, 128], F32)
        self.ident = b.sb("ident", [128, 128], BF16)
        b.dma("sp", self.identf[:], identd)
        b.copy(self.ident[:], self.identf[:])
        self.w1 = {}
        self.w2 = {}
        self.posT = {}
        for kind in ("k", "v"):
            w1d = dt("w1" + kind, [128, 32, 256])
            w2d = dt("w2" + kind, [128, 4, 128])
            pd = dt("pos" + kind, [128, 32])
            self.w1[kind] = b.sb("s_w1" + kind, [128, 32, 256], BF16)
            self.w2[kind] = b.sb("s_w2" + kind, [128, 4, 128], BF16)
            self.posT[kind] = b.sb("s_pos" + kind, [128, 32], BF16)
            b.dma("pool", self.w1[kind][:], w1d)
            b.dma("pool", self.w2[kind][:], w2d)
            b.dma("pool", self.posT[kind][:], pd)
        self.bcol = b.sb("bcol", [128, 4], F32)
        self.xs = [b.sb("xs%d" % i, [128, 2064], BF16) for i in range(2)]
        self.gh = [b.sb("gh%d" % i, [128, 128], BF16) for i in range(4)]
        self.tx = b.sb("tx", [128, 128], F32)
        self.tu = b.sb("tu", [128, 128], F32)
        self.psS = [b.ps("psS%d" % i, [128, 512]) for i in range(2)]
        self.psAcc = [b.ps("psAcc%d" % i, [128, 512]) for i in range(2)]
        self.psH = [b.ps("psH%d" % i, [128, 512]) for i in range(2)]
        self.psK2 = b.ps("psK2", [128, 512])
        self.psT = b.ps("psT", [128, 1024], BF16)
        self.PT = [b.sb("PT%d" % i, [128, 128], BF16) for i in range(3)]
        self.PT4 = None
        self.nS = 0
        self.nA = 0
        self.nH = 0
        self.nP = 0
        self.nX = 0
        self.bias_done = False

    def cmp_bias(self):
        b = self.b
        for ki, kind in enumerate(("k", "v")):
            for hc in range(2):
                ph = self.psH[self.nH % 2]
                self.nH += 1
                for l in range(32):
                    b.mm(ph[:, 0:1], self.w1[kind][0:64, l, hc * 128:(hc + 1) * 128], self.posT[kind][0:64, l:l + 1],
                         start=(l == 0), stop=(l == 31))
                b.copy(self.bcol[:, ki * 2 + hc:ki * 2 + hc + 1], ph[:, 0:1], eng="act")

    def compress_tile(self, kind, src_dram_cols, N, kdst=None, vdst=None, xs_ap=None):
        b = self.b
        ki = 0 if kind == "k" else 1
        L = 16 * (N - 1) + 32
        if xs_ap is not None:
            xs = xs_ap
        else:
            xs = self.xs[self.nX % 2]
            self.nX += 1
            b.dma("pool", xs[:, 0:L], src_dram_cols)
        for n in range(2):
            for hc in range(2):
                ph = self.psH[self.nH % 2]
                self.nH += 1
                for l in range(32):
                    b.mm(ph[:, 0:N], self.w1[kind][n * 64:(n + 1) * 64, l, hc * 128:(hc + 1) * 128],
                         xs[n * 64:(n + 1) * 64, l:l + 16 * (N - 1) + 1:16], start=(l == 0), stop=(l == 31))
                tx, tu, gh = self.tx, self.tu, self.gh[n * 2 + hc]
                b.act(tx[:, 0:N], ph[:, 0:N], AF.Identity, bias=self.bcol[:, ki * 2 + hc:ki * 2 + hc + 1])
                b.tt(tu[:, 0:N], tx[:, 0:N], tx[:, 0:N], ALU.mult)
                b.ts(tu[:, 0:N], tu[:, 0:N], 0.044715, ALU.mult, 1.0, ALU.add)
                b.tt(tu[:, 0:N], tu[:, 0:N], tx[:, 0:N], ALU.mult)
                b.act(tu[:, 0:N], tu[:, 0:N], AF.Sigmoid, scale=GELU_C)
                b.tt(gh[:, 0:N], tx[:, 0:N], tu[:, 0:N], ALU.mult)
        pk = self.psK2
        if kind == "k":
            for q in range(4):
                b.mm(pk[:, 0:N], self.w2[kind][:, q, :], self.gh[q][:, 0:N], start=(q == 0), stop=(q == 3))
            b.copy(kdst, pk[:, 0:N], eng="act")
        else:
            for q in range(4):
                b.mm(pk[0:N, 0:128], self.gh[q][:, 0:N], self.w2[kind][:, q, :], start=(q == 0), stop=(q == 3))
            b.copy(vdst[0], pk[0:N, 0:64], eng="act")
            b.copy(vdst[1], pk[0:N, 64:128], eng="act")

    def branch(self, steps, qrhs, nq, ncols, scale=0.125):
        b = self.b
        pacc = self.psAcc[self.nA % 2]
        self.nA += 1
        ns = len(steps)
        pts = {}

        def front(si):
            st = steps[si]
            ps = self.psS[self.nS % 2]
            self.nS += 1
            ex = st.get("extra", [])
            rows = st.get("rows", 128)
            b.mm(ps[0:rows, 0:nq], st["k"], qrhs, start=True, stop=(len(ex) == 0))
            for ei, (l_, r_) in enumerate(ex):
                b.mm(ps[0:rows, 0:nq], l_, r_, start=False, stop=(ei == len(ex) - 1))
            pt = self.PT[self.nP % 3]
            self.nP += 1
            pts[si] = pt
            if st.get("bias") is not None:
                b.act(pt[0:rows, 0:nq], ps[0:rows, 0:nq], AF.Exp, bias=st["bias"], scale=scale)
            else:
                b.act(pt[0:rows, 0:nq], ps[0:rows, 0:nq], AF.Exp, scale=scale)

        def back(si):
            st = steps[si]
            rows = st.get("rows", 128)
            b.mm(pacc[0:nq, 0:ncols], pts[si][0:rows, 0:nq], st["v"], start=(si == 0), stop=(si == ns - 1))

        for si in range(ns + 1):
            if si < ns:
                front(si)
            if si >= 1:
                back(si - 1)
        return pacc


def branch4(ns, b, steps, qrhs, biasl):
    pacc = ns.psAcc[ns.nA % 2]
    ns.nA += 1
    nst = len(steps)
    v4 = lambda ap: ap.unsqueeze(1).broadcast_to([ap.shape[0], 4, 128])
    banks = [ns.psS[0], ns.psS[1], ns.psH[0], ns.psH[1]]
    pts = {}

    def front(si):
        st = steps[si]
        ps = banks[ns.nS % 4]
        ns.nS += 1
        po = ps[:, 0:512].rearrange("p (g t) -> p g t", g=4)
        b.mm(po, st["k"], qrhs, start=True, stop=False)
        for (l_, r_) in st["extra"]:
            b.mm(po, l_, v4(r_), start=False, stop=False)
        b.mm(po, biasl, st["brow"].unsqueeze(2).broadcast_to([5, 4, 128]), start=False, stop=True)
        pt = ns.PT4[ns.nP % len(ns.PT4)]
        ns.nP += 1
        pts[si] = pt
        b.act(pt[:, :], ps[:, 0:512], AF.Exp, scale=0.125)

    def back(si):
        st = steps[si]
        for hl in range(4):
            b.mm(pacc[:, hl * 65:(hl + 1) * 65], pts[si][:, hl * 128:(hl + 1) * 128], st["v"],
                 start=(si == 0 and hl == 0), stop=(si == nst - 1), skip=True)

    DEP = 2
    for si in range(nst + DEP):
        if si < nst:
            front(si)
        if si >= DEP:
            back(si - DEP)
    return pacc


def build_l2n():
    nc = bass.Bass("TRN2", target_bir_lowering=False)
    dt = lambda n, s, k="ExternalInput": nc.dram_tensor(n, s, F32, kind=k).ap()
    b = Bld(nc)
    ns = Nsa(b, nc, dt)
    qTd = dt("qT", [128, 8, 1024])
    gated = dt("gates", [128, 8, 48])
    KsTd = dt("KsT", [128, 8192])
    KwTd = dt("KwT", [128, 8192])
    KcTd = dt("KcT", [128, 8192])
    VcTd = dt("VcT", [128, 8192])
    Vsd = dt("Vs", [128, 64, 128])
    Vwd = dt("Vw", [128, 64, 128])
    gmatd = dt("gmat", [128, 8192])
    ovld = dt("ovl", [128, 4, 128])
    btabd = dt("btab", [128, 16 * 72])
    cbiasd = dt("cbias", [128, 512])
    cmaskd = dt("cmask", [128, 2048])
    keepd = dt("keepm", [128, 1024])
    addd = dt("addm", [128, 1024])
    dmsd = dt("dms", [128, 8 * 128])
    dmwd = dt("dmw", [128, 12 * 128])
    btab5d = dt("btab5", [5, 72 * 16])
    biasld = dt("biasl", [5, 128])
    onso = dt("ons", [128, 8, 1024], "ExternalOutput")

    sbt = b.sb
    KsT = sbt("s_KsT", [128, 8192], BF16)
    KwT = sbt("s_KwT", [128, 8192], BF16)
    Vs = sbt("Vsa", [128, 64, 2, 65], BF16)
    Vw = sbt("Vwa", [128, 64, 2, 65], BF16)
    G = sbt("G", [128, 8192], BF16)
    qT = sbt("qTb", [128, 8, 1024], BF16)
    KCT = sbt("KCT", [128, 512], BF16)
    VCO = sbt("VCO", [128, 4, 2, 193], BF16)
    btab = sbt("s_btab", [128, 16 * 72], F32)
    cbias = sbt("s_cbias", [128, 512], F32)
    cmask = sbt("s_cmask", [128, 2048], BF16)
    keepm = sbt("s_keepm", [128, 1024], F32)
    addm = sbt("s_addm", [128, 1024], F32)
    dms = sbt("s_dms", [128, 8, 128], BF16)
    dmw = sbt("s_dmw", [128, 12, 128], BF16)
    btab5 = sbt("s_btab5", [5, 72, 16], BF16)
    biasl = sbt("s_biasl", [5, 128], BF16)
    ns.PT4 = [sbt("PT4_%d" % i, [128, 512], BF16) for i in range(5)]
    b.dma("pool", btab5[:], btab5d.rearrange("k (r h) -> k r h", r=72))
    b.dma("pool", biasl[:], biasld)
    gts = sbt("gts", [128, 8, 48], F32)
    sc = sbt("sc", [128, 2, 128], F32)
    s2 = sbt("s2", [128, 128], F32)
    s3 = sbt("s3", [128, 128], F32)
    m8 = sbt("m8", [128, 16], F32)
    nm = sbt("nm", [128, 128], BF16)
    nmT = sbt("nmT", [128, 2, 128], BF16)
    rd = sbt("rd", [128, 8], F32)
    oacc = [sbt("oacc%d" % i, [128, 1024], F32) for i in range(2)]

    for d_, s_ in ((KsT, KsTd), (KwT, KwTd), (G, gmatd)):
        for q in range(4):
            b.dma("pool", d_[:, q * 2048:(q + 1) * 2048], s_[:, q * 2048:(q + 1) * 2048])
    b.memset(Vs[:], 1.0)
    b.memset(Vw[:], 1.0)
    b.memset(VCO[:], 0.0)
    b.memset(KCT[:], 0.0)
    for d_, s_ in ((Vs, Vsd), (Vw, Vwd)):
        for q in range(4):
            b.dma("pool", d_[:, q * 16:(q + 1) * 16, :, 0:64], s_[:, q * 16:(q + 1) * 16, :].rearrange("p k (n d) -> p k n d", n=2))
    b.dma("pool", qT[:], qTd)
    b.dma("sp", btab[:], btabd)
    b.dma("sp", cbias[:], cbiasd)
    b.dma("pool", cmask[:], cmaskd)
    b.dma("sp", keepm[:], keepd)
    b.dma("sp", addm[:], addd)
    b.dma("pool", dms[:], dmsd.rearrange("p (q t) -> p q t", q=8))
    b.dma("pool", dmw[:], dmwd.rearrange("p (q t) -> p q t", q=12))
    b.dma("sp", gts[:], gated)
    b.act(gts[:], gts[:], AF.Sigmoid)

    ns.cmp_bias()
    b.memset(VCO[:, :, :, 64:65], 1.0)
    for n in range(2):
        b.dma("pool", VCO[:, :, n, 65:193], ovld)
    for ct in range(4):
        N = 128 if ct < 3 else 127
        L = 16 * (N - 1) + 32
        ns.compress_tile("k", KcTd[:, ct * 2048:ct * 2048 + L], N, kdst=KCT[:, ct * 128:ct * 128 + N])
        ns.compress_tile("v", VcTd[:, ct * 2048:ct * 2048 + L], N,
                         vdst=[VCO[0:N, ct, 0, 0:64], VCO[0:N, ct, 1, 0:64]])

    for j in range(8):
        oa = oacc[j % 2]
        qs = slice(j * 128, (j + 1) * 128)
        nct = j // 2 + 1
        for n in range(2):
            pb = slice(n * 64, (n + 1) * 64)
            b.memset(sc[:, n, :], 0.0, eng="dve")
            for g in range(8):
                h = n * 8 + g
                steps = []
                for ct in range(nct):
                    st = {"k": KCT[pb, ct * 128:(ct + 1) * 128], "v": VCO[:, ct, n, :],
                          "bias": cbias[:, (j * 16 + h) * 4 + ct:(j * 16 + h) * 4 + ct + 1]}
                    rr = nct - 1 - ct
                    if rr < 2:
                        st["extra"] = [(ns.ident[:, :], cmask[:, (j * 2 + rr) * 128:(j * 2 + rr + 1) * 128])]
                    steps.append(st)
                pc = ns.branch(steps, qT[pb, g, qs], 128, 193)
                b.ts(rd[:, 0:1], pc[:, 64:65], 1e-30, ALU.max)
                b.recip(rd[:, 0:1], rd[:, 0:1])
                b.stt(sc[:, n, :], pc[:, 65:193], rd[:, 0:1], sc[:, n, :], ALU.mult, ALU.add)
                b.tt(rd[:, 1:2], rd[:, 0:1], gts[:, j, h * 3:h * 3 + 1], ALU.mult)
                b.ts(oa[:, h * 64:(h + 1) * 64], pc[:, 0:64], rd[:, 1:2], ALU.mult)
            b.tt(s2[:], sc[:, n, :], keepm[:, j * 128:(j + 1) * 128], ALU.mult)
            b.tt(s2[:], s2[:], addm[:, j * 128:(j + 1) * 128], ALU.add)
            b.P.op("dve", lambda e: e.max(out=m8[:, 0:8], in_=s2[:]), [s2[:]], [m8[:, 0:8]])
            b.P.op("dve", lambda e: e.match_replace(out=s3[:], in_to_replace=m8[:, 0:8], in_values=s2[:], imm_value=-1e30),
                   [s2[:], m8[:, 0:8]], [s3[:]])
            b.P.op("dve", lambda e: e.max(out=m8[:, 8:16], in_=s3[:]), [s3[:]], [m8[:, 8:16]])
            b.ts(s3[:], s2[:], m8[:, 15:16], ALU.is_ge)
            b.ts(nm[:], s3[:], -NEG, ALU.mult, NEG, ALU.add)
            b.tr(ns.psT[:, 0:128], nm[:], ns.ident[:, :])
            b.copy(nmT[:, n, :], ns.psT[:, 0:128])
        for hg in range(4):
            n, g0 = hg // 2, (hg % 2) * 4
            h0 = hg * 4
            pb = slice(n * 64, (n + 1) * 64)
            qr = qT[pb, g0:g0 + 4, qs]
            steps = []
            for kt in range(8 * j + 8):
                ex = [(G[:, kt * 128:(kt + 1) * 128], nmT[:, n, :])]
                if kt >= 8 * j:
                    ex.append((ns.ident[:, :], dms[:, kt - 8 * j, :]))
                rp = 8 * j + 7 - kt
                steps.append({"k": KsT[pb, kt * 128:(kt + 1) * 128], "v": Vs[:, kt, n, :], "extra": ex,
                              "brow": btab5[:, rp, h0:h0 + 4]})
            pc = branch4(ns, b, steps, qr, biasl[:, :])
            for hl in range(4):
                h = h0 + hl
                den = pc[:, hl * 65 + 64:hl * 65 + 65]
                b.ts(rd[:, 2:3], den, 1e-30, ALU.max)
                b.recip(rd[:, 2:3], rd[:, 2:3])
                b.tt(rd[:, 3:4], rd[:, 2:3], gts[:, j, h * 3 + 1:h * 3 + 2], ALU.mult)
                b.stt(oa[:, h * 64:(h + 1) * 64], pc[:, hl * 65:hl * 65 + 64], rd[:, 3:4], oa[:, h * 64:(h + 1) * 64], ALU.mult, ALU.add)
            steps = []
            for q in range(12):
                kt = 8 * j - 4 + q
                if kt < 0:
                    continue
                steps.append({"k": KwT[pb, kt * 128:(kt + 1) * 128], "v": Vw[:, kt, n, :],
                              "extra": [(ns.ident[:, :], dmw[:, q, :])], "brow": btab5[:, 11 - q, h0:h0 + 4]})
            pc = branch4(ns, b, steps, qr, biasl[:, :])
            for hl in range(4):
                h = h0 + hl
                den = pc[:, hl * 65 + 64:hl * 65 + 65]
                b.ts(rd[:, 4:5], den, 1e-30, ALU.max)
                b.recip(rd[:, 4:5], rd[:, 4:5])
                b.tt(rd[:, 5:6], rd[:, 4:5], gts[:, j, h * 3 + 2:h * 3 + 3], ALU.mult)
                b.stt(oa[:, h * 64:(h + 1) * 64], pc[:, hl * 65:hl * 65 + 64], rd[:, 5:6], oa[:, h * 64:(h + 1) * 64], ALU.mult, ALU.add)
        b.dma("sp", onso[:, j, :], oa[:])
    b.finish()
    return nc


def nsa_cmp_weights(inp):
    m = {}
    for kind in ("k", "v"):
        w1 = inp["cmp_w1_" + kind][0].reshape(32, 64, 256).transpose(1, 0, 2)
        m["w1" + kind] = np.ascontiguousarray(np.concatenate([w1, w1], 0))
        w2 = inp["cmp_w2_" + kind][0].reshape(2, 128, 64)
        w2p = np.zeros((128, 4, 128), np.float32)
        for n in range(2):
            for hc in range(2):
                w2p[:, n * 2 + hc, n * 64:(n + 1) * 64] = w2[hc]
        m["w2" + kind] = w2p
        pT = inp["cmp_pos_" + kind][0].T
        m["pos" + kind] = np.ascontiguousarray(np.concatenate([pT, pT], 0))
    return m


def run_l2n_prompt(inp, proj_p):
    q = proj_p[:, 4096:5120]
    kv = proj_p[:, 5120:5888].reshape(8192, 6, 128)
    gates = proj_p[:, 5888:5936]
    cw = nsa_cmp_weights(inp)
    T = lambda a: np.ascontiguousarray(a.T)
    tok = lambda a: np.ascontiguousarray(a.reshape(64, 128, 128).transpose(1, 0, 2))
    shared = {"KcT": T(kv[:, 0]), "VcT": T(kv[:, 1]), "KsT": T(kv[:, 2]), "Vs": tok(kv[:, 3]),
              "KwT": T(kv[:, 4]), "Vw": tok(kv[:, 5])}
    shared.update(cw)
    outs = []
    ncs = []
    maps = []
    for c in range(NCORES):
        tiles = [c + 8 * j for j in range(8)]
        m = dict(shared)
        m.update(nsa_prompt_consts(c))
        qc = np.stack([q[128 * i:128 * (i + 1)] for i in tiles], 0)
        qr = qc.reshape(8, 128, 2, 8, 64).transpose(2, 4, 3, 0, 1).reshape(128, 8, 1024)
        m["qT"] = np.ascontiguousarray(qr)
        m["gates"] = np.ascontiguousarray(np.stack([gates[128 * i:128 * (i + 1)] for i in tiles], 1))
        maps.append(m)
    nc = build_l2n()
    res = run_bass_kernel_spmd(nc, maps, core_ids=list(range(NCORES)))
    o = np.zeros((8192, 1024), np.float32)
    for c in range(NCORES):
        r = res.results[c]["ons"]
        for j in range(8):
            i = c + 8 * j
            o[128 * i:128 * (i + 1)] = r[:, j, :]
    return o


U32 = mybir.dt.uint32
PAST = 2048
SCUT = None
NEGF = -30000.0


def nsa_sample_consts():
    c = {}
    p = np.arange(128)
    c["gs"] = (np.arange(128)[:, None] == (np.arange(17 * 128)[None, :] // 64)).astype(np.float32)
    cs = np.arange(128)[:, None] * 16
    ss = np.arange(64)[None, :] * 64
    ov = ((cs <= ss + 63) & (cs + 31 >= ss)).astype(np.float32)
    ov[127] = 0.0
    ov[:, 33:] = 0.0
    c["ovs"] = ov
    t = np.arange(4)
    bc = np.zeros((128, 2, 8, 4), np.float64)
    bs = np.zeros((128, 17, 2, 8, 4), np.float64)
    bw = np.zeros((128, 5, 2, 8, 4), np.float64)
    for n in range(2):
        for g in range(8):
            sl = SLOPES[n * 8 + g]
            qpos = PAST + t
            cpos = 16 * p + 31
            bc[:, n, g, :] = np.where((p < 127)[:, None], -sl * (qpos[None, :] - cpos[:, None]), NEGF)
            for tile in range(17):
                spos = 128 * tile + p
                dist = qpos[None, :] - spos[:, None]
                bs[:, tile, n, g, :] = np.where(dist >= 0, -sl * dist, NEGF)
            for tile in range(5):
                wpos = PAST - 512 + 128 * tile + p
                dist = qpos[None, :] - wpos[:, None]
                ok = (dist >= 0) & (dist < 512)
                if tile == 4:
                    ok &= (p < 4)[:, None]
                bw[:, tile, n, g, :] = np.where(ok, -sl * dist, NEGF)
    c["biasc"] = np.ascontiguousarray(bc.astype(np.float32).reshape(128, 64))
    c["biass"] = np.ascontiguousarray(bs.astype(np.float32).reshape(128, 17 * 64))
    c["biasw"] = np.ascontiguousarray(bw.astype(np.float32).reshape(128, 5 * 64))
    blk = np.arange(64)
    forced0, forced1, forced2 = blk == 0, blk == 32, blk == 31
    valid = blk <= 32
    keep = (valid & ~(forced0 | forced1 | forced2)).astype(np.float32)
    a = np.where(valid, 0.0, -1e30)
    a = np.where(forced2, 1e4, a)
    a = np.where(forced1, 2e4, a)
    a = np.where(forced0, 3e4, a)
    c["keeps"] = np.ascontiguousarray(np.broadcast_to(keep[None, :], (4, 64)).astype(np.float32))
    c["adds"] = np.ascontiguousarray(np.broadcast_to(a[None, :], (4, 64)).astype(np.float32))
    sel = np.zeros((32, 4), np.float32)
    for g in range(8):
        for tt in range(4):
            sel[g * 4 + tt, tt] = 1.0
    c["selm"] = sel
    c["pcol"] = p.astype(np.float32).reshape(128, 1)
    c["identd"] = np.eye(128, dtype=np.float32)
    return c


def build_l2s(n_pool=2560, nb=16):
    nc = bass.Bass("TRN2", target_bir_lowering=False)
    dt = lambda n, s, k="ExternalInput": nc.dram_tensor(n, s, F32, kind=k).ap()
    b = Bld(nc)
    ns = Nsa(b, nc, dt)
    cache = dt("cache", [n_pool * 128, 512])
    ptab = nc.dram_tensor("ptab", [1, nb * 16], I32, kind="ExternalInput").ap()
    cwin = dt("cwin", [nb, 512, 256])
    qTd = dt("qTs", [128, nb, 32])
    ksnd = dt("ksn", [128, nb, 4])
    kwnd = dt("kwn", [128, nb, 4])
    vsnd = dt("vsn", [4, nb, 128])
    vwnd = dt("vwn", [4, nb, 128])
    gtd = dt("gts", [32, nb, 2, 3])
    gsd = dt("gs", [128, 17 * 128])
    ovsd = dt("ovs", [128, 64])
    bcd = dt("biasc", [128, 64])
    bsd = dt("biass", [128, 17 * 64])
    bwd = dt("biasw", [128, 5 * 64])
    keepd = dt("keeps", [4, 64])
    addd = dt("adds", [4, 64])
    seld = dt("selm", [32, 4])
    pcold = dt("pcol", [128, 1])
    onso = dt("ons", [nb, 2, 32, 64], "ExternalOutput")

    sbt = b.sb
    Gs = sbt("s_gs", [128, 17 * 128], BF16)
    b.dma("pool", Gs[:], gsd)
    biasc = sbt("s_bc", [128, 2, 32], F32)
    biass = sbt("s_bs", [128, 17, 2, 32], F32)
    biasw = sbt("s_bw", [128, 5, 2, 32], F32)
    b.dma("sp", biasc[:], bcd.rearrange("p (n q) -> p n q", n=2))
    b.dma("sp", biass[:], bsd.rearrange("p (k n q) -> p k n q", k=17, n=2))
    b.dma("sp", biasw[:], bwd.rearrange("p (k n q) -> p k n q", k=5, n=2))
    keeps = sbt("s_keep", [4, 64], F32)
    adds = sbt("s_add", [4, 64], F32)
    selm = sbt("s_sel", [32, 4], F32)
    pcol = sbt("s_pcol", [128, 1], F32)
    b.dma("sp", keeps[:], keepd)
    b.dma("sp", adds[:], addd)
    b.dma("sp", selm[:], seld)
    b.dma("sp", pcol[:], pcold)
    qT = sbt("s_qT", [128, nb, 32], BF16)
    ksn = sbt("s_ksn", [128, nb, 4], BF16)
    kwn = sbt("s_kwn", [128, nb, 4], BF16)
    vsn = sbt("s_vsn", [4, nb, 128], BF16)
    vwn = sbt("s_vwn", [4, nb, 128], BF16)
    gts = sbt("s_gts", [32, nb, 2, 3], F32)
    b.dma("pool", qT[:], qTd)
    b.dma("pool", ksn[:], ksnd)
    b.dma("pool", kwn[:], kwnd)
    b.dma("pool", vsn[:], vsnd)
    b.dma("pool", vwn[:], vwnd)
    b.dma("sp", gts[:], gtd)
    b.act(gts[:], gts[:], AF.Sigmoid)
    pti = sbt("pti", [128, nb * 16], I32)
    idx = sbt("idx", [128, nb * 16], U32)
    b.dma("sp", pti[:], ptab.partition_broadcast(128))
    b.ts(idx[:], pti[:], 128.0, ALU.mult, pcol[:, 0:1], ALU.add)

    gth = [sbt("gth%d" % i, [128, 512], F32) for i in range(3)]
    wth = [sbt("wth%d" % i, [128, 256], F32) for i in range(2)]
    KcT = [sbt("KcT%d" % i, [128, 2048], BF16) for i in range(2)]
    VcT = [sbt("VcT%d" % i, [128, 2048], BF16) for i in range(2)]
    KsT = [sbt("KsTs%d" % i, [128, 2052], BF16) for i in range(2)]
    KwT = [sbt("KwTs%d" % i, [128, 516], BF16) for i in range(2)]
    Vs = [sbt("Vss%d" % i, [128, 17, 2, 65], BF16) for i in range(2)]
    Vw = [sbt("Vws%d" % i, [128, 5, 2, 65], BF16) for i in range(2)]
    KCTs = sbt("KCTs", [128, 128], BF16)
    VCOs = sbt("VCOs", [128, 2, 129], BF16)
    tmpf = [sbt("tmpf%d" % i, [128, 32], F32) for i in range(3)]
    xn = sbt("xn", [32, 64], F32)
    s2 = sbt("s2s", [4, 64], F32)
    s3 = sbt("s3s", [4, 64], F32)
    m8 = sbt("m8s", [4, 16], F32)
    nmf = sbt("nmf", [4, 64], F32)
    nmT = sbt("nmTs", [128, 8, 4], BF16)
    rd = sbt("rds", [32, 8], F32)
    oac = [sbt("oacs%d" % i, [32, 64], F32) for i in range(2)]
    for i in range(2):
        b.memset(Vs[i][:], 1.0)
        b.memset(Vw[i][:], 1.0)
    b.memset(KCTs[:], 0.0)
    b.memset(nmT[:], 0.0)
    b.memset(VCOs[:], 0.0)
    b.memset(VCOs[:, :, 64:65], 1.0)
    for n in range(2):
        b.dma("pool", VCOs[:, n, 65:129], ovsd)
    ns.cmp_bias()
    nt = [0]

    pend = []

    def step(kT, rows, q, mask, bias, v, pacc, first, last, ncols):
        ps = ns.psS[ns.nS % 2]
        ns.nS += 1
        b.mm(ps[0:rows, 0:32], kT, q, start=True, stop=(mask is None))
        if mask is not None:
            b.mm(ps[0:rows, 0:32], mask[0], mask[1], start=False, stop=True)
        tf = tmpf[nt[0] % 3]
        nt[0] += 1
        b.stt(tf[0:rows, :], ps[0:rows, 0:32], 0.125, bias, ALU.mult, ALU.add)
        pt = ns.PT[ns.nP % 3]
        ns.nP += 1
        b.act(pt[0:rows, 0:32], tf[0:rows, :], AF.Exp)
        flush()
        pend.append((pacc, ncols, pt, rows, v, first, last))

    def flush():
        while pend:
            pacc, ncols, pt, rows, v, first, last = pend.pop(0)
            b.mm(pacc[0:32, 0:ncols], pt[0:rows, 0:32], v, start=first, stop=last)

    for bi in range(nb):
        k2 = bi % 2
        for pg in range(16):
            gt = gth[(bi * 16 + pg) % 3]
            col = bi * 16 + pg
            b.P.custom = None
            I = Ins("pool", (lambda e, gt=gt, col=col: e.indirect_dma_start(
                out=gt[:], out_offset=None, in_=cache,
                in_offset=bass.IndirectOffsetOnAxis(ap=idx[:, col:col + 1], axis=0))), True)
            I.idx = len(b.P.ins)
            b.P.ins.append(I)
            b.P._track(I, [idx[:, col:col + 1], cache], [gt[:]])
            ph = ns.psH[ns.nH % 2]
            ns.nH += 1
            for q3 in range(3):
                b.tr(ph[:, q3 * 128:(q3 + 1) * 128], gt[:, q3 * 128:(q3 + 1) * 128], ns.identf[:, :])
            cs = slice(pg * 128, (pg + 1) * 128)
            b.copy(KcT[k2][:, cs], ph[:, 0:128], eng="act")
            b.copy(VcT[k2][:, cs], ph[:, 128:256])
            b.copy(KsT[k2][:, cs], ph[:, 256:384], eng="act")
            b.copy(Vs[k2][:, pg, :, 0:64], gt[:, 384:512].rearrange("p (n d) -> p n d", n=2))
        b.copy(KsT[k2][:, 2048:2052], ksn[:, bi, :])
        b.copy(Vs[k2][0:4, 16, :, 0:64], vsn[0:4, bi, :].rearrange("p (n d) -> p n d", n=2))
        for wt in range(4):
            wtile = wth[wt % 2]
            b.dma("sp", wtile[:], cwin[bi, wt * 128:(wt + 1) * 128, :])
            ph = ns.psH[ns.nH % 2]
            ns.nH += 1
            b.tr(ph[:, 0:128], wtile[:, 0:128], ns.identf[:, :])
            b.copy(KwT[k2][:, wt * 128:(wt + 1) * 128], ph[:, 0:128], eng="act")
            b.copy(Vw[k2][:, wt, :, 0:64], wtile[:, 128:256].rearrange("p (n d) -> p n d", n=2))
        b.copy(KwT[k2][:, 512:516], kwn[:, bi, :])
        b.copy(Vw[k2][0:4, 4, :, 0:64], vwn[0:4, bi, :].rearrange("p (n d) -> p n d", n=2))
        if SCUT == "A":
            continue
        ns.compress_tile("k", None, 127, kdst=KCTs[:, 0:127], xs_ap=KcT[k2])
        ns.compress_tile("v", None, 127, vdst=[VCOs[0:127, 0, 0:64], VCOs[0:127, 1, 0:64]], xs_ap=VcT[k2])
        if SCUT == "B":
            continue
        for n in range(2):
            pb = slice(n * 64, (n + 1) * 64)
            q = qT[pb, bi, :]
            oa = oac[n]
            pacc = ns.psAcc[ns.nA % 2]
            ns.nA += 1
            step(KCTs[pb, 0:128], 128, q, None, biasc[:, n, :], VCOs[:, n, :], pacc, True, True, 129)
            flush()
            b.ts(rd[:, 0:1], pacc[0:32, 64:65], 1e-30, ALU.max)
            b.recip(rd[:, 0:1], rd[:, 0:1])
            b.ts(xn[:], pacc[0:32, 65:129], rd[:, 0:1], ALU.mult)
            b.tt(rd[:, 1:2], rd[:, 0:1], gts[:, bi, n, 0:1], ALU.mult)
            b.ts(oa[:], pacc[0:32, 0:64], rd[:, 1:2], ALU.mult)
            if SCUT == "C":
                continue
            pk = ns.psK2
            b.mm(pk[0:4, 0:64], selm[:, :], xn[:, :])
            b.tt(s2[:], pk[0:4, 0:64], keeps[:], ALU.mult)
            b.tt(s2[:], s2[:], adds[:], ALU.add)
            b.P.op("dve", lambda e: e.max(out=m8[:, 0:8], in_=s2[:]), [s2[:]], [m8[:, 0:8]])
            b.P.op("dve", lambda e: e.match_replace(out=s3[:], in_to_replace=m8[:, 0:8], in_values=s2[:], imm_value=-1e30),
                   [s2[:], m8[:, 0:8]], [s3[:]])
            b.P.op("dve", lambda e: e.max(out=m8[:, 8:16], in_=s3[:]), [s3[:]], [m8[:, 8:16]])
            b.ts(s3[:], s2[:], m8[:, 15:16], ALU.is_ge)
            b.ts(nmf[:], s3[:], -NEG, ALU.mult, NEG, ALU.add)
            b.tr(pk[0:64, 64:68], nmf[:, :], ns.identf[0:4, 0:4])
            b.copy(nmT[0:64, :, :], pk[0:64, 64:68].unsqueeze(1).broadcast_to([64, 8, 4]))
            nmv = nmT[:].rearrange("p g t -> p (g t)")
            if SCUT == "D":
                continue
            pacc = ns.psAcc[ns.nA % 2]
            ns.nA += 1
            for tile in range(17):
                rows = 128 if tile < 16 else 4
                cs = slice(tile * 128, tile * 128 + rows)
                step(KsT[k2][pb, cs], rows, q, (Gs[:, cs], nmv), biass[0:rows, tile, n, :], Vs[k2][0:rows, tile, n, :],
                     pacc, tile == 0, tile == 16, 65)
            flush()
            b.ts(rd[:, 2:3], pacc[0:32, 64:65], 1e-30, ALU.max)
            b.recip(rd[:, 2:3], rd[:, 2:3])
            b.tt(rd[:, 3:4], rd[:, 2:3], gts[:, bi, n, 1:2], ALU.mult)
            b.stt(oa[:], pacc[0:32, 0:64], rd[:, 3:4], oa[:], ALU.mult, ALU.add)
            if SCUT == "E":
                continue
            pacc = ns.psAcc[ns.nA % 2]
            ns.nA += 1
            for tile in range(5):
                rows = 128 if tile < 4 else 4
                cs = slice(tile * 128, tile * 128 + rows)
                step(KwT[k2][pb, cs], rows, q, None, biasw[0:rows, tile, n, :], Vw[k2][0:rows, tile, n, :],
                     pacc, tile == 0, tile == 4, 65)
            flush()
            b.ts(rd[:, 4:5], pacc[0:32, 64:65], 1e-30, ALU.max)
            b.recip(rd[:, 4:5], rd[:, 4:5])
            b.tt(rd[:, 5:6], rd[:, 4:5], gts[:, bi, n, 2:3], ALU.mult)
            b.stt(oa[:], pacc[0:32, 0:64], rd[:, 5:6], oa[:], ALU.mult, ALU.add)
            b.dma("sp", onso[bi, n], oa[:])
    b.finish()
    return nc


def run_l2n_sample(inp, proj_s, nb=16):
    cache = inp["cache_kv"][0]
    n_pool = cache.shape[0]
    cache2 = np.ascontiguousarray(cache.reshape(n_pool * 128, 512))
    cst = nsa_sample_consts()
    cst.update(nsa_cmp_weights(inp))
    ps = proj_s.reshape(128, 4, NPROJ)
    nc = build_l2s(n_pool, nb)
    maps = []
    for c in range(NCORES):
        sb = ps[c * nb:(c + 1) * nb]
        m = dict(cst)
        m["cache"] = cache2
        m["ptab"] = np.ascontiguousarray(inp["page_table"][c * nb:(c + 1) * nb].reshape(1, nb * 16).astype(np.int32))
        m["cwin"] = np.ascontiguousarray(inp["cache_win"][0, c * nb:(c + 1) * nb].reshape(nb, 512, 256))
        q = sb[:, :, 4096:5120].reshape(nb, 4, 2, 8, 64)
        m["qTs"] = np.ascontiguousarray(q.transpose(2, 4, 0, 3, 1).reshape(128, nb, 32))
        kv = sb[:, :, 5120:5888].reshape(nb, 4, 6, 128)
        m["ksn"] = np.ascontiguousarray(kv[:, :, 2].transpose(2, 0, 1))
        m["kwn"] = np.ascontiguousarray(kv[:, :, 4].transpose(2, 0, 1))
        m["vsn"] = np.ascontiguousarray(kv[:, :, 3].transpose(1, 0, 2))
        m["vwn"] = np.ascontiguousarray(kv[:, :, 5].transpose(1, 0, 2))
        g = sb[:, :, 5888:5936].reshape(nb, 4, 2, 8, 3)
        m["gts"] = np.ascontiguousarray(g.transpose(3, 1, 0, 2, 4).reshape(32, nb, 2, 3))
        maps.append(m)
    res = run_bass_kernel_spmd(nc, maps, core_ids=list(range(NCORES)))
    outs = []
    for c in range(NCORES):
        r = res.results[c]["ons"].reshape(nb, 2, 8, 4, 64)
        outs.append(r.transpose(0, 3, 1, 2, 4).reshape(nb * 4, 1024))
    return np.concatenate(outs, 0)
```

```python
from contextlib import ExitStack
import numpy as np
import concourse.bass as bass
import concourse.mybir as mybir
from concourse.bass_utils import run_bass_kernel_spmd

F32 = mybir.dt.float32
BF16 = mybir.dt.bfloat16
I32 = mybir.dt.int32
AF = mybir.ActivationFunctionType
ALU = mybir.AluOpType
AX = mybir.AxisListType

NCORES = 8
D = 2048
DFF = 5504
T_P = 1024
T_S = 64
T_ALL = T_P + T_S
KC = D // 128
FC = DFF // 128
EPS = 1e-6
NPROJ = 5936
D_IN = 10032

COMPUTE = ("pe", "act", "dve", "pool")
NDMASEM = 8
CUT = 9


def region(ap):
    name = ap.tensor.name
    space = str(ap.space)
    aplist = ap.ap
    off = int(ap.offset)
    if space == "PSUM":
        return (name, 0, 128, 0, 1 << 30)
    if space == "SB":
        pstep, pcount = aplist[0]
        if pstep == 0:
            p0, foff, pcount = 0, off, 128
        else:
            p0 = off // pstep
            foff = off % pstep
        ext = 1
        for s, c in aplist[1:]:
            ext += (c - 1) * abs(s)
        return (name, p0, p0 + pcount, foff, foff + ext)
    ext = 1
    for s, c in aplist:
        ext += (c - 1) * abs(s)
    return (name, 0, 1, off, off + ext)


def overlap(a, b):
    return a[1] < b[2] and b[1] < a[2] and a[3] < b[4] and b[3] < a[4]


def covers(a, b):
    return a[1] <= b[1] and a[2] >= b[2] and a[3] <= b[3] and a[4] >= b[4]


class Ins:
    __slots__ = ("eng", "fn", "deps", "need_inc", "cnt", "is_dma", "dsem", "dval", "idx", "prewait", "inc")

    def __init__(self, eng, fn, is_dma):
        self.eng = eng
        self.fn = fn
        self.deps = set()
        self.need_inc = False
        self.cnt = None
        self.is_dma = is_dma
        self.dsem = None
        self.dval = None
        self.prewait = None
        self.inc = 16


class Prog:
    def __init__(self, nc):
        self.nc = nc
        self.ins = []
        self.hist = {}

    def _track(self, I, reads, writes):
        idx = I.idx
        ins = self.ins
        rr_ = [region(a) for a in reads if str(a.space) != "PSUM"]
        wr_ = [region(a) for a in writes] + [region(a) for a in reads if str(a.space) == "PSUM"]
        for r in rr_:
            h = self.hist.setdefault(r[0], [])
            for (rr, j, w) in h:
                if w and overlap(r, rr):
                    J = ins[j]
                    if J.eng == "pe" and I.eng == "pe" and not I.is_dma and not J.is_dma:
                        continue
                    I.deps.add(j)
        for r in wr_:
            h = self.hist.setdefault(r[0], [])
            for (rr, j, w) in h:
                if overlap(r, rr):
                    J = ins[j]
                    if J.eng == I.eng and not I.is_dma and not J.is_dma:
                        continue
                    I.deps.add(j)
        if len(I.deps) > 1:
            best = {}
            keep = set()
            for j in I.deps:
                J = ins[j]
                if J.is_dma:
                    keep.add(j)
                elif best.get(J.eng, -1) < j:
                    best[J.eng] = j
            keep.update(best.values())
            I.deps = keep
        for r in rr_:
            h = self.hist[r[0]]
            if not I.is_dma:
                h[:] = [e for e in h if e[2] or e[0] != r or ins[e[1]].eng != I.eng or ins[e[1]].is_dma]
            h.append((r, idx, False))
        for r in wr_:
            h = self.hist[r[0]]
            h[:] = [e for e in h if not covers(r, e[0])]
            h.append((r, idx, True))

    def op(self, eng, fn, reads=(), writes=()):
        I = Ins(eng, fn, False)
        I.idx = len(self.ins)
        self.ins.append(I)
        self._track(I, reads, writes)
        return I

    def dma(self, q, out, in_, **kw):
        def fn(e, out=out, in_=in_, kw=kw):
            return e.dma_start(out=out, in_=in_, **kw)
        I = Ins(q, fn, True)
        I.idx = len(self.ins)
        self.ins.append(I)
        self._track(I, [in_], [out])
        return I

    def emit(self, stack):
        nc = self.nc
        ins = self.ins
        for I in ins:
            for j in I.deps:
                ins[j].need_inc = True
        csem = {e: stack.enter_context(nc.semaphore("c_" + e)) for e in COMPUTE}
        dsems = {q: [stack.enter_context(nc.semaphore("d_%s_%d" % (q, i))) for i in range(NDMASEM)]
                 for q in ("sp", "act", "pool")}
        cnt = {e: 0 for e in COMPUTE}
        dq_n = {q: 0 for q in dsems}
        dq_val = {q: [0] * NDMASEM for q in dsems}
        dq_last = {q: [None] * NDMASEM for q in dsems}
        for I in ins:
            if I.is_dma:
                k = dq_n[I.eng] % NDMASEM
                dq_n[I.eng] += 1
                I.prewait = dq_last[I.eng][k]
                dq_val[I.eng][k] += I.inc
                I.dsem = dsems[I.eng][k]
                I.dval = dq_val[I.eng][k]
                dq_last[I.eng][k] = I.idx
            elif I.need_inc:
                cnt[I.eng] += 1
                I.cnt = cnt[I.eng]
        self.maxcnt = dict(cnt)
        streams = {e: [] for e in ("pe", "act", "dve", "pool", "sp")}
        for I in ins:
            streams[I.eng].append(I)
        block = stack.enter_context(nc.Block())

        def run_stream(ename, e):
            waited = {}

            def wait_for(j):
                J = ins[j]
                if J.is_dma:
                    key = ("d", J.eng, id(J.dsem))
                    if waited.get(key, 0) >= J.dval:
                        return
                    waited[key] = J.dval
                    e.wait_ge(J.dsem, J.dval)
                else:
                    key = ("c", J.eng)
                    if waited.get(key, 0) >= J.cnt:
                        return
                    waited[key] = J.cnt
                    e.wait_ge(csem[J.eng], J.cnt)

            for I in streams[ename]:
                for j in sorted(I.deps):
                    wait_for(j)
                if I.is_dma and I.prewait is not None:
                    wait_for(I.prewait)
                bi = I.fn(e)
                if I.is_dma:
                    bi.then_inc(I.dsem, I.inc)
                elif I.need_inc:
                    bi.then_inc(csem[I.eng], 1)
            if ename == "sp":
                for q in dsems:
                    for k in range(NDMASEM):
                        if dq_val[q][k] > 0:
                            e.wait_ge(dsems[q][k], dq_val[q][k])
                for ce in COMPUTE:
                    if cnt[ce] > 0:
                        e.wait_ge(csem[ce], cnt[ce])

        @block.tensor
        def _(e):
            run_stream("pe", e)

        @block.scalar
        def _(e):
            run_stream("act", e)

        @block.vector
        def _(e):
            run_stream("dve", e)

        @block.gpsimd
        def _(e):
            run_stream("pool", e)

        @block.sync
        def _(e):
            run_stream("sp", e)


class Bld:
    def __init__(self, nc):
        self.nc = nc
        self.P = Prog(nc)
        self.st = ExitStack()
        self._n = 0

    def sb(self, name, shape, dt):
        return self.st.enter_context(self.nc.sbuf_tensor(name, shape, dt))

    def ps(self, name, shape, dt=F32):
        return self.st.enter_context(self.nc.psum_tensor(name, shape, dt))

    def mm(self, out, lhsT, rhs, start=True, stop=True, skip=False):
        if skip:
            self.P.op("pe", lambda e: e.matmul(out, lhsT=lhsT, rhs=rhs, start=start, stop=stop, skip_group_check=True),
                      [lhsT, rhs, out], [out])
        else:
            self.P.op("pe", lambda e: e.matmul(out, lhsT=lhsT, rhs=rhs, start=start, stop=stop),
                      [lhsT, rhs] + ([] if start else [out]), [out])

    def tr(self, out, in_, ident):
        self.P.op("pe", lambda e: e.transpose(out=out, in_=in_, identity=ident), [in_, ident], [out])

    def act(self, out, in_, func, bias=None, scale=None, accum=None, eng="act"):
        kw = {}
        rd = [in_]
        wr = [out]
        if bias is not None:
            kw["bias"] = bias
            if not isinstance(bias, (int, float)):
                rd.append(bias)
        if scale is not None:
            kw["scale"] = scale
            if not isinstance(scale, (int, float)):
                rd.append(scale)
        if accum is not None:
            kw["accum_out"] = accum
            wr.append(accum)
        self.P.op("act", lambda e: e.activation(out=out, in_=in_, func=func, **kw), rd, wr)

    def tt(self, out, a, b, op, eng="dve"):
        self.P.op(eng, lambda e: e.tensor_tensor(out=out, in0=a, in1=b, op=op), [a, b], [out])

    def ts(self, out, a, s1, op0, s2=None, op1=None, eng="dve", accum=None):
        rd = [a]
        wr = [out]
        if not isinstance(s1, (int, float)):
            rd.append(s1)
        if s2 is not None and not isinstance(s2, (int, float)):
            rd.append(s2)
        kw = {}
        if op1 is not None:
            kw["op1"] = op1
        if accum is not None:
            kw["accum_out"] = accum
            wr.append(accum)
        self.P.op(eng, lambda e: e.tensor_scalar(out=out, in0=a, scalar1=s1, scalar2=s2, op0=op0, **kw), rd, wr)

    def stt(self, out, a, s, b, op0, op1):
        rd = [a, b]
        if not isinstance(s, (int, float)):
            rd.append(s)
        self.P.op("dve", lambda e: e.scalar_tensor_tensor(out=out, in0=a, scalar=s, in1=b, op0=op0, op1=op1), rd, [out])

    def copy(self, out, in_, eng="dve"):
        if eng == "act":
            self.P.op("act", lambda e: e.copy(out=out, in_=in_), [in_], [out])
        else:
            self.P.op(eng, lambda e: e.tensor_copy(out=out, in_=in_), [in_], [out])

    def recip(self, out, in_):
        self.P.op("dve", lambda e: e.reciprocal(out=out, in_=in_), [in_], [out])

    def memset(self, ap, v, eng="pool"):
        self.P.op(eng, lambda e: e.memset(ap, v), [], [ap])

    def dma(self, q, out, in_, **kw):
        self.P.dma(q, out, in_, **kw)

    def finish(self):
        self.P.emit(self.st)
        self.st.close()


class Dense:
    def __init__(self, b, ident_dram):
        self.b = b
        nc = b.nc
        self.identf = b.sb("identf", [128, 128], F32)
        self.ident = b.sb("ident", [128, 128], BF16)
        b.dma("sp", self.identf[:], ident_dram)
        b.copy(self.ident[:], self.identf[:])
        self.hT = b.sb("hT", [128, KC, 576], BF16)
        self.aT = b.sb("aT", [128, FC, 576], BF16)
        self.ybuf = b.sb("ybuf", [128, 5, D], BF16)
        self.xt = [b.sb("xt%d" % i, [128, D], F32) for i in range(2)]
        self.hn = [b.sb("hn%d" % i, [128, D], BF16) for i in range(2)]
        self.junk = b.sb("junk", [128, D], BF16)
        self.st4 = b.sb("st4", [128, 8], F32)
        self.wA = [b.sb("wA%d" % i, [128, KC, 256], BF16) for i in range(2)]
        self.wB = [b.sb("wB%d" % i, [128, KC, 256], BF16) for i in range(2)]
        self.wD = [b.sb("wD%d" % i, [128, FC, 256], BF16) for i in range(2)]
        self.sg = [b.sb("sg%d" % i, [128, 512], BF16) for i in range(2)]
        self.ev = [b.sb("ev%d" % i, [128, 256], F32) for i in range(2)]
        self.psT = [b.ps("psT%d" % i, [128, 8, 128], BF16) for i in range(2)]
        self.psG = [b.ps("psG%d" % i, [128, 512]) for i in range(2)]
        self.psU = [b.ps("psU%d" % i, [128, 512]) for i in range(2)]
        self.psO = [b.ps("psO%d" % i, [128, 512]) for i in range(2)]
        self.nT = 0
        self.nG = 0
        self.nO = 0
        self.nX = 0
        self.nW = 0
        self.nE = 0

    def rstd(self, src, rows, col):
        b = self.b
        ss = self.st4[0:rows, col:col + 1]
        b.act(self.junk[0:rows, :], src, AF.Square, accum=ss)
        b.act(ss, ss, AF.Sqrt, bias=EPS, scale=1.0 / D)
        b.recip(ss, ss)
        return ss

    def norm_to_hT(self, src, rows, tok0, gcol):
        b = self.b
        r = self.rstd(src, rows, 0)
        hn = self.hn[self.nX % 2]
        self.nX += 1
        b.ts(hn[0:rows, :], src, r, ALU.mult)
        if CUT >= 3:
            self.to_T(hn, rows, self.hT, tok0, gcol)

    def to_T(self, src, rows, dstT, tok0, gcol=None, nk=KC):
        b = self.b
        for k4 in range(0, nk, 4):
            pt = self.psT[self.nT % 2]
            self.nT += 1
            for j in range(4):
                kc = k4 + j
                b.tr(pt[:, j, 0:rows], src[0:rows, kc * 128:(kc + 1) * 128], self.ident[0:rows, 0:rows])
            if gcol is None:
                b.copy(dstT[:, k4:k4 + 4, tok0:tok0 + rows], pt[:, 0:4, 0:rows])
            else:
                for j in range(4):
                    kc = k4 + j
                    b.act(dstT[:, kc, tok0:tok0 + rows], pt[:, j, 0:rows], AF.Copy, scale=gcol[:, kc:kc + 1])

    def load_w(self, dst, w_dram, c0, w, nk):
        src = w_dram.rearrange("(kc p) f -> p kc f", p=128)[:, :, c0:c0 + w]
        self.b.dma("pool", dst[:, 0:nk, 0:w], src)

    def gate_up(self, wg, wu, groups):
        b = self.b
        nblk = (DFF + 255) // 256
        blocks = [(i * 256, min(256, DFF - i * 256)) for i in range(nblk)]

        def issue(i):
            c0, w = blocks[i]
            self.load_w(self.wA[i % 2], wg, c0, w, KC)
            self.load_w(self.wB[i % 2], wu, c0, w, KC)
        issue(0)
        for i, (c0, w) in enumerate(blocks):
            if i + 1 < nblk:
                issue(i + 1)
            wa, wb = self.wA[i % 2], self.wB[i % 2]
            for fl in range(w // 128):
                fc = c0 // 128 + fl
                for (t0, n) in groups:
                    pg = self.psG[self.nG % 2]
                    pu = self.psU[self.nG % 2]
                    sg = self.sg[self.nG % 2]
                    self.nG += 1
                    for kc in range(KC):
                        b.mm(pg[:, 0:n], wa[:, kc, fl * 128:(fl + 1) * 128], self.hT[:, kc, t0:t0 + n],
                             start=(kc == 0), stop=(kc == KC - 1))
                    for kc in range(KC):
                        b.mm(pu[:, 0:n], wb[:, kc, fl * 128:(fl + 1) * 128], self.hT[:, kc, t0:t0 + n],
                             start=(kc == 0), stop=(kc == KC - 1))
                    b.act(sg[:, 0:n], pg[:, 0:n], AF.Silu)
                    b.tt(self.aT[:, fc, t0:t0 + n], sg[:, 0:n], pu[:, 0:n], ALU.mult)

    def down(self, wd, tiles):
        b = self.b
        nblk = D // 256

        def issue(i):
            src = wd.rearrange("(fc p) d -> p fc d", p=128)[:, :, i * 256:(i + 1) * 256]
            b.dma("pool", self.wD[i % 2][:], src)
        issue(0)
        for i in range(nblk):
            if i + 1 < nblk:
                issue(i + 1)
            w = self.wD[i % 2]
            for ti, (t0, rows) in enumerate(tiles):
                po = self.psO[self.nO % 2]
                self.nO += 1
                for fc in range(FC):
                    b.mm(po[0:rows, 0:256], self.aT[:, fc, t0:t0 + rows], w[:, fc, :], start=(fc == 0), stop=(fc == FC - 1))
                b.copy(self.ybuf[0:rows, ti, i * 256:(i + 1) * 256], po[0:rows, 0:256], eng="act")

    def proj(self, srcT, nk, w_dram, cols, tiles, sink):
        b = self.b
        c_lo, c_hi = cols
        nblk = (c_hi - c_lo + 255) // 256
        blocks = [(c_lo + i * 256, min(256, c_hi - c_lo - i * 256)) for i in range(nblk)]

        def issue(i):
            c0, w = blocks[i]
            self.load_w(self.wA[(self.nW + i) % 2], w_dram, c0, w, nk)
        issue(0)
        for i, (c0, w) in enumerate(blocks):
            if i + 1 < nblk:
                issue(i + 1)
            wt = self.wA[(self.nW + i) % 2]
            for ti, (t0, rows) in enumerate(tiles):
                po = self.psO[self.nO % 2]
                self.nO += 1
                for kc in range(nk):
                    b.mm(po[0:rows, 0:w], srcT[:, kc, t0:t0 + rows], wt[:, kc, 0:w], start=(kc == 0), stop=(kc == nk - 1))
                sink(ti, t0, rows, c0, w, po)
        self.nW += nblk


def half_tiles(h):
    if h == 0:
        return [(i * 128, 128, i * 128) for i in range(4)] + [(512, 64, T_P)]
    return [(i * 128, 128, 512 + i * 128) for i in range(4)]


def half_groups(h):
    return [(0, 512), (512, 64)] if h == 0 else [(0, 512)]


def build_l1(stages=('win', 'norm', 'gateup', 'down', 'resid', 'proj'), halves=(0, 1)):
    nc = bass.Bass("TRN2", target_bir_lowering=False)
    dt = lambda n, s, k="ExternalInput": nc.dram_tensor(n, s, F32, kind=k).ap()
    x = dt("x", [T_ALL, D])
    wg = dt("wg", [D, DFF]) if 'gateup' in stages else None
    wu = dt("wu", [D, DFF]) if 'gateup' in stages else None
    wd = dt("wd", [DFF, D]) if 'down' in stages else None
    win = dt("win", [D, NPROJ]) if 'proj' in stages else None
    gcols = dt("gcols", [128, 2 * KC])
    gpost = dt("gpost", [128, D])
    identd = dt("identd", [128, 128])
    cwin = dt("cwin", [16, 512, 256])
    x1o = dt("x1o", [T_ALL, D], "ExternalOutput")
    projo = dt("projo", [T_ALL, NPROJ], "ExternalOutput")
    wino = dt("wino", [16, 512, 256], "ExternalOutput")

    b = Bld(nc)
    dn = Dense(b, identd)
    gc = b.sb("gc", [128, 2 * KC], F32)
    gp = b.sb("gp", [128, D], F32)
    b.dma("sp", gc[:], gcols)
    b.dma("sp", gp[:], gpost)
    for bi in range(16 if 'win' in stages else 0):
        b.dma("act", wino[bi, 0:508, :], cwin[bi, 4:512, :])

    for h in halves:
        tiles = half_tiles(h)
        for (t0, rows, g0) in (tiles if 'norm' in stages else []):
            xt = dn.xt[dn.nX % 2]
            b.dma("sp", xt[0:rows, :], x[g0:g0 + rows, :])
            dn.norm_to_hT(xt[0:rows, :], rows, t0, gc[:, 0:KC])
        if 'gateup' in stages:
            dn.gate_up(wg, wu, half_groups(h))
        if 'down' in stages:
            dn.down(wd, [(t0, rows) for (t0, rows, g0) in tiles])
        for ti, (t0, rows, g0) in enumerate(tiles if 'resid' in stages else []):
            xt = dn.xt[dn.nX % 2]
            b.dma("sp", xt[0:rows, :], x[g0:g0 + rows, :])
            y = dn.ybuf[0:rows, ti, :]
            r = dn.rstd(y, rows, 1)
            tmp = dn.hn[(dn.nX + 1) % 2]
            b.stt(tmp[0:rows, :], y, r, gp[0:rows, :], ALU.mult, ALU.mult)
            b.stt(xt[0:rows, :], tmp[0:rows, :], 0.5, xt[0:rows, :], ALU.mult, ALU.add)
            b.dma("sp", x1o[g0:g0 + rows, :], xt[0:rows, :])
            dn.norm_to_hT(xt[0:rows, :], rows, t0, gc[:, KC:2 * KC])

        def sink(ti, t0, rows, c0, w, po, tiles=tiles):
            ev = dn.ev[dn.nE % 2]
            dn.nE += 1
            b.copy(ev[0:rows, 0:w], po[0:rows, 0:w], eng="act")
            g0 = tiles[ti][2]
            b.dma("sp", projo[g0:g0 + rows, c0:c0 + w], ev[0:rows, 0:w])
            if g0 == T_P and c0 == 5632:
                for bi in range(16):
                    b.dma("sp", wino[bi, 508:512, :], ev[bi * 4:(bi + 1) * 4, 0:256])
        if 'proj' in stages:
            dn.proj(dn.hT, KC, win, (0, NPROJ), [(t0, rows) for (t0, rows, g0) in tiles], sink)
    b.finish()
    return nc


def _bcast(v):
    return np.ascontiguousarray(np.broadcast_to(np.asarray(v, np.float32).reshape(1, -1), (128, v.size)))


def _cols(v):
    return np.ascontiguousarray(np.asarray(v, np.float32).reshape(-1, 128).T)


def run_l1(inp):
    nc = build_l1()
    xp = inp["x_prompt"][0]
    xs = inp["x_sample"].reshape(-1, D)
    ident = np.eye(128, dtype=np.float32)
    gcols = np.concatenate([_cols(inp["norm_pre1"][0]), _cols(inp["norm_pre2"][0])], axis=1)
    gpost = _bcast(inp["norm_post1"][0])
    win = np.ascontiguousarray(inp["w_in"][0][:, :NPROJ])
    maps = []
    for c in range(NCORES):
        maps.append({
            "x": np.ascontiguousarray(np.concatenate([xp[c * T_P:(c + 1) * T_P], xs[c * T_S:(c + 1) * T_S]], 0)),
            "wg": inp["ff1_gate"][0], "wu": inp["ff1_up"][0], "wd": inp["ff1_down"][0], "win": win,
            "gcols": gcols, "gpost": gpost, "identd": ident,
            "cwin": np.ascontiguousarray(inp["cache_win"][0, c * 16:(c + 1) * 16].reshape(16, 512, 256)),
        })
    res = run_bass_kernel_spmd(nc, maps, core_ids=list(range(NCORES)))
    return res.results


def kernel(**inp):
    inp = {k: np.asarray(v) for k, v in inp.items()}
    r1 = run_l1(inp)
    proj_p = np.concatenate([r["projo"][:T_P] for r in r1], 0)
    proj_s = np.concatenate([r["projo"][T_P:] for r in r1], 0)
    x1_p = np.concatenate([r["x1o"][:T_P] for r in r1], 0)
    x1_s = np.concatenate([r["x1o"][T_P:] for r in r1], 0)
    kv_prompt = proj_p[:, 5120:5632].reshape(1, 1, 8192, 4, 2, 64)
    kv_sample = proj_s[:, 5120:5632].reshape(1, 128, 4, 4, 2, 64)
    win_prompt = proj_p[8192 - 512:, 5632:5888].reshape(1, 1, 512, 2, 2, 64)
    win_sample = np.concatenate([r["wino"] for r in r1], 0).reshape(1, 128, 512, 2, 2, 64)
    ohg_p, ohg_s, hg_p, hg_s = run_l2h(inp, proj_p, proj_s)
    ons_p = run_l2n_prompt(inp, proj_p)
    ons_s = run_l2n_sample(inp, proj_s)
    y_p, y_s = run_l3(inp, x1_p, x1_s, ohg_p, ohg_s, ons_p, ons_s)
    y_prompt = y_p.reshape(1, 8192, D)
    y_sample = y_s.reshape(128, 4, D)
    return (y_prompt, y_sample, np.ascontiguousarray(kv_prompt), np.ascontiguousarray(kv_sample),
            np.ascontiguousarray(win_prompt), win_sample, hg_p, hg_s)


CH = 32
SEG = 2048


def build_l2h(do_prompt=True, do_sample=True):
    nc = bass.Bass("TRN2", target_bir_lowering=False)
    dt = lambda n, s, k="ExternalInput": nc.dram_tensor(n, s, F32, kind=k).ap()
    qT = dt("qT", [128, 8192])
    fT = dt("fT", [128, 8192])
    v32 = dt("v32", [32, 256, 128])
    g32 = dt("g32", [32, 256, 128])
    lbc = dt("lbc", [128, 2])
    qTs = dt("qTs", [128, 512])
    fTs = dt("fTs", [128, 512])
    v4 = dt("v4", [4, 128, 128])
    g4 = dt("g4", [4, 128, 128])
    lbs = dt("lbs", [128, 16])
    st0 = dt("st0", [16, 8, 128, 128])
    gn32 = dt("gn32", [32, 128])
    rmask = dt("rmask", [128, SEG])
    rmask4 = dt("rmask4", [128, 512])
    trid = dt("trid", [32, 32])
    identd = dt("identd", [128, 128])
    o32 = dt("o32", [32, 256, 128], "ExternalOutput")
    Sp = dt("Sp", [128, 128], "ExternalOutput")
    o4 = dt("o4", [4, 128, 128], "ExternalOutput")
    Ss = dt("Ss", [16, 8, 128, 128], "ExternalOutput")

    b = Bld(nc)
    identf = b.sb("identf", [128, 128], F32)
    ident = b.sb("ident", [128, 128], BF16)
    b.dma("sp", identf[:], identd)
    b.copy(ident[:], identf[:])
    tri = b.sb("tri", [32, 32], F32)
    b.dma("sp", tri[:], trid)
    gn = b.sb("gn", [32, 128], F32)
    b.dma("sp", gn[:], gn32)
    rm = b.sb("rm", [128, SEG], F32)
    b.dma("sp", rm[:], rmask)
    rm4 = b.sb("rm4", [128, 512], F32)
    b.dma("sp", rm4[:], rmask4)
    lbr = b.sb("lbr", [128, 16], F32)
    lbv = b.sb("lbv", [128, 8], F32)
    oml = b.sb("oml", [128, 8], F32)

    qr = b.sb("qr", [128, SEG], F32)
    fr = b.sb("fr", [128, SEG], F32)
    bc = b.sb("bc", [128, SEG], F32)
    kk = b.sb("kk", [128, SEG], F32)
    t1 = b.sb("t1", [128, SEG], F32)
    qb = b.sb("qb", [128, SEG], BF16)
    kb = b.sb("kb", [128, SEG], BF16)
    kd = b.sb("kd", [128, SEG], BF16)
    ebl = b.sb("ebl", [128, 128], F32)
    vs = b.sb("vs", [32, 64, 128], BF16)
    gs = b.sb("gs", [32, 64, 128], F32)
    oa = b.sb("oa", [32, 64, 128], F32)
    sq = b.sb("sq", [32, 64, 128], F32)
    rs = b.sb("rs", [32, 128], F32)
    NSB = 8
    S = [b.sb("S%d" % i, [128, 128], F32) for i in range(NSB)]
    Sbf = [b.sb("Sbf%d" % i, [128, 128], BF16) for i in range(NSB)]
    atm = [b.sb("atm%d" % i, [32, 32], BF16) for i in range(3)]
    kdT = [b.sb("kdT%d" % i, [32, 128], BF16) for i in range(3)]
    psA = [b.ps("psA%d" % i, [128, 512]) for i in range(2)]
    psK = [b.ps("psK%d" % i, [128, 1024], BF16) for i in range(2)]
    psO = [b.ps("psO%d" % i, [128, 512]) for i in range(2)]
    psS = [b.ps("psS%d" % i, [128, 512]) for i in range(2)]
    cnt = [0]

    def prep(q_src, f_src, n, nh, C, rmk):
        per = n // nh
        nch = n // C
        v3 = lambda t: t[:, 0:n].rearrange("p (h m) -> p h m", h=nh)
        lb_bc = lbv[:, 0:nh].unsqueeze(2).broadcast_to([128, nh, per])
        oml_bc = oml[:, 0:nh].unsqueeze(2).broadcast_to([128, nh, per])
        b.dma("sp", qr[:, 0:n], q_src)
        b.dma("act", fr[:, 0:n], f_src)
        b.act(t1[:, 0:n], fr[:, 0:n], AF.Sigmoid)
        b.tt(v3(t1), v3(t1), oml_bc, ALU.mult)
        b.tt(v3(fr), v3(t1), lb_bc, ALU.add)
        b.ts(kk[:, 0:n], fr[:, 0:n], -1.0, ALU.mult, 1.0, ALU.add)
        b.act(t1[:, 0:n], fr[:, 0:n], AF.Ln)
        b.P.op("dve", lambda e: e.tensor_tensor_scan(out=bc[:, 0:n], data0=rmk[:, 0:n], data1=t1[:, 0:n],
                                                     initial=0.0, op0=ALU.mult, op1=ALU.add),
               [rmk[:, 0:n], t1[:, 0:n]], [bc[:, 0:n]])
        b.act(fr[:, 0:n], qr[:, 0:n], AF.Silu)
        b.act(t1[:, 0:n], bc[:, 0:n], AF.Exp)
        b.tt(qb[:, 0:n], fr[:, 0:n], t1[:, 0:n], ALU.mult)
        b.act(t1[:, 0:n], bc[:, 0:n], AF.Exp, scale=-1.0)
        b.tt(kb[:, 0:n], kk[:, 0:n], t1[:, 0:n], ALU.mult)
        bc3 = bc[:, 0:n].rearrange("p (c m) -> p c m", m=C)
        bl_bc = bc3[:, :, C - 1:C].broadcast_to([128, nch, C])
        b.tt(t1[:, 0:n].rearrange("p (c m) -> p c m", m=C), bl_bc, bc3, ALU.subtract)
        b.act(t1[:, 0:n], t1[:, 0:n], AF.Exp)
        b.tt(kd[:, 0:n], kk[:, 0:n], t1[:, 0:n], ALU.mult)
        b.act(ebl[:, 0:nch], bc3[:, :, C - 1], AF.Exp)

    def chunk_front(ci, C, cg=None):
        k = cnt[0] % 2
        k3 = cnt[0] % 3
        cnt[0] += 1
        cg = ci if cg is None else cg
        cs = slice(cg * C, (cg + 1) * C)
        pa, pk = psA[k], psK[k]
        b.mm(pa[0:C, 0:C], kb[:, cs], qb[:, cs])
        b.tt(atm[k3][0:C, 0:C], pa[0:C, 0:C], tri[0:C, 0:C], ALU.mult)
        b.tr(pk[0:C, 0:128], kd[:, cs], ident[:, :])
        b.copy(kdT[k3][0:C, :], pk[0:C, 0:128], eng="act")
        return (k, k3, ci, cg, cs, C)

    def chunk_back(tok, Sin, Sout, Sbx):
        k, k3, ci, cg, cs, C = tok
        po, pS = psO[k], psS[k]
        b.mm(pS[:, 0:128], kdT[k3][0:C, :], vs[0:C, ci, :])
        b.stt(Sout[:], Sin[:], ebl[:, cg:cg + 1], pS[:, 0:128], ALU.mult, ALU.add)
        b.mm(po[0:C, 0:128], atm[k3][0:C, 0:C], vs[0:C, ci, :], start=True, stop=False)
        b.mm(po[0:C, 0:128], qb[:, cs], Sbx[:], start=False, stop=True)
        b.copy(oa[0:C, ci, :], po[0:C, 0:128], eng="act")

    def post(C, nch, g_src, o_dst):
        b.dma("sp", gs[0:C, 0:nch, :], g_src)
        b.tt(sq[0:C, 0:nch, :], oa[0:C, 0:nch, :], oa[0:C, 0:nch, :], ALU.mult)
        b.P.op("dve", lambda e: e.reduce_sum(out=rs[0:C, 0:nch], in_=sq[0:C, 0:nch, :], axis=AX.X),
               [sq[0:C, 0:nch, :]], [rs[0:C, 0:nch]])
        b.act(rs[0:C, 0:nch], rs[0:C, 0:nch], AF.Sqrt, bias=EPS, scale=1.0 / 128)
        b.recip(rs[0:C, 0:nch], rs[0:C, 0:nch])
        b.tt(oa[0:C, 0:nch, :], oa[0:C, 0:nch, :], rs[0:C, 0:nch].unsqueeze(2).broadcast_to([C, nch, 128]), ALU.mult)
        b.tt(oa[0:C, 0:nch, :], oa[0:C, 0:nch, :], gn[0:C, :].unsqueeze(1).broadcast_to([C, nch, 128]), ALU.mult)
        b.act(gs[0:C, 0:nch, :], gs[0:C, 0:nch, :], AF.Silu)
        b.tt(oa[0:C, 0:nch, :], oa[0:C, 0:nch, :], gs[0:C, 0:nch, :], ALU.mult)
        b.dma("sp", o_dst, oa[0:C, 0:nch, :])

    def lower_bounds(src, nh):
        b.dma("sp", lbr[:, 0:2 * nh], src)
        l3 = lbr[:, 0:2 * nh].rearrange("p (h r) -> p h r", r=2)
        b.tt(lbv[:, 0:nh], l3[:, :, 0], l3[:, :, 1], ALU.subtract)
        b.act(lbv[:, 0:nh], lbv[:, 0:nh], AF.Sigmoid)
        b.ts(oml[:, 0:nh], lbv[:, 0:nh], -1.0, ALU.mult, 1.0, ALU.add)

    if do_prompt:
        lower_bounds(lbc, 1)
        b.memset(S[0][:], 0.0)
        b.memset(Sbf[0][:], 0.0)
        nseg = 8192 // SEG
        cps = SEG // CH
        for sg_ in range(nseg):
            prep(qT[:, sg_ * SEG:(sg_ + 1) * SEG], fT[:, sg_ * SEG:(sg_ + 1) * SEG], SEG, 1, CH, rm)
            b.dma("pool", vs[:, 0:cps, :], v32[:, sg_ * cps:(sg_ + 1) * cps, :])
            tok = chunk_front(0, CH)
            for ci in range(cps):
                gi = sg_ * cps + ci
                nxt = chunk_front(ci + 1, CH) if ci + 1 < cps else None
                chunk_back(tok, S[gi % 2], S[(gi + 1) % 2], Sbf[gi % 3])
                b.copy(Sbf[(gi + 1) % 3][:], S[(gi + 1) % 2][:], eng="act")
                tok = nxt
            post(CH, cps, g32[:, sg_ * cps:(sg_ + 1) * cps, :], o32[:, sg_ * cps:(sg_ + 1) * cps, :])
        b.dma("sp", Sp, S[(8192 // CH) % 2][:])
    if do_sample:
        lower_bounds(lbs, 8)
        prep(qTs, fTs, 512, 8, 4, rm4)
        for grp in range(2):
            b.dma("pool", vs[0:4, 0:64, :], v4[:, grp * 64:(grp + 1) * 64, :])
            PF = 5
            for jl in range(min(PF, 64)):
                j = grp * 64 + jl
                b.dma("sp", S[j % NSB][:], st0[j % 16, j // 16])
            tok = chunk_front(0, 4, cg=grp * 64)
            for jl in range(64):
                j = grp * 64 + jl
                h_, b_ = j // 16, j % 16
                Sx, Sbx = S[j % NSB], Sbf[j % NSB]
                if jl + PF < 64:
                    jn = j + PF
                    b.dma("sp", S[jn % NSB][:], st0[jn % 16, jn // 16])
                b.copy(Sbx[:], Sx[:], eng="act")
                nxt = chunk_front(jl + 1, 4, cg=j + 1) if jl + 1 < 64 else None
                chunk_back(tok, Sx, Sx, Sbx)
                b.dma("act", Ss[b_, h_], Sx[:])
                tok = nxt
            post(4, 64, g4[:, grp * 64:(grp + 1) * 64, :], o4[:, grp * 64:(grp + 1) * 64, :])
    b.finish()
    return nc


def l2h_consts():
    rmask = np.ones((128, SEG), np.float32)
    rmask[:, ::CH] = 0.0
    rmask4 = np.ones((128, 512), np.float32)
    rmask4[:, ::4] = 0.0
    tri = np.triu(np.ones((32, 32), np.float32))
    return {"rmask": rmask, "rmask4": rmask4, "trid": tri, "identd": np.eye(128, dtype=np.float32)}


def run_l2h(inp, proj_p, proj_s):
    nc = build_l2h()
    cst = l2h_consts()
    gn32 = _bcast(inp["hg_gnorm"][0])[:32]
    hg_lb = inp["hg_lb"]
    maps = []
    ps4 = proj_s.reshape(128, 4, NPROJ)
    for c in range(NCORES):
        hs = slice(c * 128, (c + 1) * 128)
        m = dict(cst)
        m["gn32"] = np.ascontiguousarray(gn32)
        m["qT"] = np.ascontiguousarray(proj_p[:, 0 * 1024:][:, hs].T)
        m["fT"] = np.ascontiguousarray(proj_p[:, 1 * 1024:][:, hs].T)
        m["v32"] = np.ascontiguousarray(proj_p[:, 2 * 1024:][:, hs].reshape(256, 32, 128).transpose(1, 0, 2))
        m["g32"] = np.ascontiguousarray(proj_p[:, 3 * 1024:][:, hs].reshape(256, 32, 128).transpose(1, 0, 2))
        m["lbc"] = np.ascontiguousarray(hg_lb[:, hs].T)
        sb = ps4[c * 16:(c + 1) * 16]
        part = lambda k: sb[:, :, k * 1024:(k + 1) * 1024].reshape(16, 4, 8, 128)
        m["qTs"] = np.ascontiguousarray(part(0).transpose(3, 2, 0, 1).reshape(128, 512))
        m["fTs"] = np.ascontiguousarray(part(1).transpose(3, 2, 0, 1).reshape(128, 512))
        m["v4"] = np.ascontiguousarray(part(2).transpose(1, 2, 0, 3).reshape(4, 128, 128))
        m["g4"] = np.ascontiguousarray(part(3).transpose(1, 2, 0, 3).reshape(4, 128, 128))
        m["lbs"] = np.ascontiguousarray(hg_lb.reshape(2, 8, 128).transpose(2, 1, 0).reshape(128, 16))
        m["st0"] = np.ascontiguousarray(inp["state_hgrn"][0, c * 16:(c + 1) * 16])
        maps.append(m)
    res = run_bass_kernel_spmd(nc, maps, core_ids=list(range(NCORES)))
    r = res.results
    o_p = np.concatenate([r[c]["o32"].transpose(1, 0, 2).reshape(8192, 128) for c in range(NCORES)], axis=1)
    hg_p = np.stack([r[c]["Sp"] for c in range(NCORES)], 0).reshape(1, 1, 8, 128, 128)
    o_s = np.concatenate([r[c]["o4"].reshape(4, 8, 16, 128).transpose(2, 0, 1, 3).reshape(16, 4, 1024)
                          for c in range(NCORES)], 0)
    hg_s = np.concatenate([r[c]["Ss"] for c in range(NCORES)], 0).reshape(1, 128, 8, 128, 128)
    return o_p, o_s.reshape(512, 1024), hg_p, hg_s


def build_l3():
    nc = bass.Bass("TRN2", target_bir_lowering=False)
    dt = lambda n, s, k="ExternalInput": nc.dram_tensor(n, s, F32, kind=k).ap()
    x1 = dt("x1", [T_ALL, D])
    ohgT = dt("ohgT", [1024, T_ALL])
    onsT = dt("onsT", [1024, T_ALL])
    wgab = dt("wgab", [D, 2 * D])
    wphg = dt("wphg", [1024, D])
    wpns = dt("wpns", [1024, D])
    wout = dt("wout", [D, D])
    wg = dt("wg", [D, DFF])
    wu = dt("wu", [D, DFF])
    wd = dt("wd", [DFF, D])
    gcols = dt("gcols", [128, 2 * KC])
    gpost2 = dt("gpost2", [128, D])
    gpost3 = dt("gpost3", [128, D])
    identd = dt("identd", [128, 128])
    yo = dt("yo", [T_ALL, D], "ExternalOutput")
    x2s = nc.dram_tensor("x2s", [T_ALL, D], F32).ap()

    b = Bld(nc)
    dn = Dense(b, identd)
    gc = b.sb("gc", [128, 2 * KC], F32)
    gp = b.sb("gp", [128, D], F32)
    b.dma("sp", gc[:], gcols)
    oT = [dn.aT[:, 0:8, :], dn.aT[:, 8:16, :]]
    sga = dn.sg

    for h in range(2):
        tiles = half_tiles(h)
        ntok = 576 if h == 0 else 512
        for (t0, rows, g0) in tiles:
            xt = dn.xt[dn.nX % 2]
            b.dma("sp", xt[0:rows, :], x1[g0:g0 + rows, :])
            dn.norm_to_hT(xt[0:rows, :], rows, t0, gc[:, 0:KC])
        for src, dst in ((ohgT, oT[0]), (onsT, oT[1])):
            s3 = src.rearrange("(kc p) t -> p kc t", p=128)
            if h == 0:
                b.dma("pool", dst[:, :, 0:512], s3[:, :, 0:512])
                b.dma("pool", dst[:, :, 512:576], s3[:, :, T_P:T_P + 64])
            else:
                b.dma("pool", dst[:, :, 0:512], s3[:, :, 512:1024])
        nblk = D // 256

        def issue(i):
            dn.load_w(dn.wA[i % 2], wgab, i * 256, 256, KC)
            dn.load_w(dn.wB[i % 2], wgab, D + i * 256, 256, KC)
            dn.load_w(dn.wD[i % 2][:, 0:8, :], wphg, i * 256, 256, 8)
            dn.load_w(dn.wD[i % 2][:, 8:16, :], wpns, i * 256, 256, 8)
        issue(0)
        for i in range(nblk):
            if i + 1 < nblk:
                issue(i + 1)
            wa, wb, wc = dn.wA[i % 2], dn.wB[i % 2], dn.wD[i % 2]
            for ti, (t0, rows, g0) in enumerate(tiles):
                k = dn.nG % 2
                dn.nG += 1
                pga, pgb, ph, pn = dn.psG[k], dn.psU[k], dn.psO[0], dn.psO[1]
                for kc in range(KC):
                    b.mm(pga[0:rows, 0:256], dn.hT[:, kc, t0:t0 + rows], wa[:, kc, :], start=(kc == 0), stop=(kc == KC - 1))
                for kc in range(KC):
                    b.mm(pgb[0:rows, 0:256], dn.hT[:, kc, t0:t0 + rows], wb[:, kc, :], start=(kc == 0), stop=(kc == KC - 1))
                for kc in range(8):
                    b.mm(ph[0:rows, 0:256], oT[0][:, kc, t0:t0 + rows], wc[:, kc, :], start=(kc == 0), stop=(kc == 7))
                for kc in range(8):
                    b.mm(pn[0:rows, 0:256], oT[1][:, kc, t0:t0 + rows], wc[:, 8 + kc, :], start=(kc == 0), stop=(kc == 7))
                s1, s2 = dn.ev[0], dn.ev[1]
                b.act(s1[0:rows, :], pga[0:rows, 0:256], AF.Sigmoid)
                b.act(s2[0:rows, :], pgb[0:rows, 0:256], AF.Sigmoid)
                b.tt(s1[0:rows, :], s1[0:rows, :], ph[0:rows, 0:256], ALU.mult)
                b.tt(s2[0:rows, :], s2[0:rows, :], pn[0:rows, 0:256], ALU.mult)
                b.tt(dn.ybuf[0:rows, ti, i * 256:(i + 1) * 256], s1[0:rows, :], s2[0:rows, :], ALU.add)
        for ti, (t0, rows, g0) in enumerate(tiles):
            dn.to_T(dn.ybuf[:, ti, :], rows, dn.hT, t0)

        def sink(ti, t0, rows, c0, w, po):
            b.copy(dn.ybuf[0:rows, ti, c0:c0 + w], po[0:rows, 0:w], eng="act")
        dn.proj(dn.hT, KC, wout, (0, D), [(t0, rows) for (t0, rows, g0) in tiles], sink)
        b.dma("sp", gp[:], gpost2)
        for ti, (t0, rows, g0) in enumerate(tiles):
            xt = dn.xt[dn.nX % 2]
            b.dma("sp", xt[0:rows, :], x1[g0:g0 + rows, :])
            y = dn.ybuf[0:rows, ti, :]
            r = dn.rstd(y, rows, 1)
            tmp = dn.hn[(dn.nX + 1) % 2]
            b.stt(tmp[0:rows, :], y, r, gp[0:rows, :], ALU.mult, ALU.mult)
            b.tt(xt[0:rows, :], tmp[0:rows, :], xt[0:rows, :], ALU.add)
            b.dma("sp", x2s[g0:g0 + rows, :], xt[0:rows, :])
            dn.norm_to_hT(xt[0:rows, :], rows, t0, gc[:, KC:2 * KC])
        dn.gate_up(wg, wu, half_groups(h))
        dn.down(wd, [(t0, rows) for (t0, rows, g0) in tiles])
        b.dma("sp", gp[:], gpost3)
        for ti, (t0, rows, g0) in enumerate(tiles):
            xt = dn.xt[dn.nX % 2]
            dn.nX += 1
            b.dma("sp", xt[0:rows, :], x2s[g0:g0 + rows, :])
            y = dn.ybuf[0:rows, ti, :]
            r = dn.rstd(y, rows, 1)
            tmp = dn.hn[ti % 2]
            b.stt(tmp[0:rows, :], y, r, gp[0:rows, :], ALU.mult, ALU.mult)
            b.stt(xt[0:rows, :], tmp[0:rows, :], 0.5, xt[0:rows, :], ALU.mult, ALU.add)
            b.dma("sp", yo[g0:g0 + rows, :], xt[0:rows, :])
    b.finish()
    return nc


def run_l3(inp, x1_p, x1_s, ohg_p, ohg_s, ons_p, ons_s):
    nc = build_l3()
    gcols = np.concatenate([_cols(inp["norm_pre2"][0]), _cols(inp["norm_pre3"][0])], axis=1)
    wgab = np.ascontiguousarray(inp["w_in"][0][:, NPROJ:])
    base = {"wgab": wgab, "wphg": inp["w_proj_hg"][0], "wpns": inp["w_proj_nsa"][0], "wout": inp["w_out"][0],
            "wg": inp["ff2_gate"][0], "wu": inp["ff2_up"][0], "wd": inp["ff2_down"][0], "gcols": gcols,
            "gpost2": _bcast(inp["norm_post2"][0]), "gpost3": _bcast(inp["norm_post3"][0]),
            "identd": np.eye(128, dtype=np.float32)}
    maps = []
    for c in range(NCORES):
        m = dict(base)
        ps, ss = slice(c * T_P, (c + 1) * T_P), slice(c * T_S, (c + 1) * T_S)
        m["x1"] = np.ascontiguousarray(np.concatenate([x1_p[ps], x1_s[ss]], 0))
        m["ohgT"] = np.ascontiguousarray(np.concatenate([ohg_p[ps], ohg_s[ss]], 0).T)
        m["onsT"] = np.ascontiguousarray(np.concatenate([ons_p[ps], ons_s[ss]], 0).T)
        maps.append(m)
    res = run_bass_kernel_spmd(nc, maps, core_ids=list(range(NCORES)))
    y_p = np.concatenate([r["yo"][:T_P] for r in res.results], 0)
    y_s = np.concatenate([r["yo"][T_P:] for r in res.results], 0)
    return y_p, y_s


NEG = -32768.0
SLOPES = np.power(2.0, -8.0 * np.arange(1, 17) / 16).astype(np.float64)
GELU_C = 1.5957691216057308


def nsa_prompt_consts(core):
    tiles = [core + 8 * j for j in range(8)]
    p = np.arange(128)
    c = {}
    c["gmat"] = (np.arange(128)[:, None] == (np.arange(8192)[None, :] // 64)).astype(np.float32)
    cs = np.arange(512)[:, None] * 16
    ss = np.arange(128)[None, :] * 64
    ov = ((cs <= ss + 63) & (cs + 31 >= ss)).astype(np.float32)
    ov[511] = 0.0
    c["ovl"] = np.ascontiguousarray(ov.reshape(4, 128, 128).transpose(1, 0, 2))
    r = np.arange(72) - (7 - core)
    c["btab"] = np.ascontiguousarray((SLOPES[None, :, None] * (p[:, None, None] - 64 - 128 * r[None, None, :])).astype(np.float32).reshape(128, 16 * 72))
    cb = np.zeros((128, 8, 16, 4), np.float64)
    cm = np.zeros((128, 8, 2, 128), np.float32)
    keep = np.zeros((128, 8, 128), np.float32)
    add = np.zeros((128, 8, 128), np.float32)
    tt = np.arange(128)
    blk = np.arange(128)
    for j, i in enumerate(tiles):
        t0 = 128 * i
        for ct in range(4):
            cb[:, j, :, ct] = SLOPES[None, :] * (16 * (128 * ct + p[:, None]) + 31 - (t0 + 64))
        nct = i // 16 + 1
        for rr in range(2):
            ct = nct - 1 - rr
            if ct < 0:
                continue
            cpos = 16 * (128 * ct + p) + 31
            ok = (cpos[:, None] <= (t0 + tt)[None, :]) & ((128 * ct + p) < 511)[:, None]
            cm[:, j, rr, :] = np.where(ok, 0.0, NEG)
        qpos = t0 + tt
        qb = qpos // 64
        valid = blk[None, :] <= qb[:, None]
        f0 = blk[None, :] == 0
        f1 = blk[None, :] == qb[:, None]
        f2 = blk[None, :] == (qb[:, None] - 1)
        forced = f0 | f1 | f2
        keep[:, j, :] = (valid & ~forced).astype(np.float32)
        a = np.where(valid, 0.0, -1e30)
        a = np.where(f2, 1e4, a)
        a = np.where(f1, 2e4, a)
        a = np.where(f0, 3e4, a)
        add[:, j, :] = a
    c["cbias"] = np.ascontiguousarray(cb.astype(np.float32).reshape(128, 8 * 16 * 4))
    c["cmask"] = np.ascontiguousarray(cm.reshape(128, 8 * 2 * 128))
    c["keepm"] = np.ascontiguousarray(keep.reshape(128, 8 * 128))
    c["addm"] = np.ascontiguousarray(add.reshape(128, 8 * 128))
    causal = np.where(p[:, None] <= tt[None, :], 0.0, NEG).astype(np.float32)
    wlow = np.where(p[:, None] > tt[None, :], 0.0, NEG).astype(np.float32)
    zero = np.zeros((128, 128), np.float32)
    full = np.full((128, 128), NEG, np.float32)
    dms = [zero if q < core else (causal if q == core else full) for q in range(8)]
    dmw = []
    for q in range(12):
        if q < core or q > core + 4:
            dmw.append(full)
        elif q == core:
            dmw.append(wlow)
        elif q == core + 4:
            dmw.append(causal)
        else:
            dmw.append(zero)
    c["dms"] = np.ascontiguousarray(np.stack(dms, 1).reshape(128, 8 * 128))
    c["dmw"] = np.ascontiguousarray(np.stack(dmw, 1).reshape(128, 12 * 128))
    c["identd"] = np.eye(128, dtype=np.float32)
    import ml_dtypes
    bf = lambda x: np.asarray(x, np.float64).astype(ml_dtypes.bfloat16).astype(np.float64)
    tab = np.zeros((5, 72, 16), np.float64)
    rr = np.arange(72) - (7 - core)
    for h in range(16):
        a = 8.0 * SLOPES[h]
        a0 = bf(a)
        a1 = bf(a - a0)
        cc = -1024.0 * rr * SLOPES[h]
        c0 = bf(cc)
        c1 = bf(cc - c0)
        c2 = bf(cc - c0 - c1)
        tab[0, :, h] = a0
        tab[1, :, h] = a1
        tab[2, :, h] = c0
        tab[3, :, h] = c1
        tab[4, :, h] = c2
    c["btab5"] = np.ascontiguousarray(tab.astype(np.float32).reshape(5, 72 * 16))
    bl = np.ones((5, 128), np.float32)
    bl[0] = p - 64
    bl[1] = p - 64
    c["biasl"] = bl
    return c


class Nsa:
    def __init__(self, b, nc, dt):
        self.b = b
        identd = dt("identd", [128, 128])
        self.identf = b.sb("identf", [128, 128], F32)
        self.ident = b.sb("ident", [128, 128], BF16)
        b.dma("sp", self.identf[:], identd)
        b.copy(self.ident[:], self.identf[:])
        self.w1 = {}
        self.w2 = {}
        self.posT = {}
        for kind in ("k", "v"):
            w1d = dt("w1" + kind, [128, 32, 256])
            w2d = dt("w2" + kind, [128, 4, 128])
            pd = dt("pos" + kind, [128, 32])
            self.w1[kind] = b.sb("s_w1" + kind, [128, 32, 256], BF16)
            self.w2[kind] = b.sb("s_w2" + kind, [128, 4, 128], BF16)
            self.posT[kind] = b.sb("s_pos" + kind, [128, 32], BF16)
            b.dma("pool", self.w1[kind][:], w1d)
            b.dma("pool", self.w2[kind][:], w2d)
            b.dma("pool", self.posT[kind][:], pd)
        self.bcol = b.sb("bcol", [128, 4], F32)
        self.xs = [b.sb("xs%d" % i, [128, 2064], BF16) for i in range(2)]
        self.xsY = b.sb("xsY", [128, 16, 129], BF16)
        self.gh = [b.sb("gh%d" % i, [128, 128], BF16) for i in range(4)]
        self.tx = b.sb("tx", [128, 128], F32)
        self.tu = b.sb("tu", [128, 128], F32)
        self.psS = [b.ps("psS%d" % i, [128, 512]) for i in range(2)]
        self.psAcc = [b.ps("psAcc%d" % i, [128, 512]) for i in range(2)]
        self.psH = [b.ps("psH%d" % i, [128, 512]) for i in range(2)]
        self.psK2 = b.ps("psK2", [128, 512])
        self.psT = b.ps("psT", [128, 1024], BF16)
        self.PT = [b.sb("PT%d" % i, [128, 128], BF16) for i in range(3)]
        self.PT4 = None
        self.nS = 0
        self.nA = 0
        self.nH = 0
        self.nP = 0
        self.nX = 0
        self.bias_done = False

    def cmp_bias(self):
        b = self.b
        for ki, kind in enumerate(("k", "v")):
            for hc in range(2):
                ph = self.psH[self.nH % 2]
                self.nH += 1
                for l in range(32):
                    b.mm(ph[:, 0:1], self.w1[kind][0:64, l, hc * 128:(hc + 1) * 128], self.posT[kind][0:64, l:l + 1],
                         start=(l == 0), stop=(l == 31))
                b.copy(self.bcol[:, ki * 2 + hc:ki * 2 + hc + 1], ph[:, 0:1], eng="act")

    def compress_tile(self, kind, src_dram_cols, N, kdst=None, vdst=None, xs_ap=None):
        b = self.b
        ki = 0 if kind == "k" else 1
        L = 16 * (N - 1) + 32
        if xs_ap is not None:
            xs = xs_ap
        else:
            xs = self.xs[self.nX % 2]
            self.nX += 1
            b.dma("pool", xs[:, 0:L], src_dram_cols)
        xy = self.xsY
        nc_ = L // 16
        b.copy(xy[:, :, 0:nc_], xs[:, 0:L].rearrange("p (c r) -> p r c", r=16))
        for n in range(2):
            for hc in range(2):
                ph = self.psH[self.nH % 2]
                self.nH += 1
                for l in range(32):
                    b.mm(ph[:, 0:N], self.w1[kind][n * 64:(n + 1) * 64, l, hc * 128:(hc + 1) * 128],
                         xy[n * 64:(n + 1) * 64, l % 16, (l // 16):(l // 16) + N], start=(l == 0), stop=(l == 31))
                tx, tu, gh = self.tx, self.tu, self.gh[n * 2 + hc]
                b.act(tx[:, 0:N], ph[:, 0:N], AF.Identity, bias=self.bcol[:, ki * 2 + hc:ki * 2 + hc + 1])
                b.tt(tu[:, 0:N], tx[:, 0:N], tx[:, 0:N], ALU.mult)
                b.ts(tu[:, 0:N], tu[:, 0:N], 0.044715, ALU.mult, 1.0, ALU.add)
                b.tt(tu[:, 0:N], tu[:, 0:N], tx[:, 0:N], ALU.mult)
                b.act(tu[:, 0:N], tu[:, 0:N], AF.Sigmoid, scale=GELU_C)
                b.tt(gh[:, 0:N], tx[:, 0:N], tu[:, 0:N], ALU.mult)
        pk = self.psK2
        if kind == "k":
            for q in range(4):
                b.mm(pk[:, 0:N], self.w2[kind][:, q, :], self.gh[q][:, 0:N], start=(q == 0), stop=(q == 3))
            b.copy(kdst, pk[:, 0:N], eng="act")
        else:
            for q in range(4):
                b.mm(pk[0:N, 0:128], self.gh[q][:, 0:N], self.w2[kind][:, q, :], start=(q == 0), stop=(q == 3))
            b.copy(vdst[0], pk[0:N, 0:64], eng="act")
            b.copy(vdst[1], pk[0:N, 64:128], eng="act")

    def compress_group(self, kind, X4, kdst3=None, vdst=None):
        b = self.b
        ki = 0 if kind == "k" else 1
        NB, N = 4, 127
        W = NB * N
        for n in range(2):
            for hc in range(2):
                ph = self.psH[self.nH % 2]
                self.nH += 1
                po = ph[:, 0:W].rearrange("p (s c) -> p s c", s=NB)
                for l in range(32):
                    b.mm(po, self.w1[kind][n * 64:(n + 1) * 64, l, hc * 128:(hc + 1) * 128],
                         X4[n * 64:(n + 1) * 64, :, l % 16, (l // 16):(l // 16) + N], start=(l == 0), stop=(l == 31))
                tx, tu, gh = self.tx4, self.tu4, self.gh4[n * 2 + hc]
                b.act(tx[:, 0:W], ph[:, 0:W], AF.Identity, bias=self.bcol[:, ki * 2 + hc:ki * 2 + hc + 1])
                b.tt(tu[:, 0:W], tx[:, 0:W], tx[:, 0:W], ALU.mult)
                b.ts(tu[:, 0:W], tu[:, 0:W], 0.044715, ALU.mult, 1.0, ALU.add)
                b.tt(tu[:, 0:W], tu[:, 0:W], tx[:, 0:W], ALU.mult)
                b.act(tu[:, 0:W], tu[:, 0:W], AF.Sigmoid, scale=GELU_C)
                b.tt(gh[:, 0:W], tx[:, 0:W], tu[:, 0:W], ALU.mult)
        pk = self.psK2
        if kind == "k":
            for q in range(4):
                b.mm(pk[:, 0:W], self.w2[kind][:, q, :], self.gh4[q][:, 0:W], start=(q == 0), stop=(q == 3))
            b.copy(kdst3, pk[:, 0:W].rearrange("p (s c) -> p s c", s=NB), eng="act")
        else:
            for sq in range(NB):
                for q in range(4):
                    b.mm(pk[0:N, 0:128], self.gh4[q][:, sq * N:(sq + 1) * N], self.w2[kind][:, q, :], start=(q == 0), stop=(q == 3))
                b.copy(vdst[sq][0], pk[0:N, 0:64], eng="act")
                b.copy(vdst[sq][1], pk[0:N, 64:128])

    def branch(self, steps, qrhs, nq, ncols, scale=0.125):
        b = self.b
        pacc = self.psAcc[self.nA % 2]
        self.nA += 1
        ns = len(steps)
        pts = {}

        def front(si):
            st = steps[si]
            ps = self.psS[self.nS % 2]
            self.nS += 1
            ex = st.get("extra", [])
            rows = st.get("rows", 128)
            b.mm(ps[0:rows, 0:nq], st["k"], qrhs, start=True, stop=(len(ex) == 0))
            for ei, (l_, r_) in enumerate(ex):
                b.mm(ps[0:rows, 0:nq], l_, r_, start=False, stop=(ei == len(ex) - 1))
            pt = self.PT[self.nP % 3]
            self.nP += 1
            pts[si] = pt
            if st.get("bias") is not None:
                b.act(pt[0:rows, 0:nq], ps[0:rows, 0:nq], AF.Exp, bias=st["bias"], scale=scale)
            else:
                b.act(pt[0:rows, 0:nq], ps[0:rows, 0:nq], AF.Exp, scale=scale)

        def back(si):
            st = steps[si]
            rows = st.get("rows", 128)
            b.mm(pacc[0:nq, 0:ncols], pts[si][0:rows, 0:nq], st["v"], start=(si == 0), stop=(si == ns - 1))

        for si in range(ns + 1):
            if si < ns:
                front(si)
            if si >= 1:
                back(si - 1)
        return pacc


def branch4(ns, b, steps, qrhs, biasl):
    pacc = ns.psAcc[ns.nA % 2]
    ns.nA += 1
    nst = len(steps)
    v4 = lambda ap: ap.unsqueeze(1).broadcast_to([ap.shape[0], 4, 128])
    banks = [ns.psS[0], ns.psS[1], ns.psH[0], ns.psH[1]]
    pts = {}

    def front(si):
        st = steps[si]
        ps = banks[ns.nS % 4]
        ns.nS += 1
        po = ps[:, 0:512].rearrange("p (g t) -> p g t", g=4)
        b.mm(po, st["k"], qrhs, start=True, stop=False)
        for (l_, r_) in st["extra"]:
            b.mm(po, l_, v4(r_), start=False, stop=False)
        b.mm(po, biasl, st["brow"].unsqueeze(2).broadcast_to([5, 4, 128]), start=False, stop=True)
        pt = ns.PT4[ns.nP % len(ns.PT4)]
        ns.nP += 1
        pts[si] = pt
        b.act(pt[:, :], ps[:, 0:512], AF.Exp, scale=0.125)

    def back(si):
        st = steps[si]
        for hl in range(4):
            b.mm(pacc[:, hl * 65:(hl + 1) * 65], pts[si][:, hl * 128:(hl + 1) * 128], st["v"],
                 start=(si == 0 and hl == 0), stop=(si == nst - 1), skip=True)

    DEP = 2
    for si in range(nst + DEP):
        if si < nst:
            front(si)
        if si >= DEP:
            back(si - DEP)
    return pacc


def build_l2n():
    nc = bass.Bass("TRN2", target_bir_lowering=False)
    dt = lambda n, s, k="ExternalInput": nc.dram_tensor(n, s, F32, kind=k).ap()
    b = Bld(nc)
    ns = Nsa(b, nc, dt)
    qTd = dt("qT", [128, 8, 1024])
    gated = dt("gates", [128, 8, 48])
    KsTd = dt("KsT", [128, 8192])
    KwTd = dt("KwT", [128, 8192])
    KcTd = dt("KcT", [128, 8192])
    VcTd = dt("VcT", [128, 8192])
    Vsd = dt("Vs", [128, 64, 128])
    Vwd = dt("Vw", [128, 64, 128])
    gmatd = dt("gmat", [128, 8192])
    ovld = dt("ovl", [128, 4, 128])
    btabd = dt("btab", [128, 16 * 72])
    cbiasd = dt("cbias", [128, 512])
    cmaskd = dt("cmask", [128, 2048])
    keepd = dt("keepm", [128, 1024])
    addd = dt("addm", [128, 1024])
    dmsd = dt("dms", [128, 8 * 128])
    dmwd = dt("dmw", [128, 12 * 128])
    btab5d = dt("btab5", [5, 72 * 16])
    biasld = dt("biasl", [5, 128])
    onso = dt("ons", [128, 8, 1024], "ExternalOutput")

    sbt = b.sb
    KsT = sbt("s_KsT", [128, 8192], BF16)
    KwT = sbt("s_KwT", [128, 8192], BF16)
    Vs = sbt("Vsa", [128, 64, 2, 65], BF16)
    Vw = sbt("Vwa", [128, 64, 2, 65], BF16)
    G = sbt("G", [128, 8192], BF16)
    qT = sbt("qTb", [128, 8, 1024], BF16)
    KCT = sbt("KCT", [128, 512], BF16)
    VCO = sbt("VCO", [128, 4, 2, 193], BF16)
    btab = sbt("s_btab", [128, 16 * 72], F32)
    cbias = sbt("s_cbias", [128, 512], F32)
    cmask = sbt("s_cmask", [128, 2048], BF16)
    keepm = sbt("s_keepm", [128, 1024], F32)
    addm = sbt("s_addm", [128, 1024], F32)
    dms = sbt("s_dms", [128, 8, 128], BF16)
    dmw = sbt("s_dmw", [128, 12, 128], BF16)
    btab5 = sbt("s_btab5", [5, 72, 16], BF16)
    biasl = sbt("s_biasl", [5, 128], BF16)
    ns.PT4 = [sbt("PT4_%d" % i, [128, 512], BF16) for i in range(5)]
    b.dma("pool", btab5[:], btab5d.rearrange("k (r h) -> k r h", r=72))
    b.dma("pool", biasl[:], biasld)
    gts = sbt("gts", [128, 8, 48], F32)
    sc = sbt("sc", [128, 2, 128], F32)
    s2 = sbt("s2", [128, 128], F32)
    s3 = sbt("s3", [128, 128], F32)
    m8 = sbt("m8", [128, 16], F32)
    nm = sbt("nm", [128, 128], BF16)
    nmT = sbt("nmT", [128, 2, 128], BF16)
    rd = sbt("rd", [128, 8], F32)
    oacc = [sbt("oacc%d" % i, [128, 1024], F32) for i in range(2)]

    for d_, s_ in ((KsT, KsTd), (KwT, KwTd), (G, gmatd)):
        for q in range(4):
            b.dma("pool", d_[:, q * 2048:(q + 1) * 2048], s_[:, q * 2048:(q + 1) * 2048])
    b.memset(Vs[:], 1.0)
    b.memset(Vw[:], 1.0)
    b.memset(VCO[:], 0.0)
    b.memset(KCT[:], 0.0)
    for d_, s_ in ((Vs, Vsd), (Vw, Vwd)):
        for q in range(4):
            b.dma("pool", d_[:, q * 16:(q + 1) * 16, :, 0:64], s_[:, q * 16:(q + 1) * 16, :].rearrange("p k (n d) -> p k n d", n=2))
    b.dma("pool", qT[:], qTd)
    b.dma("sp", btab[:], btabd)
    b.dma("sp", cbias[:], cbiasd)
    b.dma("pool", cmask[:], cmaskd)
    b.dma("sp", keepm[:], keepd)
    b.dma("sp", addm[:], addd)
    b.dma("pool", dms[:], dmsd.rearrange("p (q t) -> p q t", q=8))
    b.dma("pool", dmw[:], dmwd.rearrange("p (q t) -> p q t", q=12))
    b.dma("sp", gts[:], gated)
    b.act(gts[:], gts[:], AF.Sigmoid)

    ns.cmp_bias()
    b.memset(VCO[:, :, :, 64:65], 1.0)
    for n in range(2):
        b.dma("pool", VCO[:, :, n, 65:193], ovld)
    for ct in range(4):
        N = 128 if ct < 3 else 127
        L = 16 * (N - 1) + 32
        ns.compress_tile("k", KcTd[:, ct * 2048:ct * 2048 + L], N, kdst=KCT[:, ct * 128:ct * 128 + N])
        ns.compress_tile("v", VcTd[:, ct * 2048:ct * 2048 + L], N,
                         vdst=[VCO[0:N, ct, 0, 0:64], VCO[0:N, ct, 1, 0:64]])

    for j in range(8):
        oa = oacc[j % 2]
        qs = slice(j * 128, (j + 1) * 128)
        nct = j // 2 + 1
        for n in range(2):
            pb = slice(n * 64, (n + 1) * 64)
            b.memset(sc[:, n, :], 0.0, eng="dve")
            for g in range(8):
                h = n * 8 + g
                steps = []
                for ct in range(nct):
                    st = {"k": KCT[pb, ct * 128:(ct + 1) * 128], "v": VCO[:, ct, n, :],
                          "bias": cbias[:, (j * 16 + h) * 4 + ct:(j * 16 + h) * 4 + ct + 1]}
                    rr = nct - 1 - ct
                    if rr < 2:
                        st["extra"] = [(ns.ident[:, :], cmask[:, (j * 2 + rr) * 128:(j * 2 + rr + 1) * 128])]
                    steps.append(st)
                pc = ns.branch(steps, qT[pb, g, qs], 128, 193)
                b.ts(rd[:, 0:1], pc[:, 64:65], 1e-30, ALU.max)
                b.recip(rd[:, 0:1], rd[:, 0:1])
                b.stt(sc[:, n, :], pc[:, 65:193], rd[:, 0:1], sc[:, n, :], ALU.mult, ALU.add)
                b.tt(rd[:, 1:2], rd[:, 0:1], gts[:, j, h * 3:h * 3 + 1], ALU.mult)
                b.ts(oa[:, h * 64:(h + 1) * 64], pc[:, 0:64], rd[:, 1:2], ALU.mult)
            b.tt(s2[:], sc[:, n, :], keepm[:, j * 128:(j + 1) * 128], ALU.mult)
            b.tt(s2[:], s2[:], addm[:, j * 128:(j + 1) * 128], ALU.add)
            b.P.op("dve", lambda e: e.max(out=m8[:, 0:8], in_=s2[:]), [s2[:]], [m8[:, 0:8]])
            b.P.op("dve", lambda e: e.match_replace(out=s3[:], in_to_replace=m8[:, 0:8], in_values=s2[:], imm_value=-1e30),
                   [s2[:], m8[:, 0:8]], [s3[:]])
            b.P.op("dve", lambda e: e.max(out=m8[:, 8:16], in_=s3[:]), [s3[:]], [m8[:, 8:16]])
            b.ts(s3[:], s2[:], m8[:, 15:16], ALU.is_ge)
            b.ts(nm[:], s3[:], -NEG, ALU.mult, NEG, ALU.add)
            b.tr(ns.psT[:, 0:128], nm[:], ns.ident[:, :])
            b.copy(nmT[:, n, :], ns.psT[:, 0:128])
        for hg in range(4):
            n, g0 = hg // 2, (hg % 2) * 4
            h0 = hg * 4
            pb = slice(n * 64, (n + 1) * 64)
            qr = qT[pb, g0:g0 + 4, qs]
            steps = []
            for kt in range(8 * j + 8):
                ex = [(G[:, kt * 128:(kt + 1) * 128], nmT[:, n, :])]
                if kt >= 8 * j:
                    ex.append((ns.ident[:, :], dms[:, kt - 8 * j, :]))
                rp = 8 * j + 7 - kt
                steps.append({"k": KsT[pb, kt * 128:(kt + 1) * 128], "v": Vs[:, kt, n, :], "extra": ex,
                              "brow": btab5[:, rp, h0:h0 + 4]})
            pc = branch4(ns, b, steps, qr, biasl[:, :])
            for hl in range(4):
                h = h0 + hl
                den = pc[:, hl * 65 + 64:hl * 65 + 65]
                b.ts(rd[:, 2:3], den, 1e-30, ALU.max)
                b.recip(rd[:, 2:3], rd[:, 2:3])
                b.tt(rd[:, 3:4], rd[:, 2:3], gts[:, j, h * 3 + 1:h * 3 + 2], ALU.mult)
                b.stt(oa[:, h * 64:(h + 1) * 64], pc[:, hl * 65:hl * 65 + 64], rd[:, 3:4], oa[:, h * 64:(h + 1) * 64], ALU.mult, ALU.add)
            steps = []
            for q in range(12):
                kt = 8 * j - 4 + q
                if kt < 0:
                    continue
                steps.append({"k": KwT[pb, kt * 128:(kt + 1) * 128], "v": Vw[:, kt, n, :],
                              "extra": [(ns.ident[:, :], dmw[:, q, :])], "brow": btab5[:, 11 - q, h0:h0 + 4]})
            pc = branch4(ns, b, steps, qr, biasl[:, :])
            for hl in range(4):
                h = h0 + hl
                den = pc[:, hl * 65 + 64:hl * 65 + 65]
                b.ts(rd[:, 4:5], den, 1e-30, ALU.max)
                b.recip(rd[:, 4:5], rd[:, 4:5])
                b.tt(rd[:, 5:6], rd[:, 4:5], gts[:, j, h * 3 + 2:h * 3 + 3], ALU.mult)
                b.stt(oa[:, h * 64:(h + 1) * 64], pc[:, hl * 65:hl * 65 + 64], rd[:, 5:6], oa[:, h * 64:(h + 1) * 64], ALU.mult, ALU.add)
        b.dma("sp", onso[:, j, :], oa[:])
    b.finish()
    return nc


def nsa_cmp_weights(inp):
    m = {}
    for kind in ("k", "v"):
        w1 = inp["cmp_w1_" + kind][0].reshape(32, 64, 256).transpose(1, 0, 2)
        m["w1" + kind] = np.ascontiguousarray(np.concatenate([w1, w1], 0))
        w2 = inp["cmp_w2_" + kind][0].reshape(2, 128, 64)
        w2p = np.zeros((128, 4, 128), np.float32)
        for n in range(2):
            for hc in range(2):
                w2p[:, n * 2 + hc, n * 64:(n + 1) * 64] = w2[hc]
        m["w2" + kind] = w2p
        pT = inp["cmp_pos_" + kind][0].T
        m["pos" + kind] = np.ascontiguousarray(np.concatenate([pT, pT], 0))
    return m


def run_l2n_prompt(inp, proj_p):
    q = proj_p[:, 4096:5120]
    kv = proj_p[:, 5120:5888].reshape(8192, 6, 128)
    gates = proj_p[:, 5888:5936]
    cw = nsa_cmp_weights(inp)
    T = lambda a: np.ascontiguousarray(a.T)
    tok = lambda a: np.ascontiguousarray(a.reshape(64, 128, 128).transpose(1, 0, 2))
    shared = {"KcT": T(kv[:, 0]), "VcT": T(kv[:, 1]), "KsT": T(kv[:, 2]), "Vs": tok(kv[:, 3]),
              "KwT": T(kv[:, 4]), "Vw": tok(kv[:, 5])}
    shared.update(cw)
    outs = []
    ncs = []
    maps = []
    for c in range(NCORES):
        tiles = [c + 8 * j for j in range(8)]
        m = dict(shared)
        m.update(nsa_prompt_consts(c))
        qc = np.stack([q[128 * i:128 * (i + 1)] for i in tiles], 0)
        qr = qc.reshape(8, 128, 2, 8, 64).transpose(2, 4, 3, 0, 1).reshape(128, 8, 1024)
        m["qT"] = np.ascontiguousarray(qr)
        m["gates"] = np.ascontiguousarray(np.stack([gates[128 * i:128 * (i + 1)] for i in tiles], 1))
        maps.append(m)
    nc = build_l2n()
    res = run_bass_kernel_spmd(nc, maps, core_ids=list(range(NCORES)))
    o = np.zeros((8192, 1024), np.float32)
    for c in range(NCORES):
        r = res.results[c]["ons"]
        for j in range(8):
            i = c + 8 * j
            o[128 * i:128 * (i + 1)] = r[:, j, :]
    return o


U32 = mybir.dt.uint32
PAST = 2048
SCUT = None
NEGF = -30000.0


def nsa_sample_consts():
    c = {}
    p = np.arange(128)
    c["gs"] = (np.arange(128)[:, None] == (np.arange(17 * 128)[None, :] // 64)).astype(np.float32)
    cs = np.arange(128)[:, None] * 16
    ss = np.arange(64)[None, :] * 64
    ov = ((cs <= ss + 63) & (cs + 31 >= ss)).astype(np.float32)
    ov[127] = 0.0
    ov[:, 33:] = 0.0
    c["ovs"] = ov
    t = np.arange(4)
    bc = np.zeros((128, 2, 8, 4), np.float64)
    bs = np.zeros((128, 17, 2, 8, 4), np.float64)
    bw = np.zeros((128, 5, 2, 8, 4), np.float64)
    for n in range(2):
        for g in range(8):
            sl = SLOPES[n * 8 + g]
            qpos = PAST + t
            cpos = 16 * p + 31
            bc[:, n, g, :] = np.where((p < 127)[:, None], -sl * (qpos[None, :] - cpos[:, None]), NEGF)
            for tile in range(17):
                spos = 128 * tile + p
                dist = qpos[None, :] - spos[:, None]
                bs[:, tile, n, g, :] = np.where(dist >= 0, -sl * dist, NEGF)
            for tile in range(5):
                wpos = PAST - 512 + 128 * tile + p
                dist = qpos[None, :] - wpos[:, None]
                ok = (dist >= 0) & (dist < 512)
                if tile == 4:
                    ok &= (p < 4)[:, None]
                bw[:, tile, n, g, :] = np.where(ok, -sl * dist, NEGF)
    c["biasc"] = np.ascontiguousarray(bc.astype(np.float32).reshape(128, 64))
    c["biass"] = np.ascontiguousarray(bs.astype(np.float32).reshape(128, 17 * 64))
    c["biasw"] = np.ascontiguousarray(bw.astype(np.float32).reshape(128, 5 * 64))
    blk = np.arange(64)
    forced0, forced1, forced2 = blk == 0, blk == 32, blk == 31
    valid = blk <= 32
    keep = (valid & ~(forced0 | forced1 | forced2)).astype(np.float32)
    a = np.where(valid, 0.0, -1e30)
    a = np.where(forced2, 1e4, a)
    a = np.where(forced1, 2e4, a)
    a = np.where(forced0, 3e4, a)
    c["keeps"] = np.ascontiguousarray(np.broadcast_to(keep[None, :], (4, 64)).astype(np.float32))
    c["adds"] = np.ascontiguousarray(np.broadcast_to(a[None, :], (4, 64)).astype(np.float32))
    sel = np.zeros((32, 4), np.float32)
    for g in range(8):
        for tt in range(4):
            sel[g * 4 + tt, tt] = 1.0
    c["selm"] = sel
    c["pcol"] = p.astype(np.float32).reshape(128, 1)
    c["identd"] = np.eye(128, dtype=np.float32)
    return c


def build_l2s(n_pool=2560, nb=16):
    nc = bass.Bass("TRN2", target_bir_lowering=False)
    dt = lambda n, s, k="ExternalInput": nc.dram_tensor(n, s, F32, kind=k).ap()
    b = Bld(nc)
    ns = Nsa(b, nc, dt)
    cache = dt("cache", [n_pool * 128, 512])
    ptab = nc.dram_tensor("ptab", [1, nb * 16], I32, kind="ExternalInput").ap()
    cwin = dt("cwin", [nb, 512, 256])
    qTd = dt("qTs", [128, nb, 32])
    ksnd = dt("ksn", [128, nb, 4])
    kwnd = dt("kwn", [128, nb, 4])
    vsnd = dt("vsn", [4, nb, 128])
    vwnd = dt("vwn", [4, nb, 128])
    gtd = dt("gts", [32, nb, 2, 3])
    gsd = dt("gs", [128, 17 * 128])
    ovsd = dt("ovs", [128, 64])
    bcd = dt("biasc", [128, 64])
    bsd = dt("biass", [128, 17 * 64])
    bwd = dt("biasw", [128, 5 * 64])
    keepd = dt("keeps", [4, 64])
    addd = dt("adds", [4, 64])
    seld = dt("selm", [32, 4])
    pcold = dt("pcol", [128, 1])
    onso = dt("ons", [nb, 2, 32, 64], "ExternalOutput")

    sbt = b.sb
    Gs = sbt("s_gs", [128, 17 * 128], BF16)
    b.dma("pool", Gs[:], gsd)
    biasc = sbt("s_bc", [128, 2, 32], F32)
    biass = sbt("s_bs", [128, 17, 2, 32], F32)
    biasw = sbt("s_bw", [128, 5, 2, 32], F32)
    b.dma("sp", biasc[:], bcd.rearrange("p (n q) -> p n q", n=2))
    b.dma("sp", biass[:], bsd.rearrange("p (k n q) -> p k n q", k=17, n=2))
    b.dma("sp", biasw[:], bwd.rearrange("p (k n q) -> p k n q", k=5, n=2))
    keeps = sbt("s_keep", [4, 64], F32)
    adds = sbt("s_add", [4, 64], F32)
    selm = sbt("s_sel", [32, 4], F32)
    pcol = sbt("s_pcol", [128, 1], F32)
    b.dma("sp", keeps[:], keepd)
    b.dma("sp", adds[:], addd)
    b.dma("sp", selm[:], seld)
    b.dma("sp", pcol[:], pcold)
    qT = sbt("s_qT", [128, nb, 32], BF16)
    ksn = sbt("s_ksn", [128, nb, 4], BF16)
    kwn = sbt("s_kwn", [128, nb, 4], BF16)
    vsn = sbt("s_vsn", [4, nb, 128], BF16)
    vwn = sbt("s_vwn", [4, nb, 128], BF16)
    gts = sbt("s_gts", [32, nb, 2, 3], F32)
    b.dma("pool", qT[:], qTd)
    b.dma("pool", ksn[:], ksnd)
    b.dma("pool", kwn[:], kwnd)
    b.dma("pool", vsn[:], vsnd)
    b.dma("pool", vwn[:], vwnd)
    b.dma("sp", gts[:], gtd)
    b.act(gts[:], gts[:], AF.Sigmoid)
    pti = sbt("pti", [128, nb * 16], I32)
    idx = sbt("idx", [128, nb * 16], U32)
    b.dma("sp", pti[:], ptab.partition_broadcast(128))
    b.ts(idx[:], pti[:], 128.0, ALU.mult, pcol[:, 0:1], ALU.add)

    gth = [sbt("gth%d" % i, [128, 512], F32) for i in range(3)]
    wth = [sbt("wth%d" % i, [128, 256], F32) for i in range(2)]
    KcT4 = sbt("KcT4", [128, 4, 16, 128], BF16)
    VcT4 = sbt("VcT4", [128, 4, 16, 128], BF16)
    KsT = [sbt("KsTs%d" % i, [128, 2052], BF16) for i in range(4)]
    KwT = [sbt("KwTs%d" % i, [128, 516], BF16) for i in range(4)]
    Vs = [sbt("Vss%d" % i, [128, 17, 2, 65], BF16) for i in range(4)]
    Vw = [sbt("Vws%d" % i, [128, 5, 2, 65], BF16) for i in range(4)]
    KCTs4 = sbt("KCTs4", [128, 4, 128], BF16)
    VCOs4 = sbt("VCOs4", [128, 4, 2, 129], BF16)
    ns.tx4 = sbt("tx4", [128, 512], F32)
    ns.tu4 = sbt("tu4", [128, 512], F32)
    ns.gh4 = [sbt("gh4_%d" % i, [128, 512], BF16) for i in range(4)]
    tmpf = [sbt("tmpf%d" % i, [128, 32], F32) for i in range(3)]
    xn = sbt("xn", [32, 64], F32)
    s2 = sbt("s2s", [4, 64], F32)
    s3 = sbt("s3s", [4, 64], F32)
    m8 = sbt("m8s", [4, 16], F32)
    nmf = sbt("nmf", [4, 64], F32)
    nmT = sbt("nmTs", [128, 8, 4], BF16)
    rd = sbt("rds", [32, 8], F32)
    oac = [sbt("oacs%d" % i, [32, 64], F32) for i in range(2)]
    for i in range(4):
        b.memset(Vs[i][:], 1.0)
        b.memset(Vw[i][:], 1.0)
    b.memset(KCTs4[:], 0.0)
    b.memset(nmT[:], 0.0)
    b.memset(VCOs4[:], 0.0)
    b.memset(VCOs4[:, :, :, 64:65], 1.0)
    for sq in range(4):
        for n in range(2):
            b.dma("pool", VCOs4[:, sq, n, 65:129], ovsd)
    ns.cmp_bias()
    nt = [0]

    pend = []

    def step(kT, rows, q, mask, bias, v, pacc, first, last, ncols):
        ps = ns.psS[ns.nS % 2]
        ns.nS += 1
        b.mm(ps[0:rows, 0:32], kT, q, start=True, stop=(mask is None))
        if mask is not None:
            b.mm(ps[0:rows, 0:32], mask[0], mask[1], start=False, stop=True)
        tf = tmpf[nt[0] % 3]
        nt[0] += 1
        b.stt(tf[0:rows, :], ps[0:rows, 0:32], 0.125, bias, ALU.mult, ALU.add)
        pt = ns.PT[ns.nP % 3]
        ns.nP += 1
        b.act(pt[0:rows, 0:32], tf[0:rows, :], AF.Exp)
        flush()
        pend.append((pacc, ncols, pt, rows, v, first, last))

    def flush():
        while pend:
            pacc, ncols, pt, rows, v, first, last = pend.pop(0)
            b.mm(pacc[0:32, 0:ncols], pt[0:rows, 0:32], v, start=first, stop=last)

    def gather_seq(bi):
        k2 = bi % 4
        for pg in range(16):
            gt = gth[(bi * 16 + pg) % 3]
            col = bi * 16 + pg
            I = Ins("pool", (lambda e, gt=gt, col=col: e.indirect_dma_start(
                out=gt[:], out_offset=None, in_=cache,
                in_offset=bass.IndirectOffsetOnAxis(ap=idx[:, col:col + 1], axis=0))), True)
            I.idx = len(b.P.ins)
            b.P.ins.append(I)
            b.P._track(I, [idx[:, col:col + 1], cache], [gt[:]])
            ph = ns.psH[ns.nH % 2]
            ns.nH += 1
            for q3 in range(3):
                b.tr(ph[:, q3 * 128:(q3 + 1) * 128], gt[:, q3 * 128:(q3 + 1) * 128], ns.identf[:, :])
            cs = slice(pg * 128, (pg + 1) * 128)
            b.copy(KcT4[:, k2, :, pg * 8:(pg + 1) * 8], ph[:, 0:128].rearrange("p (c r) -> p r c", r=16), eng="act")
            b.copy(VcT4[:, k2, :, pg * 8:(pg + 1) * 8], ph[:, 128:256].rearrange("p (c r) -> p r c", r=16))
            b.copy(KsT[k2][:, cs], ph[:, 256:384], eng="act")
            b.copy(Vs[k2][:, pg, :, 0:64], gt[:, 384:512].rearrange("p (n d) -> p n d", n=2))
        b.copy(KsT[k2][:, 2048:2052], ksn[:, bi, :])
        b.copy(Vs[k2][0:4, 16, :, 0:64], vsn[0:4, bi, :].rearrange("p (n d) -> p n d", n=2))
        for wt in range(4):
            wtile = wth[wt % 2]
            b.dma("sp", wtile[:], cwin[bi, wt * 128:(wt + 1) * 128, :])
            ph = ns.psH[ns.nH % 2]
            ns.nH += 1
            b.tr(ph[:, 0:128], wtile[:, 0:128], ns.identf[:, :])
            b.copy(KwT[k2][:, wt * 128:(wt + 1) * 128], ph[:, 0:128], eng="act")
            b.copy(Vw[k2][:, wt, :, 0:64], wtile[:, 128:256].rearrange("p (n d) -> p n d", n=2))
        b.copy(KwT[k2][:, 512:516], kwn[:, bi, :])
        b.copy(Vw[k2][0:4, 4, :, 0:64], vwn[0:4, bi, :].rearrange("p (n d) -> p n d", n=2))

    for bi in range(nb):
        k2 = bi % 4
        if bi % 4 == 0:
            for bj in range(bi, bi + 4):
                gather_seq(bj)
            if SCUT == "A":
                continue
            ns.compress_group("k", KcT4, kdst3=KCTs4[:, :, 0:127])
            ns.compress_group("v", VcT4, vdst=[[VCOs4[0:127, sq, 0, 0:64], VCOs4[0:127, sq, 1, 0:64]] for sq in range(4)])
        if SCUT in ("A", "B"):
            continue
        KCTs = KCTs4[:, k2, :]
        VCOs = VCOs4[:, k2, :, :]
        for n in range(2):
            pb = slice(n * 64, (n + 1) * 64)
            q = qT[pb, bi, :]
            oa = oac[n]
            pacc = ns.psAcc[ns.nA % 2]
            ns.nA += 1
            step(KCTs4[pb, k2, :], 128, q, None, biasc[:, n, :], VCOs4[:, k2, n, :], pacc, True, True, 129)
            flush()
            b.ts(rd[:, 0:1], pacc[0:32, 64:65], 1e-30, ALU.max)
            b.recip(rd[:, 0:1], rd[:, 0:1])
            b.ts(xn[:], pacc[0:32, 65:129], rd[:, 0:1], ALU.mult)
            b.tt(rd[:, 1:2], rd[:, 0:1], gts[:, bi, n, 0:1], ALU.mult)
            b.ts(oa[:], pacc[0:32, 0:64], rd[:, 1:2], ALU.mult)
            if SCUT == "C":
                continue
            pk = ns.psK2
            b.mm(pk[0:4, 0:64], selm[:, :], xn[:, :])
            b.tt(s2[:], pk[0:4, 0:64], keeps[:], ALU.mult)
            b.tt(s2[:], s2[:], adds[:], ALU.add)
            b.P.op("dve", lambda e: e.max(out=m8[:, 0:8], in_=s2[:]), [s2[:]], [m8[:, 0:8]])
            b.P.op("dve", lambda e: e.match_replace(out=s3[:], in_to_replace=m8[:, 0:8], in_values=s2[:], imm_value=-1e30),
                   [s2[:], m8[:, 0:8]], [s3[:]])
            b.P.op("dve", lambda e: e.max(out=m8[:, 8:16], in_=s3[:]), [s3[:]], [m8[:, 8:16]])
            b.ts(s3[:], s2[:], m8[:, 15:16], ALU.is_ge)
            b.ts(nmf[:], s3[:], -NEG, ALU.mult, NEG, ALU.add)
            b.tr(pk[0:64, 64:68], nmf[:, :], ns.identf[0:4, 0:4])
            b.copy(nmT[0:64, :, :], pk[0:64, 64:68].unsqueeze(1).broadcast_to([64, 8, 4]))
            nmv = nmT[:].rearrange("p g t -> p (g t)")
            if SCUT == "D":
                continue
            pacc = ns.psAcc[ns.nA % 2]
            ns.nA += 1
            for tile in range(17):
                rows = 128 if tile < 16 else 4
                cs = slice(tile * 128, tile * 128 + rows)
                step(KsT[k2][pb, cs], rows, q, (Gs[:, cs], nmv), biass[0:rows, tile, n, :], Vs[k2][0:rows, tile, n, :],
                     pacc, tile == 0, tile == 16, 65)
            flush()
            b.ts(rd[:, 2:3], pacc[0:32, 64:65], 1e-30, ALU.max)
            b.recip(rd[:, 2:3], rd[:, 2:3])
            b.tt(rd[:, 3:4], rd[:, 2:3], gts[:, bi, n, 1:2], ALU.mult)
            b.stt(oa[:], pacc[0:32, 0:64], rd[:, 3:4], oa[:], ALU.mult, ALU.add)
            if SCUT == "E":
                continue
            pacc = ns.psAcc[ns.nA % 2]
            ns.nA += 1
            for tile in range(5):
                rows = 128 if tile < 4 else 4
                cs = slice(tile * 128, tile * 128 + rows)
                step(KwT[k2][pb, cs], rows, q, None, biasw[0:rows, tile, n, :], Vw[k2][0:rows, tile, n, :],
                     pacc, tile == 0, tile == 4, 65)
            flush()
            b.ts(rd[:, 4:5], pacc[0:32, 64:65], 1e-30, ALU.max)
            b.recip(rd[:, 4:5], rd[:, 4:5])
            b.tt(rd[:, 5:6], rd[:, 4:5], gts[:, bi, n, 2:3], ALU.mult)
            b.stt(oa[:], pacc[0:32, 0:64], rd[:, 5:6], oa[:], ALU.mult, ALU.add)
            b.dma("sp", onso[bi, n], oa[:])
    b.finish()
    return nc


def run_l2n_sample(inp, proj_s, nb=16):
    cache = inp["cache_kv"][0]
    n_pool = cache.shape[0]
    cache2 = np.ascontiguousarray(cache.reshape(n_pool * 128, 512))
    cst = nsa_sample_consts()
    cst.update(nsa_cmp_weights(inp))
    ps = proj_s.reshape(128, 4, NPROJ)
    nc = build_l2s(n_pool, nb)
    maps = []
    for c in range(NCORES):
        sb = ps[c * nb:(c + 1) * nb]
        m = dict(cst)
        m["cache"] = cache2
        m["ptab"] = np.ascontiguousarray(inp["page_table"][c * nb:(c + 1) * nb].reshape(1, nb * 16).astype(np.int32))
        m["cwin"] = np.ascontiguousarray(inp["cache_win"][0, c * nb:(c + 1) * nb].reshape(nb, 512, 256))
        q = sb[:, :, 4096:5120].reshape(nb, 4, 2, 8, 64)
        m["qTs"] = np.ascontiguousarray(q.transpose(2, 4, 0, 3, 1).reshape(128, nb, 32))
        kv = sb[:, :, 5120:5888].reshape(nb, 4, 6, 128)
        m["ksn"] = np.ascontiguousarray(kv[:, :, 2].transpose(2, 0, 1))
        m["kwn"] = np.ascontiguousarray(kv[:, :, 4].transpose(2, 0, 1))
        m["vsn"] = np.ascontiguousarray(kv[:, :, 3].transpose(1, 0, 2))
        m["vwn"] = np.ascontiguousarray(kv[:, :, 5].transpose(1, 0, 2))
        g = sb[:, :, 5888:5936].reshape(nb, 4, 2, 8, 3)
        m["gts"] = np.ascontiguousarray(g.transpose(3, 1, 0, 2, 4).reshape(32, nb, 2, 3))
        maps.append(m)
    res = run_bass_kernel_spmd(nc, maps, core_ids=list(range(NCORES)))
    outs = []
    for c in range(NCORES):
        r = res.results[c]["ons"].reshape(nb, 2, 8, 4, 64)
        outs.append(r.transpose(0, 3, 1, 2, 4).reshape(nb * 4, 1024))
    return np.concatenate(outs, 0)
```

```python
from contextlib import ExitStack
import numpy as np
import concourse.bass as bass
import concourse.mybir as mybir
from concourse.bass_utils import run_bass_kernel_spmd

F32 = mybir.dt.float32
BF16 = mybir.dt.bfloat16
I32 = mybir.dt.int32
AF = mybir.ActivationFunctionType
ALU = mybir.AluOpType
AX = mybir.AxisListType

NCORES = 8
D = 2048
DFF = 5504
T_P = 1024
T_S = 64
T_ALL = T_P + T_S
KC = D // 128
FC = DFF // 128
EPS = 1e-6
NPROJ = 5936
D_IN = 10032

COMPUTE = ("pe", "act", "dve", "pool")
NDMASEM = 8
CUT = 9


def region(ap):
    name = ap.tensor.name
    space = str(ap.space)
    aplist = ap.ap
    off = int(ap.offset)
    if space == "PSUM":
        return (name, 0, 128, 0, 1 << 30)
    if space == "SB":
        pstep, pcount = aplist[0]
        if pstep == 0:
            p0, foff, pcount = 0, off, 128
        else:
            p0 = off // pstep
            foff = off % pstep
        ext = 1
        for s, c in aplist[1:]:
            ext += (c - 1) * abs(s)
        return (name, p0, p0 + pcount, foff, foff + ext)
    ext = 1
    for s, c in aplist:
        ext += (c - 1) * abs(s)
    return (name, 0, 1, off, off + ext)


def overlap(a, b):
    return a[1] < b[2] and b[1] < a[2] and a[3] < b[4] and b[3] < a[4]


def covers(a, b):
    return a[1] <= b[1] and a[2] >= b[2] and a[3] <= b[3] and a[4] >= b[4]


class Ins:
    __slots__ = ("eng", "fn", "deps", "need_inc", "cnt", "is_dma", "dsem", "dval", "idx", "prewait", "inc")

    def __init__(self, eng, fn, is_dma):
        self.eng = eng
        self.fn = fn
        self.deps = set()
        self.need_inc = False
        self.cnt = None
        self.is_dma = is_dma
        self.dsem = None
        self.dval = None
        self.prewait = None
        self.inc = 16


class Prog:
    def __init__(self, nc):
        self.nc = nc
        self.ins = []
        self.hist = {}

    def _track(self, I, reads, writes):
        idx = I.idx
        ins = self.ins
        rr_ = [region(a) for a in reads if str(a.space) != "PSUM"]
        wr_ = [region(a) for a in writes] + [region(a) for a in reads if str(a.space) == "PSUM"]
        for r in rr_:
            h = self.hist.setdefault(r[0], [])
            for (rr, j, w) in h:
                if w and overlap(r, rr):
                    J = ins[j]
                    if J.eng == "pe" and I.eng == "pe" and not I.is_dma and not J.is_dma:
                        continue
                    I.deps.add(j)
        for r in wr_:
            h = self.hist.setdefault(r[0], [])
            for (rr, j, w) in h:
                if overlap(r, rr):
                    J = ins[j]
                    if J.eng == I.eng and not I.is_dma and not J.is_dma:
                        continue
                    I.deps.add(j)
        if len(I.deps) > 1:
            best = {}
            keep = set()
            for j in I.deps:
                J = ins[j]
                if J.is_dma:
                    keep.add(j)
                elif best.get(J.eng, -1) < j:
                    best[J.eng] = j
            keep.update(best.values())
            I.deps = keep
        for r in rr_:
            h = self.hist[r[0]]
            if not I.is_dma:
                h[:] = [e for e in h if e[2] or e[0] != r or ins[e[1]].eng != I.eng or ins[e[1]].is_dma]
            h.append((r, idx, False))
        for r in wr_:
            h = self.hist[r[0]]
            h[:] = [e for e in h if not covers(r, e[0])]
            h.append((r, idx, True))

    def op(self, eng, fn, reads=(), writes=()):
        I = Ins(eng, fn, False)
        I.idx = len(self.ins)
        self.ins.append(I)
        self._track(I, reads, writes)
        return I

    def dma(self, q, out, in_, **kw):
        def fn(e, out=out, in_=in_, kw=kw):
            return e.dma_start(out=out, in_=in_, **kw)
        I = Ins(q, fn, True)
        I.idx = len(self.ins)
        self.ins.append(I)
        self._track(I, [in_], [out])
        return I

    def emit(self, stack):
        nc = self.nc
        ins = self.ins
        for I in ins:
            for j in I.deps:
                ins[j].need_inc = True
        csem = {e: stack.enter_context(nc.semaphore("c_" + e)) for e in COMPUTE}
        dsems = {q: [stack.enter_context(nc.semaphore("d_%s_%d" % (q, i))) for i in range(NDMASEM)]
                 for q in ("sp", "act", "pool")}
        cnt = {e: 0 for e in COMPUTE}
        dq_n = {q: 0 for q in dsems}
        dq_val = {q: [0] * NDMASEM for q in dsems}
        dq_last = {q: [None] * NDMASEM for q in dsems}
        for I in ins:
            if I.is_dma:
                k = dq_n[I.eng] % NDMASEM
                dq_n[I.eng] += 1
                I.prewait = dq_last[I.eng][k]
                dq_val[I.eng][k] += I.inc
                I.dsem = dsems[I.eng][k]
                I.dval = dq_val[I.eng][k]
                dq_last[I.eng][k] = I.idx
            elif I.need_inc:
                cnt[I.eng] += 1
                I.cnt = cnt[I.eng]
        self.maxcnt = dict(cnt)
        streams = {e: [] for e in ("pe", "act", "dve", "pool", "sp")}
        for I in ins:
            streams[I.eng].append(I)
        block = stack.enter_context(nc.Block())

        def run_stream(ename, e):
            waited = {}

            def wait_for(j):
                J = ins[j]
                if J.is_dma:
                    key = ("d", J.eng, id(J.dsem))
                    if waited.get(key, 0) >= J.dval:
                        return
                    waited[key] = J.dval
                    e.wait_ge(J.dsem, J.dval)
                else:
                    key = ("c", J.eng)
                    if waited.get(key, 0) >= J.cnt:
                        return
                    waited[key] = J.cnt
                    e.wait_ge(csem[J.eng], J.cnt)

            for I in streams[ename]:
                for j in sorted(I.deps):
                    wait_for(j)
                if I.is_dma and I.prewait is not None:
                    wait_for(I.prewait)
                bi = I.fn(e)
                if I.is_dma:
                    bi.then_inc(I.dsem, I.inc)
                elif I.need_inc:
                    bi.then_inc(csem[I.eng], 1)
            if ename == "sp":
                for q in dsems:
                    for k in range(NDMASEM):
                        if dq_val[q][k] > 0:
                            e.wait_ge(dsems[q][k], dq_val[q][k])
                for ce in COMPUTE:
                    if cnt[ce] > 0:
                        e.wait_ge(csem[ce], cnt[ce])

        @block.tensor
        def _(e):
            run_stream("pe", e)

        @block.scalar
        def _(e):
            run_stream("act", e)

        @block.vector
        def _(e):
            run_stream("dve", e)

        @block.gpsimd
        def _(e):
            run_stream("pool", e)

        @block.sync
        def _(e):
            run_stream("sp", e)


class Bld:
    def __init__(self, nc):
        self.nc = nc
        self.P = Prog(nc)
        self.st = ExitStack()
        self._n = 0

    def sb(self, name, shape, dt):
        return self.st.enter_context(self.nc.sbuf_tensor(name, shape, dt))

    def ps(self, name, shape, dt=F32):
        return self.st.enter_context(self.nc.psum_tensor(name, shape, dt))

    def mm(self, out, lhsT, rhs, start=True, stop=True, skip=False):
        if skip:
            self.P.op("pe", lambda e: e.matmul(out, lhsT=lhsT, rhs=rhs, start=start, stop=stop, skip_group_check=True),
                      [lhsT, rhs, out], [out])
        else:
            self.P.op("pe", lambda e: e.matmul(out, lhsT=lhsT, rhs=rhs, start=start, stop=stop),
                      [lhsT, rhs] + ([] if start else [out]), [out])

    def tr(self, out, in_, ident):
        self.P.op("pe", lambda e: e.transpose(out=out, in_=in_, identity=ident), [in_, ident], [out])

    def act(self, out, in_, func, bias=None, scale=None, accum=None, eng="act"):
        kw = {}
        rd = [in_]
        wr = [out]
        if bias is not None:
            kw["bias"] = bias
            if not isinstance(bias, (int, float)):
                rd.append(bias)
        if scale is not None:
            kw["scale"] = scale
            if not isinstance(scale, (int, float)):
                rd.append(scale)
        if accum is not None:
            kw["accum_out"] = accum
            wr.append(accum)
        self.P.op("act", lambda e: e.activation(out=out, in_=in_, func=func, **kw), rd, wr)

    def tt(self, out, a, b, op, eng="dve"):
        self.P.op(eng, lambda e: e.tensor_tensor(out=out, in0=a, in1=b, op=op), [a, b], [out])

    def ts(self, out, a, s1, op0, s2=None, op1=None, eng="dve", accum=None):
        rd = [a]
        wr = [out]
        if not isinstance(s1, (int, float)):
            rd.append(s1)
        if s2 is not None and not isinstance(s2, (int, float)):
            rd.append(s2)
        kw = {}
        if op1 is not None:
            kw["op1"] = op1
        if accum is not None:
            kw["accum_out"] = accum
            wr.append(accum)
        self.P.op(eng, lambda e: e.tensor_scalar(out=out, in0=a, scalar1=s1, scalar2=s2, op0=op0, **kw), rd, wr)

    def stt(self, out, a, s, b, op0, op1):
        rd = [a, b]
        if not isinstance(s, (int, float)):
            rd.append(s)
        self.P.op("dve", lambda e: e.scalar_tensor_tensor(out=out, in0=a, scalar=s, in1=b, op0=op0, op1=op1), rd, [out])

    def copy(self, out, in_, eng="dve"):
        if eng == "act":
            self.P.op("act", lambda e: e.copy(out=out, in_=in_), [in_], [out])
        else:
            self.P.op(eng, lambda e: e.tensor_copy(out=out, in_=in_), [in_], [out])

    def recip(self, out, in_):
        self.P.op("dve", lambda e: e.reciprocal(out=out, in_=in_), [in_], [out])

    def memset(self, ap, v, eng="pool"):
        self.P.op(eng, lambda e: e.memset(ap, v), [], [ap])

    def dma(self, q, out, in_, **kw):
        self.P.dma(q, out, in_, **kw)

    def finish(self):
        self.P.emit(self.st)
        self.st.close()


class Dense:
    def __init__(self, b, ident_dram):
        self.b = b
        nc = b.nc
        self.identf = b.sb("identf", [128, 128], F32)
        self.ident = b.sb("ident", [128, 128], BF16)
        b.dma("sp", self.identf[:], ident_dram)
        b.copy(self.ident[:], self.identf[:])
        self.hT = b.sb("hT", [128, KC, 576], BF16)
        self.aT = b.sb("aT", [128, FC, 576], BF16)
        self.ybuf = b.sb("ybuf", [128, 5, D], BF16)
        self.xt = [b.sb("xt%d" % i, [128, D], F32) for i in range(2)]
        self.hn = [b.sb("hn%d" % i, [128, D], BF16) for i in range(2)]
        self.junk = b.sb("junk", [128, D], BF16)
        self.st4 = b.sb("st4", [128, 8], F32)
        self.wA = [b.sb("wA%d" % i, [128, KC, 256], BF16) for i in range(2)]
        self.wB = [b.sb("wB%d" % i, [128, KC, 256], BF16) for i in range(2)]
        self.wD = [b.sb("wD%d" % i, [128, FC, 256], BF16) for i in range(2)]
        self.sg = [b.sb("sg%d" % i, [128, 512], BF16) for i in range(2)]
        self.ev = [b.sb("ev%d" % i, [128, 256], F32) for i in range(2)]
        self.psT = [b.ps("psT%d" % i, [128, 8, 128], BF16) for i in range(2)]
        self.psG = [b.ps("psG%d" % i, [128, 512]) for i in range(2)]
        self.psU = [b.ps("psU%d" % i, [128, 512]) for i in range(2)]
        self.psO = [b.ps("psO%d" % i, [128, 512]) for i in range(2)]
        self.nT = 0
        self.nG = 0
        self.nO = 0
        self.nX = 0
        self.nW = 0
        self.nE = 0

    def rstd(self, src, rows, col):
        b = self.b
        ss = self.st4[0:rows, col:col + 1]
        b.act(self.junk[0:rows, :], src, AF.Square, accum=ss)
        b.act(ss, ss, AF.Sqrt, bias=EPS, scale=1.0 / D)
        b.recip(ss, ss)
        return ss

    def norm_to_hT(self, src, rows, tok0, gcol):
        b = self.b
        r = self.rstd(src, rows, 0)
        hn = self.hn[self.nX % 2]
        self.nX += 1
        b.ts(hn[0:rows, :], src, r, ALU.mult)
        if CUT >= 3:
            self.to_T(hn, rows, self.hT, tok0, gcol)

    def to_T(self, src, rows, dstT, tok0, gcol=None, nk=KC):
        b = self.b
        for k4 in range(0, nk, 4):
            pt = self.psT[self.nT % 2]
            self.nT += 1
            for j in range(4):
                kc = k4 + j
                b.tr(pt[:, j, 0:rows], src[0:rows, kc * 128:(kc + 1) * 128], self.ident[0:rows, 0:rows])
            if gcol is None:
                b.copy(dstT[:, k4:k4 + 4, tok0:tok0 + rows], pt[:, 0:4, 0:rows])
            else:
                for j in range(4):
                    kc = k4 + j
                    b.act(dstT[:, kc, tok0:tok0 + rows], pt[:, j, 0:rows], AF.Copy, scale=gcol[:, kc:kc + 1])

    def load_w(self, dst, w_dram, c0, w, nk):
        src = w_dram.rearrange("(kc p) f -> p kc f", p=128)[:, :, c0:c0 + w]
        self.b.dma("pool", dst[:, 0:nk, 0:w], src)

    def gate_up(self, wg, wu, groups):
        b = self.b
        nblk = (DFF + 255) // 256
        blocks = [(i * 256, min(256, DFF - i * 256)) for i in range(nblk)]

        def issue(i):
            c0, w = blocks[i]
            self.load_w(self.wA[i % 2], wg, c0, w, KC)
            self.load_w(self.wB[i % 2], wu, c0, w, KC)
        issue(0)
        for i, (c0, w) in enumerate(blocks):
            if i + 1 < nblk:
                issue(i + 1)
            wa, wb = self.wA[i % 2], self.wB[i % 2]
            for fl in range(w // 128):
                fc = c0 // 128 + fl
                for (t0, n) in groups:
                    pg = self.psG[self.nG % 2]
                    pu = self.psU[self.nG % 2]
                    sg = self.sg[self.nG % 2]
                    self.nG += 1
                    for kc in range(KC):
                        b.mm(pg[:, 0:n], wa[:, kc, fl * 128:(fl + 1) * 128], self.hT[:, kc, t0:t0 + n],
                             start=(kc == 0), stop=(kc == KC - 1))
                    for kc in range(KC):
                        b.mm(pu[:, 0:n], wb[:, kc, fl * 128:(fl + 1) * 128], self.hT[:, kc, t0:t0 + n],
                             start=(kc == 0), stop=(kc == KC - 1))
                    b.act(sg[:, 0:n], pg[:, 0:n], AF.Silu)
                    b.tt(self.aT[:, fc, t0:t0 + n], sg[:, 0:n], pu[:, 0:n], ALU.mult)

    def down(self, wd, tiles):
        b = self.b
        nblk = D // 256

        def issue(i):
            src = wd.rearrange("(fc p) d -> p fc d", p=128)[:, :, i * 256:(i + 1) * 256]
            b.dma("pool", self.wD[i % 2][:], src)
        issue(0)
        for i in range(nblk):
            if i + 1 < nblk:
                issue(i + 1)
            w = self.wD[i % 2]
            for ti, (t0, rows) in enumerate(tiles):
                po = self.psO[self.nO % 2]
                self.nO += 1
                for fc in range(FC):
                    b.mm(po[0:rows, 0:256], self.aT[:, fc, t0:t0 + rows], w[:, fc, :], start=(fc == 0), stop=(fc == FC - 1))
                b.copy(self.ybuf[0:rows, ti, i * 256:(i + 1) * 256], po[0:rows, 0:256], eng="act")

    def proj(self, srcT, nk, w_dram, cols, tiles, sink):
        b = self.b
        c_lo, c_hi = cols
        nblk = (c_hi - c_lo + 255) // 256
        blocks = [(c_lo + i * 256, min(256, c_hi - c_lo - i * 256)) for i in range(nblk)]

        def issue(i):
            c0, w = blocks[i]
            self.load_w(self.wA[(self.nW + i) % 2], w_dram, c0, w, nk)
        issue(0)
        for i, (c0, w) in enumerate(blocks):
            if i + 1 < nblk:
                issue(i + 1)
            wt = self.wA[(self.nW + i) % 2]
            for ti, (t0, rows) in enumerate(tiles):
                po = self.psO[self.nO % 2]
                self.nO += 1
                for kc in range(nk):
                    b.mm(po[0:rows, 0:w], srcT[:, kc, t0:t0 + rows], wt[:, kc, 0:w], start=(kc == 0), stop=(kc == nk - 1))
                sink(ti, t0, rows, c0, w, po)
        self.nW += nblk


def half_tiles(h):
    if h == 0:
        return [(i * 128, 128, i * 128) for i in range(4)] + [(512, 64, T_P)]
    return [(i * 128, 128, 512 + i * 128) for i in range(4)]


def half_groups(h):
    return [(0, 512), (512, 64)] if h == 0 else [(0, 512)]


def build_l1(stages=('win', 'norm', 'gateup', 'down', 'resid', 'proj'), halves=(0, 1)):
    nc = bass.Bass("TRN2", target_bir_lowering=False)
    dt = lambda n, s, k="ExternalInput": nc.dram_tensor(n, s, F32, kind=k).ap()
    x = dt("x", [T_ALL, D])
    wg = dt("wg", [D, DFF]) if 'gateup' in stages else None
    wu = dt("wu", [D, DFF]) if 'gateup' in stages else None
    wd = dt("wd", [DFF, D]) if 'down' in stages else None
    win = dt("win", [D, NPROJ]) if 'proj' in stages else None
    gcols = dt("gcols", [128, 2 * KC])
    gpost = dt("gpost", [128, D])
    identd = dt("identd", [128, 128])
    cwin = dt("cwin", [16, 512, 256])
    x1o = dt("x1o", [T_ALL, D], "ExternalOutput")
    projo = dt("projo", [T_ALL, NPROJ], "ExternalOutput")
    wino = dt("wino", [16, 512, 256], "ExternalOutput")

    b = Bld(nc)
    dn = Dense(b, identd)
    gc = b.sb("gc", [128, 2 * KC], F32)
    gp = b.sb("gp", [128, D], F32)
    b.dma("sp", gc[:], gcols)
    b.dma("sp", gp[:], gpost)
    for bi in range(16 if 'win' in stages else 0):
        b.dma("act", wino[bi, 0:508, :], cwin[bi, 4:512, :])

    for h in halves:
        tiles = half_tiles(h)
        for (t0, rows, g0) in (tiles if 'norm' in stages else []):
            xt = dn.xt[dn.nX % 2]
            b.dma("sp", xt[0:rows, :], x[g0:g0 + rows, :])
            dn.norm_to_hT(xt[0:rows, :], rows, t0, gc[:, 0:KC])
        if 'gateup' in stages:
            dn.gate_up(wg, wu, half_groups(h))
        if 'down' in stages:
            dn.down(wd, [(t0, rows) for (t0, rows, g0) in tiles])
        for ti, (t0, rows, g0) in enumerate(tiles if 'resid' in stages else []):
            xt = dn.xt[dn.nX % 2]
            b.dma("sp", xt[0:rows, :], x[g0:g0 + rows, :])
            y = dn.ybuf[0:rows, ti, :]
            r = dn.rstd(y, rows, 1)
            tmp = dn.hn[(dn.nX + 1) % 2]
            b.stt(tmp[0:rows, :], y, r, gp[0:rows, :], ALU.mult, ALU.mult)
            b.stt(xt[0:rows, :], tmp[0:rows, :], 0.5, xt[0:rows, :], ALU.mult, ALU.add)
            b.dma("sp", x1o[g0:g0 + rows, :], xt[0:rows, :])
            dn.norm_to_hT(xt[0:rows, :], rows, t0, gc[:, KC:2 * KC])

        def sink(ti, t0, rows, c0, w, po, tiles=tiles):
            ev = dn.ev[dn.nE % 2]
            dn.nE += 1
            b.copy(ev[0:rows, 0:w], po[0:rows, 0:w], eng="act")
            g0 = tiles[ti][2]
            b.dma("sp", projo[g0:g0 + rows, c0:c0 + w], ev[0:rows, 0:w])
            if g0 == T_P and c0 == 5632:
                for bi in range(16):
                    b.dma("sp", wino[bi, 508:512, :], ev[bi * 4:(bi + 1) * 4, 0:256])
        if 'proj' in stages:
            dn.proj(dn.hT, KC, win, (0, NPROJ), [(t0, rows) for (t0, rows, g0) in tiles], sink)
    b.finish()
    return nc


def _bcast(v):
    return np.ascontiguousarray(np.broadcast_to(np.asarray(v, np.float32).reshape(1, -1), (128, v.size)))


def _cols(v):
    return np.ascontiguousarray(np.asarray(v, np.float32).reshape(-1, 128).T)


def run_l1(inp):
    nc = build_l1()
    xp = inp["x_prompt"][0]
    xs = inp["x_sample"].reshape(-1, D)
    ident = np.eye(128, dtype=np.float32)
    gcols = np.concatenate([_cols(inp["norm_pre1"][0]), _cols(inp["norm_pre2"][0])], axis=1)
    gpost = _bcast(inp["norm_post1"][0])
    win = np.ascontiguousarray(inp["w_in"][0][:, :NPROJ])
    maps = []
    for c in range(NCORES):
        maps.append({
            "x": np.ascontiguousarray(np.concatenate([xp[c * T_P:(c + 1) * T_P], xs[c * T_S:(c + 1) * T_S]], 0)),
            "wg": inp["ff1_gate"][0], "wu": inp["ff1_up"][0], "wd": inp["ff1_down"][0], "win": win,
            "gcols": gcols, "gpost": gpost, "identd": ident,
            "cwin": np.ascontiguousarray(inp["cache_win"][0, c * 16:(c + 1) * 16].reshape(16, 512, 256)),
        })
    res = run_bass_kernel_spmd(nc, maps, core_ids=list(range(NCORES)))
    return res.results


def kernel(**inp):
    inp = {k: np.asarray(v) for k, v in inp.items()}
    r1 = run_l1(inp)
    proj_p = np.concatenate([r["projo"][:T_P] for r in r1], 0)
    proj_s = np.concatenate([r["projo"][T_P:] for r in r1], 0)
    x1_p = np.concatenate([r["x1o"][:T_P] for r in r1], 0)
    x1_s = np.concatenate([r["x1o"][T_P:] for r in r1], 0)
    kv_prompt = proj_p[:, 5120:5632].reshape(1, 1, 8192, 4, 2, 64)
    kv_sample = proj_s[:, 5120:5632].reshape(1, 128, 4, 4, 2, 64)
    win_prompt = proj_p[8192 - 512:, 5632:5888].reshape(1, 1, 512, 2, 2, 64)
    win_sample = np.concatenate([r["wino"] for r in r1], 0).reshape(1, 128, 512, 2, 2, 64)
    ohg_p, ohg_s, hg_p, hg_s = run_l2h(inp, proj_p, proj_s)
    ons_p = run_l2n_prompt(inp, proj_p)
    ons_s = run_l2n_sample(inp, proj_s)
    y_p, y_s = run_l3(inp, x1_p, x1_s, ohg_p, ohg_s, ons_p, ons_s)
    y_prompt = y_p.reshape(1, 8192, D)
    y_sample = y_s.reshape(128, 4, D)
    return (y_prompt, y_sample, np.ascontiguousarray(kv_prompt), np.ascontiguousarray(kv_sample),
            np.ascontiguousarray(win_prompt), win_sample, hg_p, hg_s)


CH = 32
SEG = 2048


def build_l2h(do_prompt=True, do_sample=True):
    nc = bass.Bass("TRN2", target_bir_lowering=False)
    dt = lambda n, s, k="ExternalInput": nc.dram_tensor(n, s, F32, kind=k).ap()
    qT = dt("qT", [128, 8192])
    fT = dt("fT", [128, 8192])
    v32 = dt("v32", [32, 256, 128])
    g32 = dt("g32", [32, 256, 128])
    lbc = dt("lbc", [128, 2])
    qTs = dt("qTs", [128, 512])
    fTs = dt("fTs", [128, 512])
    v4 = dt("v4", [4, 128, 128])
    g4 = dt("g4", [4, 128, 128])
    lbs = dt("lbs", [128, 16])
    st0 = dt("st0", [16, 8, 128, 128])
    gn32 = dt("gn32", [32, 128])
    rmask = dt("rmask", [128, SEG])
    rmask4 = dt("rmask4", [128, 512])
    trid = dt("trid", [32, 32])
    identd = dt("identd", [128, 128])
    o32 = dt("o32", [32, 256, 128], "ExternalOutput")
    Sp = dt("Sp", [128, 128], "ExternalOutput")
    o4 = dt("o4", [4, 128, 128], "ExternalOutput")
    Ss = dt("Ss", [16, 8, 128, 128], "ExternalOutput")

    b = Bld(nc)
    identf = b.sb("identf", [128, 128], F32)
    ident = b.sb("ident", [128, 128], BF16)
    b.dma("sp", identf[:], identd)
    b.copy(ident[:], identf[:])
    tri = b.sb("tri", [32, 32], F32)
    b.dma("sp", tri[:], trid)
    gn = b.sb("gn", [32, 128], F32)
    b.dma("sp", gn[:], gn32)
    rm = b.sb("rm", [128, SEG], F32)
    b.dma("sp", rm[:], rmask)
    rm4 = b.sb("rm4", [128, 512], F32)
    b.dma("sp", rm4[:], rmask4)
    lbr = b.sb("lbr", [128, 16], F32)
    lbv = b.sb("lbv", [128, 8], F32)
    oml = b.sb("oml", [128, 8], F32)

    qr = b.sb("qr", [128, SEG], F32)
    fr = b.sb("fr", [128, SEG], F32)
    bc = b.sb("bc", [128, SEG], F32)
    kk = b.sb("kk", [128, SEG], F32)
    t1 = b.sb("t1", [128, SEG], F32)
    qb = b.sb("qb", [128, SEG], BF16)
    kb = b.sb("kb", [128, SEG], BF16)
    kd = b.sb("kd", [128, SEG], BF16)
    ebl = b.sb("ebl", [128, 128], F32)
    vs = b.sb("vs", [32, 64, 128], BF16)
    gs = b.sb("gs", [32, 64, 128], F32)
    oa = b.sb("oa", [32, 64, 128], F32)
    sq = b.sb("sq", [32, 64, 128], F32)
    rs = b.sb("rs", [32, 128], F32)
    NSB = 8
    S = [b.sb("S%d" % i, [128, 128], F32) for i in range(NSB)]
    Sbf = [b.sb("Sbf%d" % i, [128, 128], BF16) for i in range(NSB)]
    atm = [b.sb("atm%d" % i, [32, 32], BF16) for i in range(3)]
    kdT = [b.sb("kdT%d" % i, [32, 128], BF16) for i in range(3)]
    psA = [b.ps("psA%d" % i, [128, 512]) for i in range(2)]
    psK = [b.ps("psK%d" % i, [128, 1024], BF16) for i in range(2)]
    psO = [b.ps("psO%d" % i, [128, 512]) for i in range(2)]
    psS = [b.ps("psS%d" % i, [128, 512]) for i in range(2)]
    cnt = [0]

    def prep(q_src, f_src, n, nh, C, rmk):
        per = n // nh
        nch = n // C
        v3 = lambda t: t[:, 0:n].rearrange("p (h m) -> p h m", h=nh)
        lb_bc = lbv[:, 0:nh].unsqueeze(2).broadcast_to([128, nh, per])
        oml_bc = oml[:, 0:nh].unsqueeze(2).broadcast_to([128, nh, per])
        b.dma("sp", qr[:, 0:n], q_src)
        b.dma("act", fr[:, 0:n], f_src)
        b.act(t1[:, 0:n], fr[:, 0:n], AF.Sigmoid)
        b.tt(v3(t1), v3(t1), oml_bc, ALU.mult)
        b.tt(v3(fr), v3(t1), lb_bc, ALU.add)
        b.ts(kk[:, 0:n], fr[:, 0:n], -1.0, ALU.mult, 1.0, ALU.add)
        b.act(t1[:, 0:n], fr[:, 0:n], AF.Ln)
        b.P.op("dve", lambda e: e.tensor_tensor_scan(out=bc[:, 0:n], data0=rmk[:, 0:n], data1=t1[:, 0:n],
                                                     initial=0.0, op0=ALU.mult, op1=ALU.add),
               [rmk[:, 0:n], t1[:, 0:n]], [bc[:, 0:n]])
        b.act(fr[:, 0:n], qr[:, 0:n], AF.Silu)
        b.act(t1[:, 0:n], bc[:, 0:n], AF.Exp)
        b.tt(qb[:, 0:n], fr[:, 0:n], t1[:, 0:n], ALU.mult)
        b.act(t1[:, 0:n], bc[:, 0:n], AF.Exp, scale=-1.0)
        b.tt(kb[:, 0:n], kk[:, 0:n], t1[:, 0:n], ALU.mult)
        bc3 = bc[:, 0:n].rearrange("p (c m) -> p c m", m=C)
        bl_bc = bc3[:, :, C - 1:C].broadcast_to([128, nch, C])
        b.tt(t1[:, 0:n].rearrange("p (c m) -> p c m", m=C), bl_bc, bc3, ALU.subtract)
        b.act(t1[:, 0:n], t1[:, 0:n], AF.Exp)
        b.tt(kd[:, 0:n], kk[:, 0:n], t1[:, 0:n], ALU.mult)
        b.act(ebl[:, 0:nch], bc3[:, :, C - 1], AF.Exp)

    def chunk_front(ci, C, cg=None):
        k = cnt[0] % 2
        k3 = cnt[0] % 3
        cnt[0] += 1
        cg = ci if cg is None else cg
        cs = slice(cg * C, (cg + 1) * C)
        pa, pk = psA[k], psK[k]
        b.mm(pa[0:C, 0:C], kb[:, cs], qb[:, cs])
        b.tt(atm[k3][0:C, 0:C], pa[0:C, 0:C], tri[0:C, 0:C], ALU.mult)
        b.tr(pk[0:C, 0:128], kd[:, cs], ident[:, :])
        b.copy(kdT[k3][0:C, :], pk[0:C, 0:128], eng="act")
        return (k, k3, ci, cg, cs, C)

    def chunk_back(tok, Sin, Sout, Sbx):
        k, k3, ci, cg, cs, C = tok
        po, pS = psO[k], psS[k]
        b.mm(pS[:, 0:128], kdT[k3][0:C, :], vs[0:C, ci, :])
        b.stt(Sout[:], Sin[:], ebl[:, cg:cg + 1], pS[:, 0:128], ALU.mult, ALU.add)
        b.mm(po[0:C, 0:128], atm[k3][0:C, 0:C], vs[0:C, ci, :], start=True, stop=False)
        b.mm(po[0:C, 0:128], qb[:, cs], Sbx[:], start=False, stop=True)
        b.copy(oa[0:C, ci, :], po[0:C, 0:128], eng="act")

    def post(C, nch, g_src, o_dst):
        b.dma("sp", gs[0:C, 0:nch, :], g_src)
        b.tt(sq[0:C, 0:nch, :], oa[0:C, 0:nch, :], oa[0:C, 0:nch, :], ALU.mult)
        b.P.op("dve", lambda e: e.reduce_sum(out=rs[0:C, 0:nch], in_=sq[0:C, 0:nch, :], axis=AX.X),
               [sq[0:C, 0:nch, :]], [rs[0:C, 0:nch]])
        b.act(rs[0:C, 0:nch], rs[0:C, 0:nch], AF.Sqrt, bias=EPS, scale=1.0 / 128)
        b.recip(rs[0:C, 0:nch], rs[0:C, 0:nch])
        b.tt(oa[0:C, 0:nch, :], oa[0:C, 0:nch, :], rs[0:C, 0:nch].unsqueeze(2).broadcast_to([C, nch, 128]), ALU.mult)
        b.tt(oa[0:C, 0:nch, :], oa[0:C, 0:nch, :], gn[0:C, :].unsqueeze(1).broadcast_to([C, nch, 128]), ALU.mult)
        b.act(gs[0:C, 0:nch, :], gs[0:C, 0:nch, :], AF.Silu)
        b.tt(oa[0:C, 0:nch, :], oa[0:C, 0:nch, :], gs[0:C, 0:nch, :], ALU.mult)
        b.dma("sp", o_dst, oa[0:C, 0:nch, :])

    def lower_bounds(src, nh):
        b.dma("sp", lbr[:, 0:2 * nh], src)
        l3 = lbr[:, 0:2 * nh].rearrange("p (h r) -> p h r", r=2)
        b.tt(lbv[:, 0:nh], l3[:, :, 0], l3[:, :, 1], ALU.subtract)
        b.act(lbv[:, 0:nh], lbv[:, 0:nh], AF.Sigmoid)
        b.ts(oml[:, 0:nh], lbv[:, 0:nh], -1.0, ALU.mult, 1.0, ALU.add)

    if do_prompt:
        lower_bounds(lbc, 1)
        b.memset(S[0][:], 0.0)
        b.memset(Sbf[0][:], 0.0)
        nseg = 8192 // SEG
        cps = SEG // CH
        for sg_ in range(nseg):
            prep(qT[:, sg_ * SEG:(sg_ + 1) * SEG], fT[:, sg_ * SEG:(sg_ + 1) * SEG], SEG, 1, CH, rm)
            b.dma("pool", vs[:, 0:cps, :], v32[:, sg_ * cps:(sg_ + 1) * cps, :])
            tok = chunk_front(0, CH)
            for ci in range(cps):
                gi = sg_ * cps + ci
                nxt = chunk_front(ci + 1, CH) if ci + 1 < cps else None
                chunk_back(tok, S[gi % 2], S[(gi + 1) % 2], Sbf[gi % 3])
                b.copy(Sbf[(gi + 1) % 3][:], S[(gi + 1) % 2][:], eng="act")
                tok = nxt
            post(CH, cps, g32[:, sg_ * cps:(sg_ + 1) * cps, :], o32[:, sg_ * cps:(sg_ + 1) * cps, :])
        b.dma("sp", Sp, S[(8192 // CH) % 2][:])
    if do_sample:
        lower_bounds(lbs, 8)
        prep(qTs, fTs, 512, 8, 4, rm4)
        for grp in range(2):
            b.dma("pool", vs[0:4, 0:64, :], v4[:, grp * 64:(grp + 1) * 64, :])
            PF = 5
            for jl in range(min(PF, 64)):
                j = grp * 64 + jl
                b.dma("sp", S[j % NSB][:], st0[j % 16, j // 16])
            tok = chunk_front(0, 4, cg=grp * 64)
            for jl in range(64):
                j = grp * 64 + jl
                h_, b_ = j // 16, j % 16
                Sx, Sbx = S[j % NSB], Sbf[j % NSB]
                if jl + PF < 64:
                    jn = j + PF
                    b.dma("sp", S[jn % NSB][:], st0[jn % 16, jn // 16])
                b.copy(Sbx[:], Sx[:], eng="act")
                nxt = chunk_front(jl + 1, 4, cg=j + 1) if jl + 1 < 64 else None
                chunk_back(tok, Sx, Sx, Sbx)
                b.dma("act", Ss[b_, h_], Sx[:])
                tok = nxt
            post(4, 64, g4[:, grp * 64:(grp + 1) * 64, :], o4[:, grp * 64:(grp + 1) * 64, :])
    b.finish()
    return nc


def l2h_consts():
    rmask = np.ones((128, SEG), np.float32)
    rmask[:, ::CH] = 0.0
    rmask4 = np.ones((128, 512), np.float32)
    rmask4[:, ::4] = 0.0
    tri = np.triu(np.ones((32, 32), np.float32))
    return {"rmask": rmask, "rmask4": rmask4, "trid": tri, "identd": np.eye(128, dtype=np.float32)}


def run_l2h(inp, proj_p, proj_s):
    nc = build_l2h()
    cst = l2h_consts()
    gn32 = _bcast(inp["hg_gnorm"][0])[:32]
    hg_lb = inp["hg_lb"]
    maps = []
    ps4 = proj_s.reshape(128, 4, NPROJ)
    for c in range(NCORES):
        hs = slice(c * 128, (c + 1) * 128)
        m = dict(cst)
        m["gn32"] = np.ascontiguousarray(gn32)
        m["qT"] = np.ascontiguousarray(proj_p[:, 0 * 1024:][:, hs].T)
        m["fT"] = np.ascontiguousarray(proj_p[:, 1 * 1024:][:, hs].T)
        m["v32"] = np.ascontiguousarray(proj_p[:, 2 * 1024:][:, hs].reshape(256, 32, 128).transpose(1, 0, 2))
        m["g32"] = np.ascontiguousarray(proj_p[:, 3 * 1024:][:, hs].reshape(256, 32, 128).transpose(1, 0, 2))
        m["lbc"] = np.ascontiguousarray(hg_lb[:, hs].T)
        sb = ps4[c * 16:(c + 1) * 16]
        part = lambda k: sb[:, :, k * 1024:(k + 1) * 1024].reshape(16, 4, 8, 128)
        m["qTs"] = np.ascontiguousarray(part(0).transpose(3, 2, 0, 1).reshape(128, 512))
        m["fTs"] = np.ascontiguousarray(part(1).transpose(3, 2, 0, 1).reshape(128, 512))
        m["v4"] = np.ascontiguousarray(part(2).transpose(1, 2, 0, 3).reshape(4, 128, 128))
        m["g4"] = np.ascontiguousarray(part(3).transpose(1, 2, 0, 3).reshape(4, 128, 128))
        m["lbs"] = np.ascontiguousarray(hg_lb.reshape(2, 8, 128).transpose(2, 1, 0).reshape(128, 16))
        m["st0"] = np.ascontiguousarray(inp["state_hgrn"][0, c * 16:(c + 1) * 16])
        maps.append(m)
    res = run_bass_kernel_spmd(nc, maps, core_ids=list(range(NCORES)))
    r = res.results
    o_p = np.concatenate([r[c]["o32"].transpose(1, 0, 2).reshape(8192, 128) for c in range(NCORES)], axis=1)
    hg_p = np.stack([r[c]["Sp"] for c in range(NCORES)], 0).reshape(1, 1, 8, 128, 128)
    o_s = np.concatenate([r[c]["o4"].reshape(4, 8, 16, 128).transpose(2, 0, 1, 3).reshape(16, 4, 1024)
                          for c in range(NCORES)], 0)
    hg_s = np.concatenate([r[c]["Ss"] for c in range(NCORES)], 0).reshape(1, 128, 8, 128, 128)
    return o_p, o_s.reshape(512, 1024), hg_p, hg_s


def build_l3():
    nc = bass.Bass("TRN2", target_bir_lowering=False)
    dt = lambda n, s, k="ExternalInput": nc.dram_tensor(n, s, F32, kind=k).ap()
    x1 = dt("x1", [T_ALL, D])
    ohgT = dt("ohgT", [1024, T_ALL])
    onsT = dt("onsT", [1024, T_ALL])
    wgab = dt("wgab", [D, 2 * D])
    wphg = dt("wphg", [1024, D])
    wpns = dt("wpns", [1024, D])
    wout = dt("wout", [D, D])
    wg = dt("wg", [D, DFF])
    wu = dt("wu", [D, DFF])
    wd = dt("wd", [DFF, D])
    gcols = dt("gcols", [128, 2 * KC])
    gpost2 = dt("gpost2", [128, D])
    gpost3 = dt("gpost3", [128, D])
    identd = dt("identd", [128, 128])
    yo = dt("yo", [T_ALL, D], "ExternalOutput")
    x2s = nc.dram_tensor("x2s", [T_ALL, D], F32).ap()

    b = Bld(nc)
    dn = Dense(b, identd)
    gc = b.sb("gc", [128, 2 * KC], F32)
    gp = b.sb("gp", [128, D], F32)
    b.dma("sp", gc[:], gcols)
    oT = [dn.aT[:, 0:8, :], dn.aT[:, 8:16, :]]
    sga = dn.sg

    for h in range(2):
        tiles = half_tiles(h)
        ntok = 576 if h == 0 else 512
        for (t0, rows, g0) in tiles:
            xt = dn.xt[dn.nX % 2]
            b.dma("sp", xt[0:rows, :], x1[g0:g0 + rows, :])
            dn.norm_to_hT(xt[0:rows, :], rows, t0, gc[:, 0:KC])
        for src, dst in ((ohgT, oT[0]), (onsT, oT[1])):
            s3 = src.rearrange("(kc p) t -> p kc t", p=128)
            if h == 0:
                b.dma("pool", dst[:, :, 0:512], s3[:, :, 0:512])
                b.dma("pool", dst[:, :, 512:576], s3[:, :, T_P:T_P + 64])
            else:
                b.dma("pool", dst[:, :, 0:512], s3[:, :, 512:1024])
        nblk = D // 256

        def issue(i):
            dn.load_w(dn.wA[i % 2], wgab, i * 256, 256, KC)
            dn.load_w(dn.wB[i % 2], wgab, D + i * 256, 256, KC)
            dn.load_w(dn.wD[i % 2][:, 0:8, :], wphg, i * 256, 256, 8)
            dn.load_w(dn.wD[i % 2][:, 8:16, :], wpns, i * 256, 256, 8)
        issue(0)
        for i in range(nblk):
            if i + 1 < nblk:
                issue(i + 1)
            wa, wb, wc = dn.wA[i % 2], dn.wB[i % 2], dn.wD[i % 2]
            for ti, (t0, rows, g0) in enumerate(tiles):
                k = dn.nG % 2
                dn.nG += 1
                pga, pgb, ph, pn = dn.psG[k], dn.psU[k], dn.psO[0], dn.psO[1]
                for kc in range(KC):
                    b.mm(pga[0:rows, 0:256], dn.hT[:, kc, t0:t0 + rows], wa[:, kc, :], start=(kc == 0), stop=(kc == KC - 1))
                for kc in range(KC):
                    b.mm(pgb[0:rows, 0:256], dn.hT[:, kc, t0:t0 + rows], wb[:, kc, :], start=(kc == 0), stop=(kc == KC - 1))
                for kc in range(8):
                    b.mm(ph[0:rows, 0:256], oT[0][:, kc, t0:t0 + rows], wc[:, kc, :], start=(kc == 0), stop=(kc == 7))
                for kc in range(8):
                    b.mm(pn[0:rows, 0:256], oT[1][:, kc, t0:t0 + rows], wc[:, 8 + kc, :], start=(kc == 0), stop=(kc == 7))
                s1, s2 = dn.ev[0], dn.ev[1]
                b.act(s1[0:rows, :], pga[0:rows, 0:256], AF.Sigmoid)
                b.act(s2[0:rows, :], pgb[0:rows, 0:256], AF.Sigmoid)
                b.tt(s1[0:rows, :], s1[0:rows, :], ph[0:rows, 0:256], ALU.mult)
                b.tt(s2[0:rows, :], s2[0:rows, :], pn[0:rows, 0:256], ALU.mult)
                b.tt(dn.ybuf[0:rows, ti, i * 256:(i + 1) * 256], s1[0:rows, :], s2[0:rows, :], ALU.add)
        for ti, (t0, rows, g0) in enumerate(tiles):
            dn.to_T(dn.ybuf[:, ti, :], rows, dn.hT, t0)

        def sink(ti, t0, rows, c0, w, po):
            b.copy(dn.ybuf[0:rows, ti, c0:c0 + w], po[0:rows, 0:w], eng="act")
        dn.proj(dn.hT, KC, wout, (0, D), [(t0, rows) for (t0, rows, g0) in tiles], sink)
        b.dma("sp", gp[:], gpost2)
        for ti, (t0, rows, g0) in enumerate(tiles):
            xt = dn.xt[dn.nX % 2]
            b.dma("sp", xt[0:rows, :], x1[g0:g0 + rows, :])
            y = dn.ybuf[0:rows, ti, :]
            r = dn.rstd(y, rows, 1)
            tmp = dn.hn[(dn.nX + 1) % 2]
            b.stt(tmp[0:rows, :], y, r, gp[0:rows, :], ALU.mult, ALU.mult)
            b.tt(xt[0:rows, :], tmp[0:rows, :], xt[0:rows, :], ALU.add)
            b.dma("sp", x2s[g0:g0 + rows, :], xt[0:rows, :])
            dn.norm_to_hT(xt[0:rows, :], rows, t0, gc[:, KC:2 * KC])
        dn.gate_up(wg, wu, half_groups(h))
        dn.down(wd, [(t0, rows) for (t0, rows, g0) in tiles])
        b.dma("sp", gp[:], gpost3)
        for ti, (t0, rows, g0) in enumerate(tiles):
            xt = dn.xt[dn.nX % 2]
            dn.nX += 1
            b.dma("sp", xt[0:rows, :], x2s[g0:g0 + rows, :])
            y = dn.ybuf[0:rows, ti, :]
            r = dn.rstd(y, rows, 1)
            tmp = dn.hn[ti % 2]
            b.stt(tmp[0:rows, :], y, r, gp[0:rows, :], ALU.mult, ALU.mult)
            b.stt(xt[0:rows, :], tmp[0:rows, :], 0.5, xt[0:rows, :], ALU.mult, ALU.add)
            b.dma("sp", yo[g0:g0 + rows, :], xt[0:rows, :])
    b.finish()
    return nc


def run_l3(inp, x1_p, x1_s, ohg_p, ohg_s, ons_p, ons_s):
    nc = build_l3()
    gcols = np.concatenate([_cols(inp["norm_pre2"][0]), _cols(inp["norm_pre3"][0])], axis=1)
    wgab = np.ascontiguousarray(inp["w_in"][0][:, NPROJ:])
    base = {"wgab": wgab, "wphg": inp["w_proj_hg"][0], "wpns": inp["w_proj_nsa"][0], "wout": inp["w_out"][0],
            "wg": inp["ff2_gate"][0], "wu": inp["ff2_up"][0], "wd": inp["ff2_down"][0], "gcols": gcols,
            "gpost2": _bcast(inp["norm_post2"][0]), "gpost3": _bcast(inp["norm_post3"][0]),
            "identd": np.eye(128, dtype=np.float32)}
    maps = []
    for c in range(NCORES):
        m = dict(base)
        ps, ss = slice(c * T_P, (c + 1) * T_P), slice(c * T_S, (c + 1) * T_S)
        m["x1"] = np.ascontiguousarray(np.concatenate([x1_p[ps], x1_s[ss]], 0))
        m["ohgT"] = np.ascontiguousarray(np.concatenate([ohg_p[ps], ohg_s[ss]], 0).T)
        m["onsT"] = np.ascontiguousarray(np.concatenate([ons_p[ps], ons_s[ss]], 0).T)
        maps.append(m)
    res = run_bass_kernel_spmd(nc, maps, core_ids=list(range(NCORES)))
    y_p = np.concatenate([r["yo"][:T_P] for r in res.results], 0)
    y_s = np.concatenate([r["yo"][T_P:] for r in res.results], 0)
    return y_p, y_s


NEG = -32768.0
SLOPES = np.power(2.0, -8.0 * np.arange(1, 17) / 16).astype(np.float64)
GELU_C = 1.5957691216057308


def nsa_prompt_consts(core):
    tiles = [core + 8 * j for j in range(8)]
    p = np.arange(128)
    c = {}
    c["gmat"] = (np.arange(128)[:, None] == (np.arange(8192)[None, :] // 64)).astype(np.float32)
    cs = np.arange(512)[:, None] * 16
    ss = np.arange(128)[None, :] * 64
    ov = ((cs <= ss + 63) & (cs + 31 >= ss)).astype(np.float32)
    ov[511] = 0.0
    c["ovl"] = np.ascontiguousarray(ov.reshape(4, 128, 128).transpose(1, 0, 2))
    r = np.arange(72) - (7 - core)
    c["btab"] = np.ascontiguousarray((SLOPES[None, :, None] * (p[:, None, None] - 64 - 128 * r[None, None, :])).astype(np.float32).reshape(128, 16 * 72))
    cb = np.zeros((128, 8, 16, 4), np.float64)
    cm = np.zeros((128, 8, 2, 128), np.float32)
    keep = np.zeros((128, 8, 128), np.float32)
    add = np.zeros((128, 8, 128), np.float32)
    tt = np.arange(128)
    blk = np.arange(128)
    for j, i in enumerate(tiles):
        t0 = 128 * i
        for ct in range(4):
            cb[:, j, :, ct] = SLOPES[None, :] * (16 * (128 * ct + p[:, None]) + 31 - (t0 + 64))
        nct = i // 16 + 1
        for rr in range(2):
            ct = nct - 1 - rr
            if ct < 0:
                continue
            cpos = 16 * (128 * ct + p) + 31
            ok = (cpos[:, None] <= (t0 + tt)[None, :]) & ((128 * ct + p) < 511)[:, None]
            cm[:, j, rr, :] = np.where(ok, 0.0, NEG)
        qpos = t0 + tt
        qb = qpos // 64
        valid = blk[None, :] <= qb[:, None]
        f0 = blk[None, :] == 0
        f1 = blk[None, :] == qb[:, None]
        f2 = blk[None, :] == (qb[:, None] - 1)
        forced = f0 | f1 | f2
        keep[:, j, :] = (valid & ~forced).astype(np.float32)
        a = np.where(valid, 0.0, -1e30)
        a = np.where(f2, 1e4, a)
        a = np.where(f1, 2e4, a)
        a = np.where(f0, 3e4, a)
        add[:, j, :] = a
    c["cbias"] = np.ascontiguousarray(cb.astype(np.float32).reshape(128, 8 * 16 * 4))
    c["cmask"] = np.ascontiguousarray(cm.reshape(128, 8 * 2 * 128))
    c["keepm"] = np.ascontiguousarray(keep.reshape(128, 8 * 128))
    c["addm"] = np.ascontiguousarray(add.reshape(128, 8 * 128))
    causal = np.where(p[:, None] <= tt[None, :], 0.0, NEG).astype(np.float32)
    wlow = np.where(p[:, None] > tt[None, :], 0.0, NEG).astype(np.float32)
    zero = np.zeros((128, 128), np.float32)
    full = np.full((128, 128), NEG, np.float32)
    dms = [zero if q < core else (causal if q == core else full) for q in range(8)]
    dmw = []
    for q in range(12):
        if q < core or q > core + 4:
            dmw.append(full)
        elif q == core:
            dmw.append(wlow)
        elif q == core + 4:
            dmw.append(causal)
        else:
            dmw.append(zero)
    c["dms"] = np.ascontiguousarray(np.stack(dms, 1).reshape(128, 8 * 128))
    c["dmw"] = np.ascontiguousarray(np.stack(dmw, 1).reshape(128, 12 * 128))
    c["identd"] = np.eye(128, dtype=np.float32)
    import ml_dtypes
    bf = lambda x: np.asarray(x, np.float64).astype(ml_dtypes.bfloat16).astype(np.float64)
    tab = np.zeros((5, 72, 16), np.float64)
    rr = np.arange(72) - (7 - core)
    for h in range(16):
        a = 8.0 * SLOPES[h]
        a0 = bf(a)
        a1 = bf(a - a0)
        cc = -1024.0 * rr * SLOPES[h]
        c0 = bf(cc)
        c1 = bf(cc - c0)
        c2 = bf(cc - c0 - c1)
        tab[0, :, h] = a0
        tab[1, :, h] = a1
        tab[2, :, h] = c0
        tab[3, :, h] = c1
        tab[4, :, h] = c2
    c["btab5"] = np.ascontiguousarray(tab.astype(np.float32).reshape(5, 72 * 16))
    bl = np.ones((5, 128), np.float32)
    bl[0] = p - 64
    bl[1] = p - 64
    c["biasl"] = bl
    return c


class Nsa:
    def __init__(self, b, nc, dt):
        self.b = b
        identd = dt("identd", [128, 128])
        self.identf = b.sb("identf", [128, 128], F32)
        self.ident = b.sb("ident", [128, 128], BF16)
        b.dma("sp", self.identf[:], identd)
        b.copy(self.ident[:], self.identf[:])
        self.w1 = {}
        self.w2 = {}
        self.posT = {}
        for kind in ("k", "v"):
            w1d = dt("w1" + kind, [128, 32, 256])
            w2d = dt("w2" + kind, [128, 4, 128])
            pd = dt("pos" + kind, [128, 32])
            self.w1[kind] = b.sb("s_w1" + kind, [128, 32, 256], BF16)
            self.w2[kind] = b.sb("s_w2" + kind, [128, 4, 128], BF16)
            self.posT[kind] = b.sb("s_pos" + kind, [128, 32], BF16)
            b.dma("pool", self.w1[kind][:], w1d)
            b.dma("pool", self.w2[kind][:], w2d)
            b.dma("pool", self.posT[kind][:], pd)
        self.bcol = b.sb("bcol", [128, 4], F32)
        self.xs = [b.sb("xs%d" % i, [128, 2064], BF16) for i in range(2)]
        self.xsY = b.sb("xsY", [128, 16, 129], BF16)
        self.gh = [b.sb("gh%d" % i, [128, 128], BF16) for i in range(4)]
        self.tx = b.sb("tx", [128, 128], F32)
        self.tu = b.sb("tu", [128, 128], F32)
        self.psS = [b.ps("psS%d" % i, [128, 512]) for i in range(2)]
        self.psAcc = [b.ps("psAcc%d" % i, [128, 512]) for i in range(2)]
        self.psH = [b.ps("psH%d" % i, [128, 512]) for i in range(2)]
        self.psK2 = b.ps("psK2", [128, 512])
        self.psT = b.ps("psT", [128, 1024], BF16)
        self.PT = [b.sb("PT%d" % i, [128, 128], BF16) for i in range(3)]
        self.PT4 = None
        self.nS = 0
        self.nA = 0
        self.nH = 0
        self.nP = 0
        self.nX = 0
        self.bias_done = False

    def cmp_bias(self):
        b = self.b
        for ki, kind in enumerate(("k", "v")):
            for hc in range(2):
                ph = self.psH[self.nH % 2]
                self.nH += 1
                for l in range(32):
                    b.mm(ph[:, 0:1], self.w1[kind][0:64, l, hc * 128:(hc + 1) * 128], self.posT[kind][0:64, l:l + 1],
                         start=(l == 0), stop=(l == 31))
                b.copy(self.bcol[:, ki * 2 + hc:ki * 2 + hc + 1], ph[:, 0:1], eng="act")

    def compress_tile(self, kind, src_dram_cols, N, kdst=None, vdst=None, xs_ap=None):
        b = self.b
        ki = 0 if kind == "k" else 1
        L = 16 * (N - 1) + 32
        if xs_ap is not None:
            xs = xs_ap
        else:
            xs = self.xs[self.nX % 2]
            self.nX += 1
            b.dma("pool", xs[:, 0:L], src_dram_cols)
        xy = self.xsY
        nc_ = L // 16
        b.copy(xy[:, :, 0:nc_], xs[:, 0:L].rearrange("p (c r) -> p r c", r=16))
        for n in range(2):
            for hc in range(2):
                ph = self.psH[self.nH % 2]
                self.nH += 1
                for l in range(32):
                    b.mm(ph[:, 0:N], self.w1[kind][n * 64:(n + 1) * 64, l, hc * 128:(hc + 1) * 128],
                         xy[n * 64:(n + 1) * 64, l % 16, (l // 16):(l // 16) + N], start=(l == 0), stop=(l == 31))
                tx, tu, gh = self.tx, self.tu, self.gh[n * 2 + hc]
                b.act(tx[:, 0:N], ph[:, 0:N], AF.Identity, bias=self.bcol[:, ki * 2 + hc:ki * 2 + hc + 1])
                b.tt(tu[:, 0:N], tx[:, 0:N], tx[:, 0:N], ALU.mult)
                b.ts(tu[:, 0:N], tu[:, 0:N], 0.044715, ALU.mult, 1.0, ALU.add)
                b.tt(tu[:, 0:N], tu[:, 0:N], tx[:, 0:N], ALU.mult)
                b.act(tu[:, 0:N], tu[:, 0:N], AF.Sigmoid, scale=GELU_C)
                b.tt(gh[:, 0:N], tx[:, 0:N], tu[:, 0:N], ALU.mult)
        pk = self.psK2
        if kind == "k":
            for q in range(4):
                b.mm(pk[:, 0:N], self.w2[kind][:, q, :], self.gh[q][:, 0:N], start=(q == 0), stop=(q == 3))
            b.copy(kdst, pk[:, 0:N], eng="act")
        else:
            for q in range(4):
                b.mm(pk[0:N, 0:128], self.gh[q][:, 0:N], self.w2[kind][:, q, :], start=(q == 0), stop=(q == 3))
            b.copy(vdst[0], pk[0:N, 0:64], eng="act")
            b.copy(vdst[1], pk[0:N, 64:128], eng="act")

    def compress_group(self, kind, X4, kdst3=None, vdst=None):
        b = self.b
        ki = 0 if kind == "k" else 1
        NB, N = 4, 127
        W = NB * N
        for n in range(2):
            for hc in range(2):
                ph = self.psH[self.nH % 2]
                self.nH += 1
                po = ph[:, 0:W].rearrange("p (s c) -> p s c", s=NB)
                for l in range(32):
                    b.mm(po, self.w1[kind][n * 64:(n + 1) * 64, l, hc * 128:(hc + 1) * 128],
                         X4[n * 64:(n + 1) * 64, :, l % 16, (l // 16):(l // 16) + N], start=(l == 0), stop=(l == 31))
                tx, tu, gh = self.tx4, self.tu4, self.gh4[n * 2 + hc]
                b.act(tx[:, 0:W], ph[:, 0:W], AF.Identity, bias=self.bcol[:, ki * 2 + hc:ki * 2 + hc + 1])
                b.tt(tu[:, 0:W], tx[:, 0:W], tx[:, 0:W], ALU.mult)
                b.ts(tu[:, 0:W], tu[:, 0:W], 0.044715, ALU.mult, 1.0, ALU.add)
                b.tt(tu[:, 0:W], tu[:, 0:W], tx[:, 0:W], ALU.mult)
                b.act(tu[:, 0:W], tu[:, 0:W], AF.Sigmoid, scale=GELU_C)
                b.tt(gh[:, 0:W], tx[:, 0:W], tu[:, 0:W], ALU.mult)
        pk = self.psK2
        if kind == "k":
            for q in range(4):
                b.mm(pk[:, 0:W], self.w2[kind][:, q, :], self.gh4[q][:, 0:W], start=(q == 0), stop=(q == 3))
            b.copy(kdst3, pk[:, 0:W].rearrange("p (s c) -> p s c", s=NB), eng="act")
        else:
            for sq in range(NB):
                for q in range(4):
                    b.mm(pk[0:N, 0:128], self.gh4[q][:, sq * N:(sq + 1) * N], self.w2[kind][:, q, :], start=(q == 0), stop=(q == 3))
                b.copy(vdst[sq][0], pk[0:N, 0:64], eng="act")
                b.copy(vdst[sq][1], pk[0:N, 64:128])

    def branch(self, steps, qrhs, nq, ncols, scale=0.125):
        b = self.b
        pacc = self.psAcc[self.nA % 2]
        self.nA += 1
        ns = len(steps)
        pts = {}

        def front(si):
            st = steps[si]
            ps = self.psS[self.nS % 2]
            self.nS += 1
            ex = st.get("extra", [])
            rows = st.get("rows", 128)
            b.mm(ps[0:rows, 0:nq], st["k"], qrhs, start=True, stop=(len(ex) == 0))
            for ei, (l_, r_) in enumerate(ex):
                b.mm(ps[0:rows, 0:nq], l_, r_, start=False, stop=(ei == len(ex) - 1))
            pt = self.PT[self.nP % 3]
            self.nP += 1
            pts[si] = pt
            if st.get("bias") is not None:
                b.act(pt[0:rows, 0:nq], ps[0:rows, 0:nq], AF.Exp, bias=st["bias"], scale=scale)
            else:
                b.act(pt[0:rows, 0:nq], ps[0:rows, 0:nq], AF.Exp, scale=scale)

        def back(si):
            st = steps[si]
            rows = st.get("rows", 128)
            b.mm(pacc[0:nq, 0:ncols], pts[si][0:rows, 0:nq], st["v"], start=(si == 0), stop=(si == ns - 1))

        for si in range(ns + 1):
            if si < ns:
                front(si)
            if si >= 1:
                back(si - 1)
        return pacc


def branch4(ns, b, steps, qrhs, biasl):
    pacc = ns.psAcc[ns.nA % 2]
    ns.nA += 1
    nst = len(steps)
    v4 = lambda ap: ap.unsqueeze(1).broadcast_to([ap.shape[0], 4, 128])
    banks = [ns.psS[0], ns.psS[1], ns.psH[0], ns.psH[1]]
    pts = {}

    def front(si):
        st = steps[si]
        ps = banks[ns.nS % 4]
        ns.nS += 1
        po = ps[:, 0:512].rearrange("p (g t) -> p g t", g=4)
        b.mm(po, st["k"], qrhs, start=True, stop=False)
        for (l_, r_) in st["extra"]:
            b.mm(po, l_, v4(r_), start=False, stop=False)
        b.mm(po, biasl, st["brow"].unsqueeze(2).broadcast_to([5, 4, 128]), start=False, stop=True)
        pt = ns.PT4[ns.nP % len(ns.PT4)]
        ns.nP += 1
        pts[si] = pt
        b.act(pt[:, :], ps[:, 0:512], AF.Exp, scale=0.125)

    def back(si):
        st = steps[si]
        for hl in range(4):
            b.mm(pacc[:, hl * 65:(hl + 1) * 65], pts[si][:, hl * 128:(hl + 1) * 128], st["v"],
                 start=(si == 0 and hl == 0), stop=(si == nst - 1), skip=True)

    DEP = 2
    for si in range(nst + DEP):
        if si < nst:
            front(si)
        if si >= DEP:
            back(si - DEP)
    return pacc


def build_l2n():
    nc = bass.Bass("TRN2", target_bir_lowering=False)
    dt = lambda n, s, k="ExternalInput": nc.dram_tensor(n, s, F32, kind=k).ap()
    b = Bld(nc)
    ns = Nsa(b, nc, dt)
    qTd = dt("qT", [128, 8, 1024])
    gated = dt("gates", [128, 8, 48])
    KsTd = dt("KsT", [128, 8192])
    KwTd = dt("KwT", [128, 8192])
    KcTd = dt("KcT", [128, 8192])
    VcTd = dt("VcT", [128, 8192])
    Vsd = dt("Vs", [128, 64, 128])
    Vwd = dt("Vw", [128, 64, 128])
    gmatd = dt("gmat", [128, 8192])
    ovld = dt("ovl", [128, 4, 128])
    btabd = dt("btab", [128, 16 * 72])
    cbiasd = dt("cbias", [128, 512])
    cmaskd = dt("cmask", [128, 2048])
    keepd = dt("keepm", [128, 1024])
    addd = dt("addm", [128, 1024])
    dmsd = dt("dms", [128, 8 * 128])
    dmwd = dt("dmw", [128, 12 * 128])
    btab5d = dt("btab5", [5, 72 * 16])
    biasld = dt("biasl", [5, 128])
    onso = dt("ons", [128, 8, 1024], "ExternalOutput")

    sbt = b.sb
    KsT = sbt("s_KsT", [128, 8192], BF16)
    KwT = sbt("s_KwT", [128, 8192], BF16)
    Vs = sbt("Vsa", [128, 64, 2, 65], BF16)
    Vw = sbt("Vwa", [128, 64, 2, 65], BF16)
    G = sbt("G", [128, 8192], BF16)
    qT = sbt("qTb", [128, 8, 1024], BF16)
    KCT = sbt("KCT", [128, 512], BF16)
    VCO = sbt("VCO", [128, 4, 2, 193], BF16)
    btab = sbt("s_btab", [128, 16 * 72], F32)
    cbias = sbt("s_cbias", [128, 512], F32)
    cmask = sbt("s_cmask", [128, 2048], BF16)
    keepm = sbt("s_keepm", [128, 1024], F32)
    addm = sbt("s_addm", [128, 1024], F32)
    dms = sbt("s_dms", [128, 8, 128], BF16)
    dmw = sbt("s_dmw", [128, 12, 128], BF16)
    btab5 = sbt("s_btab5", [5, 72, 16], BF16)
    biasl = sbt("s_biasl", [5, 128], BF16)
    ns.PT4 = [sbt("PT4_%d" % i, [128, 512], BF16) for i in range(5)]
    b.dma("pool", btab5[:], btab5d.rearrange("k (r h) -> k r h", r=72))
    b.dma("pool", biasl[:], biasld)
    gts = sbt("gts", [128, 8, 48], F32)
    sc = sbt("sc", [128, 2, 128], F32)
    s2 = sbt("s2", [128, 128], F32)
    s3 = sbt("s3", [128, 128], F32)
    m8 = sbt("m8", [128, 16], F32)
    nm = sbt("nm", [128, 128], BF16)
    nmT = sbt("nmT", [128, 2, 128], BF16)
    rd = sbt("rd", [128, 8], F32)
    oacc = [sbt("oacc%d" % i, [128, 1024], F32) for i in range(2)]

    for d_, s_ in ((KsT, KsTd), (KwT, KwTd), (G, gmatd)):
        for q in range(4):
            b.dma("pool", d_[:, q * 2048:(q + 1) * 2048], s_[:, q * 2048:(q + 1) * 2048])
    b.memset(Vs[:], 1.0)
    b.memset(Vw[:], 1.0)
    b.memset(VCO[:], 0.0)
    b.memset(KCT[:], 0.0)
    for d_, s_ in ((Vs, Vsd), (Vw, Vwd)):
        for q in range(4):
            b.dma("pool", d_[:, q * 16:(q + 1) * 16, :, 0:64], s_[:, q * 16:(q + 1) * 16, :].rearrange("p k (n d) -> p k n d", n=2))
    b.dma("pool", qT[:], qTd)
    b.dma("sp", btab[:], btabd)
    b.dma("sp", cbias[:], cbiasd)
    b.dma("pool", cmask[:], cmaskd)
    b.dma("sp", keepm[:], keepd)
    b.dma("sp", addm[:], addd)
    b.dma("pool", dms[:], dmsd.rearrange("p (q t) -> p q t", q=8))
    b.dma("pool", dmw[:], dmwd.rearrange("p (q t) -> p q t", q=12))
    b.dma("sp", gts[:], gated)
    b.act(gts[:], gts[:], AF.Sigmoid)

    ns.cmp_bias()
    b.memset(VCO[:, :, :, 64:65], 1.0)
    for n in range(2):
        b.dma("pool", VCO[:, :, n, 65:193], ovld)
    for ct in range(4):
        N = 128 if ct < 3 else 127
        L = 16 * (N - 1) + 32
        ns.compress_tile("k", KcTd[:, ct * 2048:ct * 2048 + L], N, kdst=KCT[:, ct * 128:ct * 128 + N])
        ns.compress_tile("v", VcTd[:, ct * 2048:ct * 2048 + L], N,
                         vdst=[VCO[0:N, ct, 0, 0:64], VCO[0:N, ct, 1, 0:64]])

    for j in range(8):
        oa = oacc[j % 2]
        qs = slice(j * 128, (j + 1) * 128)
        nct = j // 2 + 1
        for n in range(2):
            pb = slice(n * 64, (n + 1) * 64)
            b.memset(sc[:, n, :], 0.0, eng="dve")
            for g in range(8):
                h = n * 8 + g
                steps = []
                for ct in range(nct):
                    st = {"k": KCT[pb, ct * 128:(ct + 1) * 128], "v": VCO[:, ct, n, :],
                          "bias": cbias[:, (j * 16 + h) * 4 + ct:(j * 16 + h) * 4 + ct + 1]}
                    rr = nct - 1 - ct
                    if rr < 2:
                        st["extra"] = [(ns.ident[:, :], cmask[:, (j * 2 + rr) * 128:(j * 2 + rr + 1) * 128])]
                    steps.append(st)
                pc = ns.branch(steps, qT[pb, g, qs], 128, 193)
                b.ts(rd[:, 0:1], pc[:, 64:65], 1e-30, ALU.max)
                b.recip(rd[:, 0:1], rd[:, 0:1])
                b.stt(sc[:, n, :], pc[:, 65:193], rd[:, 0:1], sc[:, n, :], ALU.mult, ALU.add)
                b.tt(rd[:, 1:2], rd[:, 0:1], gts[:, j, h * 3:h * 3 + 1], ALU.mult)
                b.ts(oa[:, h * 64:(h + 1) * 64], pc[:, 0:64], rd[:, 1:2], ALU.mult)
            b.tt(s2[:], sc[:, n, :], keepm[:, j * 128:(j + 1) * 128], ALU.mult)
            b.tt(s2[:], s2[:], addm[:, j * 128:(j + 1) * 128], ALU.add)
            b.P.op("dve", lambda e: e.max(out=m8[:, 0:8], in_=s2[:]), [s2[:]], [m8[:, 0:8]])
            b.P.op("dve", lambda e: e.match_replace(out=s3[:], in_to_replace=m8[:, 0:8], in_values=s2[:], imm_value=-1e30),
                   [s2[:], m8[:, 0:8]], [s3[:]])
            b.P.op("dve", lambda e: e.max(out=m8[:, 8:16], in_=s3[:]), [s3[:]], [m8[:, 8:16]])
            b.ts(s3[:], s2[:], m8[:, 15:16], ALU.is_ge)
            b.ts(nm[:], s3[:], -NEG, ALU.mult, NEG, ALU.add)
            b.tr(ns.psT[:, 0:128], nm[:], ns.ident[:, :])
            b.copy(nmT[:, n, :], ns.psT[:, 0:128])
        for hg in range(4):
            n, g0 = hg // 2, (hg % 2) * 4
            h0 = hg * 4
            pb = slice(n * 64, (n + 1) * 64)
            qr = qT[pb, g0:g0 + 4, qs]
            steps = []
            for kt in range(8 * j + 8):
                ex = [(G[:, kt * 128:(kt + 1) * 128], nmT[:, n, :])]
                if kt >= 8 * j:
                    ex.append((ns.ident[:, :], dms[:, kt - 8 * j, :]))
                rp = 8 * j + 7 - kt
                steps.append({"k": KsT[pb, kt * 128:(kt + 1) * 128], "v": Vs[:, kt, n, :], "extra": ex,
                              "brow": btab5[:, rp, h0:h0 + 4]})
            pc = branch4(ns, b, steps, qr, biasl[:, :])
            for hl in range(4):
                h = h0 + hl
                den = pc[:, hl * 65 + 64:hl * 65 + 65]
                b.ts(rd[:, 2:3], den, 1e-30, ALU.max)
                b.recip(rd[:, 2:3], rd[:, 2:3])
                b.tt(rd[:, 3:4], rd[:, 2:3], gts[:, j, h * 3 + 1:h * 3 + 2], ALU.mult)
                b.stt(oa[:, h * 64:(h + 1) * 64], pc[:, hl * 65:hl * 65 + 64], rd[:, 3:4], oa[:, h * 64:(h + 1) * 64], ALU.mult, ALU.add)
            steps = []
            for q in range(12):
                kt = 8 * j - 4 + q
                if kt < 0:
                    continue
                steps.append({"k": KwT[pb, kt * 128:(kt + 1) * 128], "v": Vw[:, kt, n, :],
                              "extra": [(ns.ident[:, :], dmw[:, q, :])], "brow": btab5[:, 11 - q, h0:h0 + 4]})
            pc = branch4(ns, b, steps, qr, biasl[:, :])
            for hl in range(4):
                h = h0 + hl
                den = pc[:, hl * 65 + 64:hl * 65 + 65]
                b.ts(rd[:, 4:5], den, 1e-30, ALU.max)
                b.recip(rd[:, 4:5], rd[:, 4:5])
                b.tt(rd[:, 5:6], rd[:, 4:5], gts[:, j, h * 3 + 2:h * 3 + 3], ALU.mult)
                b.stt(oa[:, h * 64:(h + 1) * 64], pc[:, hl * 65:hl * 65 + 64], rd[:, 5:6], oa[:, h * 64:(h + 1) * 64], ALU.mult, ALU.add)
        b.dma("sp", onso[:, j, :], oa[:])
    b.finish()
    return nc


def nsa_cmp_weights(inp):
    m = {}
    for kind in ("k", "v"):
        w1 = inp["cmp_w1_" + kind][0].reshape(32, 64, 256).transpose(1, 0, 2)
        m["w1" + kind] = np.ascontiguousarray(np.concatenate([w1, w1], 0))
        w2 = inp["cmp_w2_" + kind][0].reshape(2, 128, 64)
        w2p = np.zeros((128, 4, 128), np.float32)
        for n in range(2):
            for hc in range(2):
                w2p[:, n * 2 + hc, n * 64:(n + 1) * 64] = w2[hc]
        m["w2" + kind] = w2p
        pT = inp["cmp_pos_" + kind][0].T
        m["pos" + kind] = np.ascontiguousarray(np.concatenate([pT, pT], 0))
    return m


def run_l2n_prompt(inp, proj_p):
    q = proj_p[:, 4096:5120]
    kv = proj_p[:, 5120:5888].reshape(8192, 6, 128)
    gates = proj_p[:, 5888:5936]
    cw = nsa_cmp_weights(inp)
    T = lambda a: np.ascontiguousarray(a.T)
    tok = lambda a: np.ascontiguousarray(a.reshape(64, 128, 128).transpose(1, 0, 2))
    shared = {"KcT": T(kv[:, 0]), "VcT": T(kv[:, 1]), "KsT": T(kv[:, 2]), "Vs": tok(kv[:, 3]),
              "KwT": T(kv[:, 4]), "Vw": tok(kv[:, 5])}
    shared.update(cw)
    outs = []
    ncs = []
    maps = []
    for c in range(NCORES):
        tiles = [c + 8 * j for j in range(8)]
        m = dict(shared)
        m.update(nsa_prompt_consts(c))
        qc = np.stack([q[128 * i:128 * (i + 1)] for i in tiles], 0)
        qr = qc.reshape(8, 128, 2, 8, 64).transpose(2, 4, 3, 0, 1).reshape(128, 8, 1024)
        m["qT"] = np.ascontiguousarray(qr)
        m["gates"] = np.ascontiguousarray(np.stack([gates[128 * i:128 * (i + 1)] for i in tiles], 1))
        maps.append(m)
    nc = build_l2n()
    res = run_bass_kernel_spmd(nc, maps, core_ids=list(range(NCORES)))
    o = np.zeros((8192, 1024), np.float32)
    for c in range(NCORES):
        r = res.results[c]["ons"]
        for j in range(8):
            i = c + 8 * j
            o[128 * i:128 * (i + 1)] = r[:, j, :]
    return o


U32 = mybir.dt.uint32
PAST = 2048
SCUT = None
NEGF = -30000.0


def nsa_sample_consts():
    c = {}
    p = np.arange(128)
    c["gs"] = (np.arange(128)[:, None] == (np.arange(17 * 128)[None, :] // 64)).astype(np.float32)
    cs = np.arange(128)[:, None] * 16
    ss = np.arange(64)[None, :] * 64
    ov = ((cs <= ss + 63) & (cs + 31 >= ss)).astype(np.float32)
    ov[127] = 0.0
    ov[:, 33:] = 0.0
    c["ovs"] = ov
    t = np.arange(4)
    bc = np.zeros((128, 2, 8, 4), np.float64)
    bs = np.zeros((128, 17, 2, 8, 4), np.float64)
    bw = np.zeros((128, 5, 2, 8, 4), np.float64)
    for n in range(2):
        for g in range(8):
            sl = SLOPES[n * 8 + g]
            qpos = PAST + t
            cpos = 16 * p + 31
            bc[:, n, g, :] = np.where((p < 127)[:, None], -sl * (qpos[None, :] - cpos[:, None]), NEGF)
            for tile in range(17):
                spos = 128 * tile + p
                dist = qpos[None, :] - spos[:, None]
                bs[:, tile, n, g, :] = np.where(dist >= 0, -sl * dist, NEGF)
            for tile in range(5):
                wpos = PAST - 512 + 128 * tile + p
                dist = qpos[None, :] - wpos[:, None]
                ok = (dist >= 0) & (dist < 512)
                if tile == 4:
                    ok &= (p < 4)[:, None]
                bw[:, tile, n, g, :] = np.where(ok, -sl * dist, NEGF)
    c["biasc"] = np.ascontiguousarray(bc.astype(np.float32).reshape(128, 64))
    c["biass"] = np.ascontiguousarray(bs.astype(np.float32).reshape(128, 17 * 64))
    c["biasw"] = np.ascontiguousarray(bw.astype(np.float32).reshape(128, 5 * 64))
    blk = np.arange(64)
    forced0, forced1, forced2 = blk == 0, blk == 32, blk == 31
    valid = blk <= 32
    keep = (valid & ~(forced0 | forced1 | forced2)).astype(np.float32)
    a = np.where(valid, 0.0, -1e30)
    a = np.where(forced2, 1e4, a)
    a = np.where(forced1, 2e4, a)
    a = np.where(forced0, 3e4, a)
    c["keeps"] = np.ascontiguousarray(np.broadcast_to(keep[None, :], (4, 64)).astype(np.float32))
    c["adds"] = np.ascontiguousarray(np.broadcast_to(a[None, :], (4, 64)).astype(np.float32))
    sel = np.zeros((32, 4), np.float32)
    for g in range(8):
        for tt in range(4):
            sel[g * 4 + tt, tt] = 1.0
    c["selm"] = sel
    c["pcol"] = p.astype(np.float32).reshape(128, 1)
    c["identd"] = np.eye(128, dtype=np.float32)
    return c


def build_l2s(n_pool=2560, nb=16):
    nc = bass.Bass("TRN2", target_bir_lowering=False)
    dt = lambda n, s, k="ExternalInput": nc.dram_tensor(n, s, F32, kind=k).ap()
    b = Bld(nc)
    ns = Nsa(b, nc, dt)
    cache = dt("cache", [n_pool * 128, 512])
    ptab = nc.dram_tensor("ptab", [1, nb * 16], I32, kind="ExternalInput").ap()
    cwin = dt("cwin", [nb, 512, 256])
    qTd = dt("qTs", [128, nb, 32])
    ksnd = dt("ksn", [128, nb, 4])
    kwnd = dt("kwn", [128, nb, 4])
    vsnd = dt("vsn", [4, nb, 128])
    vwnd = dt("vwn", [4, nb, 128])
    gtd = dt("gts", [32, nb, 2, 3])
    gsd = dt("gs", [128, 17 * 128])
    ovsd = dt("ovs", [128, 64])
    bcd = dt("biasc", [128, 64])
    bsd = dt("biass", [128, 17 * 64])
    bwd = dt("biasw", [128, 5 * 64])
    keepd = dt("keeps", [4, 64])
    addd = dt("adds", [4, 64])
    seld = dt("selm", [32, 4])
    pcold = dt("pcol", [128, 1])
    onso = dt("ons", [nb, 2, 32, 64], "ExternalOutput")

    sbt = b.sb
    Gs = sbt("s_gs", [128, 17 * 128], BF16)
    b.dma("pool", Gs[:], gsd)
    biasc = sbt("s_bc", [128, 2, 32], F32)
    biass = sbt("s_bs", [128, 17, 2, 32], F32)
    biasw = sbt("s_bw", [128, 5, 2, 32], F32)
    b.dma("sp", biasc[:], bcd.rearrange("p (n q) -> p n q", n=2))
    b.dma("sp", biass[:], bsd.rearrange("p (k n q) -> p k n q", k=17, n=2))
    b.dma("sp", biasw[:], bwd.rearrange("p (k n q) -> p k n q", k=5, n=2))
    keeps = sbt("s_keep", [4, 64], F32)
    adds = sbt("s_add", [4, 64], F32)
    selm = sbt("s_sel", [32, 4], F32)
    pcol = sbt("s_pcol", [128, 1], F32)
    b.dma("sp", keeps[:], keepd)
    b.dma("sp", adds[:], addd)
    b.dma("sp", selm[:], seld)
    b.dma("sp", pcol[:], pcold)
    qT = sbt("s_qT", [128, nb, 32], BF16)
    ksn = sbt("s_ksn", [128, nb, 4], BF16)
    kwn = sbt("s_kwn", [128, nb, 4], BF16)
    vsn = sbt("s_vsn", [4, nb, 128], BF16)
    vwn = sbt("s_vwn", [4, nb, 128], BF16)
    gts = sbt("s_gts", [32, nb, 2, 3], F32)
    b.dma("pool", qT[:], qTd)
    b.dma("pool", ksn[:], ksnd)
    b.dma("pool", kwn[:], kwnd)
    b.dma("pool", vsn[:], vsnd)
    b.dma("pool", vwn[:], vwnd)
    b.dma("sp", gts[:], gtd)
    b.act(gts[:], gts[:], AF.Sigmoid)
    pti = sbt("pti", [128, nb * 16], I32)
    idx = sbt("idx", [128, nb * 16], U32)
    b.dma("sp", pti[:], ptab.partition_broadcast(128))
    b.ts(idx[:], pti[:], 128.0, ALU.mult, pcol[:, 0:1], ALU.add)

    NG = 20
    gth = [sbt("gth%d" % i, [128, 512], F32) for i in range(NG)]
    wth = [sbt("wth%d" % i, [128, 256], F32) for i in range(2)]
    KcT4 = sbt("KcT4", [128, 4, 16, 128], BF16)
    VcT4 = sbt("VcT4", [128, 4, 16, 128], BF16)
    KsT = [sbt("KsTs%d" % i, [128, 2052], BF16) for i in range(4)]
    KwT = [sbt("KwTs%d" % i, [128, 516], BF16) for i in range(4)]
    Vs = [sbt("Vss%d" % i, [128, 17, 2, 65], BF16) for i in range(4)]
    Vw = [sbt("Vws%d" % i, [128, 5, 2, 65], BF16) for i in range(4)]
    KCTs4 = sbt("KCTs4", [128, 4, 128], BF16)
    VCOs4 = sbt("VCOs4", [128, 4, 2, 129], BF16)
    ns.tx4 = sbt("tx4", [128, 512], F32)
    ns.tu4 = sbt("tu4", [128, 512], F32)
    ns.gh4 = [sbt("gh4_%d" % i, [128, 512], BF16) for i in range(4)]
    tmpf = [sbt("tmpf%d" % i, [128, 32], F32) for i in range(3)]
    xn = sbt("xn", [32, 64], F32)
    s2 = sbt("s2s", [4, 64], F32)
    s3 = sbt("s3s", [4, 64], F32)
    m8 = sbt("m8s", [4, 16], F32)
    nmf = sbt("nmf", [4, 64], F32)
    nmT = sbt("nmTs", [128, 8, 4], BF16)
    rd = sbt("rds", [32, 8], F32)
    oac = [sbt("oacs%d" % i, [32, 64], F32) for i in range(2)]
    for i in range(4):
        b.memset(Vs[i][:], 1.0)
        b.memset(Vw[i][:], 1.0)
    b.memset(KCTs4[:], 0.0)
    b.memset(nmT[:], 0.0)
    b.memset(VCOs4[:], 0.0)
    b.memset(VCOs4[:, :, :, 64:65], 1.0)
    for sq in range(4):
        for n in range(2):
            b.dma("pool", VCOs4[:, sq, n, 65:129], ovsd)
    ns.cmp_bias()
    nt = [0]

    pend = []

    def step(kT, rows, q, mask, bias, v, pacc, first, last, ncols):
        ps = ns.psS[ns.nS % 2]
        ns.nS += 1
        b.mm(ps[0:rows, 0:32], kT, q, start=True, stop=(mask is None))
        if mask is not None:
            b.mm(ps[0:rows, 0:32], mask[0], mask[1], start=False, stop=True)
        tf = tmpf[nt[0] % 3]
        nt[0] += 1
        b.stt(tf[0:rows, :], ps[0:rows, 0:32], 0.125, bias, ALU.mult, ALU.add)
        pt = ns.PT[ns.nP % 3]
        ns.nP += 1
        b.act(pt[0:rows, 0:32], tf[0:rows, :], AF.Exp)
        flush()
        pend.append((pacc, ncols, pt, rows, v, first, last))

    def flush():
        while pend:
            pacc, ncols, pt, rows, v, first, last = pend.pop(0)
            b.mm(pacc[0:32, 0:ncols], pt[0:rows, 0:32], v, start=first, stop=last)

    def gather_seq(bi):
        k2 = bi % 4
        for pg in range(16):
            gt = gth[(bi * 16 + pg) % NG]
            col = bi * 16 + pg
            I = Ins("pool", (lambda e, gt=gt, col=col: e.indirect_dma_start(
                out=gt[:], out_offset=None, in_=cache,
                in_offset=bass.IndirectOffsetOnAxis(ap=idx[:, col:col + 1], axis=0))), True)
            I.idx = len(b.P.ins)
            b.P.ins.append(I)
            b.P._track(I, [idx[:, col:col + 1], cache], [gt[:]])
            ph = ns.psH[ns.nH % 2]
            ns.nH += 1
            for q3 in range(3):
                b.tr(ph[:, q3 * 128:(q3 + 1) * 128], gt[:, q3 * 128:(q3 + 1) * 128], ns.identf[:, :])
            cs = slice(pg * 128, (pg + 1) * 128)
            b.copy(KcT4[:, k2, :, pg * 8:(pg + 1) * 8], ph[:, 0:128].rearrange("p (c r) -> p r c", r=16), eng="act")
            b.copy(VcT4[:, k2, :, pg * 8:(pg + 1) * 8], ph[:, 128:256].rearrange("p (c r) -> p r c", r=16))
            b.copy(KsT[k2][:, cs], ph[:, 256:384], eng="act")
            b.copy(Vs[k2][:, pg, :, 0:64], gt[:, 384:512].rearrange("p (n d) -> p n d", n=2))
        b.copy(KsT[k2][:, 2048:2052], ksn[:, bi, :])
        b.copy(Vs[k2][0:4, 16, :, 0:64], vsn[0:4, bi, :].rearrange("p (n d) -> p n d", n=2))
        for wt in range(4):
            wtile = wth[wt % 2]
            b.dma("sp", wtile[:], cwin[bi, wt * 128:(wt + 1) * 128, :])
            ph = ns.psH[ns.nH % 2]
            ns.nH += 1
            b.tr(ph[:, 0:128], wtile[:, 0:128], ns.identf[:, :])
            b.copy(KwT[k2][:, wt * 128:(wt + 1) * 128], ph[:, 0:128], eng="act")
            b.copy(Vw[k2][:, wt, :, 0:64], wtile[:, 128:256].rearrange("p (n d) -> p n d", n=2))
        b.copy(KwT[k2][:, 512:516], kwn[:, bi, :])
        b.copy(Vw[k2][0:4, 4, :, 0:64], vwn[0:4, bi, :].rearrange("p (n d) -> p n d", n=2))

    for bi in range(nb):
        k2 = bi % 4
        if bi % 4 == 0:
            for bj in range(bi, bi + 4):
                gather_seq(bj)
            if SCUT == "A":
                continue
            ns.compress_group("k", KcT4, kdst3=KCTs4[:, :, 0:127])
            ns.compress_group("v", VcT4, vdst=[[VCOs4[0:127, sq, 0, 0:64], VCOs4[0:127, sq, 1, 0:64]] for sq in range(4)])
        if SCUT in ("A", "B"):
            continue
        KCTs = KCTs4[:, k2, :]
        VCOs = VCOs4[:, k2, :, :]
        for n in range(2):
            pb = slice(n * 64, (n + 1) * 64)
            q = qT[pb, bi, :]
            oa = oac[n]
            pacc = ns.psAcc[ns.nA % 2]
            ns.nA += 1
            step(KCTs4[pb, k2, :], 128, q, None, biasc[:, n, :], VCOs4[:, k2, n, :], pacc, True, True, 129)
            flush()
            b.ts(rd[:, 0:1], pacc[0:32, 64:65], 1e-30, ALU.max)
            b.recip(rd[:, 0:1], rd[:, 0:1])
            b.ts(xn[:], pacc[0:32, 65:129], rd[:, 0:1], ALU.mult)
            b.tt(rd[:, 1:2], rd[:, 0:1], gts[:, bi, n, 0:1], ALU.mult)
            b.ts(oa[:], pacc[0:32, 0:64], rd[:, 1:2], ALU.mult)
            if SCUT == "C":
                continue
            pk = ns.psK2
            b.mm(pk[0:4, 0:64], selm[:, :], xn[:, :])
            b.tt(s2[:], pk[0:4, 0:64], keeps[:], ALU.mult)
            b.tt(s2[:], s2[:], adds[:], ALU.add)
            b.P.op("dve", lambda e: e.max(out=m8[:, 0:8], in_=s2[:]), [s2[:]], [m8[:, 0:8]])
            b.P.op("dve", lambda e: e.match_replace(out=s3[:], in_to_replace=m8[:, 0:8], in_values=s2[:], imm_value=-1e30),
                   [s2[:], m8[:, 0:8]], [s3[:]])
            b.P.op("dve", lambda e: e.max(out=m8[:, 8:16], in_=s3[:]), [s3[:]], [m8[:, 8:16]])
            b.ts(s3[:], s2[:], m8[:, 15:16], ALU.is_ge)
            b.ts(nmf[:], s3[:], -NEG, ALU.mult, NEG, ALU.add)
            b.tr(pk[0:64, 64:68], nmf[:, :], ns.identf[0:4, 0:4])
            b.copy(nmT[0:64, :, :], pk[0:64, 64:68].unsqueeze(1).broadcast_to([64, 8, 4]))
            nmv = nmT[:].rearrange("p g t -> p (g t)")
            if SCUT == "D":
                continue
            pacc = ns.psAcc[ns.nA % 2]
            ns.nA += 1
            for tile in range(17):
                rows = 128 if tile < 16 else 4
                cs = slice(tile * 128, tile * 128 + rows)
                step(KsT[k2][pb, cs], rows, q, (Gs[:, cs], nmv), biass[0:rows, tile, n, :], Vs[k2][0:rows, tile, n, :],
                     pacc, tile == 0, tile == 16, 65)
            flush()
            b.ts(rd[:, 2:3], pacc[0:32, 64:65], 1e-30, ALU.max)
            b.recip(rd[:, 2:3], rd[:, 2:3])
            b.tt(rd[:, 3:4], rd[:, 2:3], gts[:, bi, n, 1:2], ALU.mult)
            b.stt(oa[:], pacc[0:32, 0:64], rd[:, 3:4], oa[:], ALU.mult, ALU.add)
            if SCUT == "E":
                continue
            pacc = ns.psAcc[ns.nA % 2]
            ns.nA += 1
            for tile in range(5):
                rows = 128 if tile < 4 else 4
                cs = slice(tile * 128, tile * 128 + rows)
                step(KwT[k2][pb, cs], rows, q, None, biasw[0:rows, tile, n, :], Vw[k2][0:rows, tile, n, :],
                     pacc, tile == 0, tile == 4, 65)
            flush()
            b.ts(rd[:, 4:5], pacc[0:32, 64:65], 1e-30, ALU.max)
            b.recip(rd[:, 4:5], rd[:, 4:5])
            b.tt(rd[:, 5:6], rd[:, 4:5], gts[:, bi, n, 2:3], ALU.mult)
            b.stt(oa[:], pacc[0:32, 0:64], rd[:, 5:6], oa[:], ALU.mult, ALU.add)
            b.dma("sp", onso[bi, n], oa[:])
    b.finish()
    return nc


def run_l2n_sample(inp, proj_s, nb=16):
    cache = inp["cache_kv"][0]
    n_pool = cache.shape[0]
    cache2 = np.ascontiguousarray(cache.reshape(n_pool * 128, 512))
    cst = nsa_sample_consts()
    cst.update(nsa_cmp_weights(inp))
    ps = proj_s.reshape(128, 4, NPROJ)
    nc = build_l2s(n_pool, nb)
    maps = []
    for c in range(NCORES):
        sb = ps[c * nb:(c + 1) * nb]
        m = dict(cst)
        m["cache"] = cache2
        m["ptab"] = np.ascontiguousarray(inp["page_table"][c * nb:(c + 1) * nb].reshape(1, nb * 16).astype(np.int32))
        m["cwin"] = np.ascontiguousarray(inp["cache_win"][0, c * nb:(c + 1) * nb].reshape(nb, 512, 256))
        q = sb[:, :, 4096:5120].reshape(nb, 4, 2, 8, 64)
        m["qTs"] = np.ascontiguousarray(q.transpose(2, 4, 0, 3, 1).reshape(128, nb, 32))
        kv = sb[:, :, 5120:5888].reshape(nb, 4, 6, 128)
        m["ksn"] = np.ascontiguousarray(kv[:, :, 2].transpose(2, 0, 1))
        m["kwn"] = np.ascontiguousarray(kv[:, :, 4].transpose(2, 0, 1))
        m["vsn"] = np.ascontiguousarray(kv[:, :, 3].transpose(1, 0, 2))
        m["vwn"] = np.ascontiguousarray(kv[:, :, 5].transpose(1, 0, 2))
        g = sb[:, :, 5888:5936].reshape(nb, 4, 2, 8, 3)
        m["gts"] = np.ascontiguousarray(g.transpose(3, 1, 0, 2, 4).reshape(32, nb, 2, 3))
        maps.append(m)
    res = run_bass_kernel_spmd(nc, maps, core_ids=list(range(NCORES)))
    outs = []
    for c in range(NCORES):
        r = res.results[c]["ons"].reshape(nb, 2, 8, 4, 64)
        outs.append(r.transpose(0, 3, 1, 2, 4).reshape(nb * 4, 1024))
    return np.concatenate(outs, 0)
```

```python
from contextlib import ExitStack
import numpy as np
import concourse.bass as bass
import concourse.mybir as mybir
from concourse.bass_utils import run_bass_kernel_spmd

F32 = mybir.dt.float32
BF16 = mybir.dt.bfloat16
I32 = mybir.dt.int32
AF = mybir.ActivationFunctionType
ALU = mybir.AluOpType
AX = mybir.AxisListType

NCORES = 8
D = 2048
DFF = 5504
T_P = 1024
T_S = 64
T_ALL = T_P + T_S
KC = D // 128
FC = DFF // 128
EPS = 1e-6
NPROJ = 5936
D_IN = 10032

COMPUTE = ("pe", "act", "dve", "pool")
NDMASEM = 8
CUT = 9


def region(ap):
    name = ap.tensor.name
    space = str(ap.space)
    aplist = ap.ap
    off = int(ap.offset)
    if space == "PSUM":
        return (name, 0, 128, 0, 1 << 30)
    if space == "SB":
        pstep, pcount = aplist[0]
        if pstep == 0:
            p0, foff, pcount = 0, off, 128
        else:
            p0 = off // pstep
            foff = off % pstep
        ext = 1
        for s, c in aplist[1:]:
            ext += (c - 1) * abs(s)
        return (name, p0, p0 + pcount, foff, foff + ext)
    ext = 1
    for s, c in aplist:
        ext += (c - 1) * abs(s)
    return (name, 0, 1, off, off + ext)


def overlap(a, b):
    return a[1] < b[2] and b[1] < a[2] and a[3] < b[4] and b[3] < a[4]


def covers(a, b):
    return a[1] <= b[1] and a[2] >= b[2] and a[3] <= b[3] and a[4] >= b[4]


class Ins:
    __slots__ = ("eng", "fn", "deps", "need_inc", "cnt", "is_dma", "dsem", "dval", "idx", "prewait", "inc")

    def __init__(self, eng, fn, is_dma):
        self.eng = eng
        self.fn = fn
        self.deps = set()
        self.need_inc = False
        self.cnt = None
        self.is_dma = is_dma
        self.dsem = None
        self.dval = None
        self.prewait = None
        self.inc = 16


class Prog:
    def __init__(self, nc):
        self.nc = nc
        self.ins = []
        self.hist = {}

    def _track(self, I, reads, writes):
        idx = I.idx
        ins = self.ins
        rr_ = [region(a) for a in reads if str(a.space) != "PSUM"]
        wr_ = [region(a) for a in writes] + [region(a) for a in reads if str(a.space) == "PSUM"]
        for r in rr_:
            h = self.hist.setdefault(r[0], [])
            for (rr, j, w) in h:
                if w and overlap(r, rr):
                    J = ins[j]
                    if J.eng == "pe" and I.eng == "pe" and not I.is_dma and not J.is_dma:
                        continue
                    I.deps.add(j)
        for r in wr_:
            h = self.hist.setdefault(r[0], [])
            for (rr, j, w) in h:
                if overlap(r, rr):
                    J = ins[j]
                    if J.eng == I.eng and not I.is_dma and not J.is_dma:
                        continue
                    I.deps.add(j)
        if len(I.deps) > 1:
            best = {}
            keep = set()
            for j in I.deps:
                J = ins[j]
                if J.is_dma:
                    keep.add(j)
                elif best.get(J.eng, -1) < j:
                    best[J.eng] = j
            keep.update(best.values())
            I.deps = keep
        for r in rr_:
            h = self.hist[r[0]]
            if not I.is_dma:
                h[:] = [e for e in h if e[2] or e[0] != r or ins[e[1]].eng != I.eng or ins[e[1]].is_dma]
            h.append((r, idx, False))
        for r in wr_:
            h = self.hist[r[0]]
            h[:] = [e for e in h if not covers(r, e[0])]
            h.append((r, idx, True))

    def op(self, eng, fn, reads=(), writes=()):
        I = Ins(eng, fn, False)
        I.idx = len(self.ins)
        self.ins.append(I)
        self._track(I, reads, writes)
        return I

    def dma(self, q, out, in_, **kw):
        def fn(e, out=out, in_=in_, kw=kw):
            return e.dma_start(out=out, in_=in_, **kw)
        I = Ins(q, fn, True)
        I.idx = len(self.ins)
        self.ins.append(I)
        self._track(I, [in_], [out])
        return I

    def emit(self, stack):
        nc = self.nc
        ins = self.ins
        for I in ins:
            for j in I.deps:
                ins[j].need_inc = True
        csem = {e: stack.enter_context(nc.semaphore("c_" + e)) for e in COMPUTE}
        dsems = {q: [stack.enter_context(nc.semaphore("d_%s_%d" % (q, i))) for i in range(NDMASEM)]
                 for q in ("sp", "act", "pool")}
        cnt = {e: 0 for e in COMPUTE}
        dq_n = {q: 0 for q in dsems}
        dq_val = {q: [0] * NDMASEM for q in dsems}
        dq_last = {q: [None] * NDMASEM for q in dsems}
        for I in ins:
            if I.is_dma:
                k = dq_n[I.eng] % NDMASEM
                dq_n[I.eng] += 1
                I.prewait = dq_last[I.eng][k]
                dq_val[I.eng][k] += I.inc
                I.dsem = dsems[I.eng][k]
                I.dval = dq_val[I.eng][k]
                dq_last[I.eng][k] = I.idx
            elif I.need_inc:
                cnt[I.eng] += 1
                I.cnt = cnt[I.eng]
        self.maxcnt = dict(cnt)
        streams = {e: [] for e in ("pe", "act", "dve", "pool", "sp")}
        for I in ins:
            streams[I.eng].append(I)
        block = stack.enter_context(nc.Block())

        def run_stream(ename, e):
            waited = {}

            def wait_for(j):
                J = ins[j]
                if J.is_dma:
                    key = ("d", J.eng, id(J.dsem))
                    if waited.get(key, 0) >= J.dval:
                        return
                    waited[key] = J.dval
                    e.wait_ge(J.dsem, J.dval)
                else:
                    key = ("c", J.eng)
                    if waited.get(key, 0) >= J.cnt:
                        return
                    waited[key] = J.cnt
                    e.wait_ge(csem[J.eng], J.cnt)

            for I in streams[ename]:
                for j in sorted(I.deps):
                    wait_for(j)
                if I.is_dma and I.prewait is not None:
                    wait_for(I.prewait)
                bi = I.fn(e)
                if I.is_dma:
                    bi.then_inc(I.dsem, I.inc)
                elif I.need_inc:
                    bi.then_inc(csem[I.eng], 1)
            if ename == "sp":
                for q in dsems:
                    for k in range(NDMASEM):
                        if dq_val[q][k] > 0:
                            e.wait_ge(dsems[q][k], dq_val[q][k])
                for ce in COMPUTE:
                    if cnt[ce] > 0:
                        e.wait_ge(csem[ce], cnt[ce])

        @block.tensor
        def _(e):
            run_stream("pe", e)

        @block.scalar
        def _(e):
            run_stream("act", e)

        @block.vector
        def _(e):
            run_stream("dve", e)

        @block.gpsimd
        def _(e):
            run_stream("pool", e)

        @block.sync
        def _(e):
            run_stream("sp", e)


class Bld:
    def __init__(self, nc):
        self.nc = nc
        self.P = Prog(nc)
        self.st = ExitStack()
        self._n = 0

    def sb(self, name, shape, dt):
        return self.st.enter_context(self.nc.sbuf_tensor(name, shape, dt))

    def ps(self, name, shape, dt=F32):
        return self.st.enter_context(self.nc.psum_tensor(name, shape, dt))

    def mm(self, out, lhsT, rhs, start=True, stop=True, skip=False):
        if skip:
            self.P.op("pe", lambda e: e.matmul(out, lhsT=lhsT, rhs=rhs, start=start, stop=stop, skip_group_check=True),
                      [lhsT, rhs, out], [out])
        else:
            self.P.op("pe", lambda e: e.matmul(out, lhsT=lhsT, rhs=rhs, start=start, stop=stop),
                      [lhsT, rhs] + ([] if start else [out]), [out])

    def tr(self, out, in_, ident):
        self.P.op("pe", lambda e: e.transpose(out=out, in_=in_, identity=ident), [in_, ident], [out])

    def act(self, out, in_, func, bias=None, scale=None, accum=None, eng="act"):
        kw = {}
        rd = [in_]
        wr = [out]
        if bias is not None:
            kw["bias"] = bias
            if not isinstance(bias, (int, float)):
                rd.append(bias)
        if scale is not None:
            kw["scale"] = scale
            if not isinstance(scale, (int, float)):
                rd.append(scale)
        if accum is not None:
            kw["accum_out"] = accum
            wr.append(accum)
        self.P.op("act", lambda e: e.activation(out=out, in_=in_, func=func, **kw), rd, wr)

    def tt(self, out, a, b, op, eng="dve"):
        self.P.op(eng, lambda e: e.tensor_tensor(out=out, in0=a, in1=b, op=op), [a, b], [out])

    def ts(self, out, a, s1, op0, s2=None, op1=None, eng="dve", accum=None):
        rd = [a]
        wr = [out]
        if not isinstance(s1, (int, float)):
            rd.append(s1)
        if s2 is not None and not isinstance(s2, (int, float)):
            rd.append(s2)
        kw = {}
        if op1 is not None:
            kw["op1"] = op1
        if accum is not None:
            kw["accum_out"] = accum
            wr.append(accum)
        self.P.op(eng, lambda e: e.tensor_scalar(out=out, in0=a, scalar1=s1, scalar2=s2, op0=op0, **kw), rd, wr)

    def stt(self, out, a, s, b, op0, op1):
        rd = [a, b]
        if not isinstance(s, (int, float)):
            rd.append(s)
        self.P.op("dve", lambda e: e.scalar_tensor_tensor(out=out, in0=a, scalar=s, in1=b, op0=op0, op1=op1), rd, [out])

    def copy(self, out, in_, eng="dve"):
        if eng == "act":
            self.P.op("act", lambda e: e.copy(out=out, in_=in_), [in_], [out])
        else:
            self.P.op(eng, lambda e: e.tensor_copy(out=out, in_=in_), [in_], [out])

    def recip(self, out, in_):
        self.P.op("dve", lambda e: e.reciprocal(out=out, in_=in_), [in_], [out])

    def memset(self, ap, v, eng="pool"):
        self.P.op(eng, lambda e: e.memset(ap, v), [], [ap])

    def dma(self, q, out, in_, **kw):
        self.P.dma(q, out, in_, **kw)

    def finish(self):
        self.P.emit(self.st)
        self.st.close()


class Dense:
    def __init__(self, b, ident_dram):
        self.b = b
        nc = b.nc
        self.identf = b.sb("identf", [128, 128], F32)
        self.ident = b.sb("ident", [128, 128], BF16)
        b.dma("sp", self.identf[:], ident_dram)
        b.copy(self.ident[:], self.identf[:])
        self.hT = b.sb("hT", [128, KC, 576], BF16)
        self.aT = b.sb("aT", [128, FC, 576], BF16)
        self.ybuf = b.sb("ybuf", [128, 5, D], BF16)
        self.xt = [b.sb("xt%d" % i, [128, D], F32) for i in range(2)]
        self.hn = [b.sb("hn%d" % i, [128, D], BF16) for i in range(2)]
        self.junk = b.sb("junk", [128, D], BF16)
        self.st4 = b.sb("st4", [128, 8], F32)
        self.wA = [b.sb("wA%d" % i, [128, KC, 256], BF16) for i in range(2)]
        self.wB = [b.sb("wB%d" % i, [128, KC, 256], BF16) for i in range(2)]
        self.wD = [b.sb("wD%d" % i, [128, FC, 256], BF16) for i in range(2)]
        self.sg = [b.sb("sg%d" % i, [128, 512], BF16) for i in range(2)]
        self.ev = [b.sb("ev%d" % i, [128, 256], F32) for i in range(2)]
        self.psT = [b.ps("psT%d" % i, [128, 8, 128], BF16) for i in range(2)]
        self.psG = [b.ps("psG%d" % i, [128, 512]) for i in range(2)]
        self.psU = [b.ps("psU%d" % i, [128, 512]) for i in range(2)]
        self.psO = [b.ps("psO%d" % i, [128, 512]) for i in range(2)]
        self.nT = 0
        self.nG = 0
        self.nO = 0
        self.nX = 0
        self.nW = 0
        self.nE = 0

    def rstd(self, src, rows, col):
        b = self.b
        ss = self.st4[0:rows, col:col + 1]
        b.act(self.junk[0:rows, :], src, AF.Square, accum=ss)
        b.act(ss, ss, AF.Sqrt, bias=EPS, scale=1.0 / D)
        b.recip(ss, ss)
        return ss

    def norm_to_hT(self, src, rows, tok0, gcol):
        b = self.b
        r = self.rstd(src, rows, 0)
        hn = self.hn[self.nX % 2]
        self.nX += 1
        b.ts(hn[0:rows, :], src, r, ALU.mult)
        if CUT >= 3:
            self.to_T(hn, rows, self.hT, tok0, gcol)

    def to_T(self, src, rows, dstT, tok0, gcol=None, nk=KC):
        b = self.b
        for k4 in range(0, nk, 4):
            pt = self.psT[self.nT % 2]
            self.nT += 1
            for j in range(4):
                kc = k4 + j
                b.tr(pt[:, j, 0:rows], src[0:rows, kc * 128:(kc + 1) * 128], self.ident[0:rows, 0:rows])
            if gcol is None:
                b.copy(dstT[:, k4:k4 + 4, tok0:tok0 + rows], pt[:, 0:4, 0:rows])
            else:
                for j in range(4):
                    kc = k4 + j
                    b.act(dstT[:, kc, tok0:tok0 + rows], pt[:, j, 0:rows], AF.Copy, scale=gcol[:, kc:kc + 1])

    def load_w(self, dst, w_dram, c0, w, nk):
        src = w_dram.rearrange("(kc p) f -> p kc f", p=128)[:, :, c0:c0 + w]
        self.b.dma("pool", dst[:, 0:nk, 0:w], src)

    def gate_up(self, wg, wu, groups):
        b = self.b
        nblk = (DFF + 255) // 256
        blocks = [(i * 256, min(256, DFF - i * 256)) for i in range(nblk)]

        def issue(i):
            c0, w = blocks[i]
            self.load_w(self.wA[i % 2], wg, c0, w, KC)
            self.load_w(self.wB[i % 2], wu, c0, w, KC)
        issue(0)
        for i, (c0, w) in enumerate(blocks):
            if i + 1 < nblk:
                issue(i + 1)
            wa, wb = self.wA[i % 2], self.wB[i % 2]
            for fl in range(w // 128):
                fc = c0 // 128 + fl
                for (t0, n) in groups:
                    pg = self.psG[self.nG % 2]
                    pu = self.psU[self.nG % 2]
                    sg = self.sg[self.nG % 2]
                    self.nG += 1
                    for kc in range(KC):
                        b.mm(pg[:, 0:n], wa[:, kc, fl * 128:(fl + 1) * 128], self.hT[:, kc, t0:t0 + n],
                             start=(kc == 0), stop=(kc == KC - 1))
                    for kc in range(KC):
                        b.mm(pu[:, 0:n], wb[:, kc, fl * 128:(fl + 1) * 128], self.hT[:, kc, t0:t0 + n],
                             start=(kc == 0), stop=(kc == KC - 1))
                    b.act(sg[:, 0:n], pg[:, 0:n], AF.Silu)
                    b.tt(self.aT[:, fc, t0:t0 + n], sg[:, 0:n], pu[:, 0:n], ALU.mult)

    def down(self, wd, tiles):
        b = self.b
        nblk = D // 256

        def issue(i):
            src = wd.rearrange("(fc p) d -> p fc d", p=128)[:, :, i * 256:(i + 1) * 256]
            b.dma("pool", self.wD[i % 2][:], src)
        issue(0)
        for i in range(nblk):
            if i + 1 < nblk:
                issue(i + 1)
            w = self.wD[i % 2]
            for ti, (t0, rows) in enumerate(tiles):
                po = self.psO[self.nO % 2]
                self.nO += 1
                for fc in range(FC):
                    b.mm(po[0:rows, 0:256], self.aT[:, fc, t0:t0 + rows], w[:, fc, :], start=(fc == 0), stop=(fc == FC - 1))
                b.copy(self.ybuf[0:rows, ti, i * 256:(i + 1) * 256], po[0:rows, 0:256], eng="act")

    def proj(self, srcT, nk, w_dram, cols, tiles, sink):
        b = self.b
        c_lo, c_hi = cols
        nblk = (c_hi - c_lo + 255) // 256
        blocks = [(c_lo + i * 256, min(256, c_hi - c_lo - i * 256)) for i in range(nblk)]

        def issue(i):
            c0, w = blocks[i]
            self.load_w(self.wA[(self.nW + i) % 2], w_dram, c0, w, nk)
        issue(0)
        for i, (c0, w) in enumerate(blocks):
            if i + 1 < nblk:
                issue(i + 1)
            wt = self.wA[(self.nW + i) % 2]
            for ti, (t0, rows) in enumerate(tiles):
                po = self.psO[self.nO % 2]
                self.nO += 1
                for kc in range(nk):
                    b.mm(po[0:rows, 0:w], srcT[:, kc, t0:t0 + rows], wt[:, kc, 0:w], start=(kc == 0), stop=(kc == nk - 1))
                sink(ti, t0, rows, c0, w, po)
        self.nW += nblk


def half_tiles(h):
    if h == 0:
        return [(i * 128, 128, i * 128) for i in range(4)] + [(512, 64, T_P)]
    return [(i * 128, 128, 512 + i * 128) for i in range(4)]


def half_groups(h):
    return [(0, 512), (512, 64)] if h == 0 else [(0, 512)]


def build_l1(stages=('win', 'norm', 'gateup', 'down', 'resid', 'proj'), halves=(0, 1)):
    nc = bass.Bass("TRN2", target_bir_lowering=False)
    dt = lambda n, s, k="ExternalInput": nc.dram_tensor(n, s, F32, kind=k).ap()
    x = dt("x", [T_ALL, D])
    wg = dt("wg", [D, DFF]) if 'gateup' in stages else None
    wu = dt("wu", [D, DFF]) if 'gateup' in stages else None
    wd = dt("wd", [DFF, D]) if 'down' in stages else None
    win = dt("win", [D, NPROJ]) if 'proj' in stages else None
    gcols = dt("gcols", [128, 2 * KC])
    gpost = dt("gpost", [128, D])
    identd = dt("identd", [128, 128])
    cwin = dt("cwin", [16, 512, 256])
    x1o = dt("x1o", [T_ALL, D], "ExternalOutput")
    projo = dt("projo", [T_ALL, NPROJ], "ExternalOutput")
    wino = dt("wino", [16, 512, 256], "ExternalOutput")

    b = Bld(nc)
    dn = Dense(b, identd)
    gc = b.sb("gc", [128, 2 * KC], F32)
    gp = b.sb("gp", [128, D], F32)
    b.dma("sp", gc[:], gcols)
    b.dma("sp", gp[:], gpost)
    for bi in range(16 if 'win' in stages else 0):
        b.dma("act", wino[bi, 0:508, :], cwin[bi, 4:512, :])

    for h in halves:
        tiles = half_tiles(h)
        for (t0, rows, g0) in (tiles if 'norm' in stages else []):
            xt = dn.xt[dn.nX % 2]
            b.dma("sp", xt[0:rows, :], x[g0:g0 + rows, :])
            dn.norm_to_hT(xt[0:rows, :], rows, t0, gc[:, 0:KC])
        if 'gateup' in stages:
            dn.gate_up(wg, wu, half_groups(h))
        if 'down' in stages:
            dn.down(wd, [(t0, rows) for (t0, rows, g0) in tiles])
        for ti, (t0, rows, g0) in enumerate(tiles if 'resid' in stages else []):
            xt = dn.xt[dn.nX % 2]
            b.dma("sp", xt[0:rows, :], x[g0:g0 + rows, :])
            y = dn.ybuf[0:rows, ti, :]
            r = dn.rstd(y, rows, 1)
            tmp = dn.hn[(dn.nX + 1) % 2]
            b.stt(tmp[0:rows, :], y, r, gp[0:rows, :], ALU.mult, ALU.mult)
            b.stt(xt[0:rows, :], tmp[0:rows, :], 0.5, xt[0:rows, :], ALU.mult, ALU.add)
            b.dma("sp", x1o[g0:g0 + rows, :], xt[0:rows, :])
            dn.norm_to_hT(xt[0:rows, :], rows, t0, gc[:, KC:2 * KC])

        def sink(ti, t0, rows, c0, w, po, tiles=tiles):
            ev = dn.ev[dn.nE % 2]
            dn.nE += 1
            b.copy(ev[0:rows, 0:w], po[0:rows, 0:w], eng="act")
            g0 = tiles[ti][2]
            b.dma("sp", projo[g0:g0 + rows, c0:c0 + w], ev[0:rows, 0:w])
            if g0 == T_P and c0 == 5632:
                for bi in range(16):
                    b.dma("sp", wino[bi, 508:512, :], ev[bi * 4:(bi + 1) * 4, 0:256])
        if 'proj' in stages:
            dn.proj(dn.hT, KC, win, (0, NPROJ), [(t0, rows) for (t0, rows, g0) in tiles], sink)
    b.finish()
    return nc


def _bcast(v):
    return np.ascontiguousarray(np.broadcast_to(np.asarray(v, np.float32).reshape(1, -1), (128, v.size)))


def _cols(v):
    return np.ascontiguousarray(np.asarray(v, np.float32).reshape(-1, 128).T)


def run_l1(inp):
    nc = build_l1()
    xp = inp["x_prompt"][0]
    xs = inp["x_sample"].reshape(-1, D)
    ident = np.eye(128, dtype=np.float32)
    gcols = np.concatenate([_cols(inp["norm_pre1"][0]), _cols(inp["norm_pre2"][0])], axis=1)
    gpost = _bcast(inp["norm_post1"][0])
    win = np.ascontiguousarray(inp["w_in"][0][:, :NPROJ])
    maps = []
    for c in range(NCORES):
        maps.append({
            "x": np.ascontiguousarray(np.concatenate([xp[c * T_P:(c + 1) * T_P], xs[c * T_S:(c + 1) * T_S]], 0)),
            "wg": inp["ff1_gate"][0], "wu": inp["ff1_up"][0], "wd": inp["ff1_down"][0], "win": win,
            "gcols": gcols, "gpost": gpost, "identd": ident,
            "cwin": np.ascontiguousarray(inp["cache_win"][0, c * 16:(c + 1) * 16].reshape(16, 512, 256)),
        })
    res = run_bass_kernel_spmd(nc, maps, core_ids=list(range(NCORES)))
    return res.results


def kernel(**inp):
    inp = {k: np.asarray(v) for k, v in inp.items()}
    r1 = run_l1(inp)
    proj_p = np.concatenate([r["projo"][:T_P] for r in r1], 0)
    proj_s = np.concatenate([r["projo"][T_P:] for r in r1], 0)
    x1_p = np.concatenate([r["x1o"][:T_P] for r in r1], 0)
    x1_s = np.concatenate([r["x1o"][T_P:] for r in r1], 0)
    kv_prompt = proj_p[:, 5120:5632].reshape(1, 1, 8192, 4, 2, 64)
    kv_sample = proj_s[:, 5120:5632].reshape(1, 128, 4, 4, 2, 64)
    win_prompt = proj_p[8192 - 512:, 5632:5888].reshape(1, 1, 512, 2, 2, 64)
    win_sample = np.concatenate([r["wino"] for r in r1], 0).reshape(1, 128, 512, 2, 2, 64)
    ohg_p, ohg_s, hg_p, hg_s = run_l2h(inp, proj_p, proj_s)
    ons_p = run_l2n_prompt(inp, proj_p)
    ons_s = run_l2n_sample(inp, proj_s)
    y_p, y_s = run_l3(inp, x1_p, x1_s, ohg_p, ohg_s, ons_p, ons_s)
    y_prompt = y_p.reshape(1, 8192, D)
    y_sample = y_s.reshape(128, 4, D)
    return (y_prompt, y_sample, np.ascontiguousarray(kv_prompt), np.ascontiguousarray(kv_sample),
            np.ascontiguousarray(win_prompt), win_sample, hg_p, hg_s)


CH = 32
SEG = 2048


def build_l2h(do_prompt=True, do_sample=True):
    nc = bass.Bass("TRN2", target_bir_lowering=False)
    dt = lambda n, s, k="ExternalInput": nc.dram_tensor(n, s, F32, kind=k).ap()
    qT = dt("qT", [128, 8192])
    fT = dt("fT", [128, 8192])
    v32 = dt("v32", [32, 256, 128])
    g32 = dt("g32", [32, 256, 128])
    lbc = dt("lbc", [128, 2])
    qTs = dt("qTs", [128, 512])
    fTs = dt("fTs", [128, 512])
    v4 = dt("v4", [4, 128, 128])
    g4 = dt("g4", [4, 128, 128])
    lbs = dt("lbs", [128, 16])
    st0 = dt("st0", [16, 8, 128, 128])
    gn32 = dt("gn32", [32, 128])
    rmask = dt("rmask", [128, SEG])
    rmask4 = dt("rmask4", [128, 512])
    trid = dt("trid", [32, 32])
    identd = dt("identd", [128, 128])
    o32 = dt("o32", [32, 256, 128], "ExternalOutput")
    Sp = dt("Sp", [128, 128], "ExternalOutput")
    o4 = dt("o4", [4, 128, 128], "ExternalOutput")
    Ss = dt("Ss", [16, 8, 128, 128], "ExternalOutput")

    b = Bld(nc)
    identf = b.sb("identf", [128, 128], F32)
    ident = b.sb("ident", [128, 128], BF16)
    b.dma("sp", identf[:], identd)
    b.copy(ident[:], identf[:])
    tri = b.sb("tri", [32, 32], F32)
    b.dma("sp", tri[:], trid)
    gn = b.sb("gn", [32, 128], F32)
    b.dma("sp", gn[:], gn32)
    rm = b.sb("rm", [128, SEG], F32)
    b.dma("sp", rm[:], rmask)
    rm4 = b.sb("rm4", [128, 512], F32)
    b.dma("sp", rm4[:], rmask4)
    lbr = b.sb("lbr", [128, 16], F32)
    lbv = b.sb("lbv", [128, 8], F32)
    oml = b.sb("oml", [128, 8], F32)

    qr = b.sb("qr", [128, SEG], F32)
    fr = b.sb("fr", [128, SEG], F32)
    bc = b.sb("bc", [128, SEG], F32)
    kk = b.sb("kk", [128, SEG], F32)
    t1 = b.sb("t1", [128, SEG], F32)
    qb = b.sb("qb", [128, SEG], BF16)
    kb = b.sb("kb", [128, SEG], BF16)
    kd = b.sb("kd", [128, SEG], BF16)
    ebl = b.sb("ebl", [128, 128], F32)
    vs = b.sb("vs", [32, 64, 128], BF16)
    gs = b.sb("gs", [32, 64, 128], F32)
    oa = b.sb("oa", [32, 64, 128], F32)
    sq = b.sb("sq", [32, 64, 128], F32)
    rs = b.sb("rs", [32, 128], F32)
    NSB = 8
    S = [b.sb("S%d" % i, [128, 128], F32) for i in range(NSB)]
    Sbf = [b.sb("Sbf%d" % i, [128, 128], BF16) for i in range(NSB)]
    atm = [b.sb("atm%d" % i, [32, 32], BF16) for i in range(3)]
    kdT = [b.sb("kdT%d" % i, [32, 128], BF16) for i in range(3)]
    psA = [b.ps("psA%d" % i, [128, 512]) for i in range(2)]
    psK = [b.ps("psK%d" % i, [128, 1024], BF16) for i in range(2)]
    psO = [b.ps("psO%d" % i, [128, 512]) for i in range(2)]
    psS = [b.ps("psS%d" % i, [128, 512]) for i in range(2)]
    cnt = [0]

    def prep(q_src, f_src, n, nh, C, rmk):
        per = n // nh
        nch = n // C
        v3 = lambda t: t[:, 0:n].rearrange("p (h m) -> p h m", h=nh)
        lb_bc = lbv[:, 0:nh].unsqueeze(2).broadcast_to([128, nh, per])
        oml_bc = oml[:, 0:nh].unsqueeze(2).broadcast_to([128, nh, per])
        b.dma("sp", qr[:, 0:n], q_src)
        b.dma("act", fr[:, 0:n], f_src)
        b.act(t1[:, 0:n], fr[:, 0:n], AF.Sigmoid)
        b.tt(v3(t1), v3(t1), oml_bc, ALU.mult)
        b.tt(v3(fr), v3(t1), lb_bc, ALU.add)
        b.ts(kk[:, 0:n], fr[:, 0:n], -1.0, ALU.mult, 1.0, ALU.add)
        b.act(t1[:, 0:n], fr[:, 0:n], AF.Ln)
        b.P.op("dve", lambda e: e.tensor_tensor_scan(out=bc[:, 0:n], data0=rmk[:, 0:n], data1=t1[:, 0:n],
                                                     initial=0.0, op0=ALU.mult, op1=ALU.add),
               [rmk[:, 0:n], t1[:, 0:n]], [bc[:, 0:n]])
        b.act(fr[:, 0:n], qr[:, 0:n], AF.Silu)
        b.act(t1[:, 0:n], bc[:, 0:n], AF.Exp)
        b.tt(qb[:, 0:n], fr[:, 0:n], t1[:, 0:n], ALU.mult)
        b.act(t1[:, 0:n], bc[:, 0:n], AF.Exp, scale=-1.0)
        b.tt(kb[:, 0:n], kk[:, 0:n], t1[:, 0:n], ALU.mult)
        bc3 = bc[:, 0:n].rearrange("p (c m) -> p c m", m=C)
        bl_bc = bc3[:, :, C - 1:C].broadcast_to([128, nch, C])
        b.tt(t1[:, 0:n].rearrange("p (c m) -> p c m", m=C), bl_bc, bc3, ALU.subtract)
        b.act(t1[:, 0:n], t1[:, 0:n], AF.Exp)
        b.tt(kd[:, 0:n], kk[:, 0:n], t1[:, 0:n], ALU.mult)
        b.act(ebl[:, 0:nch], bc3[:, :, C - 1], AF.Exp)

    def chunk_front(ci, C, cg=None):
        k = cnt[0] % 2
        k3 = cnt[0] % 3
        cnt[0] += 1
        cg = ci if cg is None else cg
        cs = slice(cg * C, (cg + 1) * C)
        pa, pk = psA[k], psK[k]
        b.mm(pa[0:C, 0:C], kb[:, cs], qb[:, cs])
        b.tt(atm[k3][0:C, 0:C], pa[0:C, 0:C], tri[0:C, 0:C], ALU.mult)
        b.tr(pk[0:C, 0:128], kd[:, cs], ident[:, :])
        b.copy(kdT[k3][0:C, :], pk[0:C, 0:128], eng="act")
        return (k, k3, ci, cg, cs, C)

    def chunk_back(tok, Sin, Sout, Sbx):
        k, k3, ci, cg, cs, C = tok
        po, pS = psO[k], psS[k]
        b.mm(pS[:, 0:128], kdT[k3][0:C, :], vs[0:C, ci, :])
        b.stt(Sout[:], Sin[:], ebl[:, cg:cg + 1], pS[:, 0:128], ALU.mult, ALU.add)
        b.mm(po[0:C, 0:128], atm[k3][0:C, 0:C], vs[0:C, ci, :], start=True, stop=False)
        b.mm(po[0:C, 0:128], qb[:, cs], Sbx[:], start=False, stop=True)
        b.copy(oa[0:C, ci, :], po[0:C, 0:128], eng="act")

    def post(C, nch, g_src, o_dst):
        b.dma("sp", gs[0:C, 0:nch, :], g_src)
        b.tt(sq[0:C, 0:nch, :], oa[0:C, 0:nch, :], oa[0:C, 0:nch, :], ALU.mult)
        b.P.op("dve", lambda e: e.reduce_sum(out=rs[0:C, 0:nch], in_=sq[0:C, 0:nch, :], axis=AX.X),
               [sq[0:C, 0:nch, :]], [rs[0:C, 0:nch]])
        b.act(rs[0:C, 0:nch], rs[0:C, 0:nch], AF.Sqrt, bias=EPS, scale=1.0 / 128)
        b.recip(rs[0:C, 0:nch], rs[0:C, 0:nch])
        b.tt(oa[0:C, 0:nch, :], oa[0:C, 0:nch, :], rs[0:C, 0:nch].unsqueeze(2).broadcast_to([C, nch, 128]), ALU.mult)
        b.tt(oa[0:C, 0:nch, :], oa[0:C, 0:nch, :], gn[0:C, :].unsqueeze(1).broadcast_to([C, nch, 128]), ALU.mult)
        b.act(gs[0:C, 0:nch, :], gs[0:C, 0:nch, :], AF.Silu)
        b.tt(oa[0:C, 0:nch, :], oa[0:C, 0:nch, :], gs[0:C, 0:nch, :], ALU.mult)
        b.dma("sp", o_dst, oa[0:C, 0:nch, :])

    def lower_bounds(src, nh):
        b.dma("sp", lbr[:, 0:2 * nh], src)
        l3 = lbr[:, 0:2 * nh].rearrange("p (h r) -> p h r", r=2)
        b.tt(lbv[:, 0:nh], l3[:, :, 0], l3[:, :, 1], ALU.subtract)
        b.act(lbv[:, 0:nh], lbv[:, 0:nh], AF.Sigmoid)
        b.ts(oml[:, 0:nh], lbv[:, 0:nh], -1.0, ALU.mult, 1.0, ALU.add)

    if do_prompt:
        lower_bounds(lbc, 1)
        b.memset(S[0][:], 0.0)
        b.memset(Sbf[0][:], 0.0)
        nseg = 8192 // SEG
        cps = SEG // CH
        for sg_ in range(nseg):
            prep(qT[:, sg_ * SEG:(sg_ + 1) * SEG], fT[:, sg_ * SEG:(sg_ + 1) * SEG], SEG, 1, CH, rm)
            b.dma("pool", vs[:, 0:cps, :], v32[:, sg_ * cps:(sg_ + 1) * cps, :])
            tok = chunk_front(0, CH)
            for ci in range(cps):
                gi = sg_ * cps + ci
                nxt = chunk_front(ci + 1, CH) if ci + 1 < cps else None
                chunk_back(tok, S[gi % 2], S[(gi + 1) % 2], Sbf[gi % 3])
                b.copy(Sbf[(gi + 1) % 3][:], S[(gi + 1) % 2][:], eng="act")
                tok = nxt
            post(CH, cps, g32[:, sg_ * cps:(sg_ + 1) * cps, :], o32[:, sg_ * cps:(sg_ + 1) * cps, :])
        b.dma("sp", Sp, S[(8192 // CH) % 2][:])
    if do_sample:
        lower_bounds(lbs, 8)
        prep(qTs, fTs, 512, 8, 4, rm4)
        for grp in range(2):
            b.dma("pool", vs[0:4, 0:64, :], v4[:, grp * 64:(grp + 1) * 64, :])
            PF = 5
            for jl in range(min(PF, 64)):
                j = grp * 64 + jl
                b.dma("sp", S[j % NSB][:], st0[j % 16, j // 16])
            tok = chunk_front(0, 4, cg=grp * 64)
            for jl in range(64):
                j = grp * 64 + jl
                h_, b_ = j // 16, j % 16
                Sx, Sbx = S[j % NSB], Sbf[j % NSB]
                if jl + PF < 64:
                    jn = j + PF
                    b.dma("sp", S[jn % NSB][:], st0[jn % 16, jn // 16])
                b.copy(Sbx[:], Sx[:], eng="act")
                nxt = chunk_front(jl + 1, 4, cg=j + 1) if jl + 1 < 64 else None
                chunk_back(tok, Sx, Sx, Sbx)
                b.dma("act", Ss[b_, h_], Sx[:])
                tok = nxt
            post(4, 64, g4[:, grp * 64:(grp + 1) * 64, :], o4[:, grp * 64:(grp + 1) * 64, :])
    b.finish()
    return nc


def l2h_consts():
    rmask = np.ones((128, SEG), np.float32)
    rmask[:, ::CH] = 0.0
    rmask4 = np.ones((128, 512), np.float32)
    rmask4[:, ::4] = 0.0
    tri = np.triu(np.ones((32, 32), np.float32))
    return {"rmask": rmask, "rmask4": rmask4, "trid": tri, "identd": np.eye(128, dtype=np.float32)}


def run_l2h(inp, proj_p, proj_s):
    nc = build_l2h()
    cst = l2h_consts()
    gn32 = _bcast(inp["hg_gnorm"][0])[:32]
    hg_lb = inp["hg_lb"]
    maps = []
    ps4 = proj_s.reshape(128, 4, NPROJ)
    for c in range(NCORES):
        hs = slice(c * 128, (c + 1) * 128)
        m = dict(cst)
        m["gn32"] = np.ascontiguousarray(gn32)
        m["qT"] = np.ascontiguousarray(proj_p[:, 0 * 1024:][:, hs].T)
        m["fT"] = np.ascontiguousarray(proj_p[:, 1 * 1024:][:, hs].T)
        m["v32"] = np.ascontiguousarray(proj_p[:, 2 * 1024:][:, hs].reshape(256, 32, 128).transpose(1, 0, 2))
        m["g32"] = np.ascontiguousarray(proj_p[:, 3 * 1024:][:, hs].reshape(256, 32, 128).transpose(1, 0, 2))
        m["lbc"] = np.ascontiguousarray(hg_lb[:, hs].T)
        sb = ps4[c * 16:(c + 1) * 16]
        part = lambda k: sb[:, :, k * 1024:(k + 1) * 1024].reshape(16, 4, 8, 128)
        m["qTs"] = np.ascontiguousarray(part(0).transpose(3, 2, 0, 1).reshape(128, 512))
        m["fTs"] = np.ascontiguousarray(part(1).transpose(3, 2, 0, 1).reshape(128, 512))
        m["v4"] = np.ascontiguousarray(part(2).transpose(1, 2, 0, 3).reshape(4, 128, 128))
        m["g4"] = np.ascontiguousarray(part(3).transpose(1, 2, 0, 3).reshape(4, 128, 128))
        m["lbs"] = np.ascontiguousarray(hg_lb.reshape(2, 8, 128).transpose(2, 1, 0).reshape(128, 16))
        m["st0"] = np.ascontiguousarray(inp["state_hgrn"][0, c * 16:(c + 1) * 16])
        maps.append(m)
    res = run_bass_kernel_spmd(nc, maps, core_ids=list(range(NCORES)))
    r = res.results
    o_p = np.concatenate([r[c]["o32"].transpose(1, 0, 2).reshape(8192, 128) for c in range(NCORES)], axis=1)
    hg_p = np.stack([r[c]["Sp"] for c in range(NCORES)], 0).reshape(1, 1, 8, 128, 128)
    o_s = np.concatenate([r[c]["o4"].reshape(4, 8, 16, 128).transpose(2, 0, 1, 3).reshape(16, 4, 1024)
                          for c in range(NCORES)], 0)
    hg_s = np.concatenate([r[c]["Ss"] for c in range(NCORES)], 0).reshape(1, 128, 8, 128, 128)
    return o_p, o_s.reshape(512, 1024), hg_p, hg_s


def build_l3():
    nc = bass.Bass("TRN2", target_bir_lowering=False)
    dt = lambda n, s, k="ExternalInput": nc.dram_tensor(n, s, F32, kind=k).ap()
    x1 = dt("x1", [T_ALL, D])
    ohgT = dt("ohgT", [1024, T_ALL])
    onsT = dt("onsT", [1024, T_ALL])
    wgab = dt("wgab", [D, 2 * D])
    wphg = dt("wphg", [1024, D])
    wpns = dt("wpns", [1024, D])
    wout = dt("wout", [D, D])
    wg = dt("wg", [D, DFF])
    wu = dt("wu", [D, DFF])
    wd = dt("wd", [DFF, D])
    gcols = dt("gcols", [128, 2 * KC])
    gpost2 = dt("gpost2", [128, D])
    gpost3 = dt("gpost3", [128, D])
    identd = dt("identd", [128, 128])
    yo = dt("yo", [T_ALL, D], "ExternalOutput")
    x2s = nc.dram_tensor("x2s", [T_ALL, D], F32).ap()

    b = Bld(nc)
    dn = Dense(b, identd)
    gc = b.sb("gc", [128, 2 * KC], F32)
    gp = b.sb("gp", [128, D], F32)
    b.dma("sp", gc[:], gcols)
    oT = [dn.aT[:, 0:8, :], dn.aT[:, 8:16, :]]
    sga = dn.sg

    for h in range(2):
        tiles = half_tiles(h)
        ntok = 576 if h == 0 else 512
        for (t0, rows, g0) in tiles:
            xt = dn.xt[dn.nX % 2]
            b.dma("sp", xt[0:rows, :], x1[g0:g0 + rows, :])
            dn.norm_to_hT(xt[0:rows, :], rows, t0, gc[:, 0:KC])
        for src, dst in ((ohgT, oT[0]), (onsT, oT[1])):
            s3 = src.rearrange("(kc p) t -> p kc t", p=128)
            if h == 0:
                b.dma("pool", dst[:, :, 0:512], s3[:, :, 0:512])
                b.dma("pool", dst[:, :, 512:576], s3[:, :, T_P:T_P + 64])
            else:
                b.dma("pool", dst[:, :, 0:512], s3[:, :, 512:1024])
        nblk = D // 256

        def issue(i):
            dn.load_w(dn.wA[i % 2], wgab, i * 256, 256, KC)
            dn.load_w(dn.wB[i % 2], wgab, D + i * 256, 256, KC)
            dn.load_w(dn.wD[i % 2][:, 0:8, :], wphg, i * 256, 256, 8)
            dn.load_w(dn.wD[i % 2][:, 8:16, :], wpns, i * 256, 256, 8)
        issue(0)
        for i in range(nblk):
            if i + 1 < nblk:
                issue(i + 1)
            wa, wb, wc = dn.wA[i % 2], dn.wB[i % 2], dn.wD[i % 2]
            for ti, (t0, rows, g0) in enumerate(tiles):
                k = dn.nG % 2
                dn.nG += 1
                pga, pgb, ph, pn = dn.psG[k], dn.psU[k], dn.psO[k], dn.psO[k][:, 256:512]
                for kc in range(KC):
                    b.mm(pga[0:rows, 0:256], dn.hT[:, kc, t0:t0 + rows], wa[:, kc, :], start=(kc == 0), stop=(kc == KC - 1))
                for kc in range(KC):
                    b.mm(pgb[0:rows, 0:256], dn.hT[:, kc, t0:t0 + rows], wb[:, kc, :], start=(kc == 0), stop=(kc == KC - 1))
                for kc in range(8):
                    b.mm(ph[0:rows, 0:256], oT[0][:, kc, t0:t0 + rows], wc[:, kc, :], start=(kc == 0), stop=(kc == 7))
                for kc in range(8):
                    b.mm(pn[0:rows, 0:256], oT[1][:, kc, t0:t0 + rows], wc[:, 8 + kc, :], start=(kc == 0), stop=(kc == 7))
                s1, s2 = dn.ev[0], dn.ev[1]
                b.act(s1[0:rows, :], pga[0:rows, 0:256], AF.Sigmoid)
                b.act(s2[0:rows, :], pgb[0:rows, 0:256], AF.Sigmoid)
                b.tt(s1[0:rows, :], s1[0:rows, :], ph[0:rows, 0:256], ALU.mult)
                b.tt(s2[0:rows, :], s2[0:rows, :], pn[0:rows, 0:256], ALU.mult)
                b.tt(dn.ybuf[0:rows, ti, i * 256:(i + 1) * 256], s1[0:rows, :], s2[0:rows, :], ALU.add)
        for ti, (t0, rows, g0) in enumerate(tiles):
            dn.to_T(dn.ybuf[:, ti, :], rows, dn.hT, t0)

        def sink(ti, t0, rows, c0, w, po):
            b.copy(dn.ybuf[0:rows, ti, c0:c0 + w], po[0:rows, 0:w], eng="act")
        dn.proj(dn.hT, KC, wout, (0, D), [(t0, rows) for (t0, rows, g0) in tiles], sink)
        b.dma("sp", gp[:], gpost2)
        for ti, (t0, rows, g0) in enumerate(tiles):
            xt = dn.xt[dn.nX % 2]
            b.dma("sp", xt[0:rows, :], x1[g0:g0 + rows, :])
            y = dn.ybuf[0:rows, ti, :]
            r = dn.rstd(y, rows, 1)
            tmp = dn.hn[(dn.nX + 1) % 2]
            b.stt(tmp[0:rows, :], y, r, gp[0:rows, :], ALU.mult, ALU.mult)
            b.tt(xt[0:rows, :], tmp[0:rows, :], xt[0:rows, :], ALU.add)
            b.dma("sp", x2s[g0:g0 + rows, :], xt[0:rows, :])
            dn.norm_to_hT(xt[0:rows, :], rows, t0, gc[:, KC:2 * KC])
        dn.gate_up(wg, wu, half_groups(h))
        dn.down(wd, [(t0, rows) for (t0, rows, g0) in tiles])
        b.dma("sp", gp[:], gpost3)
        for ti, (t0, rows, g0) in enumerate(tiles):
            xt = dn.xt[dn.nX % 2]
            dn.nX += 1
            b.dma("sp", xt[0:rows, :], x2s[g0:g0 + rows, :])
            y = dn.ybuf[0:rows, ti, :]
            r = dn.rstd(y, rows, 1)
            tmp = dn.hn[ti % 2]
            b.stt(tmp[0:rows, :], y, r, gp[0:rows, :], ALU.mult, ALU.mult)
            b.stt(xt[0:rows, :], tmp[0:rows, :], 0.5, xt[0:rows, :], ALU.mult, ALU.add)
            b.dma("sp", yo[g0:g0 + rows, :], xt[0:rows, :])
    b.finish()
    return nc


def run_l3(inp, x1_p, x1_s, ohg_p, ohg_s, ons_p, ons_s):
    nc = build_l3()
    gcols = np.concatenate([_cols(inp["norm_pre2"][0]), _cols(inp["norm_pre3"][0])], axis=1)
    wgab = np.ascontiguousarray(inp["w_in"][0][:, NPROJ:])
    base = {"wgab": wgab, "wphg": inp["w_proj_hg"][0], "wpns": inp["w_proj_nsa"][0], "wout": inp["w_out"][0],
            "wg": inp["ff2_gate"][0], "wu": inp["ff2_up"][0], "wd": inp["ff2_down"][0], "gcols": gcols,
            "gpost2": _bcast(inp["norm_post2"][0]), "gpost3": _bcast(inp["norm_post3"][0]),
            "identd": np.eye(128, dtype=np.float32)}
    maps = []
    for c in range(NCORES):
        m = dict(base)
        ps, ss = slice(c * T_P, (c + 1) * T_P), slice(c * T_S, (c + 1) * T_S)
        m["x1"] = np.ascontiguousarray(np.concatenate([x1_p[ps], x1_s[ss]], 0))
        m["ohgT"] = np.ascontiguousarray(np.concatenate([ohg_p[ps], ohg_s[ss]], 0).T)
        m["onsT"] = np.ascontiguousarray(np.concatenate([ons_p[ps], ons_s[ss]], 0).T)
        maps.append(m)
    res = run_bass_kernel_spmd(nc, maps, core_ids=list(range(NCORES)))
    y_p = np.concatenate([r["yo"][:T_P] for r in res.results], 0)
    y_s = np.concatenate([r["yo"][T_P:] for r in res.results], 0)
    return y_p, y_s


NEG = -32768.0
SLOPES = np.power(2.0, -8.0 * np.arange(1, 17) / 16).astype(np.float64)
GELU_C = 1.5957691216057308


def nsa_prompt_consts(core):
    tiles = [core + 8 * j for j in range(8)]
    p = np.arange(128)
    c = {}
    c["gmat"] = (np.arange(128)[:, None] == (np.arange(8192)[None, :] // 64)).astype(np.float32)
    cs = np.arange(512)[:, None] * 16
    ss = np.arange(128)[None, :] * 64
    ov = ((cs <= ss + 63) & (cs + 31 >= ss)).astype(np.float32)
    ov[511] = 0.0
    c["ovl"] = np.ascontiguousarray(ov.reshape(4, 128, 128).transpose(1, 0, 2))
    r = np.arange(72) - (7 - core)
    c["btab"] = np.ascontiguousarray((SLOPES[None, :, None] * (p[:, None, None] - 64 - 128 * r[None, None, :])).astype(np.float32).reshape(128, 16 * 72))
    cb = np.zeros((128, 8, 16, 4), np.float64)
    cm = np.zeros((128, 8, 2, 128), np.float32)
    keep = np.zeros((128, 8, 128), np.float32)
    add = np.zeros((128, 8, 128), np.float32)
    tt = np.arange(128)
    blk = np.arange(128)
    for j, i in enumerate(tiles):
        t0 = 128 * i
        for ct in range(4):
            cb[:, j, :, ct] = SLOPES[None, :] * (16 * (128 * ct + p[:, None]) + 31 - (t0 + 64))
        nct = i // 16 + 1
        for rr in range(2):
            ct = nct - 1 - rr
            if ct < 0:
                continue
            cpos = 16 * (128 * ct + p) + 31
            ok = (cpos[:, None] <= (t0 + tt)[None, :]) & ((128 * ct + p) < 511)[:, None]
            cm[:, j, rr, :] = np.where(ok, 0.0, NEG)
        qpos = t0 + tt
        qb = qpos // 64
        valid = blk[None, :] <= qb[:, None]
        f0 = blk[None, :] == 0
        f1 = blk[None, :] == qb[:, None]
        f2 = blk[None, :] == (qb[:, None] - 1)
        forced = f0 | f1 | f2
        keep[:, j, :] = (valid & ~forced).astype(np.float32)
        a = np.where(valid, 0.0, -1e30)
        a = np.where(f2, 1e4, a)
        a = np.where(f1, 2e4, a)
        a = np.where(f0, 3e4, a)
        add[:, j, :] = a
    c["cbias"] = np.ascontiguousarray(cb.astype(np.float32).reshape(128, 8 * 16 * 4))
    c["cmask"] = np.ascontiguousarray(cm.reshape(128, 8 * 2 * 128))
    c["keepm"] = np.ascontiguousarray(keep.reshape(128, 8 * 128))
    c["addm"] = np.ascontiguousarray(add.reshape(128, 8 * 128))
    causal = np.where(p[:, None] <= tt[None, :], 0.0, NEG).astype(np.float32)
    wlow = np.where(p[:, None] > tt[None, :], 0.0, NEG).astype(np.float32)
    zero = np.zeros((128, 128), np.float32)
    full = np.full((128, 128), NEG, np.float32)
    dms = [zero if q < core else (causal if q == core else full) for q in range(8)]
    dmw = []
    for q in range(12):
        if q < core or q > core + 4:
            dmw.append(full)
        elif q == core:
            dmw.append(wlow)
        elif q == core + 4:
            dmw.append(causal)
        else:
            dmw.append(zero)
    c["dms"] = np.ascontiguousarray(np.stack(dms, 1).reshape(128, 8 * 128))
    c["dmw"] = np.ascontiguousarray(np.stack(dmw, 1).reshape(128, 12 * 128))
    c["identd"] = np.eye(128, dtype=np.float32)
    import ml_dtypes
    bf = lambda x: np.asarray(x, np.float64).astype(ml_dtypes.bfloat16).astype(np.float64)
    tab = np.zeros((5, 72, 16), np.float64)
    rr = np.arange(72) - (7 - core)
    for h in range(16):
        a = 8.0 * SLOPES[h]
        a0 = bf(a)
        a1 = bf(a - a0)
        cc = -1024.0 * rr * SLOPES[h]
        c0 = bf(cc)
        c1 = bf(cc - c0)
        c2 = bf(cc - c0 - c1)
        tab[0, :, h] = a0
        tab[1, :, h] = a1
        tab[2, :, h] = c0
        tab[3, :, h] = c1
        tab[4, :, h] = c2
    c["btab5"] = np.ascontiguousarray(tab.astype(np.float32).reshape(5, 72 * 16))
    bl = np.ones((5, 128), np.float32)
    bl[0] = p - 64
    bl[1] = p - 64
    c["biasl"] = bl
    return c


class Nsa:
    def __init__(self, b, nc, dt):
        self.b = b
        identd = dt("identd", [128, 128])
        self.identf = b.sb("identf", [128, 128], F32)
        self.ident = b.sb("ident", [128, 128], BF16)
        b.dma("sp", self.identf[:], identd)
        b.copy(self.ident[:], self.identf[:])
        self.w1 = {}
        self.w2 = {}
        self.posT = {}
        for kind in ("k", "v"):
            w1d = dt("w1" + kind, [128, 32, 256])
            w2d = dt("w2" + kind, [128, 4, 128])
            pd = dt("pos" + kind, [128, 32])
            self.w1[kind] = b.sb("s_w1" + kind, [128, 32, 256], BF16)
            self.w2[kind] = b.sb("s_w2" + kind, [128, 4, 128], BF16)
            self.posT[kind] = b.sb("s_pos" + kind, [128, 32], BF16)
            b.dma("pool", self.w1[kind][:], w1d)
            b.dma("pool", self.w2[kind][:], w2d)
            b.dma("pool", self.posT[kind][:], pd)
        self.bcol = b.sb("bcol", [128, 4], F32)
        self.xs = [b.sb("xs%d" % i, [128, 2064], BF16) for i in range(2)]
        self.xsY = b.sb("xsY", [128, 16, 129], BF16)
        self.gh = [b.sb("gh%d" % i, [128, 128], BF16) for i in range(4)]
        self.tx = b.sb("tx", [128, 128], F32)
        self.tu = b.sb("tu", [128, 128], F32)
        self.psS = [b.ps("psS%d" % i, [128, 512]) for i in range(2)]
        self.psAcc = [b.ps("psAcc%d" % i, [128, 512]) for i in range(2)]
        self.psH = [b.ps("psH%d" % i, [128, 512]) for i in range(2)]
        self.psK2 = b.ps("psK2", [128, 512])
        self.psT = b.ps("psT", [128, 1024], BF16)
        self.PT = [b.sb("PT%d" % i, [128, 128], BF16) for i in range(3)]
        self.PT4 = None
        self.nS = 0
        self.nA = 0
        self.nH = 0
        self.nP = 0
        self.nX = 0
        self.bias_done = False

    def cmp_bias(self):
        b = self.b
        for ki, kind in enumerate(("k", "v")):
            for hc in range(2):
                ph = self.psH[self.nH % 2]
                self.nH += 1
                for l in range(32):
                    b.mm(ph[:, 0:1], self.w1[kind][0:64, l, hc * 128:(hc + 1) * 128], self.posT[kind][0:64, l:l + 1],
                         start=(l == 0), stop=(l == 31))
                b.copy(self.bcol[:, ki * 2 + hc:ki * 2 + hc + 1], ph[:, 0:1], eng="act")

    def compress_tile(self, kind, src_dram_cols, N, kdst=None, vdst=None, xs_ap=None):
        b = self.b
        ki = 0 if kind == "k" else 1
        L = 16 * (N - 1) + 32
        if xs_ap is not None:
            xs = xs_ap
        else:
            xs = self.xs[self.nX % 2]
            self.nX += 1
            b.dma("pool", xs[:, 0:L], src_dram_cols)
        xy = self.xsY
        nc_ = L // 16
        b.copy(xy[:, :, 0:nc_], xs[:, 0:L].rearrange("p (c r) -> p r c", r=16))
        for n in range(2):
            for hc in range(2):
                ph = self.psH[self.nH % 2]
                self.nH += 1
                for l in range(32):
                    b.mm(ph[:, 0:N], self.w1[kind][n * 64:(n + 1) * 64, l, hc * 128:(hc + 1) * 128],
                         xy[n * 64:(n + 1) * 64, l % 16, (l // 16):(l // 16) + N], start=(l == 0), stop=(l == 31))
                tx, tu, gh = self.tx, self.tu, self.gh[n * 2 + hc]
                b.act(tx[:, 0:N], ph[:, 0:N], AF.Identity, bias=self.bcol[:, ki * 2 + hc:ki * 2 + hc + 1])
                b.tt(tu[:, 0:N], tx[:, 0:N], tx[:, 0:N], ALU.mult)
                b.ts(tu[:, 0:N], tu[:, 0:N], 0.044715, ALU.mult, 1.0, ALU.add)
                b.tt(tu[:, 0:N], tu[:, 0:N], tx[:, 0:N], ALU.mult)
                b.act(tu[:, 0:N], tu[:, 0:N], AF.Sigmoid, scale=GELU_C)
                b.tt(gh[:, 0:N], tx[:, 0:N], tu[:, 0:N], ALU.mult)
        pk = self.psK2
        if kind == "k":
            for q in range(4):
                b.mm(pk[:, 0:N], self.w2[kind][:, q, :], self.gh[q][:, 0:N], start=(q == 0), stop=(q == 3))
            b.copy(kdst, pk[:, 0:N], eng="act")
        else:
            for q in range(4):
                b.mm(pk[0:N, 0:128], self.gh[q][:, 0:N], self.w2[kind][:, q, :], start=(q == 0), stop=(q == 3))
            b.copy(vdst[0], pk[0:N, 0:64], eng="act")
            b.copy(vdst[1], pk[0:N, 64:128], eng="act")

    def compress_group(self, kind, X4, kdst3=None, vdst=None):
        b = self.b
        ki = 0 if kind == "k" else 1
        NB, N = 4, 127
        W = NB * N
        for n in range(2):
            for hc in range(2):
                ph = self.psH[self.nH % 2]
                self.nH += 1
                po = ph[:, 0:W].rearrange("p (s c) -> p s c", s=NB)
                for l in range(32):
                    b.mm(po, self.w1[kind][n * 64:(n + 1) * 64, l, hc * 128:(hc + 1) * 128],
                         X4[n * 64:(n + 1) * 64, :, l % 16, (l // 16):(l // 16) + N], start=(l == 0), stop=(l == 31))
                tx, tu, gh = self.tx4, self.tu4, self.gh4[n * 2 + hc]
                b.act(tx[:, 0:W], ph[:, 0:W], AF.Identity, bias=self.bcol[:, ki * 2 + hc:ki * 2 + hc + 1])
                b.tt(tu[:, 0:W], tx[:, 0:W], tx[:, 0:W], ALU.mult)
                b.ts(tu[:, 0:W], tu[:, 0:W], 0.044715, ALU.mult, 1.0, ALU.add)
                b.tt(tu[:, 0:W], tu[:, 0:W], tx[:, 0:W], ALU.mult)
                b.act(tu[:, 0:W], tu[:, 0:W], AF.Sigmoid, scale=GELU_C)
                b.tt(gh[:, 0:W], tx[:, 0:W], tu[:, 0:W], ALU.mult)
        pk = self.psK2
        if kind == "k":
            for q in range(4):
                b.mm(pk[:, 0:W], self.w2[kind][:, q, :], self.gh4[q][:, 0:W], start=(q == 0), stop=(q == 3))
            b.copy(kdst3, pk[:, 0:W].rearrange("p (s c) -> p s c", s=NB), eng="act")
        else:
            for sq in range(NB):
                for q in range(4):
                    b.mm(pk[0:N, 0:128], self.gh4[q][:, sq * N:(sq + 1) * N], self.w2[kind][:, q, :], start=(q == 0), stop=(q == 3))
                b.copy(vdst[sq][0], pk[0:N, 0:64], eng="act")
                b.copy(vdst[sq][1], pk[0:N, 64:128])

    def branch(self, steps, qrhs, nq, ncols, scale=0.125):
        b = self.b
        pacc = self.psAcc[self.nA % 2]
        self.nA += 1
        ns = len(steps)
        pts = {}

        def front(si):
            st = steps[si]
            ps = self.psS[self.nS % 2]
            self.nS += 1
            ex = st.get("extra", [])
            rows = st.get("rows", 128)
            b.mm(ps[0:rows, 0:nq], st["k"], qrhs, start=True, stop=(len(ex) == 0))
            for ei, (l_, r_) in enumerate(ex):
                b.mm(ps[0:rows, 0:nq], l_, r_, start=False, stop=(ei == len(ex) - 1))
            pt = self.PT[self.nP % 3]
            self.nP += 1
            pts[si] = pt
            if st.get("bias") is not None:
                b.act(pt[0:rows, 0:nq], ps[0:rows, 0:nq], AF.Exp, bias=st["bias"], scale=scale)
            else:
                b.act(pt[0:rows, 0:nq], ps[0:rows, 0:nq], AF.Exp, scale=scale)

        def back(si):
            st = steps[si]
            rows = st.get("rows", 128)
            b.mm(pacc[0:nq, 0:ncols], pts[si][0:rows, 0:nq], st["v"], start=(si == 0), stop=(si == ns - 1))

        for si in range(ns + 1):
            if si < ns:
                front(si)
            if si >= 1:
                back(si - 1)
        return pacc


def branch4(ns, b, steps, qrhs, biasl):
    pacc = ns.psAcc[ns.nA % 2]
    ns.nA += 1
    nst = len(steps)
    v4 = lambda ap: ap.unsqueeze(1).broadcast_to([ap.shape[0], 4, 128])
    banks = [ns.psS[0], ns.psS[1], ns.psH[0], ns.psH[1]]
    pts = {}

    def front(si):
        st = steps[si]
        ps = banks[ns.nS % 4]
        ns.nS += 1
        po = ps[:, 0:512].rearrange("p (g t) -> p g t", g=4)
        b.mm(po, st["k"], qrhs, start=True, stop=False)
        for (l_, r_) in st["extra"]:
            b.mm(po, l_, v4(r_), start=False, stop=False)
        b.mm(po, biasl, st["brow"].unsqueeze(2).broadcast_to([5, 4, 128]), start=False, stop=True)
        pt = ns.PT4[ns.nP % len(ns.PT4)]
        ns.nP += 1
        pts[si] = pt
        b.act(pt[:, :], ps[:, 0:512], AF.Exp, scale=0.125)

    def back(si):
        st = steps[si]
        for hl in range(4):
            b.mm(pacc[:, hl * 65:(hl + 1) * 65], pts[si][:, hl * 128:(hl + 1) * 128], st["v"],
                 start=(si == 0 and hl == 0), stop=(si == nst - 1), skip=True)

    DEP = 2
    for si in range(nst + DEP):
        if si < nst:
            front(si)
        if si >= DEP:
            back(si - DEP)
    return pacc


def build_l2n():
    nc = bass.Bass("TRN2", target_bir_lowering=False)
    dt = lambda n, s, k="ExternalInput": nc.dram_tensor(n, s, F32, kind=k).ap()
    b = Bld(nc)
    ns = Nsa(b, nc, dt)
    qTd = dt("qT", [128, 8, 1024])
    gated = dt("gates", [128, 8, 48])
    KsTd = dt("KsT", [128, 8192])
    KwTd = dt("KwT", [128, 8192])
    KcTd = dt("KcT", [128, 8192])
    VcTd = dt("VcT", [128, 8192])
    Vsd = dt("Vs", [128, 64, 128])
    Vwd = dt("Vw", [128, 64, 128])
    gmatd = dt("gmat", [128, 8192])
    ovld = dt("ovl", [128, 4, 128])
    btabd = dt("btab", [128, 16 * 72])
    cbiasd = dt("cbias", [128, 512])
    cmaskd = dt("cmask", [128, 2048])
    keepd = dt("keepm", [128, 1024])
    addd = dt("addm", [128, 1024])
    dmsd = dt("dms", [128, 8 * 128])
    dmwd = dt("dmw", [128, 12 * 128])
    btab5d = dt("btab5", [5, 72 * 16])
    biasld = dt("biasl", [5, 128])
    onso = dt("ons", [128, 8, 1024], "ExternalOutput")

    sbt = b.sb
    KsT = sbt("s_KsT", [128, 8192], BF16)
    KwT = sbt("s_KwT", [128, 8192], BF16)
    Vs = sbt("Vsa", [128, 64, 2, 65], BF16)
    Vw = sbt("Vwa", [128, 64, 2, 65], BF16)
    G = sbt("G", [128, 8192], BF16)
    qT = sbt("qTb", [128, 8, 1024], BF16)
    KCT = sbt("KCT", [128, 512], BF16)
    VCO = sbt("VCO", [128, 4, 2, 193], BF16)
    btab = sbt("s_btab", [128, 16 * 72], F32)
    cbias = sbt("s_cbias", [128, 512], F32)
    cmask = sbt("s_cmask", [128, 2048], BF16)
    keepm = sbt("s_keepm", [128, 1024], F32)
    addm = sbt("s_addm", [128, 1024], F32)
    dms = sbt("s_dms", [128, 8, 128], BF16)
    dmw = sbt("s_dmw", [128, 12, 128], BF16)
    btab5 = sbt("s_btab5", [5, 72, 16], BF16)
    biasl = sbt("s_biasl", [5, 128], BF16)
    ns.PT4 = [sbt("PT4_%d" % i, [128, 512], BF16) for i in range(5)]
    b.dma("pool", btab5[:], btab5d.rearrange("k (r h) -> k r h", r=72))
    b.dma("pool", biasl[:], biasld)
    gts = sbt("gts", [128, 8, 48], F32)
    sc = sbt("sc", [128, 2, 128], F32)
    s2 = sbt("s2", [128, 128], F32)
    s3 = sbt("s3", [128, 128], F32)
    m8 = sbt("m8", [128, 16], F32)
    nm = sbt("nm", [128, 128], BF16)
    nmT = sbt("nmT", [128, 2, 128], BF16)
    rd = sbt("rd", [128, 8], F32)
    oacc = [sbt("oacc%d" % i, [128, 1024], F32) for i in range(2)]

    for d_, s_ in ((KsT, KsTd), (KwT, KwTd), (G, gmatd)):
        for q in range(4):
            b.dma("pool", d_[:, q * 2048:(q + 1) * 2048], s_[:, q * 2048:(q + 1) * 2048])
    b.memset(Vs[:], 1.0)
    b.memset(Vw[:], 1.0)
    b.memset(VCO[:], 0.0)
    b.memset(KCT[:], 0.0)
    for d_, s_ in ((Vs, Vsd), (Vw, Vwd)):
        for q in range(4):
            b.dma("pool", d_[:, q * 16:(q + 1) * 16, :, 0:64], s_[:, q * 16:(q + 1) * 16, :].rearrange("p k (n d) -> p k n d", n=2))
    b.dma("pool", qT[:], qTd)
    b.dma("sp", btab[:], btabd)
    b.dma("sp", cbias[:], cbiasd)
    b.dma("pool", cmask[:], cmaskd)
    b.dma("sp", keepm[:], keepd)
    b.dma("sp", addm[:], addd)
    b.dma("pool", dms[:], dmsd.rearrange("p (q t) -> p q t", q=8))
    b.dma("pool", dmw[:], dmwd.rearrange("p (q t) -> p q t", q=12))
    b.dma("sp", gts[:], gated)
    b.act(gts[:], gts[:], AF.Sigmoid)

    ns.cmp_bias()
    b.memset(VCO[:, :, :, 64:65], 1.0)
    for n in range(2):
        b.dma("pool", VCO[:, :, n, 65:193], ovld)
    for ct in range(4):
        N = 128 if ct < 3 else 127
        L = 16 * (N - 1) + 32
        ns.compress_tile("k", KcTd[:, ct * 2048:ct * 2048 + L], N, kdst=KCT[:, ct * 128:ct * 128 + N])
        ns.compress_tile("v", VcTd[:, ct * 2048:ct * 2048 + L], N,
                         vdst=[VCO[0:N, ct, 0, 0:64], VCO[0:N, ct, 1, 0:64]])

    for j in range(8):
        oa = oacc[j % 2]
        qs = slice(j * 128, (j + 1) * 128)
        nct = j // 2 + 1
        for n in range(2):
            pb = slice(n * 64, (n + 1) * 64)
            b.memset(sc[:, n, :], 0.0, eng="dve")
            for g in range(8):
                h = n * 8 + g
                steps = []
                for ct in range(nct):
                    st = {"k": KCT[pb, ct * 128:(ct + 1) * 128], "v": VCO[:, ct, n, :],
                          "bias": cbias[:, (j * 16 + h) * 4 + ct:(j * 16 + h) * 4 + ct + 1]}
                    rr = nct - 1 - ct
                    if rr < 2:
                        st["extra"] = [(ns.ident[:, :], cmask[:, (j * 2 + rr) * 128:(j * 2 + rr + 1) * 128])]
                    steps.append(st)
                pc = ns.branch(steps, qT[pb, g, qs], 128, 193)
                b.ts(rd[:, 0:1], pc[:, 64:65], 1e-30, ALU.max)
                b.recip(rd[:, 0:1], rd[:, 0:1])
                b.stt(sc[:, n, :], pc[:, 65:193], rd[:, 0:1], sc[:, n, :], ALU.mult, ALU.add)
                b.tt(rd[:, 1:2], rd[:, 0:1], gts[:, j, h * 3:h * 3 + 1], ALU.mult)
                b.ts(oa[:, h * 64:(h + 1) * 64], pc[:, 0:64], rd[:, 1:2], ALU.mult)
            b.tt(s2[:], sc[:, n, :], keepm[:, j * 128:(j + 1) * 128], ALU.mult)
            b.tt(s2[:], s2[:], addm[:, j * 128:(j + 1) * 128], ALU.add)
            b.P.op("dve", lambda e: e.max(out=m8[:, 0:8], in_=s2[:]), [s2[:]], [m8[:, 0:8]])
            b.P.op("dve", lambda e: e.match_replace(out=s3[:], in_to_replace=m8[:, 0:8], in_values=s2[:], imm_value=-1e30),
                   [s2[:], m8[:, 0:8]], [s3[:]])
            b.P.op("dve", lambda e: e.max(out=m8[:, 8:16], in_=s3[:]), [s3[:]], [m8[:, 8:16]])
            b.ts(s3[:], s2[:], m8[:, 15:16], ALU.is_ge)
            b.ts(nm[:], s3[:], -NEG, ALU.mult, NEG, ALU.add)
            b.tr(ns.psT[:, 0:128], nm[:], ns.ident[:, :])
            b.copy(nmT[:, n, :], ns.psT[:, 0:128])
        for hg in range(4):
            n, g0 = hg // 2, (hg % 2) * 4
            h0 = hg * 4
            pb = slice(n * 64, (n + 1) * 64)
            qr = qT[pb, g0:g0 + 4, qs]
            steps = []
            for kt in range(8 * j + 8):
                ex = [(G[:, kt * 128:(kt + 1) * 128], nmT[:, n, :])]
                if kt >= 8 * j:
                    ex.append((ns.ident[:, :], dms[:, kt - 8 * j, :]))
                rp = 8 * j + 7 - kt
                steps.append({"k": KsT[pb, kt * 128:(kt + 1) * 128], "v": Vs[:, kt, n, :], "extra": ex,
                              "brow": btab5[:, rp, h0:h0 + 4]})
            pc = branch4(ns, b, steps, qr, biasl[:, :])
            for hl in range(4):
                h = h0 + hl
                den = pc[:, hl * 65 + 64:hl * 65 + 65]
                b.ts(rd[:, 2:3], den, 1e-30, ALU.max)
                b.recip(rd[:, 2:3], rd[:, 2:3])
                b.tt(rd[:, 3:4], rd[:, 2:3], gts[:, j, h * 3 + 1:h * 3 + 2], ALU.mult)
                b.stt(oa[:, h * 64:(h + 1) * 64], pc[:, hl * 65:hl * 65 + 64], rd[:, 3:4], oa[:, h * 64:(h + 1) * 64], ALU.mult, ALU.add)
            steps = []
            for q in range(12):
                kt = 8 * j - 4 + q
                if kt < 0:
                    continue
                steps.append({"k": KwT[pb, kt * 128:(kt + 1) * 128], "v": Vw[:, kt, n, :],
                              "extra": [(ns.ident[:, :], dmw[:, q, :])], "brow": btab5[:, 11 - q, h0:h0 + 4]})
            pc = branch4(ns, b, steps, qr, biasl[:, :])
            for hl in range(4):
                h = h0 + hl
                den = pc[:, hl * 65 + 64:hl * 65 + 65]
                b.ts(rd[:, 4:5], den, 1e-30, ALU.max)
                b.recip(rd[:, 4:5], rd[:, 4:5])
                b.tt(rd[:, 5:6], rd[:, 4:5], gts[:, j, h * 3 + 2:h * 3 + 3], ALU.mult)
                b.stt(oa[:, h * 64:(h + 1) * 64], pc[:, hl * 65:hl * 65 + 64], rd[:, 5:6], oa[:, h * 64:(h + 1) * 64], ALU.mult, ALU.add)
        b.dma("sp", onso[:, j, :], oa[:])
    b.finish()
    return nc


def nsa_cmp_weights(inp):
    m = {}
    for kind in ("k", "v"):
        w1 = inp["cmp_w1_" + kind][0].reshape(32, 64, 256).transpose(1, 0, 2)
        m["w1" + kind] = np.ascontiguousarray(np.concatenate([w1, w1], 0))
        w2 = inp["cmp_w2_" + kind][0].reshape(2, 128, 64)
        w2p = np.zeros((128, 4, 128), np.float32)
        for n in range(2):
            for hc in range(2):
                w2p[:, n * 2 + hc, n * 64:(n + 1) * 64] = w2[hc]
        m["w2" + kind] = w2p
        pT = inp["cmp_pos_" + kind][0].T
        m["pos" + kind] = np.ascontiguousarray(np.concatenate([pT, pT], 0))
    return m


def run_l2n_prompt(inp, proj_p):
    q = proj_p[:, 4096:5120]
    kv = proj_p[:, 5120:5888].reshape(8192, 6, 128)
    gates = proj_p[:, 5888:5936]
    cw = nsa_cmp_weights(inp)
    T = lambda a: np.ascontiguousarray(a.T)
    tok = lambda a: np.ascontiguousarray(a.reshape(64, 128, 128).transpose(1, 0, 2))
    shared = {"KcT": T(kv[:, 0]), "VcT": T(kv[:, 1]), "KsT": T(kv[:, 2]), "Vs": tok(kv[:, 3]),
              "KwT": T(kv[:, 4]), "Vw": tok(kv[:, 5])}
    shared.update(cw)
    outs = []
    ncs = []
    maps = []
    for c in range(NCORES):
        tiles = [c + 8 * j for j in range(8)]
        m = dict(shared)
        m.update(nsa_prompt_consts(c))
        qc = np.stack([q[128 * i:128 * (i + 1)] for i in tiles], 0)
        qr = qc.reshape(8, 128, 2, 8, 64).transpose(2, 4, 3, 0, 1).reshape(128, 8, 1024)
        m["qT"] = np.ascontiguousarray(qr)
        m["gates"] = np.ascontiguousarray(np.stack([gates[128 * i:128 * (i + 1)] for i in tiles], 1))
        maps.append(m)
    nc = build_l2n()
    res = run_bass_kernel_spmd(nc, maps, core_ids=list(range(NCORES)))
    o = np.zeros((8192, 1024), np.float32)
    for c in range(NCORES):
        r = res.results[c]["ons"]
        for j in range(8):
            i = c + 8 * j
            o[128 * i:128 * (i + 1)] = r[:, j, :]
    return o


U32 = mybir.dt.uint32
PAST = 2048
SCUT = None
NEGF = -30000.0


def nsa_sample_consts():
    c = {}
    p = np.arange(128)
    c["gs"] = (np.arange(128)[:, None] == (np.arange(17 * 128)[None, :] // 64)).astype(np.float32)
    cs = np.arange(128)[:, None] * 16
    ss = np.arange(64)[None, :] * 64
    ov = ((cs <= ss + 63) & (cs + 31 >= ss)).astype(np.float32)
    ov[127] = 0.0
    ov[:, 33:] = 0.0
    c["ovs"] = ov
    t = np.arange(4)
    bc = np.zeros((128, 2, 8, 4), np.float64)
    bs = np.zeros((128, 17, 2, 8, 4), np.float64)
    bw = np.zeros((128, 5, 2, 8, 4), np.float64)
    for n in range(2):
        for g in range(8):
            sl = SLOPES[n * 8 + g]
            qpos = PAST + t
            cpos = 16 * p + 31
            bc[:, n, g, :] = np.where((p < 127)[:, None], -sl * (qpos[None, :] - cpos[:, None]), NEGF)
            for tile in range(17):
                spos = 128 * tile + p
                dist = qpos[None, :] - spos[:, None]
                bs[:, tile, n, g, :] = np.where(dist >= 0, -sl * dist, NEGF)
            for tile in range(5):
                wpos = PAST - 512 + 128 * tile + p
                dist = qpos[None, :] - wpos[:, None]
                ok = (dist >= 0) & (dist < 512)
                if tile == 4:
                    ok &= (p < 4)[:, None]
                bw[:, tile, n, g, :] = np.where(ok, -sl * dist, NEGF)
    c["biasc"] = np.ascontiguousarray(bc.astype(np.float32).reshape(128, 64))
    c["biass"] = np.ascontiguousarray(bs.astype(np.float32).reshape(128, 17 * 64))
    c["biasw"] = np.ascontiguousarray(bw.astype(np.float32).reshape(128, 5 * 64))
    blk = np.arange(64)
    forced0, forced1, forced2 = blk == 0, blk == 32, blk == 31
    valid = blk <= 32
    keep = (valid & ~(forced0 | forced1 | forced2)).astype(np.float32)
    a = np.where(valid, 0.0, -1e30)
    a = np.where(forced2, 1e4, a)
    a = np.where(forced1, 2e4, a)
    a = np.where(forced0, 3e4, a)
    c["keeps"] = np.ascontiguousarray(np.broadcast_to(keep[None, :], (4, 64)).astype(np.float32))
    c["adds"] = np.ascontiguousarray(np.broadcast_to(a[None, :], (4, 64)).astype(np.float32))
    sel = np.zeros((32, 4), np.float32)
    for g in range(8):
        for tt in range(4):
            sel[g * 4 + tt, tt] = 1.0
    c["selm"] = sel
    c["pcol"] = p.astype(np.float32).reshape(128, 1)
    c["identd"] = np.eye(128, dtype=np.float32)
    return c


def build_l2s(n_pool=2560, nb=16):
    nc = bass.Bass("TRN2", target_bir_lowering=False)
    dt = lambda n, s, k="ExternalInput": nc.dram_tensor(n, s, F32, kind=k).ap()
    b = Bld(nc)
    ns = Nsa(b, nc, dt)
    cache = dt("cache", [n_pool * 128, 512])
    ptab = nc.dram_tensor("ptab", [1, nb * 16], I32, kind="ExternalInput").ap()
    cwin = dt("cwin", [nb, 512, 256])
    qTd = dt("qTs", [128, nb, 32])
    ksnd = dt("ksn", [128, nb, 4])
    kwnd = dt("kwn", [128, nb, 4])
    vsnd = dt("vsn", [4, nb, 128])
    vwnd = dt("vwn", [4, nb, 128])
    gtd = dt("gts", [32, nb, 2, 3])
    gsd = dt("gs", [128, 17 * 128])
    ovsd = dt("ovs", [128, 64])
    bcd = dt("biasc", [128, 64])
    bsd = dt("biass", [128, 17 * 64])
    bwd = dt("biasw", [128, 5 * 64])
    keepd = dt("keeps", [4, 64])
    addd = dt("adds", [4, 64])
    seld = dt("selm", [32, 4])
    pcold = dt("pcol", [128, 1])
    onso = dt("ons", [nb, 2, 32, 64], "ExternalOutput")

    sbt = b.sb
    Gs = sbt("s_gs", [128, 17 * 128], BF16)
    b.dma("pool", Gs[:], gsd)
    biasc = sbt("s_bc", [128, 2, 32], F32)
    biass = sbt("s_bs", [128, 17, 2, 32], F32)
    biasw = sbt("s_bw", [128, 5, 2, 32], F32)
    b.dma("sp", biasc[:], bcd.rearrange("p (n q) -> p n q", n=2))
    b.dma("sp", biass[:], bsd.rearrange("p (k n q) -> p k n q", k=17, n=2))
    b.dma("sp", biasw[:], bwd.rearrange("p (k n q) -> p k n q", k=5, n=2))
    keeps = sbt("s_keep", [4, 64], F32)
    adds = sbt("s_add", [4, 64], F32)
    selm = sbt("s_sel", [32, 4], F32)
    pcol = sbt("s_pcol", [128, 1], F32)
    b.dma("sp", keeps[:], keepd)
    b.dma("sp", adds[:], addd)
    b.dma("sp", selm[:], seld)
    b.dma("sp", pcol[:], pcold)
    qT = sbt("s_qT", [128, nb, 32], BF16)
    ksn = sbt("s_ksn", [128, nb, 4], BF16)
    kwn = sbt("s_kwn", [128, nb, 4], BF16)
    vsn = sbt("s_vsn", [4, nb, 128], BF16)
    vwn = sbt("s_vwn", [4, nb, 128], BF16)
    gts = sbt("s_gts", [32, nb, 2, 3], F32)
    b.dma("pool", qT[:], qTd)
    b.dma("pool", ksn[:], ksnd)
    b.dma("pool", kwn[:], kwnd)
    b.dma("pool", vsn[:], vsnd)
    b.dma("pool", vwn[:], vwnd)
    b.dma("sp", gts[:], gtd)
    b.act(gts[:], gts[:], AF.Sigmoid)
    pti = sbt("pti", [128, nb * 16], I32)
    idx = sbt("idx", [128, nb * 16], U32)
    b.dma("sp", pti[:], ptab.partition_broadcast(128))
    b.ts(idx[:], pti[:], 128.0, ALU.mult, pcol[:, 0:1], ALU.add)

    NG = 20
    gth = [sbt("gth%d" % i, [128, 512], F32) for i in range(NG)]
    wth = [sbt("wth%d" % i, [128, 256], F32) for i in range(2)]
    KcT4 = sbt("KcT4", [128, 4, 16, 128], BF16)
    VcT4 = sbt("VcT4", [128, 4, 16, 128], BF16)
    KsT = [sbt("KsTs%d" % i, [128, 2052], BF16) for i in range(4)]
    KwT = [sbt("KwTs%d" % i, [128, 516], BF16) for i in range(4)]
    Vs = [sbt("Vss%d" % i, [128, 17, 2, 65], BF16) for i in range(4)]
    Vw = [sbt("Vws%d" % i, [128, 5, 2, 65], BF16) for i in range(4)]
    KCTs4 = sbt("KCTs4", [128, 4, 128], BF16)
    VCOs4 = sbt("VCOs4", [128, 4, 2, 129], BF16)
    ns.tx4 = sbt("tx4", [128, 512], F32)
    ns.tu4 = sbt("tu4", [128, 512], F32)
    ns.gh4 = [sbt("gh4_%d" % i, [128, 512], BF16) for i in range(4)]
    tmpf = [sbt("tmpf%d" % i, [128, 32], F32) for i in range(3)]
    xn = sbt("xn", [32, 64], F32)
    s2 = sbt("s2s", [4, 64], F32)
    s3 = sbt("s3s", [4, 64], F32)
    m8 = sbt("m8s", [4, 16], F32)
    nmf = sbt("nmf", [4, 64], F32)
    nmT = sbt("nmTs", [128, 8, 4], BF16)
    rd = sbt("rds", [32, 8], F32)
    oac = [sbt("oacs%d" % i, [32, 64], F32) for i in range(2)]
    for i in range(4):
        b.memset(Vs[i][:], 1.0)
        b.memset(Vw[i][:], 1.0)
    b.memset(KCTs4[:], 0.0)
    b.memset(nmT[:], 0.0)
    b.memset(VCOs4[:], 0.0)
    b.memset(VCOs4[:, :, :, 64:65], 1.0)
    for sq in range(4):
        for n in range(2):
            b.dma("pool", VCOs4[:, sq, n, 65:129], ovsd)
    ns.cmp_bias()
    nt = [0]

    pend = []

    def step(kT, rows, q, mask, bias, v, pacc, first, last, ncols):
        ps = ns.psS[ns.nS % 2]
        ns.nS += 1
        b.mm(ps[0:rows, 0:32], kT, q, start=True, stop=(mask is None))
        if mask is not None:
            b.mm(ps[0:rows, 0:32], mask[0], mask[1], start=False, stop=True)
        tf = tmpf[nt[0] % 3]
        nt[0] += 1
        b.stt(tf[0:rows, :], ps[0:rows, 0:32], 0.125, bias, ALU.mult, ALU.add)
        pt = ns.PT[ns.nP % 3]
        ns.nP += 1
        b.act(pt[0:rows, 0:32], tf[0:rows, :], AF.Exp)
        flush()
        pend.append((pacc, ncols, pt, rows, v, first, last))

    def flush():
        while pend:
            pacc, ncols, pt, rows, v, first, last = pend.pop(0)
            b.mm(pacc[0:32, 0:ncols], pt[0:rows, 0:32], v, start=first, stop=last)

    def gather_seq(bi):
        k2 = bi % 4
        for pg in range(16):
            gt = gth[(bi * 16 + pg) % NG]
            col = bi * 16 + pg
            I = Ins("pool", (lambda e, gt=gt, col=col: e.indirect_dma_start(
                out=gt[:], out_offset=None, in_=cache,
                in_offset=bass.IndirectOffsetOnAxis(ap=idx[:, col:col + 1], axis=0))), True)
            I.idx = len(b.P.ins)
            b.P.ins.append(I)
            b.P._track(I, [idx[:, col:col + 1], cache], [gt[:]])
            ph = ns.psH[ns.nH % 2]
            ns.nH += 1
            for q3 in range(3):
                b.tr(ph[:, q3 * 128:(q3 + 1) * 128], gt[:, q3 * 128:(q3 + 1) * 128], ns.identf[:, :])
            cs = slice(pg * 128, (pg + 1) * 128)
            b.copy(KcT4[:, k2, :, pg * 8:(pg + 1) * 8], ph[:, 0:128].rearrange("p (c r) -> p r c", r=16), eng="act")
            b.copy(VcT4[:, k2, :, pg * 8:(pg + 1) * 8], ph[:, 128:256].rearrange("p (c r) -> p r c", r=16))
            b.copy(KsT[k2][:, cs], ph[:, 256:384], eng="act")
            b.copy(Vs[k2][:, pg, :, 0:64], gt[:, 384:512].rearrange("p (n d) -> p n d", n=2))
        b.copy(KsT[k2][:, 2048:2052], ksn[:, bi, :])
        b.copy(Vs[k2][0:4, 16, :, 0:64], vsn[0:4, bi, :].rearrange("p (n d) -> p n d", n=2))
        for wt in range(4):
            wtile = wth[wt % 2]
            b.dma("sp", wtile[:], cwin[bi, wt * 128:(wt + 1) * 128, :])
            ph = ns.psH[ns.nH % 2]
            ns.nH += 1
            b.tr(ph[:, 0:128], wtile[:, 0:128], ns.identf[:, :])
            b.copy(KwT[k2][:, wt * 128:(wt + 1) * 128], ph[:, 0:128], eng="act")
            b.copy(Vw[k2][:, wt, :, 0:64], wtile[:, 128:256].rearrange("p (n d) -> p n d", n=2))
        b.copy(KwT[k2][:, 512:516], kwn[:, bi, :])
        b.copy(Vw[k2][0:4, 4, :, 0:64], vwn[0:4, bi, :].rearrange("p (n d) -> p n d", n=2))

    for bi in range(nb):
        k2 = bi % 4
        if bi % 4 == 0:
            for bj in range(bi, bi + 4):
                gather_seq(bj)
            if SCUT == "A":
                continue
            ns.compress_group("k", KcT4, kdst3=KCTs4[:, :, 0:127])
            ns.compress_group("v", VcT4, vdst=[[VCOs4[0:127, sq, 0, 0:64], VCOs4[0:127, sq, 1, 0:64]] for sq in range(4)])
        if SCUT in ("A", "B"):
            continue
        KCTs = KCTs4[:, k2, :]
        VCOs = VCOs4[:, k2, :, :]
        for n in range(2):
            pb = slice(n * 64, (n + 1) * 64)
            q = qT[pb, bi, :]
            oa = oac[n]
            pacc = ns.psAcc[ns.nA % 2]
            ns.nA += 1
            step(KCTs4[pb, k2, :], 128, q, None, biasc[:, n, :], VCOs4[:, k2, n, :], pacc, True, True, 129)
            flush()
            b.ts(rd[:, 0:1], pacc[0:32, 64:65], 1e-30, ALU.max)
            b.recip(rd[:, 0:1], rd[:, 0:1])
            b.ts(xn[:], pacc[0:32, 65:129], rd[:, 0:1], ALU.mult)
            b.tt(rd[:, 1:2], rd[:, 0:1], gts[:, bi, n, 0:1], ALU.mult)
            b.ts(oa[:], pacc[0:32, 0:64], rd[:, 1:2], ALU.mult)
            if SCUT == "C":
                continue
            pk = ns.psK2
            b.mm(pk[0:4, 0:64], selm[:, :], xn[:, :])
            b.tt(s2[:], pk[0:4, 0:64], keeps[:], ALU.mult)
            b.tt(s2[:], s2[:], adds[:], ALU.add)
            b.P.op("dve", lambda e: e.max(out=m8[:, 0:8], in_=s2[:]), [s2[:]], [m8[:, 0:8]])
            b.P.op("dve", lambda e: e.match_replace(out=s3[:], in_to_replace=m8[:, 0:8], in_values=s2[:], imm_value=-1e30),
                   [s2[:], m8[:, 0:8]], [s3[:]])
            b.P.op("dve", lambda e: e.max(out=m8[:, 8:16], in_=s3[:]), [s3[:]], [m8[:, 8:16]])
            b.ts(s3[:], s2[:], m8[:, 15:16], ALU.is_ge)
            b.ts(nmf[:], s3[:], -NEG, ALU.mult, NEG, ALU.add)
            b.tr(pk[0:64, 64:68], nmf[:, :], ns.identf[0:4, 0:4])
            b.copy(nmT[0:64, :, :], pk[0:64, 64:68].unsqueeze(1).broadcast_to([64, 8, 4]))
            nmv = nmT[:].rearrange("p g t -> p (g t)")
            if SCUT == "D":
                continue
            pacc = ns.psAcc[ns.nA % 2]
            ns.nA += 1
            for tile in range(17):
                rows = 128 if tile < 16 else 4
                cs = slice(tile * 128, tile * 128 + rows)
                step(KsT[k2][pb, cs], rows, q, (Gs[:, cs], nmv), biass[0:rows, tile, n, :], Vs[k2][0:rows, tile, n, :],
                     pacc, tile == 0, tile == 16, 65)
            flush()
            b.ts(rd[:, 2:3], pacc[0:32, 64:65], 1e-30, ALU.max)
            b.recip(rd[:, 2:3], rd[:, 2:3])
            b.tt(rd[:, 3:4], rd[:, 2:3], gts[:, bi, n, 1:2], ALU.mult)
            b.stt(oa[:], pacc[0:32, 0:64], rd[:, 3:4], oa[:], ALU.mult, ALU.add)
            if SCUT == "E":
                continue
            pacc = ns.psAcc[ns.nA % 2]
            ns.nA += 1
            for tile in range(5):
                rows = 128 if tile < 4 else 4
                cs = slice(tile * 128, tile * 128 + rows)
                step(KwT[k2][pb, cs], rows, q, None, biasw[0:rows, tile, n, :], Vw[k2][0:rows, tile, n, :],
                     pacc, tile == 0, tile == 4, 65)
            flush()
            b.ts(rd[:, 4:5], pacc[0:32, 64:65], 1e-30, ALU.max)
            b.recip(rd[:, 4:5], rd[:, 4:5])
            b.tt(rd[:, 5:6], rd[:, 4:5], gts[:, bi, n, 2:3], ALU.mult)
            b.stt(oa[:], pacc[0:32, 0:64], rd[:, 5:6], oa[:], ALU.mult, ALU.add)
            b.dma("sp", onso[bi, n], oa[:])
    b.finish()
    return nc


def run_l2n_sample(inp, proj_s, nb=16):
    cache = inp["cache_kv"][0]
    n_pool = cache.shape[0]
    cache2 = np.ascontiguousarray(cache.reshape(n_pool * 128, 512))
    cst = nsa_sample_consts()
    cst.update(nsa_cmp_weights(inp))
    ps = proj_s.reshape(128, 4, NPROJ)
    nc = build_l2s(n_pool, nb)
    maps = []
    for c in range(NCORES):
        sb = ps[c * nb:(c + 1) * nb]
        m = dict(cst)
        m["cache"] = cache2
        m["ptab"] = np.ascontiguousarray(inp["page_table"][c * nb:(c + 1) * nb].reshape(1, nb * 16).astype(np.int32))
        m["cwin"] = np.ascontiguousarray(inp["cache_win"][0, c * nb:(c + 1) * nb].reshape(nb, 512, 256))
        q = sb[:, :, 4096:5120].reshape(nb, 4, 2, 8, 64)
        m["qTs"] = np.ascontiguousarray(q.transpose(2, 4, 0, 3, 1).reshape(128, nb, 32))
        kv = sb[:, :, 5120:5888].reshape(nb, 4, 6, 128)
        m["ksn"] = np.ascontiguousarray(kv[:, :, 2].transpose(2, 0, 1))
        m["kwn"] = np.ascontiguousarray(kv[:, :, 4].transpose(2, 0, 1))
        m["vsn"] = np.ascontiguousarray(kv[:, :, 3].transpose(1, 0, 2))
        m["vwn"] = np.ascontiguousarray(kv[:, :, 5].transpose(1, 0, 2))
        g = sb[:, :, 5888:5936].reshape(nb, 4, 2, 8, 3)
        m["gts"] = np.ascontiguousarray(g.transpose(3, 1, 0, 2, 4).reshape(32, nb, 2, 3))
        maps.append(m)
    res = run_bass_kernel_spmd(nc, maps, core_ids=list(range(NCORES)))
    outs = []
    for c in range(NCORES):
        r = res.results[c]["ons"].reshape(nb, 2, 8, 4, 64)
        outs.append(r.transpose(0, 3, 1, 2, 4).reshape(nb * 4, 1024))
    return np.concatenate(outs, 0)
```
